# Optimizing a Trainium2 kernel written in Bass

```python
import math
import numpy as np
import jax
import jax.numpy as jnp
from jax import lax

D_MODEL = 1024
BATCH = 1
SEQ = 16384
DEPTH = 4

GRID_W = 64
CTX_LEN = 256
EPS = 1e-6
N_MOD = 9
D_FF = 2816
H_RET = 4
DK_RET = 128
DV_RET = 256
CHUNK_RET = 128
ROPE_BASE = 10000.0
H_GLA = 4
DK_GLA = 128
DV_GLA = 256
GLA_RANK = 16
GLA_NORMALIZER = 16.0
CHUNK_GLA = 64
D_INNER = 2 * D_MODEL
SSD_HEADDIM = 64
H_SSD = D_INNER // SSD_HEADDIM
SSD_GROUPS = 4
D_STATE = 128
CONV_K = 5
CHUNK_SSD = 128
D_XBC = D_INNER + 2 * SSD_GROUPS * D_STATE

IN_LAYOUT = (
    ('ret_q', H_RET * DK_RET), ('ret_k', H_RET * DK_RET),
    ('ret_v', H_RET * DV_RET), ('ret_g', H_RET * DV_RET),
    ('gla_q', H_GLA * DK_GLA), ('gla_k', H_GLA * DK_GLA),
    ('gla_v', H_GLA * DV_GLA), ('gla_r', H_GLA * DV_GLA),
    ('gla_af', GLA_RANK), ('gla_ab', GLA_RANK),
    ('ssd_z', D_INNER), ('ssd_xbc', D_XBC), ('ssd_dtf', H_SSD), ('ssd_dtb', H_SSD),
    ('merge', 3 * D_MODEL),
)
D_IN_PROJ = sum(size for _, size in IN_LAYOUT)

kernel_name = 'hybrid_retention_gla_ssd_prefix_dit'


def rmsnorm(t, g=None):
    tf = t.astype(jnp.float32)
    y = tf * lax.rsqrt(jnp.mean(tf * tf, axis=-1, keepdims=True) + EPS)
    if g is not None:
        y = y * g.astype(jnp.float32)
    return y.astype(t.dtype)


def adaln(cond, w, b):
    m = jax.nn.silu(cond) @ w + b
    return m.reshape(cond.shape[0], N_MOD, 1, D_MODEL)


def pre(t, g, shift, scale):
    return rmsnorm(t, g) * (1 + scale) + shift


def swiglu(h, wi, wo):
    a, u = jnp.split(h @ wi, 2, axis=-1)
    return (jax.nn.silu(a) * u) @ wo


def split_cols(t):
    out, start = {}, 0
    for name, size in IN_LAYOUT:
        out[name] = t[..., start:start + size]
        start += size
    return out


def flip(t):
    return jnp.flip(t, axis=1)


def rope_2d(t):
    B, L, H, dk = t.shape
    rows = L // GRID_W
    row = jnp.broadcast_to(jnp.arange(rows, dtype=jnp.float32)[:, None], (rows, GRID_W)).reshape(L)
    col = jnp.broadcast_to(jnp.arange(GRID_W, dtype=jnp.float32)[None, :], (rows, GRID_W)).reshape(L)
    nf = dk // 4
    freq = ROPE_BASE ** (-jnp.arange(nf, dtype=jnp.float32) / nf)
    ang = jnp.concatenate([row[:, None] * freq, col[:, None] * freq], axis=-1)[None, :, None, :]
    cos, sin = jnp.cos(ang), jnp.sin(ang)
    t1 = t[..., :dk // 2].astype(jnp.float32)
    t2 = t[..., dk // 2:].astype(jnp.float32)
    return jnp.concatenate([t1 * cos - t2 * sin, t1 * sin + t2 * cos], axis=-1).astype(t.dtype)


def dwconv(t, w, b):
    y = lax.conv_general_dilated(
        t, w[:, None, :].astype(t.dtype), window_strides=(1,),
        padding=[(CONV_K // 2, CONV_K // 2)], dimension_numbers=('NWC', 'WIO', 'NWC'),
        feature_group_count=t.shape[-1])
    return y + b


def chunk_scan(q, k, v, a, s0, chunk):
    B, L, H, dv = v.shape
    rep = H // q.shape[2]
    n = L // chunk

    def blocks(t):
        return t.astype(jnp.float32).reshape(B, n, chunk, t.shape[2], t.shape[3]).transpose(1, 0, 3, 2, 4)

    lower = jnp.tril(jnp.ones((chunk, chunk), dtype=bool))[:, :, None]

    def step(s, inp):
        qc, kc, vc, ac = inp
        if rep > 1:
            qc = jnp.repeat(qc, rep, axis=1)
            kc = jnp.repeat(kc, rep, axis=1)
        g = jnp.cumsum(ac, axis=2)
        decay = jnp.exp(jnp.where(lower, g[:, :, :, None, :] - g[:, :, None, :, :], -jnp.inf))
        if ac.shape[-1] == 1:
            att = jnp.einsum('bhid,bhjd->bhij', qc, kc) * decay[..., 0]
        else:
            att = jnp.einsum('bhid,bhjd,bhijd->bhij', qc, kc, decay)
        y = jnp.einsum('bhcd,bhdv->bhcv', qc * jnp.exp(g), s) + jnp.einsum('bhij,bhjv->bhiv', att, vc)
        g_end = g[:, :, -1:, :]
        s = jnp.exp(g_end[:, :, 0, :, None]) * s + jnp.einsum('bhcd,bhcv->bhdv', kc * jnp.exp(g_end - g), vc)
        return s, y

    s, y = lax.scan(step, s0, (blocks(q), blocks(k), blocks(v), blocks(a)))
    return y.transpose(1, 0, 3, 2, 4).reshape(B, L, H, dv).astype(v.dtype), s


def scan_state(k, v, a):
    rep = v.shape[2] // k.shape[2]
    kf = jnp.repeat(k.astype(jnp.float32), rep, axis=2)
    g = jnp.cumsum(a.astype(jnp.float32), axis=1)
    return jnp.einsum('blhd,blhv->bhdv', kf * jnp.exp(g[:, -1:] - g), v.astype(jnp.float32))


def bidir_scan(q, k, v_f, v_b, a_f, a_b, s0_f, s0_b, chunk):
    y_f, s_f = chunk_scan(q, k, v_f, a_f, s0_f, chunk)
    y_b, s_b = chunk_scan(flip(q), flip(k), flip(v_b), flip(a_b), s0_b, chunk)
    return y_f + flip(y_b), s_f, s_b


def prefix_bidir_scan(ctx_in, lat_in, chunk, need_ctx):
    q_c, k_c, vf_c, vb_c, af_c, ab_c = ctx_in
    if need_ctx:
        B, _, H, dv = vf_c.shape
        zeros = jnp.zeros((B, H, k_c.shape[-1], dv), jnp.float32)
        y_ctx, s_f, s_b = bidir_scan(q_c, k_c, vf_c, vb_c, af_c, ab_c, zeros, zeros, chunk)
    else:
        y_ctx = None
        s_f = scan_state(k_c, vf_c, af_c)
        s_b = scan_state(flip(k_c), flip(vb_c), flip(ab_c))
    y_lat, _, _ = bidir_scan(*lat_in, s_f, s_b, chunk)
    return y_ctx, y_lat


def retention_inputs(p, logit, positional):
    B, L, _ = p['ret_q'].shape
    q = p['ret_q'].reshape(B, L, H_RET, DK_RET)
    k = p['ret_k'].reshape(B, L, H_RET, DK_RET) * (DK_RET ** -0.5)
    v = p['ret_v'].reshape(B, L, H_RET, DV_RET)
    if positional:
        q, k = rope_2d(q), rope_2d(k)
    log_gamma = jax.nn.log_sigmoid(logit.astype(jnp.float32))
    a_f = jnp.broadcast_to(log_gamma[0][:, None], (B, L, H_RET, 1))
    a_b = jnp.broadcast_to(log_gamma[1][:, None], (B, L, H_RET, 1))
    return (q, k, v, v, a_f, a_b)


def retention_out(y, gate):
    B, L = y.shape[:2]
    return rmsnorm(y).reshape(B, L, H_RET * DV_RET) * jax.nn.silu(gate)


def gla_inputs(p, wa2, ba):
    B, L, _ = p['gla_q'].shape
    q = p['gla_q'].reshape(B, L, H_GLA, DK_GLA) * (DK_GLA ** -0.5)
    k = p['gla_k'].reshape(B, L, H_GLA, DK_GLA)
    v = p['gla_v'].reshape(B, L, H_GLA, DV_GLA)

    def log_gate(code, w, b):
        z = (code @ w + b).astype(jnp.float32)
        return (jax.nn.log_sigmoid(z) / GLA_NORMALIZER).reshape(B, L, H_GLA, DK_GLA)

    return (q, k, v, v, log_gate(p['gla_af'], wa2[0], ba[0]), log_gate(p['gla_ab'], wa2[1], ba[1]))


def gla_out(y, r, g):
    B, L = y.shape[:2]
    return rmsnorm(y, g).reshape(B, L, H_GLA * DV_GLA) * jax.nn.silu(r)


def ssd_inputs(p, conv_w, conv_b, dt_bias, a_log):
    B, L, _ = p['ssd_z'].shape
    xbc = jax.nn.silu(dwconv(p['ssd_xbc'], conv_w, conv_b))
    nbc = SSD_GROUPS * D_STATE
    xs = xbc[..., :D_INNER].reshape(B, L, H_SSD, SSD_HEADDIM)
    bm = xbc[..., D_INNER:D_INNER + nbc].reshape(B, L, SSD_GROUPS, D_STATE)
    cm = xbc[..., D_INNER + nbc:].reshape(B, L, SSD_GROUPS, D_STATE)
    a = -jnp.exp(a_log.astype(jnp.float32))

    def direction(dt_raw, i):
        dt = jax.nn.softplus(dt_raw.astype(jnp.float32) + dt_bias[i].astype(jnp.float32))
        return (xs * dt[..., None]).astype(xs.dtype), (dt * a[i])[..., None]

    v_f, a_f = direction(p['ssd_dtf'], 0)
    v_b, a_b = direction(p['ssd_dtb'], 1)
    return (cm, bm, v_f, v_b, a_f, a_b), xs


def ssd_out(y, xs, z, d, g):
    B, L = y.shape[:2]
    y = (y + d[:, None] * xs).reshape(B, L, D_INNER) * jax.nn.silu(z)
    y = rmsnorm(y.reshape(B, L, SSD_GROUPS, D_INNER // SSD_GROUPS)).reshape(B, L, D_INNER)
    return y * g


def token_mixing(h_ctx, h_lat, w_in, ret_logit, gla_wa2, gla_ba, gla_norm_g, conv_w, conv_b,
                 dt_bias, a_log, ssd_d, ssd_norm_g, wb_ret, wb_gla, wb_ssd, w_out, need_ctx):
    pc = split_cols(h_ctx @ w_in)
    pl = split_cols(h_lat @ w_in)
    yr_c, yr_l = prefix_bidir_scan(retention_inputs(pc, ret_logit, False),
                                   retention_inputs(pl, ret_logit, True), CHUNK_RET, need_ctx)
    yg_c, yg_l = prefix_bidir_scan(gla_inputs(pc, gla_wa2, gla_ba),
                                   gla_inputs(pl, gla_wa2, gla_ba), CHUNK_GLA, need_ctx)
    sc_in, xs_c = ssd_inputs(pc, conv_w, conv_b, dt_bias, a_log)
    sl_in, xs_l = ssd_inputs(pl, conv_w, conv_b, dt_bias, a_log)
    ys_c, ys_l = prefix_bidir_scan(sc_in, sl_in, CHUNK_SSD, need_ctx)

    def merge(p, yr, yg, ys, xs):
        b_ret = retention_out(yr, p['ret_g']) @ wb_ret
        b_gla = gla_out(yg, p['gla_r'], gla_norm_g) @ wb_gla
        b_ssd = ssd_out(ys, xs, p['ssd_z'], ssd_d, ssd_norm_g) @ wb_ssd
        g_ret, g_gla, g_ssd = jnp.split(jax.nn.sigmoid(p['merge']), 3, axis=-1)
        return (g_ret * b_ret + g_gla * b_gla + g_ssd * b_ssd) @ w_out

    out_lat = merge(pl, yr_l, yg_l, ys_l, xs_l)
    out_ctx = merge(pc, yr_c, yg_c, ys_c, xs_c) if need_ctx else None
    return out_ctx, out_lat


def setup_inputs(seed: int = 0) -> dict:
    key = jax.random.key(seed)
    ks = iter(jax.random.split(key, 32))

    def nrm(shape, scale):
        return jax.random.normal(next(ks), shape, jnp.float32) * scale

    x = nrm((BATCH, SEQ, D_MODEL), 1.0)
    c = nrm((BATCH, D_MODEL), 1.0)
    ctx = nrm((BATCH, CTX_LEN, D_MODEL), 1.0)
    c_ctx = nrm((D_MODEL,), 1.0)
    ada_w = nrm((DEPTH, D_MODEL, N_MOD * D_MODEL), 0.5 * D_MODEL ** -0.5)
    ada_b = nrm((DEPTH, N_MOD * D_MODEL), 0.02)
    norm_g = 1.0 + nrm((DEPTH, 3, D_MODEL), 0.02)
    final_norm_g = 1.0 + nrm((D_MODEL,), 0.02)
    ffn1_wi = nrm((DEPTH, D_MODEL, 2 * D_FF), D_MODEL ** -0.5)
    ffn1_wo = nrm((DEPTH, D_FF, D_MODEL), D_FF ** -0.5)
    ffn2_wi = nrm((DEPTH, D_MODEL, 2 * D_FF), D_MODEL ** -0.5)
    ffn2_wo = nrm((DEPTH, D_FF, D_MODEL), D_FF ** -0.5)
    w_in = nrm((DEPTH, D_MODEL, D_IN_PROJ), D_MODEL ** -0.5)
    ret_base = jnp.log(2.0 ** (5.0 + jnp.arange(H_RET, dtype=jnp.float32)) - 1.0)
    ret_logit = ret_base + nrm((DEPTH, 2, H_RET), 0.1)
    gla_wa2 = nrm((DEPTH, 2, GLA_RANK, H_GLA * DK_GLA), GLA_RANK ** -0.5)
    gla_ba = nrm((DEPTH, 2, H_GLA * DK_GLA), 0.1)
    gla_norm_g = 1.0 + nrm((DEPTH, DV_GLA), 0.02)
    conv_w = nrm((DEPTH, CONV_K, D_XBC), CONV_K ** -0.5)
    conv_b = nrm((DEPTH, D_XBC), 0.02)
    dt = jnp.exp(jax.random.uniform(next(ks), (DEPTH, 2, H_SSD), jnp.float32, math.log(1e-3), math.log(1e-1)))
    dt_bias = dt + jnp.log(-jnp.expm1(-dt))
    a_log = jnp.log(jax.random.uniform(next(ks), (DEPTH, 2, H_SSD), jnp.float32, 1.0, 16.0))
    ssd_d = 1.0 + nrm((DEPTH, H_SSD), 0.02)
    ssd_norm_g = 1.0 + nrm((DEPTH, D_INNER), 0.02)
    wb_ret = nrm((DEPTH, H_RET * DV_RET, D_MODEL), (H_RET * DV_RET) ** -0.5)
    wb_gla = nrm((DEPTH, H_GLA * DV_GLA, D_MODEL), (H_GLA * DV_GLA) ** -0.5)
    wb_ssd = nrm((DEPTH, D_INNER, D_MODEL), D_INNER ** -0.5)
    w_out = nrm((DEPTH, D_MODEL, D_MODEL), D_MODEL ** -0.5)
    return {'x': x, 'c': c, 'ctx': ctx, 'c_ctx': c_ctx, 'ada_w': ada_w, 'ada_b': ada_b,
            'norm_g': norm_g, 'final_norm_g': final_norm_g, 'ffn1_wi': ffn1_wi, 'ffn1_wo': ffn1_wo,
            'ffn2_wi': ffn2_wi, 'ffn2_wo': ffn2_wo, 'w_in': w_in, 'ret_logit': ret_logit,
            'gla_wa2': gla_wa2, 'gla_ba': gla_ba, 'gla_norm_g': gla_norm_g, 'conv_w': conv_w,
            'conv_b': conv_b, 'dt_bias': dt_bias, 'a_log': a_log, 'ssd_d': ssd_d,
            'ssd_norm_g': ssd_norm_g, 'wb_ret': wb_ret, 'wb_gla': wb_gla, 'wb_ssd': wb_ssd,
            'w_out': w_out}


def reference(x, c, ctx, c_ctx, ada_w, ada_b, norm_g, final_norm_g, ffn1_wi, ffn1_wo, ffn2_wi,
              ffn2_wo, w_in, ret_logit, gla_wa2, gla_ba, gla_norm_g, conv_w, conv_b, dt_bias,
              a_log, ssd_d, ssd_norm_g, wb_ret, wb_gla, wb_ssd, w_out):
    for l in range(DEPTH):
        last = l == DEPTH - 1
        m = adaln(c, ada_w[l], ada_b[l])
        mc = adaln(c_ctx[None], ada_w[l], ada_b[l])
        x = x + 0.5 * m[:, 2] * swiglu(pre(x, norm_g[l, 0], m[:, 0], m[:, 1]), ffn1_wi[l], ffn1_wo[l])
        ctx = ctx + 0.5 * mc[:, 2] * swiglu(pre(ctx, norm_g[l, 0], mc[:, 0], mc[:, 1]), ffn1_wi[l], ffn1_wo[l])
        mix_ctx, mix_lat = token_mixing(
            pre(ctx, norm_g[l, 1], mc[:, 3], mc[:, 4]), pre(x, norm_g[l, 1], m[:, 3], m[:, 4]),
            w_in[l], ret_logit[l], gla_wa2[l], gla_ba[l], gla_norm_g[l], conv_w[l], conv_b[l],
            dt_bias[l], a_log[l], ssd_d[l], ssd_norm_g[l], wb_ret[l], wb_gla[l], wb_ssd[l], w_out[l],
            need_ctx=not last)
        x = x + m[:, 5] * mix_lat
        x = x + 0.5 * m[:, 8] * swiglu(pre(x, norm_g[l, 2], m[:, 6], m[:, 7]), ffn2_wi[l], ffn2_wo[l])
        if not last:
            ctx = ctx + mc[:, 5] * mix_ctx
            ctx = ctx + 0.5 * mc[:, 8] * swiglu(pre(ctx, norm_g[l, 2], mc[:, 6], mc[:, 7]), ffn2_wi[l], ffn2_wo[l])
    return rmsnorm(x, final_norm_g)
```

```python
import contextlib
import numpy as np
import concourse.bass as bass
import concourse.mybir as mybir

F32 = mybir.dt.float32
BF16 = mybir.dt.bfloat16
AF = mybir.ActivationFunctionType
ALU = mybir.AluOpType
KDMA = 6


class View:
    __slots__ = ("buf", "ap")

    def __init__(self, buf, ap):
        self.buf = buf
        self.ap = ap

    def __getitem__(self, idx):
        return View(self.buf, self.ap[idx])

    def m(self, f):
        return View(self.buf, f(self.ap))


class Buf:
    __slots__ = ("ap", "w", "r", "name")

    def __init__(self, ap, name=""):
        self.ap = ap
        self.w = None
        self.r = {}
        self.name = name

    def __getitem__(self, idx):
        return View(self, self.ap[idx])

    def v(self):
        return View(self, self.ap)


class Prog:
    ENGS = ("pe", "act", "dve", "pool", "sp")

    def __init__(self, nc, stack):
        self.nc = nc
        self.stack = stack
        self.q = {e: [] for e in self.ENGS}
        self.cnt = {e: 0 for e in self.ENGS}
        self.dman = {e: 0 for e in self.ENGS}
        self.waited = {e: {} for e in self.ENGS}
        self.sems = {}
        for e in self.ENGS:
            self.sems[("eng", e)] = stack.enter_context(nc.semaphore("s_" + e))
        for e in ("sp", "pool", "act"):
            for i in range(KDMA):
                self.sems[("dma", e, i)] = stack.enter_context(nc.semaphore(f"d_{e}{i}"))
        self.nalloc = 0
        self.psn = 0

    def sb(self, shape, dtype=F32, name=None):
        self.nalloc += 1
        t = self.stack.enter_context(self.nc.sbuf_tensor(f"{name or 't'}_{self.nalloc}", list(shape), dtype))
        return Buf(t[:] if hasattr(t, "__getitem__") else t, name or "t")

    def sb_ctx(self, shape, dtype=F32, name=None):
        self.nalloc += 1
        return self.nc.sbuf_tensor(f"{name or 't'}_{self.nalloc}", list(shape), dtype)

    def psum_banks(self):
        self.ps = []
        for i in range(8):
            t = self.stack.enter_context(self.nc.psum_tensor(f"ps{i}", [128, 512], F32))
            self.ps.append(Buf(t[:], f"ps{i}"))

    def next_ps(self):
        b = self.ps[self.psn % 8]
        self.psn += 1
        return b

    def dram(self, name, shape, dtype=F32, kind="Internal"):
        t = self.nc.dram_tensor(name, list(shape), dtype, kind=kind)
        return Buf(t.ap(), name)

    def op(self, E, fn, reads, writes, dma=False):
        deps = {}

        def add(tok):
            sk, val, eng = tok
            if deps.get(sk, (0, None))[0] < val:
                deps[sk] = (val, eng)

        for b in reads:
            if b.w is not None:
                add(b.w)
        for b in writes:
            if b.w is not None:
                add(b.w)
            for sk, (val, eng) in b.r.items():
                add((sk, val, eng))
        waits = []
        for sk, (val, eng) in deps.items():
            if eng == "pe" and E == "pe" and not dma:
                continue
            if self.waited[E].get(sk, 0) >= val:
                continue
            self.waited[E][sk] = val
            waits.append((sk, val))
        if dma:
            n = self.dman[E]
            self.dman[E] += 1
            sk = ("dma", E, n % KDMA)
            val = 16 * (n // KDMA + 1)
            if val > 16 and self.waited[E].get(sk, 0) < val - 16:
                waits.append((sk, val - 16))
                self.waited[E][sk] = val - 16
            tok = (sk, val, "dma")
            inc = (sk, 16)
        else:
            self.cnt[E] += 1
            sk = ("eng", E)
            tok = (sk, self.cnt[E], E)
            inc = (sk, 1)
        self.q[E].append((waits, fn, inc))
        wset = set(id(b) for b in writes)
        for b in writes:
            b.w = tok
            b.r = {}
        for b in reads:
            if id(b) not in wset:
                b.r[tok[0]] = (tok[1], tok[2])
        return tok

    def barrier(self):
        cur = []
        for e in self.ENGS:
            if self.cnt[e] > 0:
                cur.append((("eng", e), self.cnt[e]))
        for e in ("sp", "pool", "act"):
            n = self.dman[e]
            for i in range(KDMA):
                k = (n - 1 - i)
                if k >= 0:
                    cur.append((("dma", e, k % KDMA), 16 * (k // KDMA + 1)))
        for E in self.ENGS:
            waits = []
            for sk, val in cur:
                if sk == ("eng", E):
                    continue
                if self.waited[E].get(sk, 0) >= val:
                    continue
                self.waited[E][sk] = val
                waits.append((sk, val))
            if waits:
                self.q[E].append((waits, None, None))

    def emit(self):
        nc = self.nc
        self.barrier()
        with nc.Block() as block:
            def replay(E, e):
                for waits, fn, inc in self.q[E]:
                    for sk, val in waits:
                        e.wait_ge(self.sems[sk], val)
                    if fn is not None:
                        ins = fn(e)
                        ins.then_inc(self.sems[inc[0]], inc[1])

            @block.tensor
            def _(e):
                replay("pe", e)

            @block.scalar
            def _(e):
                replay("act", e)

            @block.vector
            def _(e):
                replay("dve", e)

            @block.gpsimd
            def _(e):
                replay("pool", e)

            @block.sync
            def _(e):
                replay("sp", e)

    @staticmethod
    def _b(*xs):
        return [x.buf for x in xs if isinstance(x, View)]

    @staticmethod
    def _a(x):
        return x.ap if isinstance(x, View) else x

    def mm(self, out, lhsT, rhs, start=True, stop=True):
        o, l, r = out.ap, lhsT.ap, rhs.ap
        return self.op("pe", lambda e: e.matmul(o, l, r, start=start, stop=stop), self._b(lhsT, rhs), self._b(out))

    def tr(self, out, in_, ident):
        o, i, d = out.ap, in_.ap, ident.ap
        return self.op("pe", lambda e: e.transpose(o, i, d), self._b(in_, ident), self._b(out))

    def act(self, out, in_, func, bias=0.0, scale=1.0, accum=None):
        o, i, b, s = out.ap, in_.ap, self._a(bias), self._a(scale)
        ac = self._a(accum) if accum is not None else None
        kw = {}
        if ac is not None:
            kw["accum_out"] = ac
        return self.op("act", lambda e: e.activation(o, i, func, bias=b, scale=s, **kw),
                       self._b(in_, bias, scale), self._b(out) + (self._b(accum) if accum is not None else []))

    def tt(self, out, a, b, op, eng="dve"):
        o, x, y = out.ap, a.ap, b.ap
        return self.op(eng, lambda e: e.tensor_tensor(o, x, y, op), self._b(a, b), self._b(out))

    def ts(self, out, a, s1, s2, op0, op1=None, eng="dve", accum=None):
        o, x, p1, p2 = out.ap, a.ap, self._a(s1), self._a(s2)
        kw = {}
        if op1 is not None:
            kw["op1"] = op1
        if accum is not None:
            kw["accum_out"] = accum.ap
        return self.op(eng, lambda e: e.tensor_scalar(o, x, p1, p2, op0, **kw), self._b(a, s1, s2),
                       self._b(out) + (self._b(accum) if accum is not None else []))

    def stt(self, out, a, s, b, op0, op1, eng="dve"):
        o, x, p, y = out.ap, a.ap, self._a(s), b.ap
        return self.op(eng, lambda e: e.scalar_tensor_tensor(o, x, p, y, op0, op1), self._b(a, s, b), self._b(out))

    def copy(self, out, a, eng="dve"):
        o, x = out.ap, a.ap
        if eng == "act":
            return self.op("act", lambda e: e.copy(o, x), self._b(a), self._b(out))
        return self.op(eng, lambda e: e.tensor_copy(o, x), self._b(a), self._b(out))

    def memset(self, out, val, eng="dve"):
        o = out.ap
        return self.op(eng, lambda e: e.memset(o, val), [], self._b(out))

    def scan(self, out, d0, d1, init, op0, op1):
        o, a, b, i = out.ap, d0.ap, d1.ap, self._a(init)
        return self.op("dve", lambda e: e.tensor_tensor_scan(o, a, b, i, op0, op1), self._b(d0, d1, init), self._b(out))

    def dma(self, out, in_, q="sp"):
        o, i = out.ap, in_.ap
        return self.op(q, lambda e: e.dma_start(out=o, in_=i), self._b(in_), self._b(out), dma=True)


@contextlib.contextmanager
def scope(P):
    old = P.stack
    with contextlib.ExitStack() as st:
        P.stack = st
        try:
            yield
        finally:
            P.barrier()
            P.stack = old


D = 1024
KC = 8
DFF = 2816
NFF = 22
EPS = 1e-6


def fm_vec(v):
    v = np.asarray(v, np.float32)
    return np.ascontiguousarray(v.reshape(-1, 128).T)


class Blocks:
    def __init__(self, P, sizes, kinds, name="xT", dtype=F32):
        self.sizes = sizes
        self.kinds = kinds
        self.offs = np.concatenate([[0], np.cumsum(sizes)]).astype(int).tolist()
        self.T = self.offs[-1]
        self.t = P.stack.enter_context(P.nc.sbuf_tensor(name, [128, KC, self.T], dtype))
        self.b = [Buf(self.t[:, :, self.offs[i]:self.offs[i] + n], f"{name}{i}") for i, n in enumerate(sizes)]


def prep_mod(P, modT, gT, idx):
    out = {}
    for kind, mt in modT.items():
        gs = P.sb([128, KC], F32, "gs")
        hg = P.sb([128, KC], F32, "hg")
        P.stt(gs.v(), mt[:, 3 * idx + 1, :], 1.0, gT[:, idx, :], ALU.add, ALU.mult)
        P.ts(hg.v(), mt[:, 3 * idx + 2, :], 0.5 if idx != 1 else 1.0, None, ALU.mult)
        out[kind] = (gs, mt[:, 3 * idx + 0, :], hg)
    return out


def norm_mod(P, xb, hb, n, gs, sh, ones_bf, scr):
    sq, tmp, rstd = scr
    P.act(sq[:, :, :n], xb.v(), AF.Square)
    ps = P.next_ps()
    for k in range(KC):
        P.mm(ps[:, :n], ones_bf.v(), sq[:, k, :n], start=(k == 0), stop=(k == KC - 1))
    P.ts(rstd[:, :n], ps[:, :n], 1.0 / D, EPS, ALU.mult, ALU.add)
    P.act(rstd[:, :n], rstd[:, :n], AF.Ln)
    P.act(rstd[:, :n], rstd[:, :n], AF.Exp, scale=-0.5)
    for k in range(KC):
        P.stt(tmp[:, k, :n], xb[:, k, :], gs[:, k:k + 1], rstd[:, :n], ALU.mult, ALU.mult)
        P.act(hb[:, k, :], tmp[:, k, :n], AF.Identity, bias=sh[:, k:k + 1], scale=1.0)


def ffn(P, X, H, mods, wi, wo, ones_bf, scr, G=6):
    nblk = len(X.sizes)
    T = X.T
    with P.sb_ctx([128, G, T], BF16, "actT") as actT_t, \
            P.sb_ctx([128, 2, KC, 256], BF16, "wi_t") as wi_t, \
            P.sb_ctx([128, G, D], BF16, "wo_t") as wo_t, \
            P.sb_ctx([128, 2, 512], F32, "sil") as sil_t:
        actb = [[Buf(actT_t[:, g, X.offs[b]:X.offs[b] + X.sizes[b]]) for b in range(nblk)] for g in range(G)]
        wib = [Buf(wi_t[:, i]) for i in range(2)]
        wob = Buf(wo_t[:])
        silb = [Buf(sil_t[:, i]) for i in range(2)]
        wiv = wi.ap.rearrange("(k p) c -> p k c", p=128)
        wov = wo.ap.rearrange("(j p) f -> p j f", p=128)
        ns = 0
        nw = 0
        for g0 in range(0, NFF, G):
            gn = min(G, NFF - g0)
            for gi in range(gn):
                j = g0 + gi
                w = wib[nw % 2]
                nw += 1
                P.dma(w[:, :, 0:128], View(wi, wiv[:, :, j * 128:(j + 1) * 128]), q="pool")
                P.dma(w[:, :, 128:256], View(wi, wiv[:, :, DFF + j * 128:DFF + (j + 1) * 128]), q="pool")
                for b in range(nblk):
                    n = X.sizes[b]
                    pa = P.next_ps()
                    pu = P.next_ps()
                    for k in range(KC):
                        P.mm(pa[:, :n], w[:, k, 0:128], H.b[b][:, k, :], start=(k == 0), stop=(k == KC - 1))
                    for k in range(KC):
                        P.mm(pu[:, :n], w[:, k, 128:256], H.b[b][:, k, :], start=(k == 0), stop=(k == KC - 1))
                    s = silb[ns % 2]
                    ns += 1
                    P.act(s[:, :n], pa[:, :n], AF.Silu)
                    P.tt(actb[gi][b].v(), s[:, :n], pu[:, :n], ALU.mult)
            P.dma(wob[:, 0:gn, :], View(wo, wov[:, g0:g0 + gn, :]), q="pool")
            for b in range(nblk):
                n = X.sizes[b]
                hg = mods[X.kinds[b]][2]
                for f in range(KC):
                    ps = P.next_ps()
                    for gi in range(gn):
                        P.mm(ps[:, :n], wob[:, gi, f * 128:(f + 1) * 128], actb[gi][b].v(), start=(gi == 0), stop=(gi == gn - 1))
                    P.stt(X.b[b][:, f, :], ps[:, :n], hg[:, f:f + 1], X.b[b][:, f, :], ALU.mult, ALU.add)
        P.barrier()


TL = 2048
TC = 256
NCH_L = TL // 128
NCH_C = TC // 128
NCH = NCH_L + NCH_C
TT = TL + TC
DIN = 14432
COLS = {}
_o = 0
for _n, _s in (("ret_q", 512), ("ret_k", 512), ("ret_v", 1024), ("ret_g", 1024), ("gla_q", 512), ("gla_k", 512),
               ("gla_v", 1024), ("gla_r", 1024), ("gla_af", 16), ("gla_ab", 16), ("ssd_z", 2048), ("ssd_xbc", 3072),
               ("ssd_dt", 64), ("merge", 3072)):
    COLS[_n] = (_o, _s)
    _o += _s
assert _o == DIN
CI = {"ident": 0, "perm": 1, "mle": 2, "mge": 3, "sgt": 4, "slt": 5}


def host_consts():
    p = np.arange(128)[:, None]
    c = np.arange(128)[None, :]
    mats = [p == c, c == (p + 64) % 128, p <= c, p >= c, p > c, p < c]
    m = np.concatenate([x.astype(np.float32) for x in mats], axis=1)
    rows = np.concatenate([np.broadcast_to(np.arange(1, 129, dtype=np.float32), (128, 128)),
                           np.broadcast_to(np.arange(128, 0, -1).astype(np.float32), (128, 128))], axis=1)
    return np.ascontiguousarray(np.concatenate([m, rows], axis=1))


class Env:
    pass


def load_consts(P, env, cin):
    cf = P.sb([128, 8 * 128], F32, "cf")
    P.dma(cf.v(), cin.v())
    cb = P.sb([128, 6 * 128], BF16, "cb")
    P.copy(cb.v(), cf[:, 0:768])
    env.cf, env.cb = cf, cb
    env.ones_bf = P.sb([128, 128], BF16, "ones")
    P.memset(env.ones_bf.v(), 1.0)
    env.ones_f = P.sb([128, 128], F32, "onesf")
    P.memset(env.ones_f.v(), 1.0)

    def cm(name, bf=True):
        i = CI[name]
        return (cb if bf else cf)[:, i * 128:(i + 1) * 128]
    env.cm = cm
    env.row_up = cf[:, 768:896]
    env.row_dn = cf[:, 896:1024]


def softplus_parts(P, z, n, tmp1, tmp2):
    P.act(tmp1, z, AF.Abs)
    P.act(tmp1, tmp1, AF.Exp, scale=-1.0)
    P.act(tmp1, tmp1, AF.Ln, bias=1.0)
    P.ts(tmp2, z, 0.0, None, ALU.max)
    return tmp2, tmp1


def inproj(P, env, H, W, sp, lay, groups, do_q):
    nblk = len(H.sizes)
    tokblocks = [b for b in range(nblk) if H.kinds[b] != "halo"]
    halo = [b for b in range(nblk) if H.kinds[b] == "halo"][0]
    spoff = {}
    o = 0
    for b in tokblocks:
        spoff[b] = o
        o += H.sizes[b]
    wt_t = P.stack.enter_context(P.sb_ctx([128, 2, KC, 512], BF16, "win_t"))
    wt = [Buf(wt_t[:, i]) for i in range(2)]
    st = {"nw": 0}

    def loadw(c0, n):
        w = wt[st["nw"] % 2]
        st["nw"] += 1
        P.dma(w[:, :, 0:n], W[:, :, c0:c0 + n], q="pool")
        return w

    def fm_mm(w, wc0, m, b, ps=None):
        n = H.sizes[b]
        ps = ps or P.next_ps()
        for k in range(KC):
            P.mm(ps[0:m, :n], w[:, k, wc0:wc0 + m], H.b[b][:, k, :], start=(k == 0), stop=(k == KC - 1))
        return ps

    def do_qk(mixer):
        mi = 0 if mixer == "ret" else 1
        names = (["q"] if do_q else []) + ["k"]
        with scope(P):
            ec = P.stack.enter_context
            qf_t = ec(P.sb_ctx([128, 2, 512], F32, "qk_f"))
            qb_t = ec(P.sb_ctx([128, 2, 512], BF16, "qk_b"))
            qo_t = ec(P.sb_ctx([128, 4, 512], BF16, "qk_o"))
            g_t = ec(P.sb_ctx([128, 6, 512], F32, "g_t"))
            rope_t = ec(P.sb_ctx([128, 2, TL if mixer == "ret" else 2], F32, "rope"))
            code_t = ec(P.sb_ctx([16, 2, 512], BF16, "code"))
            la_t = ec(P.sb_ctx([128, 2, 512], F32, "la_t"))
            qf = [Buf(qf_t[:, i]) for i in range(2)]
            qb = [Buf(qb_t[:, i]) for i in range(2)]
            qo = [Buf(qo_t[:, i]) for i in range(4)]
            gt = [Buf(g_t[:, i]) for i in range(6)]
            la = [Buf(la_t[:, i]) for i in range(2)]
            code = [Buf(code_t[:, i]) for i in range(2)]
            rope = Buf(rope_t[:])
            if mixer == "ret":
                P.dma(rope.v(), lay["rope_in"].v())
            wq = {}
            for nm in names:
                c0, _ = COLS[f"{mixer}_{nm}"]
                wq[nm] = (c0,)
            if mixer == "gla":
                wc = loadw(COLS["gla_af"][0], 32)
                wcode = P.sb([128, KC, 32], BF16, "wcode")
                P.copy(wcode.v(), wc[:, :, 0:32], eng="pool")
            cnt = 0
            if mixer == "gla":
                egg = P.sb([128, 2, NCH, 4], F32, "egg")
            for b in tokblocks:
                n = H.sizes[b]
                nc_ = n // 128
                kind = H.kinds[b]
                if mixer == "gla":
                    for d in range(2):
                        ps = P.next_ps()
                        for k in range(KC):
                            P.mm(ps[0:16, :n], wcode[:, k, d * 16:(d + 1) * 16], H.b[b][:, k, :], start=(k == 0), stop=(k == KC - 1))
                        P.copy(code[d][:, :n], ps[0:16, :n], eng="act")
                for nm in names:
                    w = loadw(wq[nm][0], 512)
                    var0 = 0 if nm == "q" else 1
                    for h in range(4):
                        ps = fm_mm(w, h * 128, 128, b)
                        cnt += 1
                        f32 = qf[cnt % 2]
                        if mixer == "ret":
                            sc = 128 ** -0.5 if nm == "k" else 1.0
                            if kind == "lat":
                                bfv = qb[cnt % 2]
                                P.act(bfv[:, :n], ps[:, :n], AF.Copy, scale=sc)
                                ps2 = P.next_ps()
                                P.mm(ps2[:, :n], env.cm("perm"), bfv[:, :n])
                                t0 = H.offs[b]
                                P.tt(f32[:, :n], bfv[:, :n], rope[:, 0, t0:t0 + n], ALU.mult)
                                P.tt(gt[0][:, :n], ps2[:, :n], rope[:, 1, t0:t0 + n], ALU.mult)
                                P.tt(f32[:, :n], f32[:, :n], gt[0][:, :n], ALU.add, eng="pool")
                            else:
                                P.act(f32[:, :n], ps[:, :n], AF.Copy, scale=sc)
                            for d in range(2):
                                sgn = 0 if nm == "q" else 1
                                tab = lay["ret_tab"][:, d, sgn, h, :].m(lambda a: a.unsqueeze(1).to_broadcast([128, nc_, 128]))
                                o = qo[(cnt * 2 + d) % 4]
                                P.tt(o[:, :n].m(lambda a: a.rearrange("p (c t) -> p c t", t=128)),
                                     f32[:, :n].m(lambda a: a.rearrange("p (c t) -> p c t", t=128)), tab, ALU.mult,
                                     eng=("dve" if d == 0 else "pool"))
                                P.dma(sp["qk"][mi, h, 2 * d + var0, :, spoff[b]:spoff[b] + n], o[:, :n])
                        else:
                            sc = 128 ** -0.5 if nm == "q" else 1.0
                            P.act(f32[:, :n], ps[:, :n], AF.Copy, scale=sc)
                            for d in range(2):
                                psz = P.next_ps()
                                P.mm(psz[:, :n], lay["wa2"][:, d, h * 128:(h + 1) * 128], code[d][:, :n])
                                z = gt[1]
                                P.ts(z[:, :n], psz[:, :n], lay["gla_ba"][:, d, h:h + 1], None, ALU.add)
                                mx, l = softplus_parts(P, z[:, :n], n, gt[2][:, :n], gt[3][:, :n])
                                P.ts(gt[3][:, :n], z[:, :n], 0.0, None, ALU.min)
                                dd = gt[4]
                                P.tt(dd[:, :n], gt[3][:, :n], gt[2][:, :n], ALU.subtract)
                                cs = gt[5]
                                for c in range(nc_):
                                    sl = slice(c * 128, (c + 1) * 128)
                                    P.scan(cs[:, sl], env.ones_f[:, 0:128], dd[:, sl], 0.0, ALU.mult, ALU.add)
                                g = la[d]
                                if d == 0:
                                    P.copy(g[:, :n], cs[:, :n], eng="pool")
                                else:
                                    P.tt(g[:, :n], dd[:, :n], cs[:, :n], ALU.subtract)
                                    for c in range(nc_):
                                        sl = slice(c * 128, (c + 1) * 128)
                                        P.ts(g[:, sl], g[:, sl], cs[:, c * 128 + 127:c * 128 + 128], None, ALU.add)
                                if nm == "k":
                                    ch0 = spoff[b] // 128
                                    for c in range(nc_):
                                        P.act(egg[:, d, ch0 + c, h:h + 1], cs[:, c * 128 + 127:c * 128 + 128], AF.Exp, scale=1.0 / 16)
                                sgn = 1.0 if nm == "q" else -1.0
                                e = gt[0]
                                P.act(e[:, :n], g[:, :n], AF.Exp, scale=sgn / 16)
                                o = qo[(cnt * 2 + d) % 4]
                                P.tt(o[:, :n], f32[:, :n], e[:, :n], ALU.mult, eng=("dve" if d == 0 else "pool"))
                                P.dma(sp["qk"][mi, h, 2 * d + var0, :, spoff[b]:spoff[b] + n], o[:, :n])
            if mixer == "gla":
                for d in range(2):
                    P.dma(sp["eG"][mi * 2 + d], egg[:, d])
            if mixer == "ret":
                egr = P.sb([128, 2, NCH, 4], F32, "egr")
                for d in range(2):
                    P.copy(egr[:, d, :, :], lay["ret_eg"][:, d, :].m(lambda a: a.unsqueeze(1).to_broadcast([128, NCH, 4])))
                    P.dma(sp["eG"][mi * 2 + d], egr[:, d])
            P.barrier()

    def do_tm(name, handler, width=512):
        c0, sz = COLS[name]
        for cc in range(0, sz, width):
            n_ = min(width, sz - cc)
            w = loadw(c0 + cc, n_)
            for b in tokblocks:
                for c in range(H.sizes[b] // 128):
                    ps = P.next_ps()
                    for k in range(KC):
                        P.mm(ps[:, :n_], H.b[b][:, k, c * 128:(c + 1) * 128], w[:, k, 0:n_], start=(k == 0), stop=(k == KC - 1))
                    handler(ps, n_, cc, spoff[b] + c * 128)

    ev = {"n": 0}
    evt_t = P.stack.enter_context(P.sb_ctx([128, 4, 512], BF16, "evt"))
    evt = [Buf(evt_t[:, i]) for i in range(4)]

    def h_copy(dst, col0, func=AF.Copy):
        def h(ps, n_, cc, tok):
            e = evt[ev["n"] % 4]
            ev["n"] += 1
            if func == AF.Copy and ev["n"] % 2 == 0:
                P.copy(e[:, :n_], ps[:, :n_])
            else:
                P.act(e[:, :n_], ps[:, :n_], func)
            P.dma(dst[tok:tok + 128, col0 + cc:col0 + cc + n_], e[:, :n_])
        return h

    def h_dt(ps, n_, cc, tok):
        z = P.sb([128, 64], F32, "dtz")
        t1 = P.sb([128, 64], F32, "dt1")
        t2 = P.sb([128, 64], F32, "dt2")
        P.tt(z.v(), ps[:, 0:64], lay["dt_bias"].v(), ALU.add)
        mx, l = softplus_parts(P, z.v(), 64, t1.v(), t2.v())
        P.tt(z.v(), mx, l, ALU.add)
        P.tt(t1.v(), z.v(), lay["ssd_A"].v(), ALU.mult)
        P.dma(sp["dt"][tok:tok + 128, :], z.v())
        P.dma(sp["a"][tok:tok + 128, :], t1.v())

    def do_xbc(chunks):
        c0, _ = COLS["ssd_xbc"]
        with scope(P):
            ec = P.stack.enter_context
            rawl_t = ec(P.sb_ctx([128, 2, TL + 4], F32, "rawl"))
            rawc_t = ec(P.sb_ctx([128, 2, TC + 4], F32, "rawc"))
            acc_t = ec(P.sb_ctx([128, 2, TL], F32, "acc"))
            xs_t = ec(P.sb_ctx([128, 4, TT], BF16, "xsT"))
            xtm_t = ec(P.sb_ctx([128, 2, 512], BF16, "xtm"))
            rawl = [Buf(rawl_t[:, i]) for i in range(2)]
            rawc = [Buf(rawc_t[:, i]) for i in range(2)]
            acc = [Buf(acc_t[:, i]) for i in range(2)]
            xs = [Buf(xs_t[:, i]) for i in range(4)]
            xtm = [Buf(xtm_t[:, i]) for i in range(2)]
            for i in range(2):
                P.memset(rawc[i][:, 0:2], 0.0)
                P.memset(rawc[i][:, TC + 2:TC + 4], 0.0)
            w = None
            nx = 0
            for ci, ch in enumerate(chunks):
                if ci % 4 == 0 or w is None:
                    w = loadw(c0 + ch * 128, 512)
                    wbase = ch
                rl, rc, ac = rawl[ci % 2], rawc[ci % 2], acc[ci % 2]
                for b in range(nblk):
                    n = H.sizes[b]
                    ps = fm_mm(w, (ch - wbase) * 128, 128, b)
                    if H.kinds[b] == "lat":
                        P.copy(rl[:, 2 + H.offs[b]:2 + H.offs[b] + n], ps[:, :n], eng=("act" if b % 2 else "dve"))
                    elif H.kinds[b] == "ctx":
                        P.copy(rc[:, 2:2 + n], ps[:, :n], eng="act")
                    else:
                        P.ts(rl[:, 0:2], ps[:, 0:2], lay["halo_valid"][:, 0:1], None, ALU.mult)
                        P.ts(rl[:, TL + 2:TL + 4], ps[:, 2:4], lay["halo_valid"][:, 1:2], None, ALU.mult)
                for (raw, T_, o_) in ((rl, TL, 0), (rc, TC, TL)):
                    a = ac[:, 0:T_]
                    P.ts(a, raw[:, 0:T_], lay["conv_w"][:, 0, ch:ch + 1], lay["conv_b"][:, ch:ch + 1], ALU.mult, ALU.add)
                    for o in range(1, 5):
                        P.stt(a, raw[:, o:o + T_], lay["conv_w"][:, o, ch:ch + 1], a, ALU.mult, ALU.add,
                              eng=("dve" if o % 2 else "pool") if False else "dve")
                    if ch < 16:
                        P.act(xs[ci % 4][:, o_:o_ + T_], a, AF.Silu)
                    else:
                        e = xs[ci % 4]
                        P.act(e[:, o_:o_ + T_], a, AF.Silu)
                        which = "BT" if ch < 20 else "CT"
                        P.dma(sp[which][(ch - 16) % 4, :, o_:o_ + T_], e[:, o_:o_ + T_])
                if ch < 16 and ci % 4 == 3:
                    for tcn in range(NCH):
                        psb = P.next_ps()
                        pv = psb.v().m(lambda a: a.bitcast(BF16))
                        for q4 in range(4):
                            P.tr(pv[:, q4 * 128:(q4 + 1) * 128], xs[q4][:, tcn * 128:(tcn + 1) * 128], env.cm("ident"))
                        xo = xtm[nx % 2]
                        nx += 1
                        P.copy(xo.v(), pv[:, 0:512], eng=("act" if nx % 2 else "dve"))
                        P.dma(sp["x"][tcn * 128:(tcn + 1) * 128, (ch - 3) * 128:(ch + 1) * 128], xo.v())
            P.barrier()

    for grp in groups:
        if grp in ("ret", "gla"):
            do_qk(grp)
        elif grp == "v":
            do_tm("ret_v", h_copy(sp["v"][0], 0))
            do_tm("gla_v", h_copy(sp["v"][1], 0))
        elif grp == "gates":
            do_tm("ret_g", h_copy(sp["gate"], 0, AF.Silu))
            do_tm("gla_r", h_copy(sp["gate"], 1024, AF.Silu))
            do_tm("ssd_z", h_copy(sp["gate"], 2048, AF.Silu))
        elif grp == "dt":
            do_tm("ssd_dt", h_dt, width=64)
        elif grp == "xB":
            do_xbc(list(range(0, 20)))
        elif grp == "xBC":
            do_xbc(list(range(0, 24)))
        elif grp == "merge":
            c0, sz = COLS["merge"]
            for cc in range(0, sz, 512):
                w = loadw(c0 + cc, 512)
                for q4 in range(4):
                    for b in tokblocks:
                        n = H.sizes[b]
                        ps = fm_mm(w, q4 * 128, 128, b)
                        e = evt[ev["n"] % 4]
                        ev["n"] += 1
                        P.act(e[:, :n], ps[:, :n], AF.Sigmoid)
                        P.dma(sp["sg"][cc // 128 + q4, :, spoff[b]:spoff[b] + n], e[:, :n])
    P.barrier()


def bc(v, axis, shape):
    return v.m(lambda a: a.unsqueeze(axis).to_broadcast(list(shape)))


def r3(v, pat, **kw):
    return v.m(lambda a: a.rearrange(pat, **kw))


class SweepBufs:
    def __init__(self, P, do_y, with_post):
        st = P.stack

        def mk(shape, dt, n, name):
            t = st.enter_context(P.sb_ctx([128, n] + list(shape), dt, name))
            return [Buf(t[:, i]) for i in range(n)]
        self.kT = [mk([4, 128], BF16, 2, "kT0"), mk([4, 128], BF16, 2, "kT1")]
        self.v = [mk([1024], BF16, 2, "v0"), mk([1024], BF16, 2, "v1")]
        self.eG = [mk([4], F32, 2, "eG0"), mk([4], F32, 2, "eG1")]
        self.BT = mk([4, 128], BF16, 2, "BT")
        self.x = mk([2048], BF16, 2, "xtm")
        self.a = mk([64], F32, 2, "a")
        self.dt = mk([64], F32, 2, "dt")
        self.ktm = mk([512], BF16, 2, "ktm")
        self.btm = mk([512], BF16, 1, "btm")
        self.eT = mk([96], F32, 2, "eT")
        self.wv = mk([32], F32, 1, "wv")
        self.vs = mk([2048], BF16, 1, "vs")
        if do_y:
            self.qT = [mk([4, 128], BF16, 2, "qT0"), mk([4, 128], BF16, 2, "qT1")]
            self.CT = mk([4, 128], BF16, 2, "CT")
            self.attT = mk([4, 128], BF16, 2, "attT")
            self.vdt = mk([2048], BF16, 1, "vdt")
            self.cbm = mk([128], BF16, 2, "cbm")
            self.rhsD = mk([8, 128], BF16, 2, "rhsD")
            self.L = mk([4, 128], BF16, 2, "L")
            self.att = mk([8, 128], BF16, 2, "att")
            self.tmp = mk([512], F32, 2, "tmp")
            self.yall = mk([4096], F32, 1, "yall")
        if with_post:
            self.yb = mk([4096], BF16, 1, "yb")
            self.gate = mk([4096], BF16, 1, "gate")
            self.ss = mk([12], F32, 2, "ss")
            self.junk = mk([512], F32, 1, "junk")
            self.ygate = mk([4096], BF16, 1, "ygate")
            self.ygT = mk([32, 128], BF16, 1, "ygT")
            self.tmpf = mk([2048], F32, 1, "tmpf")
        else:
            self.ybo = mk([4096], BF16, 2, "ybo") if do_y else None


def sweep(P, env, sp, lay, d, chain, S, SB, do_y, B, mode, Dt=None):
    msk_f32 = env.cm("mle" if d == 0 else "mge", bf=False)
    msk_bf = env.cm("mle" if d == 0 else "mge")
    sl_f32 = env.cm("sgt" if d == 0 else "slt", bf=False)
    sl_bf = env.cm("sgt" if d == 0 else "slt")
    ident = env.cm("ident")
    for it, c in enumerate(chain):
        par = it % 2
        t0 = c * 128
        kT = [B.kT[m][par] for m in range(2)]
        v = [B.v[m][par] for m in range(2)]
        eG = [B.eG[m][par] for m in range(2)]
        BT, x, a, dt = B.BT[par], B.x[par], B.a[par], B.dt[par]
        for m in range(2):
            P.dma(kT[m].v(), View(sp["qk"], sp["qk"].ap[m, :, 2 * d + 1, :, t0:t0 + 128].rearrange("h p t -> p h t")))
            P.dma(v[m].v(), sp["v"][m, t0:t0 + 128, :])
            P.dma(eG[m].v(), sp["eG"][m * 2 + d, :, c, :])
        P.dma(BT.v(), View(sp["BT"], sp["BT"].ap[:, :, t0:t0 + 128].rearrange("g p t -> p g t")))
        P.dma(x.v(), sp["x"][t0:t0 + 128, :])
        P.dma(a.v(), sp["a"][t0:t0 + 128, :])
        P.dma(dt.v(), sp["dt"][t0:t0 + 128, :])
        if do_y:
            qT = [B.qT[m][par] for m in range(2)]
            CT = B.CT[par]
            for m in range(2):
                P.dma(qT[m].v(), View(sp["qk"], sp["qk"].ap[m, :, 2 * d, :, t0:t0 + 128].rearrange("h p t -> p h t")))
            P.dma(CT.v(), View(sp["CT"], sp["CT"].ap[:, :, t0:t0 + 128].rearrange("g p t -> p g t")))
        if mode == "post":
            yb, gate = B.yb[0], B.gate[0]
            P.dma(yb.v(), sp["yb"][t0:t0 + 128, :])
            P.dma(gate.v(), sp["gate"][t0:t0 + 128, :])
        a_d = a[:, d * 32:(d + 1) * 32]
        dt_d = dt[:, d * 32:(d + 1) * 32]
        yall = B.yall[0] if do_y else None

        def evac(dst, src, ybv, i):
            if ybv is not None:
                P.tt(dst, src, ybv, ALU.add, eng="dve")
            elif i % 2:
                P.copy(dst, src, eng="act")
            else:
                P.copy(dst, src, eng="dve")

        for m in range(2):
            psb = P.next_ps()
            pv = psb.v().m(lambda ap: ap.bitcast(BF16))
            for h in range(4):
                P.tr(pv[:, h * 128:(h + 1) * 128], kT[m][:, h, :], ident)
            ktm = B.ktm[m]
            P.copy(ktm.v(), pv[:, 0:512], eng="act")
            if do_y:
                aps = P.next_ps()
                for h in range(4):
                    P.mm(aps[:, h * 128:(h + 1) * 128], kT[m][:, h, :], qT[m][:, h, :])
                attT = B.attT[m]
                P.tt(attT.v(), r3(aps.v(), "p (h t) -> p h t", h=4), bc(msk_f32, 1, [128, 4, 128]), ALU.mult)
                for hp in range(2):
                    yp = P.next_ps()
                    for hh in range(2):
                        h = hp * 2 + hh
                        o = yp[:, hh * 256:(hh + 1) * 256]
                        P.mm(o, attT[:, h, :], v[m][:, h * 256:(h + 1) * 256], start=True, stop=False)
                        P.mm(o, qT[m][:, h, :], SB[m][:, h * 256:(h + 1) * 256], start=False, stop=True)
                    cs = slice(m * 1024 + hp * 512, m * 1024 + (hp + 1) * 512)
                    evac(yall[:, cs], yp.v(), yb[:, cs] if mode == "post" else None, hp)
            for hp in range(2):
                ps2 = P.next_ps()
                for hh in range(2):
                    h = hp * 2 + hh
                    P.mm(ps2[:, hh * 256:(hh + 1) * 256], ktm[:, h * 128:(h + 1) * 128], v[m][:, h * 256:(h + 1) * 256])
                for hh in range(2):
                    h = hp * 2 + hh
                    hc = slice(h * 256, (h + 1) * 256)
                    P.ts(S[m][:, hc], S[m][:, hc], eG[m][:, h:h + 1], None, ALU.mult, eng="pool")
                    P.stt(S[m][:, hc], ps2[:, hh * 256:(hh + 1) * 256], eG[m][:, h:h + 1], S[m][:, hc], ALU.mult, ALU.add)
            P.copy(SB[m].v(), S[m].v(), eng="act")
            if Dt is not None:
                P.tt(Dt[m].v(), Dt[m].v(), eG[m].v(), ALU.mult, eng="pool")
        psb = P.next_ps()
        pv = psb.v().m(lambda ap: ap.bitcast(BF16))
        for g in range(4):
            P.tr(pv[:, g * 128:(g + 1) * 128], BT[:, g, :], ident)
        btm = B.btm[0]
        P.copy(btm.v(), pv[:, 0:512], eng="act")
        pss = P.next_ps()
        P.mm(pss[:, 0:32], env.ones_f.v(), a_d)
        P.mm(pss[:, 32:64], sl_f32, a_d)
        if do_y:
            P.mm(pss[:, 64:96], msk_f32, a_d)
        ne = 96 if do_y else 64
        eT = B.eT[par]
        P.act(eT[:, 0:ne], pss[:, 0:ne], AF.Exp)
        if Dt is not None:
            P.tt(Dt[2].v(), Dt[2].v(), pss[:, 0:32], ALU.add)
        wv = B.wv[0]
        P.tt(wv.v(), eT[:, 32:64], dt_d, ALU.mult)
        vs = B.vs[0]
        x3 = r3(x.v(), "p (h e) -> p h e", e=64)
        P.tt(r3(vs.v(), "p (h e) -> p h e", e=64), x3, bc(wv.v(), 2, [128, 32, 64]), ALU.mult, eng="pool")
        if do_y:
            vdt = B.vdt[0]
            P.tt(r3(vdt.v(), "p (h e) -> p h e", e=64), x3, bc(dt_d, 2, [128, 32, 64]), ALU.mult)
            for g in range(4):
                cb = P.next_ps()
                P.mm(cb[:, 0:128], BT[:, g, :], CT[:, g, :])
                cbm = B.cbm[g % 2]
                P.tt(cbm.v(), cb[:, 0:128], msk_f32, ALU.mult)
                rhsD = B.rhsD[g % 2]
                P.tt(rhsD.v(), bc(msk_bf, 1, [128, 8, 128]), bc(a_d[:, g * 8:(g + 1) * 8], 2, [128, 8, 128]), ALU.mult, eng="pool")
                att = B.att[g % 2]
                for half in range(2):
                    dps = P.next_ps()
                    P.mm(dps.v(), sl_bf, r3(rhsD[:, half * 4:(half + 1) * 4, :], "p h t -> p (h t)"))
                    L = B.L[half]
                    P.act(r3(L.v(), "p h t -> p (h t)"), dps.v(), AF.Exp)
                    P.tt(att[:, half * 4:(half + 1) * 4, :], L.v(), bc(cbm.v(), 1, [128, 4, 128]), ALU.mult)
                yp = P.next_ps()
                for h in range(8):
                    P.mm(yp[:, h * 64:(h + 1) * 64], att[:, h, :], r3(vdt.v(), "p (h e) -> p h e", e=64)[:, g * 8 + h, :])
                yi = P.next_ps()
                P.mm(yi.v(), CT[:, g, :], SB[2][:, g * 512:(g + 1) * 512])
                tmp = B.tmp[g % 2]
                P.tt(r3(tmp.v(), "p (h e) -> p h e", e=64), r3(yi.v(), "p (h e) -> p h e", e=64),
                     bc(eT[:, 64 + g * 8:64 + (g + 1) * 8], 2, [128, 8, 64]), ALU.mult)
                cs = slice(2048 + g * 512, 2048 + (g + 1) * 512)
                P.tt(yall[:, cs], tmp.v(), yp.v(), ALU.add)
                if mode == "post":
                    P.tt(yall[:, cs], yall[:, cs], yb[:, cs], ALU.add, eng="pool")
        P.tt(r3(S[2].v(), "p (h e) -> p h e", e=64), r3(S[2].v(), "p (h e) -> p h e", e=64),
             bc(eT[:, 0:32], 2, [128, 32, 64]), ALU.mult)
        for g in range(4):
            ps = P.next_ps()
            P.mm(ps.v(), btm[:, g * 128:(g + 1) * 128], vs[:, g * 512:(g + 1) * 512])
            gc = slice(g * 512, (g + 1) * 512)
            P.tt(S[2][:, gc], S[2][:, gc], ps.v(), ALU.add)
        P.copy(SB[2].v(), S[2].v(), eng="act")
        if mode == "yb":
            ybo = B.ybo[0]
            P.copy(ybo[:, 0:2048], yall[:, 0:2048], eng="act")
            P.copy(ybo[:, 2048:4096], yall[:, 2048:4096], eng="pool")
            P.dma(sp["yb"][t0:t0 + 128, :], ybo.v())
        elif mode == "post":
            ss, junk, ygate, tmpf = B.ss[par], B.junk[0], B.ygate[0], B.tmpf[0]
            P.tt(r3(tmpf.v(), "p (h e) -> p h e", e=64), x3, bc(lay["ssd_d"].v(), 2, [128, 32, 64]), ALU.mult, eng="pool")
            P.tt(yall[:, 2048:4096], yall[:, 2048:4096], tmpf.v(), ALU.add)
            P.tt(yall[:, 2048:4096], yall[:, 2048:4096], gate[:, 2048:4096], ALU.mult)
            for h in range(8):
                P.act(junk[:, 0:256], yall[:, h * 256:(h + 1) * 256], AF.Square, accum=ss[:, h:h + 1])
            for g in range(4):
                P.act(junk[:, 0:512], yall[:, 2048 + g * 512:2048 + (g + 1) * 512], AF.Square, accum=ss[:, 8 + g:9 + g])
            P.ts(ss[:, 0:8], ss[:, 0:8], 1.0 / 256, EPS, ALU.mult, ALU.add)
            P.ts(ss[:, 8:12], ss[:, 8:12], 1.0 / 512, EPS, ALU.mult, ALU.add)
            P.act(ss.v(), ss.v(), AF.Ln)
            P.act(ss.v(), ss.v(), AF.Exp, scale=-0.5)
            for h in range(4):
                hc = slice(h * 256, (h + 1) * 256)
                P.stt(ygate[:, hc], yall[:, hc], ss[:, h:h + 1], gate[:, hc], ALU.mult, ALU.mult)
            for h in range(4):
                hc = slice(1024 + h * 256, 1024 + (h + 1) * 256)
                P.stt(tmpf[:, 0:256], yall[:, hc], ss[:, 4 + h:5 + h], gate[:, hc], ALU.mult, ALU.mult)
                P.tt(ygate[:, hc], tmpf[:, 0:256], lay["gla_g"].v(), ALU.mult, eng="pool")
            for g in range(4):
                gc = slice(2048 + g * 512, 2048 + (g + 1) * 512)
                P.stt(ygate[:, gc], yall[:, gc], ss[:, 8 + g:9 + g], lay["ssd_ng"][:, g * 512:(g + 1) * 512], ALU.mult, ALU.mult)
            ygT = B.ygT[0]
            for q in range(4):
                psb = P.next_ps()
                pv = psb.v().m(lambda ap: ap.bitcast(BF16))
                for k8 in range(8):
                    k = q * 8 + k8
                    P.tr(pv[:, k8 * 128:(k8 + 1) * 128], ygate[:, k * 128:(k + 1) * 128], ident)
                P.copy(r3(ygT[:, q * 8:(q + 1) * 8, :], "p k t -> p (k t)"), pv[:, 0:1024], eng=("act" if q % 2 else "dve"))
            P.dma(View(sp["ygT"], sp["ygT"].ap[:, :, t0:t0 + 128].rearrange("k p t -> p k t")), ygT.v())


def phase_c(P, env, sp, X, spoff, gate5, wbL, woL, blks):
    with scope(P):
        yg_t = P.stack.enter_context(P.sb_ctx([128, 32, 512], BF16, "ygblk"))
        sg_t = P.stack.enter_context(P.sb_ctx([128, 24, 512], BF16, "sgblk"))
        wb_t = P.stack.enter_context(P.sb_ctx([128, 2, 32, 128], BF16, "wbf"))
        wo_t = P.stack.enter_context(P.sb_ctx([128, 2, 8, 128], BF16, "wof"))
        mg_t = P.stack.enter_context(P.sb_ctx([128, 8, 512], BF16, "mg"))
        t_t = P.stack.enter_context(P.sb_ctx([128, 3, 512], F32, "t123"))
        yg, sg, mg = Buf(yg_t[:]), Buf(sg_t[:]), Buf(mg_t[:])
        wbf = [Buf(wb_t[:, i]) for i in range(2)]
        wof = [Buf(wo_t[:, i]) for i in range(2)]
        t = [Buf(t_t[:, i]) for i in range(3)]
        nw = 0
        for b in blks:
            n = X.sizes[b]
            o = spoff[b]
            P.dma(yg[:, :, :n], View(sp["ygT"], sp["ygT"].ap[:, :, o:o + n].rearrange("k p t -> p k t")))
            P.dma(sg[:, :, :n], View(sp["sg"], sp["sg"].ap[:, :, o:o + n].rearrange("k p t -> p k t")))
            for f in range(KC):
                w = wbf[nw % 2]
                nw += 1
                P.dma(w.v(), wbL[f], q="pool")
                for i, (k0, k1) in enumerate(((0, 8), (8, 16), (16, 32))):
                    ps = P.next_ps()
                    for k in range(k0, k1):
                        P.mm(ps[:, :n], w[:, k, :], yg[:, k, :n], start=(k == k0), stop=(k == k1 - 1))
                    P.tt(t[i][:, :n], ps[:, :n], sg[:, i * 8 + f, :n], ALU.mult, eng=("dve" if i != 1 else "pool") if False else "dve")
                P.tt(t[0][:, :n], t[0][:, :n], t[1][:, :n], ALU.add, eng="pool")
                P.tt(mg[:, f, :n], t[0][:, :n], t[2][:, :n], ALU.add, eng="pool")
            hg = gate5[X.kinds[b]]
            for f in range(KC):
                w = wof[f % 2]
                P.dma(w.v(), woL[f], q="pool")
                ps = P.next_ps()
                for k in range(KC):
                    P.mm(ps[:, :n], w[:, k, :], mg[:, k, :n], start=(k == 0), stop=(k == KC - 1))
                P.stt(X.b[b][:, f, :], ps[:, :n], hg[:, f:f + 1], X.b[b][:, f, :], ALU.mult, ALU.add)


def load_small(P, ins, names):
    lay = {}
    for nm in names:
        b = ins[nm]
        shape = list(b.ap.shape)
        t = P.sb(shape, F32, "l_" + nm)
        P.dma(t.v(), b.v())
        lay[nm] = t
    return lay


def derive_layer(P, env, lay, need_post):
    lg = P.sb([128, 8], F32, "lg")
    t1 = P.sb([128, 8], F32, "lgt1")
    t2 = P.sb([128, 8], F32, "lgt2")
    P.act(t1.v(), lay["ret_logit"].v(), AF.Abs)
    P.act(t1.v(), t1.v(), AF.Exp, scale=-1.0)
    P.act(t1.v(), t1.v(), AF.Ln, bias=1.0)
    P.ts(t2.v(), lay["ret_logit"].v(), 0.0, None, ALU.min)
    P.tt(lg.v(), t2.v(), t1.v(), ALU.subtract)
    nlg = P.sb([128, 8], F32, "nlg")
    P.ts(nlg.v(), lg.v(), -1.0, None, ALU.mult)
    tab = P.sb([128, 2, 2, 4, 128], F32, "ret_tab")
    for d in range(2):
        row = env.row_up if d == 0 else env.row_dn
        for h in range(4):
            P.act(tab[:, d, 0, h, :], row, AF.Exp, scale=lg[:, d * 4 + h:d * 4 + h + 1])
            P.act(tab[:, d, 1, h, :], row, AF.Exp, scale=nlg[:, d * 4 + h:d * 4 + h + 1])
    lay["ret_tab"] = tab
    eg = P.sb([128, 2, 4], F32, "ret_eg")
    P.act(eg.v().m(lambda a: a.rearrange("p d h -> p (d h)")), lg.v(), AF.Exp, scale=128.0)
    lay["ret_eg"] = eg
    wa2b = P.sb([16, 2, 512], BF16, "wa2b")
    P.copy(wa2b.v(), lay["wa2"].v())
    lay["wa2"] = wa2b
    A = P.sb([128, 64], F32, "ssdA")
    P.act(A.v(), lay["a_log"].v(), AF.Exp)
    P.ts(A.v(), A.v(), -1.0, None, ALU.mult)
    lay["ssd_A"] = A


SMALL_COMMON = ["mod_lat", "mod_ctx", "norm_g", "ret_logit", "wa2", "gla_ba", "conv_w", "conv_b", "dt_bias", "a_log",
                "halo_valid"]
SMALL_SHAPES = {"mod_lat": [128, 9, 8], "mod_ctx": [128, 9, 8], "norm_g": [128, 3, 8], "ret_logit": [128, 8],
                "wa2": [16, 2, 512], "gla_ba": [128, 2, 4], "conv_w": [128, 5, 24], "conv_b": [128, 24],
                "dt_bias": [128, 64], "a_log": [128, 64], "halo_valid": [128, 2],
                "gla_g": [128, 256], "ssd_d": [128, 32], "ssd_ng": [128, 2048], "final_g": [128, 8]}
SIZES = [512, 512, 512, 512, TC, 4]
KINDS = ["lat", "lat", "lat", "lat", "ctx", "halo"]
TIN = TL + TC + 4


def make_spill(P, full):
    sp = {}
    sp["qk"] = P.dram("sp_qk", [2, 4, 4, 128, TT], BF16)
    sp["eG"] = P.dram("sp_eG", [4, 128, NCH, 4], F32)
    sp["v"] = P.dram("sp_v", [2, TT, 1024], BF16)
    sp["BT"] = P.dram("sp_BT", [4, 128, TT], BF16)
    sp["x"] = P.dram("sp_x", [TT, 2048], BF16)
    sp["a"] = P.dram("sp_a", [TT, 64], F32)
    sp["dt"] = P.dram("sp_dt", [TT, 64], F32)
    if full:
        sp["CT"] = P.dram("sp_CT", [4, 128, TT], BF16)
        sp["gate"] = P.dram("sp_gate", [TT, 4096], BF16)
        sp["sg"] = P.dram("sp_sg", [24, 128, TT], BF16)
        sp["yb"] = P.dram("sp_yb", [TT, 4096], BF16)
        sp["ygT"] = P.dram("sp_ygT", [32, 128, TT], BF16)
    return sp


def alloc_states(P):
    S = [P.sb([128, 1024], F32, "S_ret"), P.sb([128, 1024], F32, "S_gla"), P.sb([128, 2048], F32, "S_ssd")]
    SB = [P.sb([128, 1024], BF16, "SB_ret"), P.sb([128, 1024], BF16, "SB_gla"), P.sb([128, 2048], BF16, "SB_ssd")]
    return S, SB


def zero_states(P, S, SB):
    for i in range(3):
        P.memset(S[i].v(), 0.0, eng="pool")
        P.memset(SB[i].v(), 0.0, eng="pool")


def build_layer(phase, last=False):
    nc = bass.Bass("TRN2", target_bir_lowering=False)
    with contextlib.ExitStack() as stack:
        P = Prog(nc, stack)
        P.psum_banks()
        env = Env()
        ins = {}

        def inp(name, shape, dt=F32):
            ins[name] = P.dram(name, shape, dt, "ExternalInput")
            return ins[name]
        inp("cin", [128, 1024])
        inp("xin", [D, TIN])
        for nm in SMALL_COMMON:
            inp(nm, SMALL_SHAPES[nm])
        inp("rope_in", [128, 2, TL])
        inp("w_inL", [128, KC, DIN])
        if phase == 1:
            inp("wiL", [NFF, 128, KC, 256])
            inp("woL", [128, NFF, D])
            x1out = P.dram("x1T", [D, TT], F32, "ExternalOutput")
            Eout = P.dram("Eout", [2, 128, 4096], F32, "ExternalOutput")
            Dout = P.dram("Dout", [2, 128, 40], F32, "ExternalOutput")
        else:
            for nm in ("gla_g", "ssd_d", "ssd_ng", "final_g"):
                inp(nm, SMALL_SHAPES[nm])
            inp("wiL", [NFF, 128, KC, 256])
            inp("woL", [128, NFF, D])
            inp("wbL", [KC, 128, 32, 128])
            inp("w_outL", [KC, 128, KC, 128])
            inp("slotE", [2, 7, 128, 4096])
            inp("slotD", [2, 7, 128, 40])
            xout = P.dram("xoutT", [D, TT], F32, "ExternalOutput")
        load_consts(P, env, ins["cin"])
        lay = load_small(P, ins, SMALL_COMMON + ([] if phase == 1 else ["gla_g", "ssd_d", "ssd_ng", "final_g"]))
        lay["rope_in"] = ins["rope_in"]
        derive_layer(P, env, lay, phase == 2)
        modT = {"lat": lay["mod_lat"], "ctx": lay["mod_ctx"]}
        sp = make_spill(P, phase == 2)
        xv = ins["xin"].ap.rearrange("(k p) t -> p k t", p=128)
        offs = np.concatenate([[0], np.cumsum(SIZES)]).astype(int).tolist()
        spoff = {b: offs[b] for b in range(5)}

        def wi_view(buf):
            return buf

        with scope(P):
            H = Blocks(P, SIZES, KINDS, "hT", BF16)
            scr = (P.sb([128, KC, 512], BF16, "sq"), P.sb([128, KC, 512], F32, "tmpn"), P.sb([128, 512], F32, "rstd"))
            if phase == 1:
                with scope(P):
                    X = Blocks(P, SIZES, KINDS, "xTs")
                    for b in range(6):
                        P.dma(X.b[b].v(), View(ins["xin"], xv[:, :, offs[b]:offs[b] + SIZES[b]]))
                    m0 = prep_mod(P, modT, lay["norm_g"], 0)
                    m0["halo"] = m0["lat"]
                    for b in range(6):
                        gs, sh, hg = m0[KINDS[b]]
                        norm_mod(P, X.b[b], H.b[b], SIZES[b], gs, sh, env.ones_bf, scr)
                    ffn_tiled(P, X, H, m0, ins["wiL"], ins["woL"])
                    ov = x1out.ap.rearrange("(k p) t -> p k t", p=128)
                    for b in range(5):
                        P.dma(View(x1out, ov[:, :, offs[b]:offs[b] + SIZES[b]]), X.b[b].v())
                    m1 = prep_mod(P, modT, lay["norm_g"], 1)
                    m1["halo"] = m1["lat"]
                    for b in range(6):
                        gs, sh, hg = m1[KINDS[b]]
                        norm_mod(P, X.b[b], H.b[b], SIZES[b], gs, sh, env.ones_bf, scr)
            else:
                with scope(P):
                    xt_t = P.stack.enter_context(P.sb_ctx([128, 2, KC, 512], F32, "xtmp"))
                    xt = [Buf(xt_t[:, i]) for i in range(2)]
                    m1 = prep_mod(P, modT, lay["norm_g"], 1)
                    m1["halo"] = m1["lat"]
                    for b in range(6):
                        n = SIZES[b]
                        xb = xt[b % 2]
                        P.dma(xb[:, :, :n], View(ins["xin"], xv[:, :, offs[b]:offs[b] + n]))
                        gs, sh, hg = m1[KINDS[b]]
                        norm_mod(P, _sub(xb, n), H.b[b], n, gs, sh, env.ones_bf, scr)
            groups = ["ret", "gla", "v", "dt", "xB"] if phase == 1 else ["ret", "gla", "v", "gates", "dt", "xBC", "merge"]
            lay["halo_valid"] = lay["halo_valid"]
            inproj(P, env, H, ins["w_inL"], sp, lay, groups, do_q=(phase == 2))
        with scope(P):
            S, SB = alloc_states(P)
            if phase == 1:
                B = SweepBufs(P, False, False)
                Dt = [P.sb([128, 4], F32, "Dt0"), P.sb([128, 4], F32, "Dt1"), P.sb([128, 32], F32, "Dt2")]
                for d in range(2):
                    zero_states(P, S, SB)
                    P.memset(Dt[0].v(), 1.0)
                    P.memset(Dt[1].v(), 1.0)
                    P.memset(Dt[2].v(), 0.0)
                    chain = list(range(NCH_L)) if d == 0 else list(range(NCH_L - 1, -1, -1))
                    sweep(P, env, sp, lay, d, chain, S, SB, False, B, "state", Dt)
                    P.dma(Eout[d, :, 0:1024], S[0].v())
                    P.dma(Eout[d, :, 1024:2048], S[1].v())
                    P.dma(Eout[d, :, 2048:4096], S[2].v())
                    dd = P.sb([128, 40], F32, "dd")
                    P.copy(dd[:, 0:4], Dt[0].v())
                    P.copy(dd[:, 4:8], Dt[1].v())
                    P.act(dd[:, 8:40], Dt[2].v(), AF.Exp)
                    P.dma(Dout[d], dd.v())
            else:
                B = SweepBufs(P, True, True)
                B.ybo = [P.sb([128, 4096], BF16, "ybo")]
                et = B.yall[0]
                dtile = P.sb([128, 40], F32, "slotD_t")
                for d in (1, 0):
                    zero_states(P, S, SB)
                    cchain = [NCH_L, NCH_L + 1] if d == 0 else [NCH_L + 1, NCH_L]
                    sweep(P, env, sp, lay, d, cchain, S, SB, True, B, "post" if d == 0 else "yb")
                    for s in range(7):
                        P.dma(et.v(), ins["slotE"][d, s])
                        P.dma(dtile.v(), ins["slotD"][d, s])
                        for h in range(4):
                            hc = slice(h * 256, (h + 1) * 256)
                            P.stt(S[0][:, hc], S[0][:, hc], dtile[:, h:h + 1], et[:, hc], ALU.mult, ALU.add)
                            hc2 = slice(1024 + h * 256, 1024 + (h + 1) * 256)
                            P.stt(S[1][:, hc], S[1][:, hc], dtile[:, 4 + h:5 + h], et[:, hc2], ALU.mult, ALU.add)
                        for h in range(32):
                            hc = slice(h * 64, (h + 1) * 64)
                            hc2 = slice(2048 + h * 64, 2048 + (h + 1) * 64)
                            P.stt(S[2][:, hc], S[2][:, hc], dtile[:, 8 + h:9 + h], et[:, hc2], ALU.mult, ALU.add)
                    for i in range(3):
                        P.copy(SB[i].v(), S[i].v(), eng="act")
                    chain = list(range(NCH_L)) if d == 0 else list(range(NCH_L - 1, -1, -1))
                    sweep(P, env, sp, lay, d, chain, S, SB, True, B, "post" if d == 0 else "yb")
        if phase == 2:
            with scope(P):
                X = Blocks(P, SIZES[:5], KINDS[:5], "xTs")
                for b in range(5):
                    P.dma(X.b[b].v(), View(ins["xin"], xv[:, :, offs[b]:offs[b] + SIZES[b]]))
                g5 = {}
                for kind in ("lat", "ctx"):
                    g5[kind] = modT[kind][:, 5, :]
                blks = [0, 1, 2, 3] + ([] if last else [4])
                phase_c(P, env, sp, X, spoff, g5, ins["wbL"], ins["w_outL"], blks)
                with scope(P):
                    H2 = Blocks(P, SIZES[:5], KINDS[:5], "h2T", BF16)
                    scr = (P.sb([128, KC, 512], BF16, "sq"), P.sb([128, KC, 512], F32, "tmpn"), P.sb([128, 512], F32, "rstd"))
                    m2 = prep_mod(P, modT, lay["norm_g"], 2)
                    for b in range(5):
                        gs, sh, hg = m2[KINDS[b]]
                        norm_mod(P, X.b[b], H2.b[b], SIZES[b], gs, sh, env.ones_bf, scr)
                    ffn_tiled(P, X, H2, m2, ins["wiL"], ins["woL"], G=4)
                ov = xout.ap.rearrange("(k p) t -> p k t", p=128)
                if last:
                    with scope(P):
                        scr = (P.sb([128, KC, 512], BF16, "sq"), P.sb([128, KC, 512], F32, "tmpn"), P.sb([128, 512], F32, "rstd"))
                        zt = P.sb([128, KC], F32, "zeros8")
                        P.memset(zt.v(), 0.0)
                        o_t = P.stack.enter_context(P.sb_ctx([128, 2, KC, 512], F32, "fo"))
                        ob = [Buf(o_t[:, i]) for i in range(2)]
                        for b in range(4):
                            norm_mod(P, X.b[b], ob[b % 2], 512, lay["final_g"], zt, env.ones_bf, scr)
                            P.dma(View(xout, ov[:, :, offs[b]:offs[b] + 512]), ob[b % 2].v())
                        P.dma(View(xout, ov[:, :, offs[4]:offs[4] + TC]), X.b[4].v())
                else:
                    for b in range(5):
                        P.dma(View(xout, ov[:, :, offs[b]:offs[b] + SIZES[b]]), X.b[b].v())
        P.emit()
    return nc


class _sub:
    def __init__(self, buf, n):
        self.buf = buf
        self.n = n

    def v(self):
        return View(self.buf, self.buf.ap[:, :, :self.n])

    def __getitem__(self, idx):
        return View(self.buf, self.buf.ap[:, :, :self.n][idx])


def ffn_tiled(P, X, H, mods, wiL, woL, G=6):
    nblk = len(X.sizes)
    T = X.T
    with scope(P):
        actT_t = P.stack.enter_context(P.sb_ctx([128, G, T], BF16, "actT"))
        wi_t = P.stack.enter_context(P.sb_ctx([128, 2, KC, 256], BF16, "wi_t"))
        wo_t = P.stack.enter_context(P.sb_ctx([128, G, D], BF16, "wo_t"))
        sil_t = P.stack.enter_context(P.sb_ctx([128, 2, 512], F32, "sil"))
        actb = [[Buf(actT_t[:, g, X.offs[b]:X.offs[b] + X.sizes[b]]) for b in range(nblk)] for g in range(G)]
        wib = [Buf(wi_t[:, i]) for i in range(2)]
        wob = Buf(wo_t[:])
        silb = [Buf(sil_t[:, i]) for i in range(2)]
        ns = 0
        nw = 0
        for g0 in range(0, NFF, G):
            gn = min(G, NFF - g0)
            for gi in range(gn):
                j = g0 + gi
                w = wib[nw % 2]
                nw += 1
                P.dma(w.v(), wiL[j], q="pool")
                for b in range(nblk):
                    n = X.sizes[b]
                    pa = P.next_ps()
                    pu = P.next_ps()
                    for k in range(KC):
                        P.mm(pa[:, :n], w[:, k, 0:128], H.b[b][:, k, :], start=(k == 0), stop=(k == KC - 1))
                    for k in range(KC):
                        P.mm(pu[:, :n], w[:, k, 128:256], H.b[b][:, k, :], start=(k == 0), stop=(k == KC - 1))
                    s = silb[ns % 2]
                    ns += 1
                    P.act(s[:, :n], pa[:, :n], AF.Silu)
                    P.tt(actb[gi][b].v(), s[:, :n], pu[:, :n], ALU.mult)
            P.dma(wob[:, 0:gn, :], woL[:, g0:g0 + gn, :], q="pool")
            for b in range(nblk):
                n = X.sizes[b]
                hg = mods[X.kinds[b]][2]
                for f in range(KC):
                    ps = P.next_ps()
                    for gi in range(gn):
                        P.mm(ps[:, :n], wob[:, gi, f * 128:(f + 1) * 128], actb[gi][b].v(), start=(gi == 0), stop=(gi == gn - 1))
                    P.stt(X.b[b][:, f, :], ps[:, :n], hg[:, f:f + 1], X.b[b][:, f, :], ALU.mult, ALU.add)


def build_p0():
    nc = bass.Bass("TRN2", target_bir_lowering=False)
    NCOL = 4608
    with contextlib.ExitStack() as stack:
        P = Prog(nc, stack)
        P.psum_banks()
        cc = P.dram("ccT", [128, KC, 2], F32, "ExternalInput")
        W = P.dram("adaW", [128, KC, NCOL], F32, "ExternalInput")
        bias = P.dram("adab", [2, NCOL], F32, "ExternalInput")
        out = P.dram("modout", [2, NCOL], F32, "ExternalOutput")
        s = P.sb([128, KC, 2], F32, "s")
        P.dma(s.v(), cc.v())
        P.act(s.v(), s.v(), AF.Silu)
        bt = P.sb([2, NCOL], F32, "bt")
        P.dma(bt.v(), bias.v())
        ot = P.sb([2, NCOL], F32, "ot")
        w_t = P.stack.enter_context(P.sb_ctx([128, 2, KC, 512], F32, "w0"))
        wb = [Buf(w_t[:, i]) for i in range(2)]
        for j in range(NCOL // 512):
            w = wb[j % 2]
            P.dma(w.v(), W[:, :, j * 512:(j + 1) * 512])
            ps = P.next_ps()
            for k in range(KC):
                P.mm(ps[0:2, :], s[:, k, :], w[:, k, :], start=(k == 0), stop=(k == KC - 1))
            P.tt(ot[:, j * 512:(j + 1) * 512], ps[0:2, :], bt[:, j * 512:(j + 1) * 512], ALU.add)
        P.dma(out.v(), ot.v())
        P.emit()
    return nc


_PROGS = {}


def get_prog(key):
    if key not in _PROGS:
        if key == "p0":
            _PROGS[key] = build_p0()
        elif key == "p1":
            _PROGS[key] = build_layer(1)
        elif key == "p2":
            _PROGS[key] = build_layer(2, last=False)
        else:
            _PROGS[key] = build_layer(2, last=True)
    return _PROGS[key]


def rep(v, n=128):
    v = np.asarray(v, np.float32).reshape(1, -1)
    return np.ascontiguousarray(np.broadcast_to(v, (n, v.shape[1])))


def rope_tables(core):
    t = core * TL + np.arange(TL)
    row = (t // 64).astype(np.float32)
    col = (t % 64).astype(np.float32)
    nf = 32
    freq = (np.float32(10000.0) ** (-np.arange(nf, dtype=np.float32) / np.float32(nf))).astype(np.float32)
    ang = np.concatenate([row[:, None] * freq[None, :], col[:, None] * freq[None, :]], axis=1).astype(np.float32)
    cos = np.cos(ang).astype(np.float32).T
    sin = np.sin(ang).astype(np.float32).T
    cos_tab = np.concatenate([cos, cos], axis=0)
    sin_tab = np.concatenate([-sin, sin], axis=0)
    return np.ascontiguousarray(np.stack([cos_tab, sin_tab], axis=1))


def run_model(inp, ncores, depth=4, run=None):
    from concourse.bass_utils import run_bass_kernel_spmd
    f32 = np.float32
    x = np.asarray(inp["x"], f32)[0]
    ctx = np.asarray(inp["ctx"], f32)[0]
    assert x.shape[0] == ncores * TL
    consts = host_consts()
    cc = np.stack([np.asarray(inp["c"], f32)[0], np.asarray(inp["c_ctx"], f32)], axis=1)
    ccT = np.ascontiguousarray(cc.reshape(KC, 128, 2).transpose(1, 0, 2))
    in_maps = []
    for c in range(8):
        l, half = (c // 2) % depth, c % 2
        Wl = np.asarray(inp["ada_w"][l], f32)[:, half * 4608:(half + 1) * 4608]
        in_maps.append({"ccT": ccT,
                        "adaW": np.ascontiguousarray(Wl.reshape(KC, 128, 4608).transpose(1, 0, 2)),
                        "adab": rep(np.asarray(inp["ada_b"][l], f32)[half * 4608:(half + 1) * 4608], 2)})
    res0 = run_bass_kernel_spmd(get_prog("p0"), in_maps, core_ids=list(range(8))).results
    mods = []
    for l in range(depth):
        m = np.concatenate([res0[2 * l]["modout"], res0[2 * l + 1]["modout"]], axis=1)
        mods.append([np.ascontiguousarray(m[s].reshape(9, KC, 128).transpose(2, 0, 1)) for s in range(2)])
    xT = [np.ascontiguousarray(x[c * TL:(c + 1) * TL].T) for c in range(ncores)]
    ctxT = np.ascontiguousarray(ctx.T)
    ropes = [rope_tables(c) for c in range(ncores)]
    zeros2 = np.zeros((D, 2), f32)

    def halos(xs):
        hs, vs = [], []
        for c in range(ncores):
            left = xs[c - 1][:, -2:] if c > 0 else zeros2
            right = xs[c + 1][:, :2] if c < ncores - 1 else zeros2
            hs.append(np.concatenate([left, right], axis=1))
            vs.append(rep(np.array([1.0 if c > 0 else 0.0, 1.0 if c < ncores - 1 else 0.0], f32)))
        return hs, vs

    for l in range(depth):
        last = l == depth - 1
        g = lambda k: np.asarray(inp[k][l], f32)
        common = {
            "cin": consts,
            "mod_lat": mods[l][0], "mod_ctx": mods[l][1],
            "norm_g": np.ascontiguousarray(g("norm_g").reshape(3, KC, 128).transpose(2, 0, 1)),
            "ret_logit": rep(g("ret_logit").reshape(-1)),
            "wa2": np.ascontiguousarray(g("gla_wa2").transpose(1, 0, 2)),
            "gla_ba": np.ascontiguousarray(g("gla_ba").reshape(2, 4, 128).transpose(2, 0, 1)),
            "conv_w": np.ascontiguousarray(g("conv_w").reshape(5, 24, 128).transpose(2, 0, 1)),
            "conv_b": np.ascontiguousarray(g("conv_b").reshape(24, 128).T),
            "dt_bias": rep(g("dt_bias").reshape(-1)),
            "a_log": rep(g("a_log").reshape(-1)),
            "w_inL": np.ascontiguousarray(g("w_in").reshape(KC, 128, DIN).transpose(1, 0, 2)),
        }

        def tile_ffn(wi, wo):
            wr = wi.reshape(KC, 128, 2 * DFF)
            a = wr[:, :, :DFF].reshape(KC, 128, NFF, 128).transpose(2, 1, 0, 3)
            u = wr[:, :, DFF:].reshape(KC, 128, NFF, 128).transpose(2, 1, 0, 3)
            return (np.ascontiguousarray(np.concatenate([a, u], axis=3)),
                    np.ascontiguousarray(wo.reshape(NFF, 128, D).transpose(1, 0, 2)))
        wiL, woL = tile_ffn(g("ffn1_wi"), g("ffn1_wo"))
        hs, vs = halos(xT)
        in_maps = []
        for c in range(ncores):
            m = dict(common)
            m.update({"xin": np.ascontiguousarray(np.concatenate([xT[c], ctxT, hs[c]], axis=1)),
                      "halo_valid": vs[c], "rope_in": ropes[c], "wiL": wiL, "woL": woL})
            in_maps.append(m)
        res1 = run_bass_kernel_spmd(get_prog("p1"), in_maps, core_ids=list(range(ncores))).results
        if run is not None:
            run["res1"] = res1
        x1 = [r["x1T"][:, :TL] for r in res1]
        ctx1 = res1[0]["x1T"][:, TL:]
        E = [r["Eout"] for r in res1]
        Dd = [r["Dout"] for r in res1]
        wiL, woL = tile_ffn(g("ffn2_wi"), g("ffn2_wo"))
        wb_all = np.concatenate([g("wb_ret"), g("wb_gla"), g("wb_ssd")], axis=0)
        wbL = np.ascontiguousarray(wb_all.reshape(32, 128, KC, 128).transpose(2, 1, 0, 3))
        w_outL = np.ascontiguousarray(g("w_out").reshape(KC, 128, KC, 128).transpose(2, 1, 0, 3))
        hs, vs = halos(x1)
        in_maps = []
        for c in range(ncores):
            slotE = np.zeros((2, 7, 128, 4096), f32)
            slotD = np.ones((2, 7, 128, 40), f32)
            for s, src in enumerate(range(0, c)):
                slotE[0, s] = E[src][0]
                slotD[0, s] = Dd[src][0]
            for s, src in enumerate(range(ncores - 1, c, -1)):
                slotE[1, s] = E[src][1]
                slotD[1, s] = Dd[src][1]
            m = dict(common)
            m.update({"xin": np.ascontiguousarray(np.concatenate([x1[c], ctx1, hs[c]], axis=1)),
                      "halo_valid": vs[c], "rope_in": ropes[c], "wiL": wiL, "woL": woL, "wbL": wbL, "w_outL": w_outL,
                      "gla_g": rep(g("gla_norm_g")), "ssd_d": rep(g("ssd_d")), "ssd_ng": rep(g("ssd_norm_g")),
                      "final_g": fm_vec(np.asarray(inp["final_norm_g"], f32)),
                      "slotE": slotE, "slotD": slotD})
            in_maps.append(m)
        res2 = run_bass_kernel_spmd(get_prog("p2last" if last else "p2"), in_maps, core_ids=list(range(ncores))).results
        if run is not None:
            run["res2"] = res2
        xT = [r["xoutT"][:, :TL] for r in res2]
        ctxT = np.ascontiguousarray(res2[0]["xoutT"][:, TL:])
    out = np.concatenate([t.T for t in xT], axis=0)[None]
    return np.ascontiguousarray(out.astype(np.float32))


def kernel(**inputs):
    return run_model(inputs, 8)
```

```python
import contextlib
import numpy as np
import concourse.bass as bass
import concourse.mybir as mybir

F32 = mybir.dt.float32
BF16 = mybir.dt.bfloat16
AF = mybir.ActivationFunctionType
ALU = mybir.AluOpType
KDMA = 6


class View:
    __slots__ = ("buf", "ap")

    def __init__(self, buf, ap):
        self.buf = buf
        self.ap = ap

    def __getitem__(self, idx):
        return View(self.buf, self.ap[idx])

    def m(self, f):
        return View(self.buf, f(self.ap))


class Buf:
    __slots__ = ("ap", "w", "r", "name")

    def __init__(self, ap, name=""):
        self.ap = ap
        self.w = None
        self.r = {}
        self.name = name

    def __getitem__(self, idx):
        return View(self, self.ap[idx])

    def v(self):
        return View(self, self.ap)


class Prog:
    ENGS = ("pe", "act", "dve", "pool", "sp")

    def __init__(self, nc, stack):
        self.nc = nc
        self.stack = stack
        self.q = {e: [] for e in self.ENGS}
        self.cnt = {e: 0 for e in self.ENGS}
        self.dman = {e: 0 for e in self.ENGS}
        self.waited = {e: {} for e in self.ENGS}
        self.sems = {}
        for e in self.ENGS:
            self.sems[("eng", e)] = stack.enter_context(nc.semaphore("s_" + e))
        for e in ("sp", "pool", "act"):
            for i in range(KDMA):
                self.sems[("dma", e, i)] = stack.enter_context(nc.semaphore(f"d_{e}{i}"))
        self.nalloc = 0
        self.psn = 0

    def sb(self, shape, dtype=F32, name=None):
        self.nalloc += 1
        t = self.stack.enter_context(self.nc.sbuf_tensor(f"{name or 't'}_{self.nalloc}", list(shape), dtype))
        return Buf(t[:] if hasattr(t, "__getitem__") else t, name or "t")

    def sb_ctx(self, shape, dtype=F32, name=None):
        self.nalloc += 1
        return self.nc.sbuf_tensor(f"{name or 't'}_{self.nalloc}", list(shape), dtype)

    def psum_banks(self):
        self.ps = []
        for i in range(8):
            t = self.stack.enter_context(self.nc.psum_tensor(f"ps{i}", [128, 512], F32))
            self.ps.append(Buf(t[:], f"ps{i}"))

    def next_ps(self):
        b = self.ps[self.psn % 8]
        self.psn += 1
        return b

    def dram(self, name, shape, dtype=F32, kind="Internal"):
        t = self.nc.dram_tensor(name, list(shape), dtype, kind=kind)
        return Buf(t.ap(), name)

    def op(self, E, fn, reads, writes, dma=False):
        deps = {}

        def add(tok):
            sk, val, eng = tok
            if deps.get(sk, (0, None))[0] < val:
                deps[sk] = (val, eng)

        for b in reads:
            if b.w is not None:
                add(b.w)
        for b in writes:
            if b.w is not None:
                add(b.w)
            for sk, (val, eng) in b.r.items():
                add((sk, val, eng))
        waits = []
        for sk, (val, eng) in deps.items():
            if eng == "pe" and E == "pe" and not dma:
                continue
            if self.waited[E].get(sk, 0) >= val:
                continue
            self.waited[E][sk] = val
            waits.append((sk, val))
        if dma:
            n = self.dman[E]
            self.dman[E] += 1
            sk = ("dma", E, n % KDMA)
            val = 16 * (n // KDMA + 1)
            if val > 16 and self.waited[E].get(sk, 0) < val - 16:
                waits.append((sk, val - 16))
                self.waited[E][sk] = val - 16
            tok = (sk, val, "dma")
            inc = (sk, 16)
        else:
            self.cnt[E] += 1
            sk = ("eng", E)
            tok = (sk, self.cnt[E], E)
            inc = (sk, 1)
        self.q[E].append((waits, fn, inc))
        wset = set(id(b) for b in writes)
        for b in writes:
            b.w = tok
            b.r = {}
        for b in reads:
            if id(b) not in wset:
                b.r[tok[0]] = (tok[1], tok[2])
        return tok

    def barrier(self):
        cur = []
        for e in self.ENGS:
            if self.cnt[e] > 0:
                cur.append((("eng", e), self.cnt[e]))
        for e in ("sp", "pool", "act"):
            n = self.dman[e]
            for i in range(KDMA):
                k = (n - 1 - i)
                if k >= 0:
                    cur.append((("dma", e, k % KDMA), 16 * (k // KDMA + 1)))
        for E in self.ENGS:
            waits = []
            for sk, val in cur:
                if sk == ("eng", E):
                    continue
                if self.waited[E].get(sk, 0) >= val:
                    continue
                self.waited[E][sk] = val
                waits.append((sk, val))
            if waits:
                self.q[E].append((waits, None, None))

    def emit(self):
        nc = self.nc
        self.barrier()
        with nc.Block() as block:
            def replay(E, e):
                for waits, fn, inc in self.q[E]:
                    for sk, val in waits:
                        e.wait_ge(self.sems[sk], val)
                    if fn is not None:
                        ins = fn(e)
                        ins.then_inc(self.sems[inc[0]], inc[1])

            @block.tensor
            def _(e):
                replay("pe", e)

            @block.scalar
            def _(e):
                replay("act", e)

            @block.vector
            def _(e):
                replay("dve", e)

            @block.gpsimd
            def _(e):
                replay("pool", e)

            @block.sync
            def _(e):
                replay("sp", e)

    @staticmethod
    def _b(*xs):
        return [x.buf for x in xs if isinstance(x, View)]

    @staticmethod
    def _a(x):
        return x.ap if isinstance(x, View) else x

    def mm(self, out, lhsT, rhs, start=True, stop=True):
        o, l, r = out.ap, lhsT.ap, rhs.ap
        return self.op("pe", lambda e: e.matmul(o, l, r, start=start, stop=stop), self._b(lhsT, rhs), self._b(out))

    def tr(self, out, in_, ident):
        o, i, d = out.ap, in_.ap, ident.ap
        return self.op("pe", lambda e: e.transpose(o, i, d), self._b(in_, ident), self._b(out))

    def act(self, out, in_, func, bias=0.0, scale=1.0, accum=None):
        o, i, b, s = out.ap, in_.ap, self._a(bias), self._a(scale)
        ac = self._a(accum) if accum is not None else None
        kw = {}
        if ac is not None:
            kw["accum_out"] = ac
        return self.op("act", lambda e: e.activation(o, i, func, bias=b, scale=s, **kw),
                       self._b(in_, bias, scale), self._b(out) + (self._b(accum) if accum is not None else []))

    def tt(self, out, a, b, op, eng="dve"):
        o, x, y = out.ap, a.ap, b.ap
        return self.op(eng, lambda e: e.tensor_tensor(o, x, y, op), self._b(a, b), self._b(out))

    def ts(self, out, a, s1, s2, op0, op1=None, eng="dve", accum=None):
        o, x, p1, p2 = out.ap, a.ap, self._a(s1), self._a(s2)
        kw = {}
        if op1 is not None:
            kw["op1"] = op1
        if accum is not None:
            kw["accum_out"] = accum.ap
        return self.op(eng, lambda e: e.tensor_scalar(o, x, p1, p2, op0, **kw), self._b(a, s1, s2),
                       self._b(out) + (self._b(accum) if accum is not None else []))

    def stt(self, out, a, s, b, op0, op1, eng="dve"):
        o, x, p, y = out.ap, a.ap, self._a(s), b.ap
        return self.op(eng, lambda e: e.scalar_tensor_tensor(o, x, p, y, op0, op1), self._b(a, s, b), self._b(out))

    def copy(self, out, a, eng="dve"):
        o, x = out.ap, a.ap
        if eng == "act":
            return self.op("act", lambda e: e.copy(o, x), self._b(a), self._b(out))
        return self.op(eng, lambda e: e.tensor_copy(o, x), self._b(a), self._b(out))

    def memset(self, out, val, eng="dve"):
        o = out.ap
        return self.op(eng, lambda e: e.memset(o, val), [], self._b(out))

    def scan(self, out, d0, d1, init, op0, op1):
        o, a, b, i = out.ap, d0.ap, d1.ap, self._a(init)
        return self.op("dve", lambda e: e.tensor_tensor_scan(o, a, b, i, op0, op1), self._b(d0, d1, init), self._b(out))

    def dma(self, out, in_, q="sp"):
        o, i = out.ap, in_.ap
        return self.op(q, lambda e: e.dma_start(out=o, in_=i), self._b(in_), self._b(out), dma=True)


@contextlib.contextmanager
def scope(P):
    old = P.stack
    with contextlib.ExitStack() as st:
        P.stack = st
        try:
            yield
        finally:
            P.barrier()
            P.stack = old


D = 1024
KC = 8
DFF = 2816
NFF = 22
EPS = 1e-6


def fm_vec(v):
    v = np.asarray(v, np.float32)
    return np.ascontiguousarray(v.reshape(-1, 128).T)


class Blocks:
    def __init__(self, P, sizes, kinds, name="xT", dtype=F32):
        self.sizes = sizes
        self.kinds = kinds
        self.offs = np.concatenate([[0], np.cumsum(sizes)]).astype(int).tolist()
        self.T = self.offs[-1]
        self.t = P.stack.enter_context(P.nc.sbuf_tensor(name, [128, KC, self.T], dtype))
        self.b = [Buf(self.t[:, :, self.offs[i]:self.offs[i] + n], f"{name}{i}") for i, n in enumerate(sizes)]


def prep_mod(P, modT, gT, idx):
    out = {}
    for kind, mt in modT.items():
        gs = P.sb([128, KC], F32, "gs")
        hg = P.sb([128, KC], F32, "hg")
        P.stt(gs.v(), mt[:, 3 * idx + 1, :], 1.0, gT[:, idx, :], ALU.add, ALU.mult)
        P.ts(hg.v(), mt[:, 3 * idx + 2, :], 0.5 if idx != 1 else 1.0, None, ALU.mult)
        out[kind] = (gs, mt[:, 3 * idx + 0, :], hg)
    return out


def norm_mod(P, xb, hb, n, gs, sh, ones_bf, scr):
    sq, tmp, rstd = scr
    P.act(sq[:, :, :n], xb.v(), AF.Square)
    ps = P.next_ps()
    for k in range(KC):
        P.mm(ps[:, :n], ones_bf.v(), sq[:, k, :n], start=(k == 0), stop=(k == KC - 1))
    P.ts(rstd[:, :n], ps[:, :n], 1.0 / D, EPS, ALU.mult, ALU.add)
    P.act(rstd[:, :n], rstd[:, :n], AF.Ln)
    P.act(rstd[:, :n], rstd[:, :n], AF.Exp, scale=-0.5)
    for k in range(KC):
        P.stt(tmp[:, k, :n], xb[:, k, :], gs[:, k:k + 1], rstd[:, :n], ALU.mult, ALU.mult)
        P.act(hb[:, k, :], tmp[:, k, :n], AF.Identity, bias=sh[:, k:k + 1], scale=1.0)


def ffn(P, X, H, mods, wi, wo, ones_bf, scr, G=6):
    nblk = len(X.sizes)
    T = X.T
    with P.sb_ctx([128, G, T], BF16, "actT") as actT_t, \
            P.sb_ctx([128, 2, KC, 256], BF16, "wi_t") as wi_t, \
            P.sb_ctx([128, G, D], BF16, "wo_t") as wo_t, \
            P.sb_ctx([128, 2, 512], F32, "sil") as sil_t:
        actb = [[Buf(actT_t[:, g, X.offs[b]:X.offs[b] + X.sizes[b]]) for b in range(nblk)] for g in range(G)]
        wib = [Buf(wi_t[:, i]) for i in range(2)]
        wob = Buf(wo_t[:])
        silb = [Buf(sil_t[:, i]) for i in range(2)]
        wiv = wi.ap.rearrange("(k p) c -> p k c", p=128)
        wov = wo.ap.rearrange("(j p) f -> p j f", p=128)
        ns = 0
        nw = 0
        for g0 in range(0, NFF, G):
            gn = min(G, NFF - g0)
            for gi in range(gn):
                j = g0 + gi
                w = wib[nw % 2]
                nw += 1
                P.dma(w[:, :, 0:128], View(wi, wiv[:, :, j * 128:(j + 1) * 128]), q="pool")
                P.dma(w[:, :, 128:256], View(wi, wiv[:, :, DFF + j * 128:DFF + (j + 1) * 128]), q="pool")
                for b in range(nblk):
                    n = X.sizes[b]
                    pa = P.next_ps()
                    pu = P.next_ps()
                    for k in range(KC):
                        P.mm(pa[:, :n], w[:, k, 0:128], H.b[b][:, k, :], start=(k == 0), stop=(k == KC - 1))
                    for k in range(KC):
                        P.mm(pu[:, :n], w[:, k, 128:256], H.b[b][:, k, :], start=(k == 0), stop=(k == KC - 1))
                    s = silb[ns % 2]
                    ns += 1
                    P.act(s[:, :n], pa[:, :n], AF.Silu)
                    P.tt(actb[gi][b].v(), s[:, :n], pu[:, :n], ALU.mult)
            P.dma(wob[:, 0:gn, :], View(wo, wov[:, g0:g0 + gn, :]), q="pool")
            for b in range(nblk):
                n = X.sizes[b]
                hg = mods[X.kinds[b]][2]
                for f in range(KC):
                    ps = P.next_ps()
                    for gi in range(gn):
                        P.mm(ps[:, :n], wob[:, gi, f * 128:(f + 1) * 128], actb[gi][b].v(), start=(gi == 0), stop=(gi == gn - 1))
                    P.stt(X.b[b][:, f, :], ps[:, :n], hg[:, f:f + 1], X.b[b][:, f, :], ALU.mult, ALU.add)
        P.barrier()


TL = 2048
TC = 256
NCH_L = TL // 128
NCH_C = TC // 128
NCH = NCH_L + NCH_C
TT = TL + TC
DIN = 14432
COLS = {}
_o = 0
for _n, _s in (("ret_q", 512), ("ret_k", 512), ("ret_v", 1024), ("ret_g", 1024), ("gla_q", 512), ("gla_k", 512),
               ("gla_v", 1024), ("gla_r", 1024), ("gla_af", 16), ("gla_ab", 16), ("ssd_z", 2048), ("ssd_xbc", 3072),
               ("ssd_dt", 64), ("merge", 3072)):
    COLS[_n] = (_o, _s)
    _o += _s
assert _o == DIN
CI = {"ident": 0, "perm": 1, "mle": 2, "mge": 3, "sgt": 4, "slt": 5}


def host_consts():
    p = np.arange(128)[:, None]
    c = np.arange(128)[None, :]
    mats = [p == c, c == (p + 64) % 128, p <= c, p >= c, p > c, p < c]
    m = np.concatenate([x.astype(np.float32) for x in mats], axis=1)
    rows = np.concatenate([np.broadcast_to(np.arange(1, 129, dtype=np.float32), (128, 128)),
                           np.broadcast_to(np.arange(128, 0, -1).astype(np.float32), (128, 128))], axis=1)
    return np.ascontiguousarray(np.concatenate([m, rows], axis=1))


class Env:
    pass


def load_consts(P, env, cin):
    cf = P.sb([128, 8 * 128], F32, "cf")
    P.dma(cf.v(), cin.v())
    cb = P.sb([128, 6 * 128], BF16, "cb")
    P.copy(cb.v(), cf[:, 0:768])
    env.cf, env.cb = cf, cb
    env.ones_bf = P.sb([128, 128], BF16, "ones")
    P.memset(env.ones_bf.v(), 1.0)
    env.ones_f = P.sb([128, 128], F32, "onesf")
    P.memset(env.ones_f.v(), 1.0)

    def cm(name, bf=True):
        i = CI[name]
        return (cb if bf else cf)[:, i * 128:(i + 1) * 128]
    env.cm = cm
    env.row_up = cf[:, 768:896]
    env.row_dn = cf[:, 896:1024]


def softplus_parts(P, z, n, tmp1, tmp2):
    P.act(tmp1, z, AF.Abs)
    P.act(tmp1, tmp1, AF.Exp, scale=-1.0)
    P.act(tmp1, tmp1, AF.Ln, bias=1.0)
    P.ts(tmp2, z, 0.0, None, ALU.max)
    return tmp2, tmp1


def inproj(P, env, H, W, sp, lay, groups, do_q):
    nblk = len(H.sizes)
    tokblocks = [b for b in range(nblk) if H.kinds[b] != "halo"]
    halo = [b for b in range(nblk) if H.kinds[b] == "halo"][0]
    spoff = {}
    o = 0
    for b in tokblocks:
        spoff[b] = o
        o += H.sizes[b]
    wt_t = P.stack.enter_context(P.sb_ctx([128, 2, KC, 512], BF16, "win_t"))
    wt = [Buf(wt_t[:, i]) for i in range(2)]
    st = {"nw": 0}

    def loadw(c0, n):
        w = wt[st["nw"] % 2]
        st["nw"] += 1
        P.dma(w[:, :, 0:n], W[:, :, c0:c0 + n], q="pool")
        return w

    def fm_mm(w, wc0, m, b, ps=None):
        n = H.sizes[b]
        ps = ps or P.next_ps()
        for k in range(KC):
            P.mm(ps[0:m, :n], w[:, k, wc0:wc0 + m], H.b[b][:, k, :], start=(k == 0), stop=(k == KC - 1))
        return ps

    def do_qk(mixer):
        mi = 0 if mixer == "ret" else 1
        names = (["q"] if do_q else []) + ["k"]
        with scope(P):
            ec = P.stack.enter_context
            qf_t = ec(P.sb_ctx([128, 2, 512], F32, "qk_f"))
            qb_t = ec(P.sb_ctx([128, 2, 512], BF16, "qk_b"))
            qo_t = ec(P.sb_ctx([128, 4, 512], BF16, "qk_o"))
            g_t = ec(P.sb_ctx([128, 6, 512], F32, "g_t"))
            rope_t = ec(P.sb_ctx([128, 2, TL if mixer == "ret" else 2], F32, "rope"))
            code_t = ec(P.sb_ctx([16, 2, 512], BF16, "code"))
            la_t = ec(P.sb_ctx([128, 2, 512], F32, "la_t"))
            epm_t = ec(P.sb_ctx([128, 4, 512], F32, "epm_t"))
            epm = [[Buf(epm_t[:, 0]), Buf(epm_t[:, 1])], [Buf(epm_t[:, 2]), Buf(epm_t[:, 3])]]
            qf = [Buf(qf_t[:, i]) for i in range(2)]
            qb = [Buf(qb_t[:, i]) for i in range(2)]
            qo = [Buf(qo_t[:, i]) for i in range(4)]
            gt = [Buf(g_t[:, i]) for i in range(6)]
            la = [Buf(la_t[:, i]) for i in range(2)]
            code = [Buf(code_t[:, i]) for i in range(2)]
            rope = Buf(rope_t[:])
            if mixer == "ret":
                P.dma(rope.v(), lay["rope_in"].v())
            wq = {}
            for nm in names:
                c0, _ = COLS[f"{mixer}_{nm}"]
                wq[nm] = (c0,)
            if mixer == "gla":
                wc = loadw(COLS["gla_af"][0], 32)
                wcode = P.sb([128, KC, 32], BF16, "wcode")
                P.copy(wcode.v(), wc[:, :, 0:32])
            cnt = 0
            if mixer == "gla":
                egg = P.sb([128, 2, NCH, 4], F32, "egg")
            for b in tokblocks:
                n = H.sizes[b]
                nc_ = n // 128
                kind = H.kinds[b]
                if mixer == "gla":
                    for d in range(2):
                        ps = P.next_ps()
                        for k in range(KC):
                            P.mm(ps[0:16, :n], wcode[:, k, d * 16:(d + 1) * 16], H.b[b][:, k, :], start=(k == 0), stop=(k == KC - 1))
                        P.copy(code[d][:, :n], ps[0:16, :n], eng="act")
                if mixer == "ret":
                    for nm in names:
                        w = loadw(wq[nm][0], 512)
                        var0 = 0 if nm == "q" else 1
                        for h in range(4):
                            ps = fm_mm(w, h * 128, 128, b)
                            cnt += 1
                            f32 = qf[cnt % 2]
                            sc = 128 ** -0.5 if nm == "k" else 1.0
                            if kind == "lat":
                                bfv = qb[cnt % 2]
                                P.act(bfv[:, :n], ps[:, :n], AF.Copy, scale=sc)
                                ps2 = P.next_ps()
                                P.mm(ps2[:, :n], env.cm("perm"), bfv[:, :n])
                                t0 = H.offs[b]
                                P.tt(f32[:, :n], bfv[:, :n], rope[:, 0, t0:t0 + n], ALU.mult)
                                P.tt(gt[0][:, :n], ps2[:, :n], rope[:, 1, t0:t0 + n], ALU.mult)
                                P.tt(f32[:, :n], f32[:, :n], gt[0][:, :n], ALU.add)
                            else:
                                P.act(f32[:, :n], ps[:, :n], AF.Copy, scale=sc)
                            for d in range(2):
                                sgn = 0 if nm == "q" else 1
                                tab = lay["ret_tab"][:, d, sgn, h, :].m(lambda a: a.unsqueeze(1).to_broadcast([128, nc_, 128]))
                                o = qo[(cnt * 2 + d) % 4]
                                P.tt(o[:, :n].m(lambda a: a.rearrange("p (c t) -> p c t", t=128)),
                                     f32[:, :n].m(lambda a: a.rearrange("p (c t) -> p c t", t=128)), tab, ALU.mult)
                                P.dma(sp["qk"][mi, h, 2 * d + var0, :, spoff[b]:spoff[b] + n], o[:, :n])
                else:
                    ws = {nm: loadw(wq[nm][0], 512) for nm in names}
                    for h in range(4):
                        for d in range(2):
                            psz = P.next_ps()
                            P.mm(psz[:, :n], lay["wa2"][:, d, h * 128:(h + 1) * 128], code[d][:, :n])
                            z = gt[1]
                            P.ts(z[:, :n], psz[:, :n], lay["gla_ba"][:, d, h:h + 1], None, ALU.add)
                            mx, l = softplus_parts(P, z[:, :n], n, gt[2][:, :n], gt[3][:, :n])
                            P.ts(gt[3][:, :n], z[:, :n], 0.0, None, ALU.min)
                            dd = gt[4]
                            P.tt(dd[:, :n], gt[3][:, :n], gt[2][:, :n], ALU.subtract)
                            cs = gt[5]
                            for c in range(nc_):
                                sl = slice(c * 128, (c + 1) * 128)
                                P.scan(cs[:, sl], env.ones_f[:, 0:128], dd[:, sl], 0.0, ALU.mult, ALU.add)
                            g = la[d]
                            if d == 0:
                                g = cs
                            else:
                                P.tt(g[:, :n], dd[:, :n], cs[:, :n], ALU.subtract)
                                for c in range(nc_):
                                    sl = slice(c * 128, (c + 1) * 128)
                                    P.ts(g[:, sl], g[:, sl], cs[:, c * 128 + 127:c * 128 + 128], None, ALU.add)
                            ch0 = spoff[b] // 128
                            for c in range(nc_):
                                P.act(egg[:, d, ch0 + c, h:h + 1], cs[:, c * 128 + 127:c * 128 + 128], AF.Exp, scale=1.0 / 16)
                            if do_q:
                                P.act(epm[0][d][:, :n], g[:, :n], AF.Exp, scale=1.0 / 16)
                            P.act(epm[1][d][:, :n], g[:, :n], AF.Exp, scale=-1.0 / 16)
                        for nm in names:
                            var0 = 0 if nm == "q" else 1
                            ps = fm_mm(ws[nm], h * 128, 128, b)
                            cnt += 1
                            f32 = qf[cnt % 2]
                            sc = 128 ** -0.5 if nm == "q" else 1.0
                            P.act(f32[:, :n], ps[:, :n], AF.Copy, scale=sc)
                            for d in range(2):
                                o = qo[(cnt * 2 + d) % 4]
                                P.tt(o[:, :n], f32[:, :n], epm[var0][d][:, :n], ALU.mult)
                                P.dma(sp["qk"][mi, h, 2 * d + var0, :, spoff[b]:spoff[b] + n], o[:, :n])
            if mixer == "gla":
                for d in range(2):
                    P.dma(sp["eG"][mi * 2 + d], egg[:, d])
            if mixer == "ret":
                egr = P.sb([128, 2, NCH, 4], F32, "egr")
                for d in range(2):
                    P.copy(egr[:, d, :, :], lay["ret_eg"][:, d, :].m(lambda a: a.unsqueeze(1).to_broadcast([128, NCH, 4])))
                    P.dma(sp["eG"][mi * 2 + d], egr[:, d])
            P.barrier()

    def do_tm(name, handler, width=512):
        c0, sz = COLS[name]
        for cc in range(0, sz, width):
            n_ = min(width, sz - cc)
            w = loadw(c0 + cc, n_)
            for b in tokblocks:
                for c in range(H.sizes[b] // 128):
                    ps = P.next_ps()
                    for k in range(KC):
                        P.mm(ps[:, :n_], H.b[b][:, k, c * 128:(c + 1) * 128], w[:, k, 0:n_], start=(k == 0), stop=(k == KC - 1))
                    handler(ps, n_, cc, spoff[b] + c * 128)

    ev = {"n": 0}
    evt_t = P.stack.enter_context(P.sb_ctx([128, 4, 512], BF16, "evt"))
    evt = [Buf(evt_t[:, i]) for i in range(4)]

    def h_copy(dst, col0, func=AF.Copy):
        def h(ps, n_, cc, tok):
            e = evt[ev["n"] % 4]
            ev["n"] += 1
            if func == AF.Copy and ev["n"] % 2 == 0:
                P.copy(e[:, :n_], ps[:, :n_])
            else:
                P.act(e[:, :n_], ps[:, :n_], func)
            P.dma(dst[tok:tok + 128, col0 + cc:col0 + cc + n_], e[:, :n_])
        return h

    def h_dt(ps, n_, cc, tok):
        z = P.sb([128, 64], F32, "dtz")
        t1 = P.sb([128, 64], F32, "dt1")
        t2 = P.sb([128, 64], F32, "dt2")
        P.tt(z.v(), ps[:, 0:64], lay["dt_bias"].v(), ALU.add)
        mx, l = softplus_parts(P, z.v(), 64, t1.v(), t2.v())
        P.tt(z.v(), mx, l, ALU.add)
        P.tt(t1.v(), z.v(), lay["ssd_A"].v(), ALU.mult)
        P.dma(sp["dt"][tok:tok + 128, :], z.v())
        P.dma(sp["a"][tok:tok + 128, :], t1.v())

    def do_xbc(chunks):
        c0, _ = COLS["ssd_xbc"]
        with scope(P):
            ec = P.stack.enter_context
            rawl_t = ec(P.sb_ctx([128, 2, TL + 4], F32, "rawl"))
            rawc_t = ec(P.sb_ctx([128, 2, TC + 4], F32, "rawc"))
            acc_t = ec(P.sb_ctx([128, 2, TL], F32, "acc"))
            xs_t = ec(P.sb_ctx([128, 4, TT], BF16, "xsT"))
            xtm_t = ec(P.sb_ctx([128, 2, 512], BF16, "xtm"))
            rawl = [Buf(rawl_t[:, i]) for i in range(2)]
            rawc = [Buf(rawc_t[:, i]) for i in range(2)]
            acc = [Buf(acc_t[:, i]) for i in range(2)]
            xs = [Buf(xs_t[:, i]) for i in range(4)]
            xtm = [Buf(xtm_t[:, i]) for i in range(2)]
            for i in range(2):
                P.memset(rawc[i][:, 0:2], 0.0)
                P.memset(rawc[i][:, TC + 2:TC + 4], 0.0)
            w = None
            nx = 0
            for ci, ch in enumerate(chunks):
                if ci % 4 == 0 or w is None:
                    w = loadw(c0 + ch * 128, 512)
                    wbase = ch
                rl, rc, ac = rawl[ci % 2], rawc[ci % 2], acc[ci % 2]
                for b in range(nblk):
                    n = H.sizes[b]
                    ps = fm_mm(w, (ch - wbase) * 128, 128, b)
                    if H.kinds[b] == "lat":
                        P.copy(rl[:, 2 + H.offs[b]:2 + H.offs[b] + n], ps[:, :n], eng=("act" if b % 2 else "dve"))
                    elif H.kinds[b] == "ctx":
                        P.copy(rc[:, 2:2 + n], ps[:, :n], eng="act")
                    else:
                        P.ts(rl[:, 0:2], ps[:, 0:2], lay["halo_valid"][:, 0:1], None, ALU.mult)
                        P.ts(rl[:, TL + 2:TL + 4], ps[:, 2:4], lay["halo_valid"][:, 1:2], None, ALU.mult)
                for (raw, T_, o_) in ((rl, TL, 0), (rc, TC, TL)):
                    a = ac[:, 0:T_]
                    P.ts(a, raw[:, 0:T_], lay["conv_w"][:, 0, ch:ch + 1], lay["conv_b"][:, ch:ch + 1], ALU.mult, ALU.add)
                    for o in range(1, 5):
                        P.stt(a, raw[:, o:o + T_], lay["conv_w"][:, o, ch:ch + 1], a, ALU.mult, ALU.add,
                              eng=("dve" if o % 2 else "pool") if False else "dve")
                    if ch < 16:
                        P.act(xs[ci % 4][:, o_:o_ + T_], a, AF.Silu)
                    else:
                        e = xs[ci % 4]
                        P.act(e[:, o_:o_ + T_], a, AF.Silu)
                        which = "BT" if ch < 20 else "CT"
                        P.dma(sp[which][(ch - 16) % 4, :, o_:o_ + T_], e[:, o_:o_ + T_])
                if ch < 16 and ci % 4 == 3:
                    for tcn in range(NCH):
                        psb = P.next_ps()
                        pv = psb.v().m(lambda a: a.bitcast(BF16))
                        for q4 in range(4):
                            P.tr(pv[:, q4 * 128:(q4 + 1) * 128], xs[q4][:, tcn * 128:(tcn + 1) * 128], env.cm("ident"))
                        xo = xtm[nx % 2]
                        nx += 1
                        P.copy(xo.v(), pv[:, 0:512], eng=("act" if nx % 2 else "dve"))
                        P.dma(sp["x"][tcn * 128:(tcn + 1) * 128, (ch - 3) * 128:(ch + 1) * 128], xo.v())
            P.barrier()

    for grp in groups:
        if grp in ("ret", "gla"):
            do_qk(grp)
        elif grp == "v":
            do_tm("ret_v", h_copy(sp["v"][0], 0))
            do_tm("gla_v", h_copy(sp["v"][1], 0))
        elif grp == "gates":
            do_tm("ret_g", h_copy(sp["gate"], 0, AF.Silu))
            do_tm("gla_r", h_copy(sp["gate"], 1024, AF.Silu))
            do_tm("ssd_z", h_copy(sp["gate"], 2048, AF.Silu))
        elif grp == "dt":
            do_tm("ssd_dt", h_dt, width=64)
        elif grp == "xB":
            do_xbc(list(range(0, 20)))
        elif grp == "xBC":
            do_xbc(list(range(0, 24)))
        elif grp == "merge":
            c0, sz = COLS["merge"]
            for cc in range(0, sz, 512):
                w = loadw(c0 + cc, 512)
                for q4 in range(4):
                    for b in tokblocks:
                        n = H.sizes[b]
                        ps = fm_mm(w, q4 * 128, 128, b)
                        e = evt[ev["n"] % 4]
                        ev["n"] += 1
                        P.act(e[:, :n], ps[:, :n], AF.Sigmoid)
                        P.dma(sp["sg"][cc // 128 + q4, :, spoff[b]:spoff[b] + n], e[:, :n])
    P.barrier()


def bc(v, axis, shape):
    return v.m(lambda a: a.unsqueeze(axis).to_broadcast(list(shape)))


def r3(v, pat, **kw):
    return v.m(lambda a: a.rearrange(pat, **kw))


class SweepBufs:
    def __init__(self, P, do_y, with_post):
        st = P.stack

        def mk(shape, dt, n, name):
            t = st.enter_context(P.sb_ctx([128, n] + list(shape), dt, name))
            return [Buf(t[:, i]) for i in range(n)]
        self.kT = [mk([4, 128], BF16, 2, "kT0"), mk([4, 128], BF16, 2, "kT1")]
        self.v = [mk([1024], BF16, 2, "v0"), mk([1024], BF16, 2, "v1")]
        self.eG = [mk([4], F32, 2, "eG0"), mk([4], F32, 2, "eG1")]
        self.BT = mk([4, 128], BF16, 2, "BT")
        self.x = mk([2048], BF16, 2, "xtm")
        self.a = mk([64], F32, 2, "a")
        self.dt = mk([64], F32, 2, "dt")
        self.ktm = mk([512], BF16, 2, "ktm")
        self.btm = mk([512], BF16, 1, "btm")
        self.eT = mk([96], F32, 2, "eT")
        self.wv = mk([32], F32, 1, "wv")
        self.vs = mk([2048], BF16, 1, "vs")
        if do_y:
            self.qT = [mk([4, 128], BF16, 2, "qT0"), mk([4, 128], BF16, 2, "qT1")]
            self.CT = mk([4, 128], BF16, 2, "CT")
            self.attT = mk([4, 128], BF16, 2, "attT")
            self.vdt = mk([2048], BF16, 1, "vdt")
            self.cbm = mk([128], BF16, 2, "cbm")
            self.rhsD = mk([8, 128], BF16, 2, "rhsD")
            self.L = mk([4, 128], BF16, 2, "L")
            self.att = mk([8, 128], BF16, 2, "att")
            self.tmp = mk([512], F32, 2, "tmp")
            self.yall = mk([4096], F32, 1, "yall")
        if with_post:
            self.yb = mk([4096], BF16, 1, "yb")
            self.gate = mk([4096], BF16, 1, "gate")
            self.ss = mk([12], F32, 2, "ss")
            self.junk = mk([512], F32, 1, "junk")
            self.ygate = mk([4096], BF16, 1, "ygate")
            self.ygT = mk([32, 128], BF16, 1, "ygT")
            self.tmpf = mk([2048], F32, 1, "tmpf")
        else:
            self.ybo = mk([4096], BF16, 2, "ybo") if do_y else None


def sweep(P, env, sp, lay, d, chain, S, SB, do_y, B, mode, Dt=None):
    msk_f32 = env.cm("mle" if d == 0 else "mge", bf=False)
    msk_bf = env.cm("mle" if d == 0 else "mge")
    sl_f32 = env.cm("sgt" if d == 0 else "slt", bf=False)
    sl_bf = env.cm("sgt" if d == 0 else "slt")
    ident = env.cm("ident")
    for it, c in enumerate(chain):
        par = it % 2
        t0 = c * 128
        kT = [B.kT[m][par] for m in range(2)]
        v = [B.v[m][par] for m in range(2)]
        eG = [B.eG[m][par] for m in range(2)]
        BT, x, a, dt = B.BT[par], B.x[par], B.a[par], B.dt[par]
        for m in range(2):
            P.dma(kT[m].v(), View(sp["qk"], sp["qk"].ap[m, :, 2 * d + 1, :, t0:t0 + 128].rearrange("h p t -> p h t")))
            P.dma(v[m].v(), sp["v"][m, t0:t0 + 128, :])
            P.dma(eG[m].v(), sp["eG"][m * 2 + d, :, c, :])
        P.dma(BT.v(), View(sp["BT"], sp["BT"].ap[:, :, t0:t0 + 128].rearrange("g p t -> p g t")))
        P.dma(x.v(), sp["x"][t0:t0 + 128, :])
        P.dma(a.v(), sp["a"][t0:t0 + 128, :])
        P.dma(dt.v(), sp["dt"][t0:t0 + 128, :])
        if do_y:
            qT = [B.qT[m][par] for m in range(2)]
            CT = B.CT[par]
            for m in range(2):
                P.dma(qT[m].v(), View(sp["qk"], sp["qk"].ap[m, :, 2 * d, :, t0:t0 + 128].rearrange("h p t -> p h t")))
            P.dma(CT.v(), View(sp["CT"], sp["CT"].ap[:, :, t0:t0 + 128].rearrange("g p t -> p g t")))
        if mode == "post":
            yb, gate = B.yb[0], B.gate[0]
            P.dma(yb.v(), sp["yb"][t0:t0 + 128, :])
            P.dma(gate.v(), sp["gate"][t0:t0 + 128, :])
        a_d = a[:, d * 32:(d + 1) * 32]
        dt_d = dt[:, d * 32:(d + 1) * 32]
        yall = B.yall[0] if do_y else None

        def evac(dst, src, ybv, i):
            if ybv is not None:
                P.tt(dst, src, ybv, ALU.add, eng="dve")
            elif i % 2:
                P.copy(dst, src, eng="act")
            else:
                P.copy(dst, src, eng="dve")

        for m in range(2):
            psb = P.next_ps()
            pv = psb.v().m(lambda ap: ap.bitcast(BF16))
            for h in range(4):
                P.tr(pv[:, h * 128:(h + 1) * 128], kT[m][:, h, :], ident)
            ktm = B.ktm[m]
            P.copy(ktm.v(), pv[:, 0:512], eng="act")
            if do_y:
                aps = P.next_ps()
                for h in range(4):
                    P.mm(aps[:, h * 128:(h + 1) * 128], kT[m][:, h, :], qT[m][:, h, :])
                attT = B.attT[m]
                P.tt(attT.v(), r3(aps.v(), "p (h t) -> p h t", h=4), bc(msk_f32, 1, [128, 4, 128]), ALU.mult)
                for hp in range(2):
                    yp = P.next_ps()
                    for hh in range(2):
                        h = hp * 2 + hh
                        o = yp[:, hh * 256:(hh + 1) * 256]
                        P.mm(o, attT[:, h, :], v[m][:, h * 256:(h + 1) * 256], start=True, stop=False)
                        P.mm(o, qT[m][:, h, :], SB[m][:, h * 256:(h + 1) * 256], start=False, stop=True)
                    cs = slice(m * 1024 + hp * 512, m * 1024 + (hp + 1) * 512)
                    evac(yall[:, cs], yp.v(), yb[:, cs] if mode == "post" else None, hp)
            for hp in range(2):
                ps2 = P.next_ps()
                for hh in range(2):
                    h = hp * 2 + hh
                    P.mm(ps2[:, hh * 256:(hh + 1) * 256], ktm[:, h * 128:(h + 1) * 128], v[m][:, h * 256:(h + 1) * 256])
                for hh in range(2):
                    h = hp * 2 + hh
                    hc = slice(h * 256, (h + 1) * 256)
                    P.act(S[m][:, hc], S[m][:, hc], AF.Copy, scale=eG[m][:, h:h + 1])
                    P.stt(S[m][:, hc], ps2[:, hh * 256:(hh + 1) * 256], eG[m][:, h:h + 1], S[m][:, hc], ALU.mult, ALU.add)
            P.copy(SB[m].v(), S[m].v(), eng="act")
            if Dt is not None:
                P.tt(Dt[m].v(), Dt[m].v(), eG[m].v(), ALU.mult)
        psb = P.next_ps()
        pv = psb.v().m(lambda ap: ap.bitcast(BF16))
        for g in range(4):
            P.tr(pv[:, g * 128:(g + 1) * 128], BT[:, g, :], ident)
        btm = B.btm[0]
        P.copy(btm.v(), pv[:, 0:512], eng="act")
        pss = P.next_ps()
        P.mm(pss[:, 0:32], env.ones_f.v(), a_d)
        P.mm(pss[:, 32:64], sl_f32, a_d)
        if do_y:
            P.mm(pss[:, 64:96], msk_f32, a_d)
        ne = 96 if do_y else 64
        eT = B.eT[par]
        P.act(eT[:, 0:ne], pss[:, 0:ne], AF.Exp)
        if Dt is not None:
            P.tt(Dt[2].v(), Dt[2].v(), pss[:, 0:32], ALU.add)
        wv = B.wv[0]
        P.tt(wv.v(), eT[:, 32:64], dt_d, ALU.mult)
        vs = B.vs[0]
        x3 = r3(x.v(), "p (h e) -> p h e", e=64)
        P.tt(r3(vs.v(), "p (h e) -> p h e", e=64), x3, bc(wv.v(), 2, [128, 32, 64]), ALU.mult)
        if do_y:
            vdt = B.vdt[0]
            P.tt(r3(vdt.v(), "p (h e) -> p h e", e=64), x3, bc(dt_d, 2, [128, 32, 64]), ALU.mult)
            for g in range(4):
                cb = P.next_ps()
                P.mm(cb[:, 0:128], BT[:, g, :], CT[:, g, :])
                cbm = B.cbm[g % 2]
                P.tt(cbm.v(), cb[:, 0:128], msk_f32, ALU.mult)
                rhsD = B.rhsD[g % 2]
                P.tt(rhsD.v(), bc(msk_bf, 1, [128, 8, 128]), bc(a_d[:, g * 8:(g + 1) * 8], 2, [128, 8, 128]), ALU.mult)
                att = B.att[g % 2]
                for half in range(2):
                    dps = P.next_ps()
                    P.mm(dps.v(), sl_bf, r3(rhsD[:, half * 4:(half + 1) * 4, :], "p h t -> p (h t)"))
                    L = B.L[half]
                    P.act(r3(L.v(), "p h t -> p (h t)"), dps.v(), AF.Exp)
                    P.tt(att[:, half * 4:(half + 1) * 4, :], L.v(), bc(cbm.v(), 1, [128, 4, 128]), ALU.mult)
                yp = P.next_ps()
                for h in range(8):
                    P.mm(yp[:, h * 64:(h + 1) * 64], att[:, h, :], r3(vdt.v(), "p (h e) -> p h e", e=64)[:, g * 8 + h, :])
                yi = P.next_ps()
                P.mm(yi.v(), CT[:, g, :], SB[2][:, g * 512:(g + 1) * 512])
                tmp = B.tmp[g % 2]
                P.tt(r3(tmp.v(), "p (h e) -> p h e", e=64), r3(yi.v(), "p (h e) -> p h e", e=64),
                     bc(eT[:, 64 + g * 8:64 + (g + 1) * 8], 2, [128, 8, 64]), ALU.mult)
                cs = slice(2048 + g * 512, 2048 + (g + 1) * 512)
                P.tt(yall[:, cs], tmp.v(), yp.v(), ALU.add)
                if mode == "post":
                    P.tt(yall[:, cs], yall[:, cs], yb[:, cs], ALU.add)
        P.tt(r3(S[2].v(), "p (h e) -> p h e", e=64), r3(S[2].v(), "p (h e) -> p h e", e=64),
             bc(eT[:, 0:32], 2, [128, 32, 64]), ALU.mult)
        for g in range(4):
            ps = P.next_ps()
            P.mm(ps.v(), btm[:, g * 128:(g + 1) * 128], vs[:, g * 512:(g + 1) * 512])
            gc = slice(g * 512, (g + 1) * 512)
            P.tt(S[2][:, gc], S[2][:, gc], ps.v(), ALU.add)
        P.copy(SB[2].v(), S[2].v(), eng="act")
        if mode == "yb":
            ybo = B.ybo[0]
            P.copy(ybo[:, 0:2048], yall[:, 0:2048], eng="act")
            P.copy(ybo[:, 2048:4096], yall[:, 2048:4096], eng="dve")
            P.dma(sp["yb"][t0:t0 + 128, :], ybo.v())
        elif mode == "post":
            ss, junk, ygate, tmpf = B.ss[par], B.junk[0], B.ygate[0], B.tmpf[0]
            P.tt(r3(tmpf.v(), "p (h e) -> p h e", e=64), x3, bc(lay["ssd_d"].v(), 2, [128, 32, 64]), ALU.mult)
            P.tt(yall[:, 2048:4096], yall[:, 2048:4096], tmpf.v(), ALU.add)
            P.tt(yall[:, 2048:4096], yall[:, 2048:4096], gate[:, 2048:4096], ALU.mult)
            for h in range(8):
                P.act(junk[:, 0:256], yall[:, h * 256:(h + 1) * 256], AF.Square, accum=ss[:, h:h + 1])
            for g in range(4):
                P.act(junk[:, 0:512], yall[:, 2048 + g * 512:2048 + (g + 1) * 512], AF.Square, accum=ss[:, 8 + g:9 + g])
            P.ts(ss[:, 0:8], ss[:, 0:8], 1.0 / 256, EPS, ALU.mult, ALU.add)
            P.ts(ss[:, 8:12], ss[:, 8:12], 1.0 / 512, EPS, ALU.mult, ALU.add)
            P.act(ss.v(), ss.v(), AF.Ln)
            P.act(ss.v(), ss.v(), AF.Exp, scale=-0.5)
            for h in range(4):
                hc = slice(h * 256, (h + 1) * 256)
                P.stt(ygate[:, hc], yall[:, hc], ss[:, h:h + 1], gate[:, hc], ALU.mult, ALU.mult)
            for h in range(4):
                hc = slice(1024 + h * 256, 1024 + (h + 1) * 256)
                P.stt(tmpf[:, 0:256], yall[:, hc], ss[:, 4 + h:5 + h], gate[:, hc], ALU.mult, ALU.mult)
                P.tt(ygate[:, hc], tmpf[:, 0:256], lay["gla_g"].v(), ALU.mult)
            for g in range(4):
                gc = slice(2048 + g * 512, 2048 + (g + 1) * 512)
                P.stt(ygate[:, gc], yall[:, gc], ss[:, 8 + g:9 + g], lay["ssd_ng"][:, g * 512:(g + 1) * 512], ALU.mult, ALU.mult)
            ygT = B.ygT[0]
            for q in range(4):
                psb = P.next_ps()
                pv = psb.v().m(lambda ap: ap.bitcast(BF16))
                for k8 in range(8):
                    k = q * 8 + k8
                    P.tr(pv[:, k8 * 128:(k8 + 1) * 128], ygate[:, k * 128:(k + 1) * 128], ident)
                P.copy(r3(ygT[:, q * 8:(q + 1) * 8, :], "p k t -> p (k t)"), pv[:, 0:1024], eng=("act" if q % 2 else "dve"))
            P.dma(View(sp["ygT"], sp["ygT"].ap[:, :, t0:t0 + 128].rearrange("k p t -> p k t")), ygT.v())


def phase_c(P, env, sp, X, spoff, gate5, wbL, woL, blks):
    with scope(P):
        yg_t = P.stack.enter_context(P.sb_ctx([128, 32, 512], BF16, "ygblk"))
        sg_t = P.stack.enter_context(P.sb_ctx([128, 24, 512], BF16, "sgblk"))
        wb_t = P.stack.enter_context(P.sb_ctx([128, 2, 32, 128], BF16, "wbf"))
        wo_t = P.stack.enter_context(P.sb_ctx([128, 2, 8, 128], BF16, "wof"))
        mg_t = P.stack.enter_context(P.sb_ctx([128, 8, 512], BF16, "mg"))
        t_t = P.stack.enter_context(P.sb_ctx([128, 3, 512], F32, "t123"))
        yg, sg, mg = Buf(yg_t[:]), Buf(sg_t[:]), Buf(mg_t[:])
        wbf = [Buf(wb_t[:, i]) for i in range(2)]
        wof = [Buf(wo_t[:, i]) for i in range(2)]
        t = [Buf(t_t[:, i]) for i in range(3)]
        nw = 0
        for b in blks:
            n = X.sizes[b]
            o = spoff[b]
            P.dma(yg[:, :, :n], View(sp["ygT"], sp["ygT"].ap[:, :, o:o + n].rearrange("k p t -> p k t")))
            P.dma(sg[:, :, :n], View(sp["sg"], sp["sg"].ap[:, :, o:o + n].rearrange("k p t -> p k t")))
            for f in range(KC):
                w = wbf[nw % 2]
                nw += 1
                P.dma(w.v(), wbL[f], q="pool")
                for i, (k0, k1) in enumerate(((0, 8), (8, 16), (16, 32))):
                    ps = P.next_ps()
                    for k in range(k0, k1):
                        P.mm(ps[:, :n], w[:, k, :], yg[:, k, :n], start=(k == k0), stop=(k == k1 - 1))
                    P.tt(t[i][:, :n], ps[:, :n], sg[:, i * 8 + f, :n], ALU.mult, eng=("dve" if i != 1 else "pool") if False else "dve")
                P.tt(t[0][:, :n], t[0][:, :n], t[1][:, :n], ALU.add)
                P.tt(mg[:, f, :n], t[0][:, :n], t[2][:, :n], ALU.add)
            hg = gate5[X.kinds[b]]
            for f in range(KC):
                w = wof[f % 2]
                P.dma(w.v(), woL[f], q="pool")
                ps = P.next_ps()
                for k in range(KC):
                    P.mm(ps[:, :n], w[:, k, :], mg[:, k, :n], start=(k == 0), stop=(k == KC - 1))
                P.stt(X.b[b][:, f, :], ps[:, :n], hg[:, f:f + 1], X.b[b][:, f, :], ALU.mult, ALU.add)


def load_small(P, ins, names):
    lay = {}
    for nm in names:
        b = ins[nm]
        shape = list(b.ap.shape)
        t = P.sb(shape, F32, "l_" + nm)
        P.dma(t.v(), b.v())
        lay[nm] = t
    return lay


def derive_layer(P, env, lay, need_post):
    lg = P.sb([128, 8], F32, "lg")
    t1 = P.sb([128, 8], F32, "lgt1")
    t2 = P.sb([128, 8], F32, "lgt2")
    P.act(t1.v(), lay["ret_logit"].v(), AF.Abs)
    P.act(t1.v(), t1.v(), AF.Exp, scale=-1.0)
    P.act(t1.v(), t1.v(), AF.Ln, bias=1.0)
    P.ts(t2.v(), lay["ret_logit"].v(), 0.0, None, ALU.min)
    P.tt(lg.v(), t2.v(), t1.v(), ALU.subtract)
    nlg = P.sb([128, 8], F32, "nlg")
    P.ts(nlg.v(), lg.v(), -1.0, None, ALU.mult)
    tab = P.sb([128, 2, 2, 4, 128], F32, "ret_tab")
    for d in range(2):
        row = env.row_up if d == 0 else env.row_dn
        for h in range(4):
            P.act(tab[:, d, 0, h, :], row, AF.Exp, scale=lg[:, d * 4 + h:d * 4 + h + 1])
            P.act(tab[:, d, 1, h, :], row, AF.Exp, scale=nlg[:, d * 4 + h:d * 4 + h + 1])
    lay["ret_tab"] = tab
    eg = P.sb([128, 2, 4], F32, "ret_eg")
    P.act(eg.v().m(lambda a: a.rearrange("p d h -> p (d h)")), lg.v(), AF.Exp, scale=128.0)
    lay["ret_eg"] = eg
    wa2b = P.sb([16, 2, 512], BF16, "wa2b")
    P.copy(wa2b.v(), lay["wa2"].v())
    lay["wa2"] = wa2b
    A = P.sb([128, 64], F32, "ssdA")
    P.act(A.v(), lay["a_log"].v(), AF.Exp)
    P.ts(A.v(), A.v(), -1.0, None, ALU.mult)
    lay["ssd_A"] = A


SMALL_COMMON = ["mod_lat", "mod_ctx", "norm_g", "ret_logit", "wa2", "gla_ba", "conv_w", "conv_b", "dt_bias", "a_log",
                "halo_valid"]
SMALL_SHAPES = {"mod_lat": [128, 9, 8], "mod_ctx": [128, 9, 8], "norm_g": [128, 3, 8], "ret_logit": [128, 8],
                "wa2": [16, 2, 512], "gla_ba": [128, 2, 4], "conv_w": [128, 5, 24], "conv_b": [128, 24],
                "dt_bias": [128, 64], "a_log": [128, 64], "halo_valid": [128, 2],
                "gla_g": [128, 256], "ssd_d": [128, 32], "ssd_ng": [128, 2048], "final_g": [128, 8]}
SIZES = [512, 512, 512, 512, TC, 4]
KINDS = ["lat", "lat", "lat", "lat", "ctx", "halo"]
TIN = TL + TC + 4


def make_spill(P, full):
    sp = {}
    sp["qk"] = P.dram("sp_qk", [2, 4, 4, 128, TT], BF16)
    sp["eG"] = P.dram("sp_eG", [4, 128, NCH, 4], F32)
    sp["v"] = P.dram("sp_v", [2, TT, 1024], BF16)
    sp["BT"] = P.dram("sp_BT", [4, 128, TT], BF16)
    sp["x"] = P.dram("sp_x", [TT, 2048], BF16)
    sp["a"] = P.dram("sp_a", [TT, 64], F32)
    sp["dt"] = P.dram("sp_dt", [TT, 64], F32)
    if full:
        sp["CT"] = P.dram("sp_CT", [4, 128, TT], BF16)
        sp["gate"] = P.dram("sp_gate", [TT, 4096], BF16)
        sp["sg"] = P.dram("sp_sg", [24, 128, TT], BF16)
        sp["yb"] = P.dram("sp_yb", [TT, 4096], BF16)
        sp["ygT"] = P.dram("sp_ygT", [32, 128, TT], BF16)
    return sp


def alloc_states(P):
    S = [P.sb([128, 1024], F32, "S_ret"), P.sb([128, 1024], F32, "S_gla"), P.sb([128, 2048], F32, "S_ssd")]
    SB = [P.sb([128, 1024], BF16, "SB_ret"), P.sb([128, 1024], BF16, "SB_gla"), P.sb([128, 2048], BF16, "SB_ssd")]
    return S, SB


def zero_states(P, S, SB):
    for i in range(3):
        P.memset(S[i].v(), 0.0, eng="pool")
        P.memset(SB[i].v(), 0.0, eng="pool")


def build_layer(phase, last=False):
    nc = bass.Bass("TRN2", target_bir_lowering=False)
    with contextlib.ExitStack() as stack:
        P = Prog(nc, stack)
        P.psum_banks()
        env = Env()
        ins = {}

        def inp(name, shape, dt=F32):
            ins[name] = P.dram(name, shape, dt, "ExternalInput")
            return ins[name]
        inp("cin", [128, 1024])
        inp("xin", [D, TIN])
        for nm in SMALL_COMMON:
            inp(nm, SMALL_SHAPES[nm])
        inp("rope_in", [128, 2, TL])
        inp("w_inL", [128, KC, DIN])
        if phase == 1:
            inp("wiL", [NFF, 128, KC, 256])
            inp("woL", [128, NFF, D])
            x1out = P.dram("x1T", [D, TT], F32, "ExternalOutput")
            Eout = P.dram("Eout", [2, 128, 4096], F32, "ExternalOutput")
            Dout = P.dram("Dout", [2, 128, 40], F32, "ExternalOutput")
        else:
            for nm in ("gla_g", "ssd_d", "ssd_ng", "final_g"):
                inp(nm, SMALL_SHAPES[nm])
            inp("wiL", [NFF, 128, KC, 256])
            inp("woL", [128, NFF, D])
            inp("wbL", [KC, 128, 32, 128])
            inp("w_outL", [KC, 128, KC, 128])
            inp("slotE", [2, 7, 128, 4096])
            inp("slotD", [2, 7, 128, 40])
            xout = P.dram("xoutT", [D, TT], F32, "ExternalOutput")
        load_consts(P, env, ins["cin"])
        lay = load_small(P, ins, SMALL_COMMON + ([] if phase == 1 else ["gla_g", "ssd_d", "ssd_ng", "final_g"]))
        lay["rope_in"] = ins["rope_in"]
        derive_layer(P, env, lay, phase == 2)
        modT = {"lat": lay["mod_lat"], "ctx": lay["mod_ctx"]}
        sp = make_spill(P, phase == 2)
        xv = ins["xin"].ap.rearrange("(k p) t -> p k t", p=128)
        offs = np.concatenate([[0], np.cumsum(SIZES)]).astype(int).tolist()
        spoff = {b: offs[b] for b in range(5)}

        def wi_view(buf):
            return buf

        with scope(P):
            H = Blocks(P, SIZES, KINDS, "hT", BF16)
            scr = (P.sb([128, KC, 512], BF16, "sq"), P.sb([128, KC, 512], F32, "tmpn"), P.sb([128, 512], F32, "rstd"))
            if phase == 1:
                with scope(P):
                    X = Blocks(P, SIZES, KINDS, "xTs")
                    for b in range(6):
                        P.dma(X.b[b].v(), View(ins["xin"], xv[:, :, offs[b]:offs[b] + SIZES[b]]))
                    m0 = prep_mod(P, modT, lay["norm_g"], 0)
                    m0["halo"] = m0["lat"]
                    for b in range(6):
                        gs, sh, hg = m0[KINDS[b]]
                        norm_mod(P, X.b[b], H.b[b], SIZES[b], gs, sh, env.ones_bf, scr)
                    ffn_tiled(P, X, H, m0, ins["wiL"], ins["woL"])
                    ov = x1out.ap.rearrange("(k p) t -> p k t", p=128)
                    for b in range(5):
                        P.dma(View(x1out, ov[:, :, offs[b]:offs[b] + SIZES[b]]), X.b[b].v())
                    m1 = prep_mod(P, modT, lay["norm_g"], 1)
                    m1["halo"] = m1["lat"]
                    for b in range(6):
                        gs, sh, hg = m1[KINDS[b]]
                        norm_mod(P, X.b[b], H.b[b], SIZES[b], gs, sh, env.ones_bf, scr)
            else:
                with scope(P):
                    xt_t = P.stack.enter_context(P.sb_ctx([128, 2, KC, 512], F32, "xtmp"))
                    xt = [Buf(xt_t[:, i]) for i in range(2)]
                    m1 = prep_mod(P, modT, lay["norm_g"], 1)
                    m1["halo"] = m1["lat"]
                    for b in range(6):
                        n = SIZES[b]
                        xb = xt[b % 2]
                        P.dma(xb[:, :, :n], View(ins["xin"], xv[:, :, offs[b]:offs[b] + n]))
                        gs, sh, hg = m1[KINDS[b]]
                        norm_mod(P, _sub(xb, n), H.b[b], n, gs, sh, env.ones_bf, scr)
            groups = ["ret", "gla", "v", "dt", "xB"] if phase == 1 else ["ret", "gla", "v", "gates", "dt", "xBC", "merge"]
            lay["halo_valid"] = lay["halo_valid"]
            inproj(P, env, H, ins["w_inL"], sp, lay, groups, do_q=(phase == 2))
        with scope(P):
            S, SB = alloc_states(P)
            if phase == 1:
                B = SweepBufs(P, False, False)
                Dt = [P.sb([128, 4], F32, "Dt0"), P.sb([128, 4], F32, "Dt1"), P.sb([128, 32], F32, "Dt2")]
                for d in range(2):
                    zero_states(P, S, SB)
                    P.memset(Dt[0].v(), 1.0)
                    P.memset(Dt[1].v(), 1.0)
                    P.memset(Dt[2].v(), 0.0)
                    chain = list(range(NCH_L)) if d == 0 else list(range(NCH_L - 1, -1, -1))
                    sweep(P, env, sp, lay, d, chain, S, SB, False, B, "state", Dt)
                    P.dma(Eout[d, :, 0:1024], S[0].v())
                    P.dma(Eout[d, :, 1024:2048], S[1].v())
                    P.dma(Eout[d, :, 2048:4096], S[2].v())
                    dd = P.sb([128, 40], F32, "dd")
                    P.copy(dd[:, 0:4], Dt[0].v())
                    P.copy(dd[:, 4:8], Dt[1].v())
                    P.act(dd[:, 8:40], Dt[2].v(), AF.Exp)
                    P.dma(Dout[d], dd.v())
            else:
                B = SweepBufs(P, True, True)
                B.ybo = [P.sb([128, 4096], BF16, "ybo")]
                et = B.yall[0]
                dtile = P.sb([128, 40], F32, "slotD_t")
                for d in (1, 0):
                    zero_states(P, S, SB)
                    cchain = [NCH_L, NCH_L + 1] if d == 0 else [NCH_L + 1, NCH_L]
                    sweep(P, env, sp, lay, d, cchain, S, SB, True, B, "post" if d == 0 else "yb")
                    for s in range(7):
                        P.dma(et.v(), ins["slotE"][d, s])
                        P.dma(dtile.v(), ins["slotD"][d, s])
                        for h in range(4):
                            hc = slice(h * 256, (h + 1) * 256)
                            P.stt(S[0][:, hc], S[0][:, hc], dtile[:, h:h + 1], et[:, hc], ALU.mult, ALU.add)
                            hc2 = slice(1024 + h * 256, 1024 + (h + 1) * 256)
                            P.stt(S[1][:, hc], S[1][:, hc], dtile[:, 4 + h:5 + h], et[:, hc2], ALU.mult, ALU.add)
                        for h in range(32):
                            hc = slice(h * 64, (h + 1) * 64)
                            hc2 = slice(2048 + h * 64, 2048 + (h + 1) * 64)
                            P.stt(S[2][:, hc], S[2][:, hc], dtile[:, 8 + h:9 + h], et[:, hc2], ALU.mult, ALU.add)
                    for i in range(3):
                        P.copy(SB[i].v(), S[i].v(), eng="act")
                    chain = list(range(NCH_L)) if d == 0 else list(range(NCH_L - 1, -1, -1))
                    sweep(P, env, sp, lay, d, chain, S, SB, True, B, "post" if d == 0 else "yb")
        if phase == 2:
            with scope(P):
                X = Blocks(P, SIZES[:5], KINDS[:5], "xTs")
                for b in range(5):
                    P.dma(X.b[b].v(), View(ins["xin"], xv[:, :, offs[b]:offs[b] + SIZES[b]]))
                g5 = {}
                for kind in ("lat", "ctx"):
                    g5[kind] = modT[kind][:, 5, :]
                blks = [0, 1, 2, 3] + ([] if last else [4])
                phase_c(P, env, sp, X, spoff, g5, ins["wbL"], ins["w_outL"], blks)
                with scope(P):
                    H2 = Blocks(P, SIZES[:5], KINDS[:5], "h2T", BF16)
                    scr = (P.sb([128, KC, 512], BF16, "sq"), P.sb([128, KC, 512], F32, "tmpn"), P.sb([128, 512], F32, "rstd"))
                    m2 = prep_mod(P, modT, lay["norm_g"], 2)
                    for b in range(5):
                        gs, sh, hg = m2[KINDS[b]]
                        norm_mod(P, X.b[b], H2.b[b], SIZES[b], gs, sh, env.ones_bf, scr)
                    ffn_tiled(P, X, H2, m2, ins["wiL"], ins["woL"], G=4)
                ov = xout.ap.rearrange("(k p) t -> p k t", p=128)
                if last:
                    with scope(P):
                        scr = (P.sb([128, KC, 512], BF16, "sq"), P.sb([128, KC, 512], F32, "tmpn"), P.sb([128, 512], F32, "rstd"))
                        zt = P.sb([128, KC], F32, "zeros8")
                        P.memset(zt.v(), 0.0)
                        o_t = P.stack.enter_context(P.sb_ctx([128, 2, KC, 512], F32, "fo"))
                        ob = [Buf(o_t[:, i]) for i in range(2)]
                        for b in range(4):
                            norm_mod(P, X.b[b], ob[b % 2], 512, lay["final_g"], zt, env.ones_bf, scr)
                            P.dma(View(xout, ov[:, :, offs[b]:offs[b] + 512]), ob[b % 2].v())
                        P.dma(View(xout, ov[:, :, offs[4]:offs[4] + TC]), X.b[4].v())
                else:
                    for b in range(5):
                        P.dma(View(xout, ov[:, :, offs[b]:offs[b] + SIZES[b]]), X.b[b].v())
        P.emit()
    return nc


class _sub:
    def __init__(self, buf, n):
        self.buf = buf
        self.n = n

    def v(self):
        return View(self.buf, self.buf.ap[:, :, :self.n])

    def __getitem__(self, idx):
        return View(self.buf, self.buf.ap[:, :, :self.n][idx])


def ffn_tiled(P, X, H, mods, wiL, woL, G=6):
    nblk = len(X.sizes)
    T = X.T
    with scope(P):
        actT_t = P.stack.enter_context(P.sb_ctx([128, G, T], BF16, "actT"))
        wi_t = P.stack.enter_context(P.sb_ctx([128, 2, KC, 256], BF16, "wi_t"))
        wo_t = P.stack.enter_context(P.sb_ctx([128, G, D], BF16, "wo_t"))
        sil_t = P.stack.enter_context(P.sb_ctx([128, 2, 512], F32, "sil"))
        actb = [[Buf(actT_t[:, g, X.offs[b]:X.offs[b] + X.sizes[b]]) for b in range(nblk)] for g in range(G)]
        wib = [Buf(wi_t[:, i]) for i in range(2)]
        wob = Buf(wo_t[:])
        silb = [Buf(sil_t[:, i]) for i in range(2)]
        ns = 0
        nw = 0
        for g0 in range(0, NFF, G):
            gn = min(G, NFF - g0)
            for gi in range(gn):
                j = g0 + gi
                w = wib[nw % 2]
                nw += 1
                P.dma(w.v(), wiL[j], q="pool")
                for b in range(nblk):
                    n = X.sizes[b]
                    pa = P.next_ps()
                    pu = P.next_ps()
                    for k in range(KC):
                        P.mm(pa[:, :n], w[:, k, 0:128], H.b[b][:, k, :], start=(k == 0), stop=(k == KC - 1))
                    for k in range(KC):
                        P.mm(pu[:, :n], w[:, k, 128:256], H.b[b][:, k, :], start=(k == 0), stop=(k == KC - 1))
                    s = silb[ns % 2]
                    ns += 1
                    P.act(s[:, :n], pa[:, :n], AF.Silu)
                    P.tt(actb[gi][b].v(), s[:, :n], pu[:, :n], ALU.mult)
            P.dma(wob[:, 0:gn, :], woL[:, g0:g0 + gn, :], q="pool")
            for b in range(nblk):
                n = X.sizes[b]
                hg = mods[X.kinds[b]][2]
                for f in range(KC):
                    ps = P.next_ps()
                    for gi in range(gn):
                        P.mm(ps[:, :n], wob[:, gi, f * 128:(f + 1) * 128], actb[gi][b].v(), start=(gi == 0), stop=(gi == gn - 1))
                    P.stt(X.b[b][:, f, :], ps[:, :n], hg[:, f:f + 1], X.b[b][:, f, :], ALU.mult, ALU.add)


def build_p0():
    nc = bass.Bass("TRN2", target_bir_lowering=False)
    NCOL = 4608
    with contextlib.ExitStack() as stack:
        P = Prog(nc, stack)
        P.psum_banks()
        cc = P.dram("ccT", [128, KC, 2], F32, "ExternalInput")
        W = P.dram("adaW", [128, KC, NCOL], F32, "ExternalInput")
        bias = P.dram("adab", [2, NCOL], F32, "ExternalInput")
        out = P.dram("modout", [2, NCOL], F32, "ExternalOutput")
        s = P.sb([128, KC, 2], F32, "s")
        P.dma(s.v(), cc.v())
        P.act(s.v(), s.v(), AF.Silu)
        bt = P.sb([2, NCOL], F32, "bt")
        P.dma(bt.v(), bias.v())
        ot = P.sb([2, NCOL], F32, "ot")
        w_t = P.stack.enter_context(P.sb_ctx([128, 2, KC, 512], F32, "w0"))
        wb = [Buf(w_t[:, i]) for i in range(2)]
        for j in range(NCOL // 512):
            w = wb[j % 2]
            P.dma(w.v(), W[:, :, j * 512:(j + 1) * 512])
            ps = P.next_ps()
            for k in range(KC):
                P.mm(ps[0:2, :], s[:, k, :], w[:, k, :], start=(k == 0), stop=(k == KC - 1))
            P.tt(ot[:, j * 512:(j + 1) * 512], ps[0:2, :], bt[:, j * 512:(j + 1) * 512], ALU.add)
        P.dma(out.v(), ot.v())
        P.emit()
    return nc


_PROGS = {}


def get_prog(key):
    if key not in _PROGS:
        if key == "p0":
            _PROGS[key] = build_p0()
        elif key == "p1":
            _PROGS[key] = build_layer(1)
        elif key == "p2":
            _PROGS[key] = build_layer(2, last=False)
        else:
            _PROGS[key] = build_layer(2, last=True)
    return _PROGS[key]


def rep(v, n=128):
    v = np.asarray(v, np.float32).reshape(1, -1)
    return np.ascontiguousarray(np.broadcast_to(v, (n, v.shape[1])))


def rope_tables(core):
    t = core * TL + np.arange(TL)
    row = (t // 64).astype(np.float32)
    col = (t % 64).astype(np.float32)
    nf = 32
    freq = (np.float32(10000.0) ** (-np.arange(nf, dtype=np.float32) / np.float32(nf))).astype(np.float32)
    ang = np.concatenate([row[:, None] * freq[None, :], col[:, None] * freq[None, :]], axis=1).astype(np.float32)
    cos = np.cos(ang).astype(np.float32).T
    sin = np.sin(ang).astype(np.float32).T
    cos_tab = np.concatenate([cos, cos], axis=0)
    sin_tab = np.concatenate([-sin, sin], axis=0)
    return np.ascontiguousarray(np.stack([cos_tab, sin_tab], axis=1))


def run_model(inp, ncores, depth=4, run=None):
    from concourse.bass_utils import run_bass_kernel_spmd
    f32 = np.float32
    x = np.asarray(inp["x"], f32)[0]
    ctx = np.asarray(inp["ctx"], f32)[0]
    assert x.shape[0] == ncores * TL
    consts = host_consts()
    cc = np.stack([np.asarray(inp["c"], f32)[0], np.asarray(inp["c_ctx"], f32)], axis=1)
    ccT = np.ascontiguousarray(cc.reshape(KC, 128, 2).transpose(1, 0, 2))
    in_maps = []
    for c in range(8):
        l, half = (c // 2) % depth, c % 2
        Wl = np.asarray(inp["ada_w"][l], f32)[:, half * 4608:(half + 1) * 4608]
        in_maps.append({"ccT": ccT,
                        "adaW": np.ascontiguousarray(Wl.reshape(KC, 128, 4608).transpose(1, 0, 2)),
                        "adab": rep(np.asarray(inp["ada_b"][l], f32)[half * 4608:(half + 1) * 4608], 2)})
    res0 = run_bass_kernel_spmd(get_prog("p0"), in_maps, core_ids=list(range(8))).results
    mods = []
    for l in range(depth):
        m = np.concatenate([res0[2 * l]["modout"], res0[2 * l + 1]["modout"]], axis=1)
        mods.append([np.ascontiguousarray(m[s].reshape(9, KC, 128).transpose(2, 0, 1)) for s in range(2)])
    xT = [np.ascontiguousarray(x[c * TL:(c + 1) * TL].T) for c in range(ncores)]
    ctxT = np.ascontiguousarray(ctx.T)
    ropes = [rope_tables(c) for c in range(ncores)]
    zeros2 = np.zeros((D, 2), f32)

    def halos(xs):
        hs, vs = [], []
        for c in range(ncores):
            left = xs[c - 1][:, -2:] if c > 0 else zeros2
            right = xs[c + 1][:, :2] if c < ncores - 1 else zeros2
            hs.append(np.concatenate([left, right], axis=1))
            vs.append(rep(np.array([1.0 if c > 0 else 0.0, 1.0 if c < ncores - 1 else 0.0], f32)))
        return hs, vs

    for l in range(depth):
        last = l == depth - 1
        g = lambda k: np.asarray(inp[k][l], f32)
        common = {
            "cin": consts,
            "mod_lat": mods[l][0], "mod_ctx": mods[l][1],
            "norm_g": np.ascontiguousarray(g("norm_g").reshape(3, KC, 128).transpose(2, 0, 1)),
            "ret_logit": rep(g("ret_logit").reshape(-1)),
            "wa2": np.ascontiguousarray(g("gla_wa2").transpose(1, 0, 2)),
            "gla_ba": np.ascontiguousarray(g("gla_ba").reshape(2, 4, 128).transpose(2, 0, 1)),
            "conv_w": np.ascontiguousarray(g("conv_w").reshape(5, 24, 128).transpose(2, 0, 1)),
            "conv_b": np.ascontiguousarray(g("conv_b").reshape(24, 128).T),
            "dt_bias": rep(g("dt_bias").reshape(-1)),
            "a_log": rep(g("a_log").reshape(-1)),
            "w_inL": np.ascontiguousarray(g("w_in").reshape(KC, 128, DIN).transpose(1, 0, 2)),
        }

        def tile_ffn(wi, wo):
            wr = wi.reshape(KC, 128, 2 * DFF)
            a = wr[:, :, :DFF].reshape(KC, 128, NFF, 128).transpose(2, 1, 0, 3)
            u = wr[:, :, DFF:].reshape(KC, 128, NFF, 128).transpose(2, 1, 0, 3)
            return (np.ascontiguousarray(np.concatenate([a, u], axis=3)),
                    np.ascontiguousarray(wo.reshape(NFF, 128, D).transpose(1, 0, 2)))
        wiL, woL = tile_ffn(g("ffn1_wi"), g("ffn1_wo"))
        hs, vs = halos(xT)
        in_maps = []
        for c in range(ncores):
            m = dict(common)
            m.update({"xin": np.ascontiguousarray(np.concatenate([xT[c], ctxT, hs[c]], axis=1)),
                      "halo_valid": vs[c], "rope_in": ropes[c], "wiL": wiL, "woL": woL})
            in_maps.append(m)
        res1 = run_bass_kernel_spmd(get_prog("p1"), in_maps, core_ids=list(range(ncores))).results
        if run is not None:
            run["res1"] = res1
        x1 = [r["x1T"][:, :TL] for r in res1]
        ctx1 = res1[0]["x1T"][:, TL:]
        E = [r["Eout"] for r in res1]
        Dd = [r["Dout"] for r in res1]
        wiL, woL = tile_ffn(g("ffn2_wi"), g("ffn2_wo"))
        wb_all = np.concatenate([g("wb_ret"), g("wb_gla"), g("wb_ssd")], axis=0)
        wbL = np.ascontiguousarray(wb_all.reshape(32, 128, KC, 128).transpose(2, 1, 0, 3))
        w_outL = np.ascontiguousarray(g("w_out").reshape(KC, 128, KC, 128).transpose(2, 1, 0, 3))
        hs, vs = halos(x1)
        in_maps = []
        for c in range(ncores):
            slotE = np.zeros((2, 7, 128, 4096), f32)
            slotD = np.ones((2, 7, 128, 40), f32)
            for s, src in enumerate(range(0, c)):
                slotE[0, s] = E[src][0]
                slotD[0, s] = Dd[src][0]
            for s, src in enumerate(range(ncores - 1, c, -1)):
                slotE[1, s] = E[src][1]
                slotD[1, s] = Dd[src][1]
            m = dict(common)
            m.update({"xin": np.ascontiguousarray(np.concatenate([x1[c], ctx1, hs[c]], axis=1)),
                      "halo_valid": vs[c], "rope_in": ropes[c], "wiL": wiL, "woL": woL, "wbL": wbL, "w_outL": w_outL,
                      "gla_g": rep(g("gla_norm_g")), "ssd_d": rep(g("ssd_d")), "ssd_ng": rep(g("ssd_norm_g")),
                      "final_g": fm_vec(np.asarray(inp["final_norm_g"], f32)),
                      "slotE": slotE, "slotD": slotD})
            in_maps.append(m)
        res2 = run_bass_kernel_spmd(get_prog("p2last" if last else "p2"), in_maps, core_ids=list(range(ncores))).results
        if run is not None:
            run["res2"] = res2
        xT = [r["xoutT"][:, :TL] for r in res2]
        ctxT = np.ascontiguousarray(res2[0]["xoutT"][:, TL:])
    out = np.concatenate([t.T for t in xT], axis=0)[None]
    return np.ascontiguousarray(out.astype(np.float32))


def kernel(**inputs):
    return run_model(inputs, 8)
```

```python
import contextlib
import numpy as np
import concourse.bass as bass
import concourse.mybir as mybir

F32 = mybir.dt.float32
BF16 = mybir.dt.bfloat16
AF = mybir.ActivationFunctionType
ALU = mybir.AluOpType
KDMA = 6


class View:
    __slots__ = ("buf", "ap")

    def __init__(self, buf, ap):
        self.buf = buf
        self.ap = ap

    def __getitem__(self, idx):
        return View(self.buf, self.ap[idx])

    def m(self, f):
        return View(self.buf, f(self.ap))


class Buf:
    __slots__ = ("ap", "w", "r", "name", "psum")

    def __init__(self, ap, name="", psum=False):
        self.ap = ap
        self.w = None
        self.r = {}
        self.name = name
        self.psum = psum

    def __getitem__(self, idx):
        return View(self, self.ap[idx])

    def v(self):
        return View(self, self.ap)


class Prog:
    ENGS = ("pe", "act", "dve", "pool", "sp")

    def __init__(self, nc, stack):
        self.nc = nc
        self.stack = stack
        self.q = {e: [] for e in self.ENGS}
        self.cnt = {e: 0 for e in self.ENGS}
        self.dman = {e: 0 for e in self.ENGS}
        self.waited = {e: {} for e in self.ENGS}
        self.sems = {}
        for e in self.ENGS:
            self.sems[("eng", e)] = stack.enter_context(nc.semaphore("s_" + e))
        for e in ("sp", "pool", "act"):
            for i in range(KDMA):
                self.sems[("dma", e, i)] = stack.enter_context(nc.semaphore(f"d_{e}{i}"))
        self.nalloc = 0
        self.psn = 0

    def sb(self, shape, dtype=F32, name=None):
        self.nalloc += 1
        t = self.stack.enter_context(self.nc.sbuf_tensor(f"{name or 't'}_{self.nalloc}", list(shape), dtype))
        return Buf(t[:] if hasattr(t, "__getitem__") else t, name or "t")

    def sb_ctx(self, shape, dtype=F32, name=None):
        self.nalloc += 1
        return self.nc.sbuf_tensor(f"{name or 't'}_{self.nalloc}", list(shape), dtype)

    def psum_banks(self):
        self.ps = []
        for i in range(8):
            t = self.stack.enter_context(self.nc.psum_tensor(f"ps{i}", [128, 512], F32))
            self.ps.append(Buf(t[:], f"ps{i}", psum=True))

    def next_ps(self):
        b = self.ps[self.psn % 8]
        self.psn += 1
        return b

    def dram(self, name, shape, dtype=F32, kind="Internal"):
        t = self.nc.dram_tensor(name, list(shape), dtype, kind=kind)
        return Buf(t.ap(), name)

    def op(self, E, fn, reads, writes, dma=False):
        deps = {}

        def add(tok):
            sk, val, eng = tok
            if deps.get(sk, (0, None))[0] < val:
                deps[sk] = (val, eng)

        for b in reads:
            if b.w is not None:
                add(b.w)
            if b.psum:
                for sk, (val, eng) in b.r.items():
                    if eng != E:
                        add((sk, val, eng))
        for b in writes:
            if b.w is not None:
                add(b.w)
            for sk, (val, eng) in b.r.items():
                add((sk, val, eng))
        waits = []
        for sk, (val, eng) in deps.items():
            if eng == "pe" and E == "pe" and not dma:
                continue
            if self.waited[E].get(sk, 0) >= val:
                continue
            self.waited[E][sk] = val
            waits.append((sk, val))
        if dma:
            n = self.dman[E]
            self.dman[E] += 1
            sk = ("dma", E, n % KDMA)
            val = 16 * (n // KDMA + 1)
            if val > 16 and self.waited[E].get(sk, 0) < val - 16:
                waits.append((sk, val - 16))
                self.waited[E][sk] = val - 16
            tok = (sk, val, "dma")
            inc = (sk, 16)
        else:
            self.cnt[E] += 1
            sk = ("eng", E)
            tok = (sk, self.cnt[E], E)
            inc = (sk, 1)
        self.q[E].append((waits, fn, inc))
        wset = set(id(b) for b in writes)
        for b in writes:
            b.w = tok
            b.r = {}
        for b in reads:
            if id(b) not in wset:
                b.r[tok[0]] = (tok[1], tok[2])
        return tok

    def barrier(self):
        cur = []
        for e in self.ENGS:
            if self.cnt[e] > 0:
                cur.append((("eng", e), self.cnt[e]))
        for e in ("sp", "pool", "act"):
            n = self.dman[e]
            for i in range(KDMA):
                k = (n - 1 - i)
                if k >= 0:
                    cur.append((("dma", e, k % KDMA), 16 * (k // KDMA + 1)))
        for E in self.ENGS:
            waits = []
            for sk, val in cur:
                if sk == ("eng", E):
                    continue
                if self.waited[E].get(sk, 0) >= val:
                    continue
                self.waited[E][sk] = val
                waits.append((sk, val))
            if waits:
                self.q[E].append((waits, None, None))

    def emit(self):
        nc = self.nc
        self.barrier()
        with nc.Block() as block:
            def replay(E, e):
                for waits, fn, inc in self.q[E]:
                    for sk, val in waits:
                        e.wait_ge(self.sems[sk], val)
                    if fn is not None:
                        ins = fn(e)
                        ins.then_inc(self.sems[inc[0]], inc[1])

            @block.tensor
            def _(e):
                replay("pe", e)

            @block.scalar
            def _(e):
                replay("act", e)

            @block.vector
            def _(e):
                replay("dve", e)

            @block.gpsimd
            def _(e):
                replay("pool", e)

            @block.sync
            def _(e):
                replay("sp", e)

    @staticmethod
    def _b(*xs):
        return [x.buf for x in xs if isinstance(x, View)]

    @staticmethod
    def _a(x):
        return x.ap if isinstance(x, View) else x

    def mm(self, out, lhsT, rhs, start=True, stop=True):
        o, l, r = out.ap, lhsT.ap, rhs.ap
        return self.op("pe", lambda e: e.matmul(o, l, r, start=start, stop=stop), self._b(lhsT, rhs), self._b(out))

    def tr(self, out, in_, ident):
        o, i, d = out.ap, in_.ap, ident.ap
        return self.op("pe", lambda e: e.transpose(o, i, d), self._b(in_, ident), self._b(out))

    def act(self, out, in_, func, bias=0.0, scale=1.0, accum=None):
        o, i, b, s = out.ap, in_.ap, self._a(bias), self._a(scale)
        ac = self._a(accum) if accum is not None else None
        kw = {}
        if ac is not None:
            kw["accum_out"] = ac
        return self.op("act", lambda e: e.activation(o, i, func, bias=b, scale=s, **kw),
                       self._b(in_, bias, scale), self._b(out) + (self._b(accum) if accum is not None else []))

    def tt(self, out, a, b, op, eng="dve"):
        o, x, y = out.ap, a.ap, b.ap
        return self.op(eng, lambda e: e.tensor_tensor(o, x, y, op), self._b(a, b), self._b(out))

    def ts(self, out, a, s1, s2, op0, op1=None, eng="dve", accum=None):
        o, x, p1, p2 = out.ap, a.ap, self._a(s1), self._a(s2)
        kw = {}
        if op1 is not None:
            kw["op1"] = op1
        if accum is not None:
            kw["accum_out"] = accum.ap
        return self.op(eng, lambda e: e.tensor_scalar(o, x, p1, p2, op0, **kw), self._b(a, s1, s2),
                       self._b(out) + (self._b(accum) if accum is not None else []))

    def stt(self, out, a, s, b, op0, op1, eng="dve"):
        o, x, p, y = out.ap, a.ap, self._a(s), b.ap
        return self.op(eng, lambda e: e.scalar_tensor_tensor(o, x, p, y, op0, op1), self._b(a, s, b), self._b(out))

    def copy(self, out, a, eng="dve"):
        o, x = out.ap, a.ap
        if eng == "act":
            return self.op("act", lambda e: e.copy(o, x), self._b(a), self._b(out))
        return self.op(eng, lambda e: e.tensor_copy(o, x), self._b(a), self._b(out))

    def memset(self, out, val, eng="dve"):
        o = out.ap
        return self.op(eng, lambda e: e.memset(o, val), [], self._b(out))

    def scan(self, out, d0, d1, init, op0, op1):
        o, a, b, i = out.ap, d0.ap, d1.ap, self._a(init)
        return self.op("dve", lambda e: e.tensor_tensor_scan(o, a, b, i, op0, op1), self._b(d0, d1, init), self._b(out))

    def dma(self, out, in_, q="sp"):
        o, i = out.ap, in_.ap
        return self.op(q, lambda e: e.dma_start(out=o, in_=i), self._b(in_), self._b(out), dma=True)


@contextlib.contextmanager
def scope(P):
    old = P.stack
    with contextlib.ExitStack() as st:
        P.stack = st
        try:
            yield
        finally:
            P.barrier()
            P.stack = old


D = 1024
KC = 8
DFF = 2816
NFF = 22
EPS = 1e-6


def fm_vec(v):
    v = np.asarray(v, np.float32)
    return np.ascontiguousarray(v.reshape(-1, 128).T)


class Blocks:
    def __init__(self, P, sizes, kinds, name="xT", dtype=F32):
        self.sizes = sizes
        self.kinds = kinds
        self.offs = np.concatenate([[0], np.cumsum(sizes)]).astype(int).tolist()
        self.T = self.offs[-1]
        self.t = P.stack.enter_context(P.nc.sbuf_tensor(name, [128, KC, self.T], dtype))
        self.b = [Buf(self.t[:, :, self.offs[i]:self.offs[i] + n], f"{name}{i}") for i, n in enumerate(sizes)]


def prep_mod(P, modT, gT, idx):
    out = {}
    for kind, mt in modT.items():
        gs = P.sb([128, KC], F32, "gs")
        hg = P.sb([128, KC], F32, "hg")
        P.stt(gs.v(), mt[:, 3 * idx + 1, :], 1.0, gT[:, idx, :], ALU.add, ALU.mult)
        P.ts(hg.v(), mt[:, 3 * idx + 2, :], 0.5 if idx != 1 else 1.0, None, ALU.mult)
        out[kind] = (gs, mt[:, 3 * idx + 0, :], hg)
    return out


def norm_mod(P, xb, hb, n, gs, sh, ones_bf, scr):
    sq, tmp, rstd = scr
    P.act(sq[:, :, :n], xb.v(), AF.Square)
    ps = P.next_ps()
    for k in range(KC):
        P.mm(ps[:, :n], ones_bf.v(), sq[:, k, :n], start=(k == 0), stop=(k == KC - 1))
    P.ts(rstd[:, :n], ps[:, :n], 1.0 / D, EPS, ALU.mult, ALU.add)
    P.act(rstd[:, :n], rstd[:, :n], AF.Ln)
    P.act(rstd[:, :n], rstd[:, :n], AF.Exp, scale=-0.5)
    for k in range(KC):
        P.stt(tmp[:, k, :n], xb[:, k, :], gs[:, k:k + 1], rstd[:, :n], ALU.mult, ALU.mult)
        P.act(hb[:, k, :], tmp[:, k, :n], AF.Identity, bias=sh[:, k:k + 1], scale=1.0)


def ffn(P, X, H, mods, wi, wo, ones_bf, scr, G=6):
    nblk = len(X.sizes)
    T = X.T
    with P.sb_ctx([128, G, T], BF16, "actT") as actT_t, \
            P.sb_ctx([128, 2, KC, 256], BF16, "wi_t") as wi_t, \
            P.sb_ctx([128, G, D], BF16, "wo_t") as wo_t, \
            P.sb_ctx([128, 2, 512], F32, "sil") as sil_t:
        actb = [[Buf(actT_t[:, g, X.offs[b]:X.offs[b] + X.sizes[b]]) for b in range(nblk)] for g in range(G)]
        wib = [Buf(wi_t[:, i]) for i in range(2)]
        wob = Buf(wo_t[:])
        silb = [Buf(sil_t[:, i]) for i in range(2)]
        wiv = wi.ap.rearrange("(k p) c -> p k c", p=128)
        wov = wo.ap.rearrange("(j p) f -> p j f", p=128)
        ns = 0
        nw = 0
        for g0 in range(0, NFF, G):
            gn = min(G, NFF - g0)
            for gi in range(gn):
                j = g0 + gi
                w = wib[nw % 2]
                nw += 1
                P.dma(w[:, :, 0:128], View(wi, wiv[:, :, j * 128:(j + 1) * 128]), q="pool")
                P.dma(w[:, :, 128:256], View(wi, wiv[:, :, DFF + j * 128:DFF + (j + 1) * 128]), q="pool")
                for b in range(nblk):
                    n = X.sizes[b]
                    pa = P.next_ps()
                    pu = P.next_ps()
                    for k in range(KC):
                        P.mm(pa[:, :n], w[:, k, 0:128], H.b[b][:, k, :], start=(k == 0), stop=(k == KC - 1))
                    for k in range(KC):
                        P.mm(pu[:, :n], w[:, k, 128:256], H.b[b][:, k, :], start=(k == 0), stop=(k == KC - 1))
                    s = silb[ns % 2]
                    ns += 1
                    P.act(s[:, :n], pa[:, :n], AF.Silu)
                    P.tt(actb[gi][b].v(), s[:, :n], pu[:, :n], ALU.mult)
            P.dma(wob[:, 0:gn, :], View(wo, wov[:, g0:g0 + gn, :]), q="pool")
            for b in range(nblk):
                n = X.sizes[b]
                hg = mods[X.kinds[b]][2]
                for f in range(KC):
                    ps = P.next_ps()
                    for gi in range(gn):
                        P.mm(ps[:, :n], wob[:, gi, f * 128:(f + 1) * 128], actb[gi][b].v(), start=(gi == 0), stop=(gi == gn - 1))
                    P.stt(X.b[b][:, f, :], ps[:, :n], hg[:, f:f + 1], X.b[b][:, f, :], ALU.mult, ALU.add)
        P.barrier()


TL = 2048
TC = 256
NCH_L = TL // 128
NCH_C = TC // 128
NCH = NCH_L + NCH_C
TT = TL + TC
DIN = 14432
COLS = {}
_o = 0
for _n, _s in (("ret_q", 512), ("ret_k", 512), ("ret_v", 1024), ("ret_g", 1024), ("gla_q", 512), ("gla_k", 512),
               ("gla_v", 1024), ("gla_r", 1024), ("gla_af", 16), ("gla_ab", 16), ("ssd_z", 2048), ("ssd_xbc", 3072),
               ("ssd_dt", 64), ("merge", 3072)):
    COLS[_n] = (_o, _s)
    _o += _s
assert _o == DIN
CI = {"ident": 0, "perm": 1, "mle": 2, "mge": 3, "sgt": 4, "slt": 5}


def host_consts():
    p = np.arange(128)[:, None]
    c = np.arange(128)[None, :]
    mats = [p == c, c == (p + 64) % 128, p <= c, p >= c, p > c, p < c]
    m = np.concatenate([x.astype(np.float32) for x in mats], axis=1)
    rows = np.concatenate([np.broadcast_to(np.arange(1, 129, dtype=np.float32), (128, 128)),
                           np.broadcast_to(np.arange(128, 0, -1).astype(np.float32), (128, 128))], axis=1)
    return np.ascontiguousarray(np.concatenate([m, rows], axis=1))


class Env:
    pass


def load_consts(P, env, cin):
    cf = P.sb([128, 8 * 128], F32, "cf")
    P.dma(cf.v(), cin.v())
    cb = P.sb([128, 6 * 128], BF16, "cb")
    P.copy(cb.v(), cf[:, 0:768])
    env.cf, env.cb = cf, cb
    env.ones_bf = P.sb([128, 128], BF16, "ones")
    P.memset(env.ones_bf.v(), 1.0)
    env.ones_f = P.sb([128, 128], F32, "onesf")
    P.memset(env.ones_f.v(), 1.0)

    def cm(name, bf=True):
        i = CI[name]
        return (cb if bf else cf)[:, i * 128:(i + 1) * 128]
    env.cm = cm
    env.row_up = cf[:, 768:896]
    env.row_dn = cf[:, 896:1024]


def softplus_parts(P, z, n, tmp1, tmp2):
    P.act(tmp1, z, AF.Abs)
    P.act(tmp1, tmp1, AF.Exp, scale=-1.0)
    P.act(tmp1, tmp1, AF.Ln, bias=1.0)
    P.ts(tmp2, z, 0.0, None, ALU.max)
    return tmp2, tmp1


def inproj(P, env, H, W, sp, lay, groups, do_q):
    nblk = len(H.sizes)
    tokblocks = [b for b in range(nblk) if H.kinds[b] != "halo"]
    halo = [b for b in range(nblk) if H.kinds[b] == "halo"][0]
    spoff = {}
    o = 0
    for b in tokblocks:
        spoff[b] = o
        o += H.sizes[b]
    wt_t = P.stack.enter_context(P.sb_ctx([128, 2, KC, 512], BF16, "win_t"))
    wt = [Buf(wt_t[:, i]) for i in range(2)]
    st = {"nw": 0}

    def loadw(c0, n):
        w = wt[st["nw"] % 2]
        st["nw"] += 1
        P.dma(w[:, :, 0:n], W[:, :, c0:c0 + n], q="pool")
        return w

    def fm_mm(w, wc0, m, b, ps=None):
        n = H.sizes[b]
        ps = ps or P.next_ps()
        for k in range(KC):
            P.mm(ps[0:m, :n], w[:, k, wc0:wc0 + m], H.b[b][:, k, :], start=(k == 0), stop=(k == KC - 1))
        return ps

    def do_qk(mixer):
        mi = 0 if mixer == "ret" else 1
        names = (["q"] if do_q else []) + ["k"]
        with scope(P):
            ec = P.stack.enter_context
            qf_t = ec(P.sb_ctx([128, 2, 512], F32, "qk_f"))
            qb_t = ec(P.sb_ctx([128, 2, 512], BF16, "qk_b"))
            qo_t = ec(P.sb_ctx([128, 4, 512], BF16, "qk_o"))
            g_t = ec(P.sb_ctx([128, 6, 512], F32, "g_t"))
            rope_t = ec(P.sb_ctx([128, 2, TL if mixer == "ret" else 2], F32, "rope"))
            code_t = ec(P.sb_ctx([16, 2, 512], BF16, "code"))
            la_t = ec(P.sb_ctx([128, 2, 512], F32, "la_t"))
            epm_t = ec(P.sb_ctx([128, 4, 512], F32, "epm_t"))
            epm = [[Buf(epm_t[:, 0]), Buf(epm_t[:, 1])], [Buf(epm_t[:, 2]), Buf(epm_t[:, 3])]]
            qf = [Buf(qf_t[:, i]) for i in range(2)]
            qb = [Buf(qb_t[:, i]) for i in range(2)]
            qo = [Buf(qo_t[:, i]) for i in range(4)]
            gt = [Buf(g_t[:, i]) for i in range(6)]
            la = [Buf(la_t[:, i]) for i in range(2)]
            code = [Buf(code_t[:, i]) for i in range(2)]
            rope = Buf(rope_t[:])
            if mixer == "ret":
                P.dma(rope.v(), lay["rope_in"].v())
            wq = {}
            for nm in names:
                c0, _ = COLS[f"{mixer}_{nm}"]
                wq[nm] = (c0,)
            if mixer == "gla":
                wc = loadw(COLS["gla_af"][0], 32)
                wcode = P.sb([128, KC, 32], BF16, "wcode")
                P.copy(wcode.v(), wc[:, :, 0:32])
            cnt = 0
            if mixer == "gla":
                egg = P.sb([128, 2, NCH, 4], F32, "egg")
            for b in tokblocks:
                n = H.sizes[b]
                nc_ = n // 128
                kind = H.kinds[b]
                if mixer == "gla":
                    for d in range(2):
                        ps = P.next_ps()
                        for k in range(KC):
                            P.mm(ps[0:16, :n], wcode[:, k, d * 16:(d + 1) * 16], H.b[b][:, k, :], start=(k == 0), stop=(k == KC - 1))
                        P.copy(code[d][:, :n], ps[0:16, :n], eng="act")
                if mixer == "ret":
                    for nm in names:
                        w = loadw(wq[nm][0], 512)
                        var0 = 0 if nm == "q" else 1
                        for h in range(4):
                            ps = fm_mm(w, h * 128, 128, b)
                            cnt += 1
                            f32 = qf[cnt % 2]
                            sc = 128 ** -0.5 if nm == "k" else 1.0
                            if kind == "lat":
                                bfv = qb[cnt % 2]
                                P.act(bfv[:, :n], ps[:, :n], AF.Copy, scale=sc)
                                ps2 = P.next_ps()
                                P.mm(ps2[:, :n], env.cm("perm"), bfv[:, :n])
                                t0 = H.offs[b]
                                P.tt(f32[:, :n], bfv[:, :n], rope[:, 0, t0:t0 + n], ALU.mult)
                                P.tt(gt[0][:, :n], ps2[:, :n], rope[:, 1, t0:t0 + n], ALU.mult)
                                P.tt(f32[:, :n], f32[:, :n], gt[0][:, :n], ALU.add)
                            else:
                                P.act(f32[:, :n], ps[:, :n], AF.Copy, scale=sc)
                            for d in range(2):
                                sgn = 0 if nm == "q" else 1
                                tab = lay["ret_tab"][:, d, sgn, h, :].m(lambda a: a.unsqueeze(1).to_broadcast([128, nc_, 128]))
                                o = qo[(cnt * 2 + d) % 4]
                                P.tt(o[:, :n].m(lambda a: a.rearrange("p (c t) -> p c t", t=128)),
                                     f32[:, :n].m(lambda a: a.rearrange("p (c t) -> p c t", t=128)), tab, ALU.mult)
                                P.dma(sp["qk"][mi, h, 2 * d + var0, :, spoff[b]:spoff[b] + n], o[:, :n])
                else:
                    ws = {nm: loadw(wq[nm][0], 512) for nm in names}
                    for h in range(4):
                        for d in range(2):
                            psz = P.next_ps()
                            P.mm(psz[:, :n], lay["wa2"][:, d, h * 128:(h + 1) * 128], code[d][:, :n])
                            z = gt[1]
                            P.ts(z[:, :n], psz[:, :n], lay["gla_ba"][:, d, h:h + 1], None, ALU.add)
                            mx, l = softplus_parts(P, z[:, :n], n, gt[2][:, :n], gt[3][:, :n])
                            P.ts(gt[3][:, :n], z[:, :n], 0.0, None, ALU.min)
                            dd = gt[4]
                            P.tt(dd[:, :n], gt[3][:, :n], gt[2][:, :n], ALU.subtract)
                            cs = gt[5]
                            for c in range(nc_):
                                sl = slice(c * 128, (c + 1) * 128)
                                P.scan(cs[:, sl], env.ones_f[:, 0:128], dd[:, sl], 0.0, ALU.mult, ALU.add)
                            g = la[d]
                            if d == 0:
                                g = cs
                            else:
                                P.tt(g[:, :n], dd[:, :n], cs[:, :n], ALU.subtract)
                                for c in range(nc_):
                                    sl = slice(c * 128, (c + 1) * 128)
                                    P.ts(g[:, sl], g[:, sl], cs[:, c * 128 + 127:c * 128 + 128], None, ALU.add)
                            ch0 = spoff[b] // 128
                            for c in range(nc_):
                                P.act(egg[:, d, ch0 + c, h:h + 1], cs[:, c * 128 + 127:c * 128 + 128], AF.Exp, scale=1.0 / 16)
                            if do_q:
                                P.act(epm[0][d][:, :n], g[:, :n], AF.Exp, scale=1.0 / 16)
                            P.act(epm[1][d][:, :n], g[:, :n], AF.Exp, scale=-1.0 / 16)
                        for nm in names:
                            var0 = 0 if nm == "q" else 1
                            ps = fm_mm(ws[nm], h * 128, 128, b)
                            cnt += 1
                            f32 = qf[cnt % 2]
                            sc = 128 ** -0.5 if nm == "q" else 1.0
                            P.act(f32[:, :n], ps[:, :n], AF.Copy, scale=sc)
                            for d in range(2):
                                o = qo[(cnt * 2 + d) % 4]
                                P.tt(o[:, :n], f32[:, :n], epm[var0][d][:, :n], ALU.mult)
                                P.dma(sp["qk"][mi, h, 2 * d + var0, :, spoff[b]:spoff[b] + n], o[:, :n])
            if mixer == "gla":
                for d in range(2):
                    P.dma(sp["eG"][mi * 2 + d], egg[:, d])
            if mixer == "ret":
                egr = P.sb([128, 2, NCH, 4], F32, "egr")
                for d in range(2):
                    P.copy(egr[:, d, :, :], lay["ret_eg"][:, d, :].m(lambda a: a.unsqueeze(1).to_broadcast([128, NCH, 4])))
                    P.dma(sp["eG"][mi * 2 + d], egr[:, d])
            P.barrier()

    def do_tm(name, handler, width=512):
        c0, sz = COLS[name]
        for cc in range(0, sz, width):
            n_ = min(width, sz - cc)
            w = loadw(c0 + cc, n_)
            for b in tokblocks:
                for c in range(H.sizes[b] // 128):
                    ps = P.next_ps()
                    for k in range(KC):
                        P.mm(ps[:, :n_], H.b[b][:, k, c * 128:(c + 1) * 128], w[:, k, 0:n_], start=(k == 0), stop=(k == KC - 1))
                    handler(ps, n_, cc, spoff[b] + c * 128)

    ev = {"n": 0}
    evt_t = P.stack.enter_context(P.sb_ctx([128, 4, 512], BF16, "evt"))
    evt = [Buf(evt_t[:, i]) for i in range(4)]

    def h_copy(dst, col0, func=AF.Copy):
        def h(ps, n_, cc, tok):
            e = evt[ev["n"] % 4]
            ev["n"] += 1
            if func == AF.Copy and ev["n"] % 2 == 0:
                P.copy(e[:, :n_], ps[:, :n_])
            else:
                P.act(e[:, :n_], ps[:, :n_], func)
            P.dma(dst[tok:tok + 128, col0 + cc:col0 + cc + n_], e[:, :n_])
        return h

    def h_dt(ps, n_, cc, tok):
        z = P.sb([128, 64], F32, "dtz")
        t1 = P.sb([128, 64], F32, "dt1")
        t2 = P.sb([128, 64], F32, "dt2")
        P.tt(z.v(), ps[:, 0:64], lay["dt_bias"].v(), ALU.add)
        mx, l = softplus_parts(P, z.v(), 64, t1.v(), t2.v())
        P.tt(z.v(), mx, l, ALU.add)
        P.tt(t1.v(), z.v(), lay["ssd_A"].v(), ALU.mult)
        P.dma(sp["dt"][tok:tok + 128, :], z.v())
        P.dma(sp["a"][tok:tok + 128, :], t1.v())

    def do_xbc(chunks):
        c0, _ = COLS["ssd_xbc"]
        with scope(P):
            ec = P.stack.enter_context
            rawl_t = ec(P.sb_ctx([128, 2, TL + 4], F32, "rawl"))
            rawc_t = ec(P.sb_ctx([128, 2, TC + 4], F32, "rawc"))
            acc_t = ec(P.sb_ctx([128, 2, TL], F32, "acc"))
            xs_t = ec(P.sb_ctx([128, 4, TT], BF16, "xsT"))
            xtm_t = ec(P.sb_ctx([128, 2, 512], BF16, "xtm"))
            rawl = [Buf(rawl_t[:, i]) for i in range(2)]
            rawc = [Buf(rawc_t[:, i]) for i in range(2)]
            acc = [Buf(acc_t[:, i]) for i in range(2)]
            xs = [Buf(xs_t[:, i]) for i in range(4)]
            xtm = [Buf(xtm_t[:, i]) for i in range(2)]
            for i in range(2):
                P.memset(rawc[i][:, 0:2], 0.0)
                P.memset(rawc[i][:, TC + 2:TC + 4], 0.0)
            w = None
            nx = 0
            for ci, ch in enumerate(chunks):
                if ci % 4 == 0 or w is None:
                    w = loadw(c0 + ch * 128, 512)
                    wbase = ch
                rl, rc, ac = rawl[ci % 2], rawc[ci % 2], acc[ci % 2]
                for b in range(nblk):
                    n = H.sizes[b]
                    ps = fm_mm(w, (ch - wbase) * 128, 128, b)
                    if H.kinds[b] == "lat":
                        P.copy(rl[:, 2 + H.offs[b]:2 + H.offs[b] + n], ps[:, :n], eng=("act" if b % 2 else "dve"))
                    elif H.kinds[b] == "ctx":
                        P.copy(rc[:, 2:2 + n], ps[:, :n], eng="act")
                    else:
                        P.ts(rl[:, 0:2], ps[:, 0:2], lay["halo_valid"][:, 0:1], None, ALU.mult)
                        P.ts(rl[:, TL + 2:TL + 4], ps[:, 2:4], lay["halo_valid"][:, 1:2], None, ALU.mult)
                for (raw, T_, o_) in ((rl, TL, 0), (rc, TC, TL)):
                    a = ac[:, 0:T_]
                    P.ts(a, raw[:, 0:T_], lay["conv_w"][:, 0, ch:ch + 1], lay["conv_b"][:, ch:ch + 1], ALU.mult, ALU.add)
                    for o in range(1, 5):
                        P.stt(a, raw[:, o:o + T_], lay["conv_w"][:, o, ch:ch + 1], a, ALU.mult, ALU.add,
                              eng=("dve" if o % 2 else "pool") if False else "dve")
                    if ch < 16:
                        P.act(xs[ci % 4][:, o_:o_ + T_], a, AF.Silu)
                    else:
                        e = xs[ci % 4]
                        P.act(e[:, o_:o_ + T_], a, AF.Silu)
                        which = "BT" if ch < 20 else "CT"
                        P.dma(sp[which][(ch - 16) % 4, :, o_:o_ + T_], e[:, o_:o_ + T_])
                if ch < 16 and ci % 4 == 3:
                    for tcn in range(NCH):
                        psb = P.next_ps()
                        pv = psb.v().m(lambda a: a.bitcast(BF16))
                        for q4 in range(4):
                            P.tr(pv[:, q4 * 128:(q4 + 1) * 128], xs[q4][:, tcn * 128:(tcn + 1) * 128], env.cm("ident"))
                        xo = xtm[nx % 2]
                        nx += 1
                        P.copy(xo.v(), pv[:, 0:512], eng=("act" if nx % 2 else "dve"))
                        P.dma(sp["x"][tcn * 128:(tcn + 1) * 128, (ch - 3) * 128:(ch + 1) * 128], xo.v())
            P.barrier()

    for grp in groups:
        if grp in ("ret", "gla"):
            do_qk(grp)
        elif grp == "v":
            do_tm("ret_v", h_copy(sp["v"][0], 0))
            do_tm("gla_v", h_copy(sp["v"][1], 0))
        elif grp == "gates":
            do_tm("ret_g", h_copy(sp["gate"], 0, AF.Silu))
            do_tm("gla_r", h_copy(sp["gate"], 1024, AF.Silu))
            do_tm("ssd_z", h_copy(sp["gate"], 2048, AF.Silu))
        elif grp == "dt":
            do_tm("ssd_dt", h_dt, width=64)
        elif grp == "xB":
            do_xbc(list(range(0, 20)))
        elif grp == "xBC":
            do_xbc(list(range(0, 24)))
        elif grp == "merge":
            c0, sz = COLS["merge"]
            for cc in range(0, sz, 512):
                w = loadw(c0 + cc, 512)
                for q4 in range(4):
                    for b in tokblocks:
                        n = H.sizes[b]
                        ps = fm_mm(w, q4 * 128, 128, b)
                        e = evt[ev["n"] % 4]
                        ev["n"] += 1
                        P.act(e[:, :n], ps[:, :n], AF.Sigmoid)
                        P.dma(sp["sg"][cc // 128 + q4, :, spoff[b]:spoff[b] + n], e[:, :n])
    P.barrier()


def bc(v, axis, shape):
    return v.m(lambda a: a.unsqueeze(axis).to_broadcast(list(shape)))


def r3(v, pat, **kw):
    return v.m(lambda a: a.rearrange(pat, **kw))


class SweepBufs:
    def __init__(self, P, do_y, with_post):
        st = P.stack

        def mk(shape, dt, n, name):
            t = st.enter_context(P.sb_ctx([128, n] + list(shape), dt, name))
            return [Buf(t[:, i]) for i in range(n)]
        self.kT = [mk([4, 128], BF16, 2, "kT0"), mk([4, 128], BF16, 2, "kT1")]
        self.v = [mk([1024], BF16, 2, "v0"), mk([1024], BF16, 2, "v1")]
        self.eG = [mk([4], F32, 2, "eG0"), mk([4], F32, 2, "eG1")]
        self.BT = mk([4, 128], BF16, 2, "BT")
        self.x = mk([2048], BF16, 2, "xtm")
        self.a = mk([64], F32, 2, "a")
        self.dt = mk([64], F32, 2, "dt")
        self.ktm = mk([512], BF16, 2, "ktm")
        self.btm = mk([512], BF16, 1, "btm")
        self.eT = mk([96], F32, 2, "eT")
        self.wv = mk([32], F32, 1, "wv")
        self.vs = mk([2048], BF16, 1, "vs")
        if do_y:
            self.qT = [mk([4, 128], BF16, 2, "qT0"), mk([4, 128], BF16, 2, "qT1")]
            self.CT = mk([4, 128], BF16, 2, "CT")
            self.attT = mk([4, 128], BF16, 2, "attT")
            self.vdt = mk([2048], BF16, 1, "vdt")
            self.cbm = mk([128], BF16, 2, "cbm")
            self.rhsD = mk([8, 128], BF16, 2, "rhsD")
            self.L = mk([4, 128], BF16, 2, "L")
            self.att = mk([8, 128], BF16, 2, "att")
            self.tmp = mk([512], F32, 2, "tmp")
            self.yall = mk([4096], F32, 2, "yall")
            self.yb = mk([4096], BF16, 2, "yb")
        if with_post:
            self.gate = mk([4096], BF16, 2, "gate")
            self.ss = mk([12], F32, 2, "ss")
            self.junk = mk([512], F32, 1, "junk")
            self.ygate = mk([4096], BF16, 1, "ygate")
            self.ygT = mk([32, 128], BF16, 1, "ygT")


def sweep(P, env, sp, lay, d, chain, S, SB, do_y, B, mode, Dt=None):
    msk_f32 = env.cm("mle" if d == 0 else "mge", bf=False)
    msk_bf = env.cm("mle" if d == 0 else "mge")
    sl_f32 = env.cm("sgt" if d == 0 else "slt", bf=False)
    sl_bf = env.cm("sgt" if d == 0 else "slt")
    ident = env.cm("ident")
    post = mode == "post"

    def bfview(psb):
        return psb.v().m(lambda ap: ap.bitcast(BF16))

    def front(it, c):
        par = it % 2
        t0 = c * 128
        kT = [B.kT[m][par] for m in range(2)]
        v = [B.v[m][par] for m in range(2)]
        eG = [B.eG[m][par] for m in range(2)]
        BT, x, a, dt = B.BT[par], B.x[par], B.a[par], B.dt[par]
        for m in range(2):
            P.dma(kT[m].v(), View(sp["qk"], sp["qk"].ap[m, :, 2 * d + 1, :, t0:t0 + 128].rearrange("h p t -> p h t")))
            P.dma(v[m].v(), sp["v"][m, t0:t0 + 128, :])
            P.dma(eG[m].v(), sp["eG"][m * 2 + d, :, c, :])
        P.dma(BT.v(), View(sp["BT"], sp["BT"].ap[:, :, t0:t0 + 128].rearrange("g p t -> p g t")))
        P.dma(x.v(), sp["x"][t0:t0 + 128, :])
        P.dma(a.v(), sp["a"][t0:t0 + 128, :])
        P.dma(dt.v(), sp["dt"][t0:t0 + 128, :])
        if do_y:
            qT = [B.qT[m][par] for m in range(2)]
            CT = B.CT[par]
            for m in range(2):
                P.dma(qT[m].v(), View(sp["qk"], sp["qk"].ap[m, :, 2 * d, :, t0:t0 + 128].rearrange("h p t -> p h t")))
            P.dma(CT.v(), View(sp["CT"], sp["CT"].ap[:, :, t0:t0 + 128].rearrange("g p t -> p g t")))
            yall = B.yall[par]
        if post:
            yb, gate = B.yb[par], B.gate[par]
            P.dma(yb.v(), sp["yb"][t0:t0 + 128, :])
            P.dma(gate.v(), sp["gate"][t0:t0 + 128, :])
        a_d = a[:, d * 32:(d + 1) * 32]
        dt_d = dt[:, d * 32:(d + 1) * 32]
        x3 = r3(x.v(), "p (h e) -> p h e", e=64)
        ktm = [B.ktm[0], B.ktm[1]]
        btm = B.btm[0]
        for m in range(2):
            pv = bfview(P.next_ps())
            for h in range(4):
                P.tr(pv[:, h * 128:(h + 1) * 128], kT[m][:, h, :], ident)
            P.copy(ktm[m].v(), pv[:, 0:512], eng="act")
        pv = bfview(P.next_ps())
        for g in range(4):
            P.tr(pv[:, g * 128:(g + 1) * 128], BT[:, g, :], ident)
        P.copy(btm.v(), pv[:, 0:512], eng="act")
        pss = P.next_ps()
        P.mm(pss[:, 0:32], env.ones_f.v(), a_d)
        P.mm(pss[:, 32:64], sl_f32, a_d)
        if do_y:
            P.mm(pss[:, 64:96], msk_f32, a_d)
        ne = 96 if do_y else 64
        eT = B.eT[par]
        P.act(eT[:, 0:ne], pss[:, 0:ne], AF.Exp)
        if Dt is not None:
            P.tt(Dt[2].v(), Dt[2].v(), pss[:, 0:32], ALU.add)
        if do_y:
            attT = [B.attT[0], B.attT[1]]
            for m in range(2):
                aps = P.next_ps()
                for h in range(4):
                    P.mm(aps[:, h * 128:(h + 1) * 128], kT[m][:, h, :], qT[m][:, h, :])
                P.tt(attT[m].v(), r3(aps.v(), "p (h t) -> p h t", h=4), bc(msk_f32, 1, [128, 4, 128]), ALU.mult)
        wv = B.wv[0]
        P.tt(wv.v(), eT[:, 32:64], dt_d, ALU.mult)
        vs = B.vs[0]
        P.tt(r3(vs.v(), "p (h e) -> p h e", e=64), x3, bc(wv.v(), 2, [128, 32, 64]), ALU.mult)
        ne_ = 0
        for m in range(2):
            if do_y:
                for hp in range(2):
                    yp = P.next_ps()
                    for hh in range(2):
                        h = hp * 2 + hh
                        o = yp[:, hh * 256:(hh + 1) * 256]
                        if post:
                            P.mm(o, ident, yb[:, m * 1024 + h * 256:m * 1024 + (h + 1) * 256], start=True, stop=False)
                        P.mm(o, attT[m][:, h, :], v[m][:, h * 256:(h + 1) * 256], start=(not post), stop=False)
                        P.mm(o, qT[m][:, h, :], SB[m][:, h * 256:(h + 1) * 256], start=False, stop=True)
                    cs = slice(m * 1024 + hp * 512, m * 1024 + (hp + 1) * 512)
                    ne_ += 1
                    P.copy(yall[:, cs], yp.v(), eng=("act" if ne_ % 2 else "dve"))
            for hp in range(2):
                ps2 = P.next_ps()
                for hh in range(2):
                    h = hp * 2 + hh
                    P.mm(ps2[:, hh * 256:(hh + 1) * 256], ktm[m][:, h * 128:(h + 1) * 128], v[m][:, h * 256:(h + 1) * 256])
                for hh in range(2):
                    h = hp * 2 + hh
                    hc = slice(h * 256, (h + 1) * 256)
                    P.act(S[m][:, hc], S[m][:, hc], AF.Copy, scale=eG[m][:, h:h + 1])
                    P.stt(S[m][:, hc], ps2[:, hh * 256:(hh + 1) * 256], eG[m][:, h:h + 1], S[m][:, hc], ALU.mult, ALU.add)
            P.copy(SB[m].v(), S[m].v(), eng="act")
            if Dt is not None:
                P.tt(Dt[m].v(), Dt[m].v(), eG[m].v(), ALU.mult)
        if do_y:
            vdt = B.vdt[0]
            vdt3 = r3(vdt.v(), "p (h e) -> p h e", e=64)
            P.tt(vdt3, x3, bc(dt_d, 2, [128, 32, 64]), ALU.mult)
            for g in range(4):
                cb = P.next_ps()
                P.mm(cb[:, 0:128], BT[:, g, :], CT[:, g, :])
                cbm = B.cbm[g % 2]
                P.tt(cbm.v(), cb[:, 0:128], msk_f32, ALU.mult)
                rhsD = B.rhsD[g % 2]
                P.tt(rhsD.v(), bc(msk_bf, 1, [128, 8, 128]), bc(a_d[:, g * 8:(g + 1) * 8], 2, [128, 8, 128]), ALU.mult)
                att = B.att[g % 2]
                for half in range(2):
                    dps = P.next_ps()
                    P.mm(dps.v(), sl_bf, r3(rhsD[:, half * 4:(half + 1) * 4, :], "p h t -> p (h t)"))
                    L = B.L[half]
                    P.act(r3(L.v(), "p h t -> p (h t)"), dps.v(), AF.Exp)
                    P.tt(att[:, half * 4:(half + 1) * 4, :], L.v(), bc(cbm.v(), 1, [128, 4, 128]), ALU.mult)
                yp = P.next_ps()
                cs = slice(2048 + g * 512, 2048 + (g + 1) * 512)
                if post:
                    P.mm(yp.v(), ident, yb[:, cs], start=True, stop=False)
                for h in range(8):
                    hs = slice(h * 64, (h + 1) * 64)
                    if post:
                        P.mm(yp[:, hs], att[:, h, :], vdt3[:, g * 8 + h, :], start=False, stop=False)
                        P.mm(yp[:, hs], lay["dI"][:, g * 8 + h, :], x3[:, g * 8 + h, :], start=False, stop=(h == 7))
                    else:
                        P.mm(yp[:, hs], att[:, h, :], vdt3[:, g * 8 + h, :])
                yi = P.next_ps()
                P.mm(yi.v(), CT[:, g, :], SB[2][:, g * 512:(g + 1) * 512])
                tmp = B.tmp[g % 2]
                P.tt(r3(tmp.v(), "p (h e) -> p h e", e=64), r3(yi.v(), "p (h e) -> p h e", e=64),
                     bc(eT[:, 64 + g * 8:64 + (g + 1) * 8], 2, [128, 8, 64]), ALU.mult)
                P.tt(yall[:, cs], tmp.v(), yp.v(), ALU.add)
        P.tt(r3(S[2].v(), "p (h e) -> p h e", e=64), r3(S[2].v(), "p (h e) -> p h e", e=64),
             bc(eT[:, 0:32], 2, [128, 32, 64]), ALU.mult)
        for g in range(4):
            ps = P.next_ps()
            P.mm(ps.v(), btm[:, g * 128:(g + 1) * 128], vs[:, g * 512:(g + 1) * 512])
            gc = slice(g * 512, (g + 1) * 512)
            P.tt(S[2][:, gc], S[2][:, gc], ps.v(), ALU.add)
        P.copy(SB[2].v(), S[2].v(), eng="act")

    def back(it, c):
        par = it % 2
        t0 = c * 128
        yall = B.yall[par]
        if mode == "yb":
            ybo = B.yb[par]
            P.copy(ybo[:, 0:2048], yall[:, 0:2048], eng="act")
            P.copy(ybo[:, 2048:4096], yall[:, 2048:4096], eng="dve")
            P.dma(sp["yb"][t0:t0 + 128, :], ybo.v())
            return
        gate = B.gate[par]
        ss, junk, ygate = B.ss[par], B.junk[0], B.ygate[0]
        P.tt(yall[:, 2048:4096], yall[:, 2048:4096], gate[:, 2048:4096], ALU.mult)
        for h in range(8):
            P.act(junk[:, 0:256], yall[:, h * 256:(h + 1) * 256], AF.Square, accum=ss[:, h:h + 1])
        for g in range(4):
            P.act(junk[:, 0:512], yall[:, 2048 + g * 512:2048 + (g + 1) * 512], AF.Square, accum=ss[:, 8 + g:9 + g])
        P.ts(ss[:, 0:8], ss[:, 0:8], 1.0 / 256, EPS, ALU.mult, ALU.add)
        P.ts(ss[:, 8:12], ss[:, 8:12], 1.0 / 512, EPS, ALU.mult, ALU.add)
        P.act(ss.v(), ss.v(), AF.Ln)
        P.act(ss.v(), ss.v(), AF.Exp, scale=-0.5)
        for h in range(4):
            hc = slice(h * 256, (h + 1) * 256)
            P.stt(ygate[:, hc], yall[:, hc], ss[:, h:h + 1], gate[:, hc], ALU.mult, ALU.mult)
        for h in range(4):
            hc = slice(1024 + h * 256, 1024 + (h + 1) * 256)
            P.stt(junk[:, 256:512], yall[:, hc], ss[:, 4 + h:5 + h], gate[:, hc], ALU.mult, ALU.mult)
            P.tt(ygate[:, hc], junk[:, 256:512], lay["gla_g"].v(), ALU.mult)
        for g in range(4):
            gc = slice(2048 + g * 512, 2048 + (g + 1) * 512)
            P.stt(ygate[:, gc], yall[:, gc], ss[:, 8 + g:9 + g], lay["ssd_ng"][:, g * 512:(g + 1) * 512], ALU.mult, ALU.mult)
        ygT = B.ygT[0]
        for q in range(4):
            pv = bfview(P.next_ps())
            for k8 in range(8):
                k = q * 8 + k8
                P.tr(pv[:, k8 * 128:(k8 + 1) * 128], ygate[:, k * 128:(k + 1) * 128], ident)
            P.copy(r3(ygT[:, q * 8:(q + 1) * 8, :], "p k t -> p (k t)"), pv[:, 0:1024], eng=("act" if q % 2 else "dve"))
        P.dma(View(sp["ygT"], sp["ygT"].ap[:, :, t0:t0 + 128].rearrange("k p t -> p k t")), ygT.v())

    prev = None
    for it, c in enumerate(chain):
        front(it, c)
        if mode != "state":
            if prev is not None:
                back(*prev)
            prev = (it, c)
    if prev is not None:
        back(*prev)


def phase_c(P, env, sp, X, spoff, gate5, wbL, woL, blks):
    with scope(P):
        yg_t = P.stack.enter_context(P.sb_ctx([128, 32, 512], BF16, "ygblk"))
        sg_t = P.stack.enter_context(P.sb_ctx([128, 24, 512], BF16, "sgblk"))
        wb_t = P.stack.enter_context(P.sb_ctx([128, 2, 32, 128], BF16, "wbf"))
        wo_t = P.stack.enter_context(P.sb_ctx([128, 2, 8, 128], BF16, "wof"))
        mg_t = P.stack.enter_context(P.sb_ctx([128, 8, 512], BF16, "mg"))
        t_t = P.stack.enter_context(P.sb_ctx([128, 3, 512], F32, "t123"))
        yg, sg, mg = Buf(yg_t[:]), Buf(sg_t[:]), Buf(mg_t[:])
        wbf = [Buf(wb_t[:, i]) for i in range(2)]
        wof = [Buf(wo_t[:, i]) for i in range(2)]
        t = [Buf(t_t[:, i]) for i in range(3)]
        nw = 0
        for b in blks:
            n = X.sizes[b]
            o = spoff[b]
            P.dma(yg[:, :, :n], View(sp["ygT"], sp["ygT"].ap[:, :, o:o + n].rearrange("k p t -> p k t")))
            P.dma(sg[:, :, :n], View(sp["sg"], sp["sg"].ap[:, :, o:o + n].rearrange("k p t -> p k t")))
            for f in range(KC):
                w = wbf[nw % 2]
                nw += 1
                P.dma(w.v(), wbL[f], q="pool")
                for i, (k0, k1) in enumerate(((0, 8), (8, 16), (16, 32))):
                    ps = P.next_ps()
                    for k in range(k0, k1):
                        P.mm(ps[:, :n], w[:, k, :], yg[:, k, :n], start=(k == k0), stop=(k == k1 - 1))
                    P.tt(t[i][:, :n], ps[:, :n], sg[:, i * 8 + f, :n], ALU.mult, eng=("dve" if i != 1 else "pool") if False else "dve")
                P.tt(t[0][:, :n], t[0][:, :n], t[1][:, :n], ALU.add)
                P.tt(mg[:, f, :n], t[0][:, :n], t[2][:, :n], ALU.add)
            hg = gate5[X.kinds[b]]
            for f in range(KC):
                w = wof[f % 2]
                P.dma(w.v(), woL[f], q="pool")
                ps = P.next_ps()
                for k in range(KC):
                    P.mm(ps[:, :n], w[:, k, :], mg[:, k, :n], start=(k == 0), stop=(k == KC - 1))
                P.stt(X.b[b][:, f, :], ps[:, :n], hg[:, f:f + 1], X.b[b][:, f, :], ALU.mult, ALU.add)


def load_small(P, ins, names):
    lay = {}
    for nm in names:
        b = ins[nm]
        shape = list(b.ap.shape)
        t = P.sb(shape, F32, "l_" + nm)
        P.dma(t.v(), b.v())
        lay[nm] = t
    return lay


def derive_layer(P, env, lay, need_post):
    lg = P.sb([128, 8], F32, "lg")
    t1 = P.sb([128, 8], F32, "lgt1")
    t2 = P.sb([128, 8], F32, "lgt2")
    P.act(t1.v(), lay["ret_logit"].v(), AF.Abs)
    P.act(t1.v(), t1.v(), AF.Exp, scale=-1.0)
    P.act(t1.v(), t1.v(), AF.Ln, bias=1.0)
    P.ts(t2.v(), lay["ret_logit"].v(), 0.0, None, ALU.min)
    P.tt(lg.v(), t2.v(), t1.v(), ALU.subtract)
    nlg = P.sb([128, 8], F32, "nlg")
    P.ts(nlg.v(), lg.v(), -1.0, None, ALU.mult)
    tab = P.sb([128, 2, 2, 4, 128], F32, "ret_tab")
    for d in range(2):
        row = env.row_up if d == 0 else env.row_dn
        for h in range(4):
            P.act(tab[:, d, 0, h, :], row, AF.Exp, scale=lg[:, d * 4 + h:d * 4 + h + 1])
            P.act(tab[:, d, 1, h, :], row, AF.Exp, scale=nlg[:, d * 4 + h:d * 4 + h + 1])
    lay["ret_tab"] = tab
    eg = P.sb([128, 2, 4], F32, "ret_eg")
    P.act(eg.v().m(lambda a: a.rearrange("p d h -> p (d h)")), lg.v(), AF.Exp, scale=128.0)
    lay["ret_eg"] = eg
    wa2b = P.sb([16, 2, 512], BF16, "wa2b")
    P.copy(wa2b.v(), lay["wa2"].v())
    lay["wa2"] = wa2b
    A = P.sb([128, 64], F32, "ssdA")
    P.act(A.v(), lay["a_log"].v(), AF.Exp)
    P.ts(A.v(), A.v(), -1.0, None, ALU.mult)
    lay["ssd_A"] = A


SMALL_COMMON = ["mod_lat", "mod_ctx", "norm_g", "ret_logit", "wa2", "gla_ba", "conv_w", "conv_b", "dt_bias", "a_log",
                "halo_valid"]
SMALL_SHAPES = {"mod_lat": [128, 9, 8], "mod_ctx": [128, 9, 8], "norm_g": [128, 3, 8], "ret_logit": [128, 8],
                "wa2": [16, 2, 512], "gla_ba": [128, 2, 4], "conv_w": [128, 5, 24], "conv_b": [128, 24],
                "dt_bias": [128, 64], "a_log": [128, 64], "halo_valid": [128, 2],
                "gla_g": [128, 256], "ssd_d": [128, 32], "ssd_ng": [128, 2048], "final_g": [128, 8]}
SIZES = [512, 512, 512, 512, TC, 4]
KINDS = ["lat", "lat", "lat", "lat", "ctx", "halo"]
TIN = TL + TC + 4


def make_spill(P, full):
    sp = {}
    sp["qk"] = P.dram("sp_qk", [2, 4, 4, 128, TT], BF16)
    sp["eG"] = P.dram("sp_eG", [4, 128, NCH, 4], F32)
    sp["v"] = P.dram("sp_v", [2, TT, 1024], BF16)
    sp["BT"] = P.dram("sp_BT", [4, 128, TT], BF16)
    sp["x"] = P.dram("sp_x", [TT, 2048], BF16)
    sp["a"] = P.dram("sp_a", [TT, 64], F32)
    sp["dt"] = P.dram("sp_dt", [TT, 64], F32)
    if full:
        sp["CT"] = P.dram("sp_CT", [4, 128, TT], BF16)
        sp["gate"] = P.dram("sp_gate", [TT, 4096], BF16)
        sp["sg"] = P.dram("sp_sg", [24, 128, TT], BF16)
        sp["yb"] = P.dram("sp_yb", [TT, 4096], BF16)
        sp["ygT"] = P.dram("sp_ygT", [32, 128, TT], BF16)
    return sp


def alloc_states(P):
    S = [P.sb([128, 1024], F32, "S_ret"), P.sb([128, 1024], F32, "S_gla"), P.sb([128, 2048], F32, "S_ssd")]
    SB = [P.sb([128, 1024], BF16, "SB_ret"), P.sb([128, 1024], BF16, "SB_gla"), P.sb([128, 2048], BF16, "SB_ssd")]
    return S, SB


def zero_states(P, S, SB):
    for i in range(3):
        P.memset(S[i].v(), 0.0, eng="pool")
        P.memset(SB[i].v(), 0.0, eng="pool")


def build_layer(phase, last=False):
    nc = bass.Bass("TRN2", target_bir_lowering=False)
    with contextlib.ExitStack() as stack:
        P = Prog(nc, stack)
        P.psum_banks()
        env = Env()
        ins = {}

        def inp(name, shape, dt=F32):
            ins[name] = P.dram(name, shape, dt, "ExternalInput")
            return ins[name]
        inp("cin", [128, 1024])
        inp("xin", [D, TIN])
        for nm in SMALL_COMMON:
            inp(nm, SMALL_SHAPES[nm])
        inp("rope_in", [128, 2, TL])
        inp("w_inL", [128, KC, DIN])
        if phase == 1:
            inp("wiL", [NFF, 128, KC, 256])
            inp("woL", [128, NFF, D])
            x1out = P.dram("x1T", [D, TT], F32, "ExternalOutput")
            Eout = P.dram("Eout", [2, 128, 4096], F32, "ExternalOutput")
            Dout = P.dram("Dout", [2, 128, 40], F32, "ExternalOutput")
        else:
            for nm in ("gla_g", "ssd_d", "ssd_ng", "final_g"):
                inp(nm, SMALL_SHAPES[nm])
            inp("wiL", [NFF, 128, KC, 256])
            inp("woL", [128, NFF, D])
            inp("wbL", [KC, 128, 32, 128])
            inp("w_outL", [KC, 128, KC, 128])
            inp("slotE", [2, 7, 128, 4096])
            inp("slotD", [2, 7, 128, 40])
            xout = P.dram("xoutT", [D, TT], F32, "ExternalOutput")
        load_consts(P, env, ins["cin"])
        lay = load_small(P, ins, SMALL_COMMON + ([] if phase == 1 else ["gla_g", "ssd_d", "ssd_ng", "final_g"]))
        lay["rope_in"] = ins["rope_in"]
        derive_layer(P, env, lay, phase == 2)
        modT = {"lat": lay["mod_lat"], "ctx": lay["mod_ctx"]}
        sp = make_spill(P, phase == 2)
        xv = ins["xin"].ap.rearrange("(k p) t -> p k t", p=128)
        offs = np.concatenate([[0], np.cumsum(SIZES)]).astype(int).tolist()
        spoff = {b: offs[b] for b in range(5)}

        def wi_view(buf):
            return buf

        with scope(P):
            H = Blocks(P, SIZES, KINDS, "hT", BF16)
            scr = (P.sb([128, KC, 512], BF16, "sq"), P.sb([128, KC, 512], F32, "tmpn"), P.sb([128, 512], F32, "rstd"))
            if phase == 1:
                with scope(P):
                    X = Blocks(P, SIZES, KINDS, "xTs")
                    for b in range(6):
                        P.dma(X.b[b].v(), View(ins["xin"], xv[:, :, offs[b]:offs[b] + SIZES[b]]))
                    m0 = prep_mod(P, modT, lay["norm_g"], 0)
                    m0["halo"] = m0["lat"]
                    for b in range(6):
                        gs, sh, hg = m0[KINDS[b]]
                        norm_mod(P, X.b[b], H.b[b], SIZES[b], gs, sh, env.ones_bf, scr)
                    ffn_tiled(P, X, H, m0, ins["wiL"], ins["woL"])
                    ov = x1out.ap.rearrange("(k p) t -> p k t", p=128)
                    for b in range(5):
                        P.dma(View(x1out, ov[:, :, offs[b]:offs[b] + SIZES[b]]), X.b[b].v())
                    m1 = prep_mod(P, modT, lay["norm_g"], 1)
                    m1["halo"] = m1["lat"]
                    for b in range(6):
                        gs, sh, hg = m1[KINDS[b]]
                        norm_mod(P, X.b[b], H.b[b], SIZES[b], gs, sh, env.ones_bf, scr)
            else:
                with scope(P):
                    xt_t = P.stack.enter_context(P.sb_ctx([128, 2, KC, 512], F32, "xtmp"))
                    xt = [Buf(xt_t[:, i]) for i in range(2)]
                    m1 = prep_mod(P, modT, lay["norm_g"], 1)
                    m1["halo"] = m1["lat"]
                    for b in range(6):
                        n = SIZES[b]
                        xb = xt[b % 2]
                        P.dma(xb[:, :, :n], View(ins["xin"], xv[:, :, offs[b]:offs[b] + n]))
                        gs, sh, hg = m1[KINDS[b]]
                        norm_mod(P, _sub(xb, n), H.b[b], n, gs, sh, env.ones_bf, scr)
            groups = ["ret", "gla", "v", "dt", "xB"] if phase == 1 else ["ret", "gla", "v", "gates", "dt", "xBC", "merge"]
            lay["halo_valid"] = lay["halo_valid"]
            inproj(P, env, H, ins["w_inL"], sp, lay, groups, do_q=(phase == 2))
        with scope(P):
            S, SB = alloc_states(P)
            if phase == 1:
                B = SweepBufs(P, False, False)
                Dt = [P.sb([128, 4], F32, "Dt0"), P.sb([128, 4], F32, "Dt1"), P.sb([128, 32], F32, "Dt2")]
                for d in range(2):
                    zero_states(P, S, SB)
                    P.memset(Dt[0].v(), 1.0)
                    P.memset(Dt[1].v(), 1.0)
                    P.memset(Dt[2].v(), 0.0)
                    chain = list(range(NCH_L)) if d == 0 else list(range(NCH_L - 1, -1, -1))
                    sweep(P, env, sp, lay, d, chain, S, SB, False, B, "state", Dt)
                    P.dma(Eout[d, :, 0:1024], S[0].v())
                    P.dma(Eout[d, :, 1024:2048], S[1].v())
                    P.dma(Eout[d, :, 2048:4096], S[2].v())
                    dd = P.sb([128, 40], F32, "dd")
                    P.copy(dd[:, 0:4], Dt[0].v())
                    P.copy(dd[:, 4:8], Dt[1].v())
                    P.act(dd[:, 8:40], Dt[2].v(), AF.Exp)
                    P.dma(Dout[d], dd.v())
            else:
                B = SweepBufs(P, True, True)
                dI = P.sb([128, 32, 128], BF16, "dI")
                for h in range(32):
                    P.ts(dI[:, h, :], env.cm("ident", bf=False), lay["ssd_d"][:, h:h + 1], None, ALU.mult)
                lay["dI"] = dI
                et = B.yall[0]
                dtile = P.sb([128, 40], F32, "slotD_t")
                for d in (1, 0):
                    zero_states(P, S, SB)
                    cchain = [NCH_L, NCH_L + 1] if d == 0 else [NCH_L + 1, NCH_L]
                    sweep(P, env, sp, lay, d, cchain, S, SB, True, B, "post" if d == 0 else "yb")
                    for s in range(7):
                        P.dma(et.v(), ins["slotE"][d, s])
                        P.dma(dtile.v(), ins["slotD"][d, s])
                        for h in range(4):
                            hc = slice(h * 256, (h + 1) * 256)
                            P.stt(S[0][:, hc], S[0][:, hc], dtile[:, h:h + 1], et[:, hc], ALU.mult, ALU.add)
                            hc2 = slice(1024 + h * 256, 1024 + (h + 1) * 256)
                            P.stt(S[1][:, hc], S[1][:, hc], dtile[:, 4 + h:5 + h], et[:, hc2], ALU.mult, ALU.add)
                        for h in range(32):
                            hc = slice(h * 64, (h + 1) * 64)
                            hc2 = slice(2048 + h * 64, 2048 + (h + 1) * 64)
                            P.stt(S[2][:, hc], S[2][:, hc], dtile[:, 8 + h:9 + h], et[:, hc2], ALU.mult, ALU.add)
                    for i in range(3):
                        P.copy(SB[i].v(), S[i].v(), eng="act")
                    chain = list(range(NCH_L)) if d == 0 else list(range(NCH_L - 1, -1, -1))
                    sweep(P, env, sp, lay, d, chain, S, SB, True, B, "post" if d == 0 else "yb")
        if phase == 2:
            with scope(P):
                X = Blocks(P, SIZES[:5], KINDS[:5], "xTs")
                for b in range(5):
                    P.dma(X.b[b].v(), View(ins["xin"], xv[:, :, offs[b]:offs[b] + SIZES[b]]))
                g5 = {}
                for kind in ("lat", "ctx"):
                    g5[kind] = modT[kind][:, 5, :]
                blks = [0, 1, 2, 3] + ([] if last else [4])
                phase_c(P, env, sp, X, spoff, g5, ins["wbL"], ins["w_outL"], blks)
                with scope(P):
                    H2 = Blocks(P, SIZES[:5], KINDS[:5], "h2T", BF16)
                    scr = (P.sb([128, KC, 512], BF16, "sq"), P.sb([128, KC, 512], F32, "tmpn"), P.sb([128, 512], F32, "rstd"))
                    m2 = prep_mod(P, modT, lay["norm_g"], 2)
                    for b in range(5):
                        gs, sh, hg = m2[KINDS[b]]
                        norm_mod(P, X.b[b], H2.b[b], SIZES[b], gs, sh, env.ones_bf, scr)
                    ffn_tiled(P, X, H2, m2, ins["wiL"], ins["woL"], G=4)
                ov = xout.ap.rearrange("(k p) t -> p k t", p=128)
                if last:
                    with scope(P):
                        scr = (P.sb([128, KC, 512], BF16, "sq"), P.sb([128, KC, 512], F32, "tmpn"), P.sb([128, 512], F32, "rstd"))
                        zt = P.sb([128, KC], F32, "zeros8")
                        P.memset(zt.v(), 0.0)
                        o_t = P.stack.enter_context(P.sb_ctx([128, 2, KC, 512], F32, "fo"))
                        ob = [Buf(o_t[:, i]) for i in range(2)]
                        for b in range(4):
                            norm_mod(P, X.b[b], ob[b % 2], 512, lay["final_g"], zt, env.ones_bf, scr)
                            P.dma(View(xout, ov[:, :, offs[b]:offs[b] + 512]), ob[b % 2].v())
                        P.dma(View(xout, ov[:, :, offs[4]:offs[4] + TC]), X.b[4].v())
                else:
                    for b in range(5):
                        P.dma(View(xout, ov[:, :, offs[b]:offs[b] + SIZES[b]]), X.b[b].v())
        P.emit()
    return nc


class _sub:
    def __init__(self, buf, n):
        self.buf = buf
        self.n = n

    def v(self):
        return View(self.buf, self.buf.ap[:, :, :self.n])

    def __getitem__(self, idx):
        return View(self.buf, self.buf.ap[:, :, :self.n][idx])


def ffn_tiled(P, X, H, mods, wiL, woL, G=6):
    nblk = len(X.sizes)
    T = X.T
    with scope(P):
        actT_t = P.stack.enter_context(P.sb_ctx([128, G, T], BF16, "actT"))
        wi_t = P.stack.enter_context(P.sb_ctx([128, 2, KC, 256], BF16, "wi_t"))
        wo_t = P.stack.enter_context(P.sb_ctx([128, G, D], BF16, "wo_t"))
        sil_t = P.stack.enter_context(P.sb_ctx([128, 2, 512], F32, "sil"))
        actb = [[Buf(actT_t[:, g, X.offs[b]:X.offs[b] + X.sizes[b]]) for b in range(nblk)] for g in range(G)]
        wib = [Buf(wi_t[:, i]) for i in range(2)]
        wob = Buf(wo_t[:])
        silb = [Buf(sil_t[:, i]) for i in range(2)]
        ns = 0
        nw = 0
        for g0 in range(0, NFF, G):
            gn = min(G, NFF - g0)
            for gi in range(gn):
                j = g0 + gi
                w = wib[nw % 2]
                nw += 1
                P.dma(w.v(), wiL[j], q="pool")
                for b in range(nblk):
                    n = X.sizes[b]
                    pa = P.next_ps()
                    pu = P.next_ps()
                    for k in range(KC):
                        P.mm(pa[:, :n], w[:, k, 0:128], H.b[b][:, k, :], start=(k == 0), stop=(k == KC - 1))
                    for k in range(KC):
                        P.mm(pu[:, :n], w[:, k, 128:256], H.b[b][:, k, :], start=(k == 0), stop=(k == KC - 1))
                    s = silb[ns % 2]
                    ns += 1
                    P.act(s[:, :n], pa[:, :n], AF.Silu)
                    P.tt(actb[gi][b].v(), s[:, :n], pu[:, :n], ALU.mult)
            P.dma(wob[:, 0:gn, :], woL[:, g0:g0 + gn, :], q="pool")
            for b in range(nblk):
                n = X.sizes[b]
                hg = mods[X.kinds[b]][2]
                for f in range(KC):
                    ps = P.next_ps()
                    for gi in range(gn):
                        P.mm(ps[:, :n], wob[:, gi, f * 128:(f + 1) * 128], actb[gi][b].v(), start=(gi == 0), stop=(gi == gn - 1))
                    P.stt(X.b[b][:, f, :], ps[:, :n], hg[:, f:f + 1], X.b[b][:, f, :], ALU.mult, ALU.add)


def build_p0():
    nc = bass.Bass("TRN2", target_bir_lowering=False)
    NCOL = 4608
    with contextlib.ExitStack() as stack:
        P = Prog(nc, stack)
        P.psum_banks()
        cc = P.dram("ccT", [128, KC, 2], F32, "ExternalInput")
        W = P.dram("adaW", [128, KC, NCOL], F32, "ExternalInput")
        bias = P.dram("adab", [2, NCOL], F32, "ExternalInput")
        out = P.dram("modout", [2, NCOL], F32, "ExternalOutput")
        s = P.sb([128, KC, 2], F32, "s")
        P.dma(s.v(), cc.v())
        P.act(s.v(), s.v(), AF.Silu)
        bt = P.sb([2, NCOL], F32, "bt")
        P.dma(bt.v(), bias.v())
        ot = P.sb([2, NCOL], F32, "ot")
        w_t = P.stack.enter_context(P.sb_ctx([128, 2, KC, 512], F32, "w0"))
        wb = [Buf(w_t[:, i]) for i in range(2)]
        for j in range(NCOL // 512):
            w = wb[j % 2]
            P.dma(w.v(), W[:, :, j * 512:(j + 1) * 512])
            ps = P.next_ps()
            for k in range(KC):
                P.mm(ps[0:2, :], s[:, k, :], w[:, k, :], start=(k == 0), stop=(k == KC - 1))
            P.tt(ot[:, j * 512:(j + 1) * 512], ps[0:2, :], bt[:, j * 512:(j + 1) * 512], ALU.add)
        P.dma(out.v(), ot.v())
        P.emit()
    return nc


_PROGS = {}


def get_prog(key):
    if key not in _PROGS:
        if key == "p0":
            _PROGS[key] = build_p0()
        elif key == "p1":
            _PROGS[key] = build_layer(1)
        elif key == "p2":
            _PROGS[key] = build_layer(2, last=False)
        else:
            _PROGS[key] = build_layer(2, last=True)
    return _PROGS[key]


def rep(v, n=128):
    v = np.asarray(v, np.float32).reshape(1, -1)
    return np.ascontiguousarray(np.broadcast_to(v, (n, v.shape[1])))


def rope_tables(core):
    t = core * TL + np.arange(TL)
    row = (t // 64).astype(np.float32)
    col = (t % 64).astype(np.float32)
    nf = 32
    freq = (np.float32(10000.0) ** (-np.arange(nf, dtype=np.float32) / np.float32(nf))).astype(np.float32)
    ang = np.concatenate([row[:, None] * freq[None, :], col[:, None] * freq[None, :]], axis=1).astype(np.float32)
    cos = np.cos(ang).astype(np.float32).T
    sin = np.sin(ang).astype(np.float32).T
    cos_tab = np.concatenate([cos, cos], axis=0)
    sin_tab = np.concatenate([-sin, sin], axis=0)
    return np.ascontiguousarray(np.stack([cos_tab, sin_tab], axis=1))


def run_model(inp, ncores, depth=4, run=None):
    from concourse.bass_utils import run_bass_kernel_spmd
    f32 = np.float32
    x = np.asarray(inp["x"], f32)[0]
    ctx = np.asarray(inp["ctx"], f32)[0]
    assert x.shape[0] == ncores * TL
    consts = host_consts()
    cc = np.stack([np.asarray(inp["c"], f32)[0], np.asarray(inp["c_ctx"], f32)], axis=1)
    ccT = np.ascontiguousarray(cc.reshape(KC, 128, 2).transpose(1, 0, 2))
    in_maps = []
    for c in range(8):
        l, half = (c // 2) % depth, c % 2
        Wl = np.asarray(inp["ada_w"][l], f32)[:, half * 4608:(half + 1) * 4608]
        in_maps.append({"ccT": ccT,
                        "adaW": np.ascontiguousarray(Wl.reshape(KC, 128, 4608).transpose(1, 0, 2)),
                        "adab": rep(np.asarray(inp["ada_b"][l], f32)[half * 4608:(half + 1) * 4608], 2)})
    res0 = run_bass_kernel_spmd(get_prog("p0"), in_maps, core_ids=list(range(8))).results
    mods = []
    for l in range(depth):
        m = np.concatenate([res0[2 * l]["modout"], res0[2 * l + 1]["modout"]], axis=1)
        mods.append([np.ascontiguousarray(m[s].reshape(9, KC, 128).transpose(2, 0, 1)) for s in range(2)])
    xT = [np.ascontiguousarray(x[c * TL:(c + 1) * TL].T) for c in range(ncores)]
    ctxT = np.ascontiguousarray(ctx.T)
    ropes = [rope_tables(c) for c in range(ncores)]
    zeros2 = np.zeros((D, 2), f32)

    def halos(xs):
        hs, vs = [], []
        for c in range(ncores):
            left = xs[c - 1][:, -2:] if c > 0 else zeros2
            right = xs[c + 1][:, :2] if c < ncores - 1 else zeros2
            hs.append(np.concatenate([left, right], axis=1))
            vs.append(rep(np.array([1.0 if c > 0 else 0.0, 1.0 if c < ncores - 1 else 0.0], f32)))
        return hs, vs

    for l in range(depth):
        last = l == depth - 1
        g = lambda k: np.asarray(inp[k][l], f32)
        common = {
            "cin": consts,
            "mod_lat": mods[l][0], "mod_ctx": mods[l][1],
            "norm_g": np.ascontiguousarray(g("norm_g").reshape(3, KC, 128).transpose(2, 0, 1)),
            "ret_logit": rep(g("ret_logit").reshape(-1)),
            "wa2": np.ascontiguousarray(g("gla_wa2").transpose(1, 0, 2)),
            "gla_ba": np.ascontiguousarray(g("gla_ba").reshape(2, 4, 128).transpose(2, 0, 1)),
            "conv_w": np.ascontiguousarray(g("conv_w").reshape(5, 24, 128).transpose(2, 0, 1)),
            "conv_b": np.ascontiguousarray(g("conv_b").reshape(24, 128).T),
            "dt_bias": rep(g("dt_bias").reshape(-1)),
            "a_log": rep(g("a_log").reshape(-1)),
            "w_inL": np.ascontiguousarray(g("w_in").reshape(KC, 128, DIN).transpose(1, 0, 2)),
        }

        def tile_ffn(wi, wo):
            wr = wi.reshape(KC, 128, 2 * DFF)
            a = wr[:, :, :DFF].reshape(KC, 128, NFF, 128).transpose(2, 1, 0, 3)
            u = wr[:, :, DFF:].reshape(KC, 128, NFF, 128).transpose(2, 1, 0, 3)
            return (np.ascontiguousarray(np.concatenate([a, u], axis=3)),
                    np.ascontiguousarray(wo.reshape(NFF, 128, D).transpose(1, 0, 2)))
        wiL, woL = tile_ffn(g("ffn1_wi"), g("ffn1_wo"))
        hs, vs = halos(xT)
        in_maps = []
        for c in range(ncores):
            m = dict(common)
            m.update({"xin": np.ascontiguousarray(np.concatenate([xT[c], ctxT, hs[c]], axis=1)),
                      "halo_valid": vs[c], "rope_in": ropes[c], "wiL": wiL, "woL": woL})
            in_maps.append(m)
        res1 = run_bass_kernel_spmd(get_prog("p1"), in_maps, core_ids=list(range(ncores))).results
        if run is not None:
            run["res1"] = res1
        x1 = [r["x1T"][:, :TL] for r in res1]
        ctx1 = res1[0]["x1T"][:, TL:]
        E = [r["Eout"] for r in res1]
        Dd = [r["Dout"] for r in res1]
        wiL, woL = tile_ffn(g("ffn2_wi"), g("ffn2_wo"))
        wb_all = np.concatenate([g("wb_ret"), g("wb_gla"), g("wb_ssd")], axis=0)
        wbL = np.ascontiguousarray(wb_all.reshape(32, 128, KC, 128).transpose(2, 1, 0, 3))
        w_outL = np.ascontiguousarray(g("w_out").reshape(KC, 128, KC, 128).transpose(2, 1, 0, 3))
        hs, vs = halos(x1)
        in_maps = []
        for c in range(ncores):
            slotE = np.zeros((2, 7, 128, 4096), f32)
            slotD = np.ones((2, 7, 128, 40), f32)
            for s, src in enumerate(range(0, c)):
                slotE[0, s] = E[src][0]
                slotD[0, s] = Dd[src][0]
            for s, src in enumerate(range(ncores - 1, c, -1)):
                slotE[1, s] = E[src][1]
                slotD[1, s] = Dd[src][1]
            m = dict(common)
            m.update({"xin": np.ascontiguousarray(np.concatenate([x1[c], ctx1, hs[c]], axis=1)),
                      "halo_valid": vs[c], "rope_in": ropes[c], "wiL": wiL, "woL": woL, "wbL": wbL, "w_outL": w_outL,
                      "gla_g": rep(g("gla_norm_g")), "ssd_d": rep(g("ssd_d")), "ssd_ng": rep(g("ssd_norm_g")),
                      "final_g": fm_vec(np.asarray(inp["final_norm_g"], f32)),
                      "slotE": slotE, "slotD": slotD})
            in_maps.append(m)
        res2 = run_bass_kernel_spmd(get_prog("p2last" if last else "p2"), in_maps, core_ids=list(range(ncores))).results
        if run is not None:
            run["res2"] = res2
        xT = [r["xoutT"][:, :TL] for r in res2]
        ctxT = np.ascontiguousarray(res2[0]["xoutT"][:, TL:])
    out = np.concatenate([t.T for t in xT], axis=0)[None]
    return np.ascontiguousarray(out.astype(np.float32))


def kernel(**inputs):
    return run_model(inputs, 8)
```

```python
import contextlib
import numpy as np
import concourse.bass as bass
import concourse.mybir as mybir

F32 = mybir.dt.float32
BF16 = mybir.dt.bfloat16
AF = mybir.ActivationFunctionType
ALU = mybir.AluOpType
KDMA = 6


class View:
    __slots__ = ("buf", "ap")

    def __init__(self, buf, ap):
        self.buf = buf
        self.ap = ap

    def __getitem__(self, idx):
        return View(self.buf, self.ap[idx])

    def m(self, f):
        return View(self.buf, f(self.ap))


class Buf:
    __slots__ = ("ap", "w", "r", "name", "psum")

    def __init__(self, ap, name="", psum=False):
        self.ap = ap
        self.w = None
        self.r = {}
        self.name = name
        self.psum = psum

    def __getitem__(self, idx):
        return View(self, self.ap[idx])

    def v(self):
        return View(self, self.ap)


class Prog:
    ENGS = ("pe", "act", "dve", "pool", "sp")

    def __init__(self, nc, stack):
        self.nc = nc
        self.stack = stack
        self.q = {e: [] for e in self.ENGS}
        self.cnt = {e: 0 for e in self.ENGS}
        self.dman = {e: 0 for e in self.ENGS}
        self.waited = {e: {} for e in self.ENGS}
        self.sems = {}
        for e in self.ENGS:
            self.sems[("eng", e)] = stack.enter_context(nc.semaphore("s_" + e))
        for e in ("sp", "pool", "act"):
            for i in range(KDMA):
                self.sems[("dma", e, i)] = stack.enter_context(nc.semaphore(f"d_{e}{i}"))
        self.nalloc = 0
        self.psn = 0

    def sb(self, shape, dtype=F32, name=None):
        self.nalloc += 1
        t = self.stack.enter_context(self.nc.sbuf_tensor(f"{name or 't'}_{self.nalloc}", list(shape), dtype))
        return Buf(t[:] if hasattr(t, "__getitem__") else t, name or "t")

    def sb_ctx(self, shape, dtype=F32, name=None):
        self.nalloc += 1
        return self.nc.sbuf_tensor(f"{name or 't'}_{self.nalloc}", list(shape), dtype)

    def psum_banks(self):
        self.ps = []
        for i in range(8):
            t = self.stack.enter_context(self.nc.psum_tensor(f"ps{i}", [128, 512], F32))
            self.ps.append(Buf(t[:], f"ps{i}", psum=True))

    def next_ps(self):
        b = self.ps[self.psn % 8]
        self.psn += 1
        return b

    def dram(self, name, shape, dtype=F32, kind="Internal"):
        t = self.nc.dram_tensor(name, list(shape), dtype, kind=kind)
        return Buf(t.ap(), name)

    def op(self, E, fn, reads, writes, dma=False):
        deps = {}

        def add(tok):
            sk, val, eng = tok
            if deps.get(sk, (0, None))[0] < val:
                deps[sk] = (val, eng)

        for b in reads:
            if b.w is not None:
                add(b.w)
            if b.psum:
                for sk, (val, eng) in b.r.items():
                    if eng != E:
                        add((sk, val, eng))
        for b in writes:
            if b.w is not None:
                add(b.w)
            for sk, (val, eng) in b.r.items():
                add((sk, val, eng))
        waits = []
        for sk, (val, eng) in deps.items():
            if eng == "pe" and E == "pe" and not dma:
                continue
            if self.waited[E].get(sk, 0) >= val:
                continue
            self.waited[E][sk] = val
            waits.append((sk, val))
        if dma:
            n = self.dman[E]
            self.dman[E] += 1
            sk = ("dma", E, n % KDMA)
            val = 16 * (n // KDMA + 1)
            if val > 16 and self.waited[E].get(sk, 0) < val - 16:
                waits.append((sk, val - 16))
                self.waited[E][sk] = val - 16
            tok = (sk, val, "dma")
            inc = (sk, 16)
        else:
            self.cnt[E] += 1
            sk = ("eng", E)
            tok = (sk, self.cnt[E], E)
            inc = (sk, 1)
        self.q[E].append((waits, fn, inc))
        wset = set(id(b) for b in writes)
        for b in writes:
            b.w = tok
            b.r = {}
        for b in reads:
            if id(b) not in wset:
                b.r[tok[0]] = (tok[1], tok[2])
        return tok

    def barrier(self):
        cur = []
        for e in self.ENGS:
            if self.cnt[e] > 0:
                cur.append((("eng", e), self.cnt[e]))
        for e in ("sp", "pool", "act"):
            n = self.dman[e]
            for i in range(KDMA):
                k = (n - 1 - i)
                if k >= 0:
                    cur.append((("dma", e, k % KDMA), 16 * (k // KDMA + 1)))
        for E in self.ENGS:
            waits = []
            for sk, val in cur:
                if sk == ("eng", E):
                    continue
                if self.waited[E].get(sk, 0) >= val:
                    continue
                self.waited[E][sk] = val
                waits.append((sk, val))
            if waits:
                self.q[E].append((waits, None, None))

    def emit(self):
        nc = self.nc
        self.barrier()
        with nc.Block() as block:
            def replay(E, e):
                for waits, fn, inc in self.q[E]:
                    for sk, val in waits:
                        e.wait_ge(self.sems[sk], val)
                    if fn is not None:
                        ins = fn(e)
                        ins.then_inc(self.sems[inc[0]], inc[1])

            @block.tensor
            def _(e):
                replay("pe", e)

            @block.scalar
            def _(e):
                replay("act", e)

            @block.vector
            def _(e):
                replay("dve", e)

            @block.gpsimd
            def _(e):
                replay("pool", e)

            @block.sync
            def _(e):
                replay("sp", e)

    @staticmethod
    def _b(*xs):
        return [x.buf for x in xs if isinstance(x, View)]

    @staticmethod
    def _a(x):
        return x.ap if isinstance(x, View) else x

    def mm(self, out, lhsT, rhs, start=True, stop=True):
        o, l, r = out.ap, lhsT.ap, rhs.ap
        return self.op("pe", lambda e: e.matmul(o, l, r, start=start, stop=stop), self._b(lhsT, rhs), self._b(out))

    def tr(self, out, in_, ident):
        o, i, d = out.ap, in_.ap, ident.ap
        return self.op("pe", lambda e: e.transpose(o, i, d), self._b(in_, ident), self._b(out))

    def act(self, out, in_, func, bias=0.0, scale=1.0, accum=None):
        o, i, b, s = out.ap, in_.ap, self._a(bias), self._a(scale)
        ac = self._a(accum) if accum is not None else None
        kw = {}
        if ac is not None:
            kw["accum_out"] = ac
        return self.op("act", lambda e: e.activation(o, i, func, bias=b, scale=s, **kw),
                       self._b(in_, bias, scale), self._b(out) + (self._b(accum) if accum is not None else []))

    def tt(self, out, a, b, op, eng="dve"):
        o, x, y = out.ap, a.ap, b.ap
        return self.op(eng, lambda e: e.tensor_tensor(o, x, y, op), self._b(a, b), self._b(out))

    def ts(self, out, a, s1, s2, op0, op1=None, eng="dve", accum=None):
        o, x, p1, p2 = out.ap, a.ap, self._a(s1), self._a(s2)
        kw = {}
        if op1 is not None:
            kw["op1"] = op1
        if accum is not None:
            kw["accum_out"] = accum.ap
        return self.op(eng, lambda e: e.tensor_scalar(o, x, p1, p2, op0, **kw), self._b(a, s1, s2),
                       self._b(out) + (self._b(accum) if accum is not None else []))

    def stt(self, out, a, s, b, op0, op1, eng="dve"):
        o, x, p, y = out.ap, a.ap, self._a(s), b.ap
        return self.op(eng, lambda e: e.scalar_tensor_tensor(o, x, p, y, op0, op1), self._b(a, s, b), self._b(out))

    def copy(self, out, a, eng="dve"):
        o, x = out.ap, a.ap
        if eng == "act":
            return self.op("act", lambda e: e.copy(o, x), self._b(a), self._b(out))
        return self.op(eng, lambda e: e.tensor_copy(o, x), self._b(a), self._b(out))

    def memset(self, out, val, eng="dve"):
        o = out.ap
        return self.op(eng, lambda e: e.memset(o, val), [], self._b(out))

    def scan(self, out, d0, d1, init, op0, op1):
        o, a, b, i = out.ap, d0.ap, d1.ap, self._a(init)
        return self.op("dve", lambda e: e.tensor_tensor_scan(o, a, b, i, op0, op1), self._b(d0, d1, init), self._b(out))

    def dma(self, out, in_, q="sp"):
        o, i = out.ap, in_.ap
        return self.op(q, lambda e: e.dma_start(out=o, in_=i), self._b(in_), self._b(out), dma=True)


@contextlib.contextmanager
def scope(P):
    old = P.stack
    with contextlib.ExitStack() as st:
        P.stack = st
        try:
            yield
        finally:
            P.barrier()
            P.stack = old


D = 1024
KC = 8
DFF = 2816
NFF = 22
EPS = 1e-6


def fm_vec(v):
    v = np.asarray(v, np.float32)
    return np.ascontiguousarray(v.reshape(-1, 128).T)


class Blocks:
    def __init__(self, P, sizes, kinds, name="xT", dtype=F32):
        self.sizes = sizes
        self.kinds = kinds
        self.offs = np.concatenate([[0], np.cumsum(sizes)]).astype(int).tolist()
        self.T = self.offs[-1]
        self.t = P.stack.enter_context(P.nc.sbuf_tensor(name, [128, KC, self.T], dtype))
        self.b = [Buf(self.t[:, :, self.offs[i]:self.offs[i] + n], f"{name}{i}") for i, n in enumerate(sizes)]


def prep_mod(P, modT, gT, idx):
    out = {}
    for kind, mt in modT.items():
        gs = P.sb([128, KC], F32, "gs")
        hg = P.sb([128, KC], F32, "hg")
        P.stt(gs.v(), mt[:, 3 * idx + 1, :], 1.0, gT[:, idx, :], ALU.add, ALU.mult)
        P.ts(hg.v(), mt[:, 3 * idx + 2, :], 0.5 if idx != 1 else 1.0, None, ALU.mult)
        out[kind] = (gs, mt[:, 3 * idx + 0, :], hg)
    return out


def norm_mod(P, xb, hb, n, gs, sh, ones_bf, scr):
    sq, tmp, rstd = scr
    P.act(sq[:, :, :n], xb.v(), AF.Square)
    ps = P.next_ps()
    for k in range(KC):
        P.mm(ps[:, :n], ones_bf.v(), sq[:, k, :n], start=(k == 0), stop=(k == KC - 1))
    P.ts(rstd[:, :n], ps[:, :n], 1.0 / D, EPS, ALU.mult, ALU.add)
    P.act(rstd[:, :n], rstd[:, :n], AF.Ln)
    P.act(rstd[:, :n], rstd[:, :n], AF.Exp, scale=-0.5)
    for k in range(KC):
        P.stt(tmp[:, k, :n], xb[:, k, :], gs[:, k:k + 1], rstd[:, :n], ALU.mult, ALU.mult)
        P.act(hb[:, k, :], tmp[:, k, :n], AF.Identity, bias=sh[:, k:k + 1], scale=1.0)


def ffn(P, X, H, mods, wi, wo, ones_bf, scr, G=6):
    nblk = len(X.sizes)
    T = X.T
    with P.sb_ctx([128, G, T], BF16, "actT") as actT_t, \
            P.sb_ctx([128, 2, KC, 256], BF16, "wi_t") as wi_t, \
            P.sb_ctx([128, G, D], BF16, "wo_t") as wo_t, \
            P.sb_ctx([128, 2, 512], F32, "sil") as sil_t:
        actb = [[Buf(actT_t[:, g, X.offs[b]:X.offs[b] + X.sizes[b]]) for b in range(nblk)] for g in range(G)]
        wib = [Buf(wi_t[:, i]) for i in range(2)]
        wob = Buf(wo_t[:])
        silb = [Buf(sil_t[:, i]) for i in range(2)]
        wiv = wi.ap.rearrange("(k p) c -> p k c", p=128)
        wov = wo.ap.rearrange("(j p) f -> p j f", p=128)
        ns = 0
        nw = 0
        for g0 in range(0, NFF, G):
            gn = min(G, NFF - g0)
            for gi in range(gn):
                j = g0 + gi
                w = wib[nw % 2]
                nw += 1
                P.dma(w[:, :, 0:128], View(wi, wiv[:, :, j * 128:(j + 1) * 128]), q="pool")
                P.dma(w[:, :, 128:256], View(wi, wiv[:, :, DFF + j * 128:DFF + (j + 1) * 128]), q="pool")
                for b in range(nblk):
                    n = X.sizes[b]
                    pa = P.next_ps()
                    pu = P.next_ps()
                    for k in range(KC):
                        P.mm(pa[:, :n], w[:, k, 0:128], H.b[b][:, k, :], start=(k == 0), stop=(k == KC - 1))
                    for k in range(KC):
                        P.mm(pu[:, :n], w[:, k, 128:256], H.b[b][:, k, :], start=(k == 0), stop=(k == KC - 1))
                    s = silb[ns % 2]
                    ns += 1
                    P.act(s[:, :n], pa[:, :n], AF.Silu)
                    P.tt(actb[gi][b].v(), s[:, :n], pu[:, :n], ALU.mult)
            P.dma(wob[:, 0:gn, :], View(wo, wov[:, g0:g0 + gn, :]), q="pool")
            for b in range(nblk):
                n = X.sizes[b]
                hg = mods[X.kinds[b]][2]
                for f in range(KC):
                    ps = P.next_ps()
                    for gi in range(gn):
                        P.mm(ps[:, :n], wob[:, gi, f * 128:(f + 1) * 128], actb[gi][b].v(), start=(gi == 0), stop=(gi == gn - 1))
                    P.stt(X.b[b][:, f, :], ps[:, :n], hg[:, f:f + 1], X.b[b][:, f, :], ALU.mult, ALU.add)
        P.barrier()


TL = 2048
TC = 256
NCH_L = TL // 128
NCH_C = TC // 128
NCH = NCH_L + NCH_C
TT = TL + TC
DIN = 14432
COLS = {}
_o = 0
for _n, _s in (("ret_q", 512), ("ret_k", 512), ("ret_v", 1024), ("ret_g", 1024), ("gla_q", 512), ("gla_k", 512),
               ("gla_v", 1024), ("gla_r", 1024), ("gla_af", 16), ("gla_ab", 16), ("ssd_z", 2048), ("ssd_xbc", 3072),
               ("ssd_dt", 64), ("merge", 3072)):
    COLS[_n] = (_o, _s)
    _o += _s
assert _o == DIN
CI = {"ident": 0, "perm": 1, "mle": 2, "mge": 3, "sgt": 4, "slt": 5}


def host_consts():
    p = np.arange(128)[:, None]
    c = np.arange(128)[None, :]
    mats = [p == c, c == (p + 64) % 128, p <= c, p >= c, p > c, p < c]
    m = np.concatenate([x.astype(np.float32) for x in mats], axis=1)
    rows = np.concatenate([np.broadcast_to(np.arange(1, 129, dtype=np.float32), (128, 128)),
                           np.broadcast_to(np.arange(128, 0, -1).astype(np.float32), (128, 128))], axis=1)
    return np.ascontiguousarray(np.concatenate([m, rows], axis=1))


class Env:
    pass


def load_consts(P, env, cin):
    cf = P.sb([128, 8 * 128], F32, "cf")
    P.dma(cf.v(), cin.v())
    cb = P.sb([128, 6 * 128], BF16, "cb")
    P.copy(cb.v(), cf[:, 0:768])
    env.cf, env.cb = cf, cb
    env.ones_bf = P.sb([128, 128], BF16, "ones")
    P.memset(env.ones_bf.v(), 1.0)
    env.ones_f = P.sb([128, 128], F32, "onesf")
    P.memset(env.ones_f.v(), 1.0)

    def cm(name, bf=True):
        i = CI[name]
        return (cb if bf else cf)[:, i * 128:(i + 1) * 128]
    env.cm = cm
    env.row_up = cf[:, 768:896]
    env.row_dn = cf[:, 896:1024]


def softplus_parts(P, z, n, tmp1, tmp2):
    P.act(tmp1, z, AF.Abs)
    P.act(tmp1, tmp1, AF.Exp, scale=-1.0)
    P.act(tmp1, tmp1, AF.Ln, bias=1.0)
    P.ts(tmp2, z, 0.0, None, ALU.max)
    return tmp2, tmp1


def inproj(P, env, H, W, sp, lay, groups, do_q):
    nblk = len(H.sizes)
    tokblocks = [b for b in range(nblk) if H.kinds[b] != "halo"]
    halo = [b for b in range(nblk) if H.kinds[b] == "halo"][0]
    spoff = {}
    o = 0
    for b in tokblocks:
        spoff[b] = o
        o += H.sizes[b]
    wt_t = P.stack.enter_context(P.sb_ctx([128, 2, KC, 512], BF16, "win_t"))
    wt = [Buf(wt_t[:, i]) for i in range(2)]
    st = {"nw": 0}

    def loadw(c0, n):
        w = wt[st["nw"] % 2]
        st["nw"] += 1
        P.dma(w[:, :, 0:n], W[:, :, c0:c0 + n], q="pool")
        return w

    def fm_mm(w, wc0, m, b, ps=None):
        n = H.sizes[b]
        ps = ps or P.next_ps()
        for k in range(KC):
            P.mm(ps[0:m, :n], w[:, k, wc0:wc0 + m], H.b[b][:, k, :], start=(k == 0), stop=(k == KC - 1))
        return ps

    def do_qk(mixer):
        mi = 0 if mixer == "ret" else 1
        names = (["q"] if do_q else []) + ["k"]
        with scope(P):
            ec = P.stack.enter_context
            qf_t = ec(P.sb_ctx([128, 2, 512], F32, "qk_f"))
            qb_t = ec(P.sb_ctx([128, 2, 512], BF16, "qk_b"))
            qo_t = ec(P.sb_ctx([128, 4, 512], BF16, "qk_o"))
            g_t = ec(P.sb_ctx([128, 6, 512], F32, "g_t"))
            rope_t = ec(P.sb_ctx([128, 2, TL if mixer == "ret" else 2], F32, "rope"))
            code_t = ec(P.sb_ctx([16, 2, 512], BF16, "code"))
            la_t = ec(P.sb_ctx([128, 2, 512], F32, "la_t"))
            epm_t = ec(P.sb_ctx([128, 4, 512], F32, "epm_t"))
            epm = [[Buf(epm_t[:, 0]), Buf(epm_t[:, 1])], [Buf(epm_t[:, 2]), Buf(epm_t[:, 3])]]
            qf = [Buf(qf_t[:, i]) for i in range(2)]
            qb = [Buf(qb_t[:, i]) for i in range(2)]
            qo = [Buf(qo_t[:, i]) for i in range(4)]
            gt = [Buf(g_t[:, i]) for i in range(6)]
            la = [Buf(la_t[:, i]) for i in range(2)]
            code = [Buf(code_t[:, i]) for i in range(2)]
            rope = Buf(rope_t[:])
            if mixer == "ret":
                P.dma(rope.v(), lay["rope_in"].v())
            wq = {}
            for nm in names:
                c0, _ = COLS[f"{mixer}_{nm}"]
                wq[nm] = (c0,)
            if mixer == "gla":
                wc = loadw(COLS["gla_af"][0], 32)
                wcode = P.sb([128, KC, 32], BF16, "wcode")
                P.copy(wcode.v(), wc[:, :, 0:32])
            cnt = 0
            if mixer == "gla":
                egg = P.sb([128, 2, NCH, 4], F32, "egg")
            for b in tokblocks:
                n = H.sizes[b]
                nc_ = n // 128
                kind = H.kinds[b]
                if mixer == "gla":
                    for d in range(2):
                        ps = P.next_ps()
                        for k in range(KC):
                            P.mm(ps[0:16, :n], wcode[:, k, d * 16:(d + 1) * 16], H.b[b][:, k, :], start=(k == 0), stop=(k == KC - 1))
                        P.copy(code[d][:, :n], ps[0:16, :n], eng="act")
                if mixer == "ret":
                    for nm in names:
                        w = loadw(wq[nm][0], 512)
                        var0 = 0 if nm == "q" else 1
                        for h in range(4):
                            ps = fm_mm(w, h * 128, 128, b)
                            cnt += 1
                            f32 = qf[cnt % 2]
                            sc = 128 ** -0.5 if nm == "k" else 1.0
                            if kind == "lat":
                                bfv = qb[cnt % 2]
                                P.act(bfv[:, :n], ps[:, :n], AF.Copy, scale=sc)
                                ps2 = P.next_ps()
                                P.mm(ps2[:, :n], env.cm("perm"), bfv[:, :n])
                                t0 = H.offs[b]
                                P.tt(f32[:, :n], bfv[:, :n], rope[:, 0, t0:t0 + n], ALU.mult)
                                P.tt(gt[0][:, :n], ps2[:, :n], rope[:, 1, t0:t0 + n], ALU.mult)
                                P.tt(f32[:, :n], f32[:, :n], gt[0][:, :n], ALU.add)
                            else:
                                P.act(f32[:, :n], ps[:, :n], AF.Copy, scale=sc)
                            for d in range(2):
                                sgn = 0 if nm == "q" else 1
                                tab = lay["ret_tab"][:, d, sgn, h, :].m(lambda a: a.unsqueeze(1).to_broadcast([128, nc_, 128]))
                                o = qo[(cnt * 2 + d) % 4]
                                P.tt(o[:, :n].m(lambda a: a.rearrange("p (c t) -> p c t", t=128)),
                                     f32[:, :n].m(lambda a: a.rearrange("p (c t) -> p c t", t=128)), tab, ALU.mult)
                                P.dma(sp["qk"][mi, h, 2 * d + var0, :, spoff[b]:spoff[b] + n], o[:, :n])
                else:
                    ws = {nm: loadw(wq[nm][0], 512) for nm in names}
                    for h in range(4):
                        for d in range(2):
                            psz = P.next_ps()
                            P.mm(psz[:, :n], lay["wa2"][:, d, h * 128:(h + 1) * 128], code[d][:, :n])
                            z = gt[1]
                            P.ts(z[:, :n], psz[:, :n], lay["gla_ba"][:, d, h:h + 1], None, ALU.add)
                            mx, l = softplus_parts(P, z[:, :n], n, gt[2][:, :n], gt[3][:, :n])
                            P.ts(gt[3][:, :n], z[:, :n], 0.0, None, ALU.min)
                            dd = gt[4]
                            P.tt(dd[:, :n], gt[3][:, :n], gt[2][:, :n], ALU.subtract)
                            cs = gt[5]
                            for c in range(nc_):
                                sl = slice(c * 128, (c + 1) * 128)
                                P.scan(cs[:, sl], env.ones_f[:, 0:128], dd[:, sl], 0.0, ALU.mult, ALU.add)
                            g = la[d]
                            if d == 0:
                                g = cs
                            else:
                                P.tt(g[:, :n], dd[:, :n], cs[:, :n], ALU.subtract)
                                for c in range(nc_):
                                    sl = slice(c * 128, (c + 1) * 128)
                                    P.ts(g[:, sl], g[:, sl], cs[:, c * 128 + 127:c * 128 + 128], None, ALU.add)
                            ch0 = spoff[b] // 128
                            for c in range(nc_):
                                P.act(egg[:, d, ch0 + c, h:h + 1], cs[:, c * 128 + 127:c * 128 + 128], AF.Exp, scale=1.0 / 16)
                            if do_q:
                                P.act(epm[0][d][:, :n], g[:, :n], AF.Exp, scale=1.0 / 16)
                            P.act(epm[1][d][:, :n], g[:, :n], AF.Exp, scale=-1.0 / 16)
                        for nm in names:
                            var0 = 0 if nm == "q" else 1
                            ps = fm_mm(ws[nm], h * 128, 128, b)
                            cnt += 1
                            f32 = qf[cnt % 2]
                            sc = 128 ** -0.5 if nm == "q" else 1.0
                            P.act(f32[:, :n], ps[:, :n], AF.Copy, scale=sc)
                            for d in range(2):
                                o = qo[(cnt * 2 + d) % 4]
                                P.tt(o[:, :n], f32[:, :n], epm[var0][d][:, :n], ALU.mult)
                                P.dma(sp["qk"][mi, h, 2 * d + var0, :, spoff[b]:spoff[b] + n], o[:, :n])
            if mixer == "gla":
                for d in range(2):
                    P.dma(sp["eG"][mi * 2 + d], egg[:, d])
            if mixer == "ret":
                egr = P.sb([128, 2, NCH, 4], F32, "egr")
                for d in range(2):
                    P.copy(egr[:, d, :, :], lay["ret_eg"][:, d, :].m(lambda a: a.unsqueeze(1).to_broadcast([128, NCH, 4])))
                    P.dma(sp["eG"][mi * 2 + d], egr[:, d])
            P.barrier()

    def do_tm(name, handler, width=512):
        c0, sz = COLS[name]
        for cc in range(0, sz, width):
            n_ = min(width, sz - cc)
            w = loadw(c0 + cc, n_)
            for b in tokblocks:
                for c in range(H.sizes[b] // 128):
                    ps = P.next_ps()
                    for k in range(KC):
                        P.mm(ps[:, :n_], H.b[b][:, k, c * 128:(c + 1) * 128], w[:, k, 0:n_], start=(k == 0), stop=(k == KC - 1))
                    handler(ps, n_, cc, spoff[b] + c * 128)

    ev = {"n": 0}
    evt_t = P.stack.enter_context(P.sb_ctx([128, 4, 512], BF16, "evt"))
    evt = [Buf(evt_t[:, i]) for i in range(4)]

    def h_copy(dst, col0, func=AF.Copy):
        def h(ps, n_, cc, tok):
            e = evt[ev["n"] % 4]
            ev["n"] += 1
            if func == AF.Copy and ev["n"] % 2 == 0:
                P.copy(e[:, :n_], ps[:, :n_])
            else:
                P.act(e[:, :n_], ps[:, :n_], func)
            P.dma(dst[tok:tok + 128, col0 + cc:col0 + cc + n_], e[:, :n_])
        return h

    def h_dt(ps, n_, cc, tok):
        z = P.sb([128, 64], F32, "dtz")
        t1 = P.sb([128, 64], F32, "dt1")
        t2 = P.sb([128, 64], F32, "dt2")
        P.tt(z.v(), ps[:, 0:64], lay["dt_bias"].v(), ALU.add)
        mx, l = softplus_parts(P, z.v(), 64, t1.v(), t2.v())
        P.tt(z.v(), mx, l, ALU.add)
        P.tt(t1.v(), z.v(), lay["ssd_A"].v(), ALU.mult)
        P.dma(sp["dt"][tok:tok + 128, :], z.v())
        P.dma(sp["a"][tok:tok + 128, :], t1.v())

    def do_xbc(chunks):
        c0, _ = COLS["ssd_xbc"]
        with scope(P):
            ec = P.stack.enter_context
            rawl_t = ec(P.sb_ctx([128, 2, TL + 4], F32, "rawl"))
            rawc_t = ec(P.sb_ctx([128, 2, TC + 4], F32, "rawc"))
            acc_t = ec(P.sb_ctx([128, 2, TL], F32, "acc"))
            xs_t = ec(P.sb_ctx([128, 4, TT], BF16, "xsT"))
            xtm_t = ec(P.sb_ctx([128, 2, 512], BF16, "xtm"))
            rawl = [Buf(rawl_t[:, i]) for i in range(2)]
            rawc = [Buf(rawc_t[:, i]) for i in range(2)]
            acc = [Buf(acc_t[:, i]) for i in range(2)]
            xs = [Buf(xs_t[:, i]) for i in range(4)]
            xtm = [Buf(xtm_t[:, i]) for i in range(2)]
            for i in range(2):
                P.memset(rawc[i][:, 0:2], 0.0)
                P.memset(rawc[i][:, TC + 2:TC + 4], 0.0)
            w = None
            nx = 0
            for ci, ch in enumerate(chunks):
                if ci % 4 == 0 or w is None:
                    w = loadw(c0 + ch * 128, 512)
                    wbase = ch
                rl, rc, ac = rawl[ci % 2], rawc[ci % 2], acc[ci % 2]
                for b in range(nblk):
                    n = H.sizes[b]
                    ps = fm_mm(w, (ch - wbase) * 128, 128, b)
                    if H.kinds[b] == "lat":
                        P.copy(rl[:, 2 + H.offs[b]:2 + H.offs[b] + n], ps[:, :n], eng=("act" if b % 2 else "dve"))
                    elif H.kinds[b] == "ctx":
                        P.copy(rc[:, 2:2 + n], ps[:, :n], eng="act")
                    else:
                        P.ts(rl[:, 0:2], ps[:, 0:2], lay["halo_valid"][:, 0:1], None, ALU.mult)
                        P.ts(rl[:, TL + 2:TL + 4], ps[:, 2:4], lay["halo_valid"][:, 1:2], None, ALU.mult)
                for (raw, T_, o_) in ((rl, TL, 0), (rc, TC, TL)):
                    a = ac[:, 0:T_]
                    P.ts(a, raw[:, 0:T_], lay["conv_w"][:, 0, ch:ch + 1], lay["conv_b"][:, ch:ch + 1], ALU.mult, ALU.add)
                    for o in range(1, 5):
                        P.stt(a, raw[:, o:o + T_], lay["conv_w"][:, o, ch:ch + 1], a, ALU.mult, ALU.add,
                              eng=("dve" if o % 2 else "pool") if False else "dve")
                    if ch < 16:
                        P.act(xs[ci % 4][:, o_:o_ + T_], a, AF.Silu)
                    else:
                        e = xs[ci % 4]
                        P.act(e[:, o_:o_ + T_], a, AF.Silu)
                        which = "BT" if ch < 20 else "CT"
                        P.dma(sp[which][(ch - 16) % 4, :, o_:o_ + T_], e[:, o_:o_ + T_])
                if ch < 16 and ci % 4 == 3:
                    for tcn in range(NCH):
                        psb = P.next_ps()
                        pv = psb.v().m(lambda a: a.bitcast(BF16))
                        for q4 in range(4):
                            P.tr(pv[:, q4 * 128:(q4 + 1) * 128], xs[q4][:, tcn * 128:(tcn + 1) * 128], env.cm("ident"))
                        xo = xtm[nx % 2]
                        nx += 1
                        P.copy(xo.v(), pv[:, 0:512], eng=("act" if nx % 2 else "dve"))
                        P.dma(sp["x"][tcn * 128:(tcn + 1) * 128, (ch - 3) * 128:(ch + 1) * 128], xo.v())
            P.barrier()

    for grp in groups:
        if grp in ("ret", "gla"):
            do_qk(grp)
        elif grp == "v":
            do_tm("ret_v", h_copy(sp["v"][0], 0))
            do_tm("gla_v", h_copy(sp["v"][1], 0))
        elif grp == "gates":
            do_tm("ret_g", h_copy(sp["gate"], 0, AF.Silu))
            do_tm("gla_r", h_copy(sp["gate"], 1024, AF.Silu))
            do_tm("ssd_z", h_copy(sp["gate"], 2048, AF.Silu))
        elif grp == "dt":
            do_tm("ssd_dt", h_dt, width=64)
        elif grp == "xB":
            do_xbc(list(range(0, 20)))
        elif grp == "xBC":
            do_xbc(list(range(0, 24)))
        elif grp == "merge":
            c0, sz = COLS["merge"]
            for cc in range(0, sz, 512):
                w = loadw(c0 + cc, 512)
                for q4 in range(4):
                    for b in tokblocks:
                        n = H.sizes[b]
                        ps = fm_mm(w, q4 * 128, 128, b)
                        e = evt[ev["n"] % 4]
                        ev["n"] += 1
                        P.act(e[:, :n], ps[:, :n], AF.Sigmoid)
                        P.dma(sp["sg"][cc // 128 + q4, :, spoff[b]:spoff[b] + n], e[:, :n])
    P.barrier()


def bc(v, axis, shape):
    return v.m(lambda a: a.unsqueeze(axis).to_broadcast(list(shape)))


def r3(v, pat, **kw):
    return v.m(lambda a: a.rearrange(pat, **kw))


class SweepBufs:
    def __init__(self, P, do_y, with_post):
        st = P.stack

        def mk(shape, dt, n, name):
            t = st.enter_context(P.sb_ctx([128, n] + list(shape), dt, name))
            return [Buf(t[:, i]) for i in range(n)]
        self.kT = [mk([4, 128], BF16, 2, "kT0"), mk([4, 128], BF16, 2, "kT1")]
        self.v = [mk([1024], BF16, 2, "v0"), mk([1024], BF16, 2, "v1")]
        self.eG = [mk([4], F32, 2, "eG0"), mk([4], F32, 2, "eG1")]
        self.BT = mk([4, 128], BF16, 2, "BT")
        self.x = mk([2048], BF16, 2, "xtm")
        self.a = mk([64], F32, 2, "a")
        self.dt = mk([64], F32, 2, "dt")
        self.ktm = mk([512], BF16, 2, "ktm")
        self.btm = mk([512], BF16, 1, "btm")
        self.eT = mk([96], F32, 2, "eT")
        self.wv = mk([32], F32, 1, "wv")
        self.vs = mk([2048], BF16, 1, "vs")
        if do_y:
            self.qT = [mk([4, 128], BF16, 2, "qT0"), mk([4, 128], BF16, 2, "qT1")]
            self.CT = mk([4, 128], BF16, 2, "CT")
            self.attT = mk([4, 128], BF16, 2, "attT")
            self.vdt = mk([2048], BF16, 1, "vdt")
            self.cbm = mk([4, 128], BF16, 1, "cbm")
            self.rhsD = mk([32, 128], BF16, 1, "rhsD")
            self.L = mk([4, 128], BF16, 4, "L")
            self.att = mk([32, 128], BF16, 1, "att")
            self.tmp = mk([512], F32, 2, "tmp")
            self.yall = mk([4096], F32, 2, "yall")
            self.yb = mk([4096], BF16, 1, "yb")
        if with_post:
            self.gate = mk([4096], BF16, 2, "gate")
            self.ss = mk([12], F32, 2, "ss")
            self.junk = mk([512], F32, 1, "junk")
            self.ygate = mk([4096], BF16, 1, "ygate")
            self.ygT = mk([32, 128], BF16, 1, "ygT")


def sweep(P, env, sp, lay, d, chain, S, SB, do_y, B, mode, Dt=None):
    msk_f32 = env.cm("mle" if d == 0 else "mge", bf=False)
    msk_bf = env.cm("mle" if d == 0 else "mge")
    sl_f32 = env.cm("sgt" if d == 0 else "slt", bf=False)
    sl_bf = env.cm("sgt" if d == 0 else "slt")
    ident = env.cm("ident")
    post = mode == "post"

    def bfview(psb):
        return psb.v().m(lambda ap: ap.bitcast(BF16))

    def front(it, c):
        par = it % 2
        t0 = c * 128
        kT = [B.kT[m][par] for m in range(2)]
        v = [B.v[m][par] for m in range(2)]
        eG = [B.eG[m][par] for m in range(2)]
        BT, x, a, dt = B.BT[par], B.x[par], B.a[par], B.dt[par]
        for m in range(2):
            P.dma(kT[m].v(), View(sp["qk"], sp["qk"].ap[m, :, 2 * d + 1, :, t0:t0 + 128].rearrange("h p t -> p h t")))
            P.dma(v[m].v(), sp["v"][m, t0:t0 + 128, :])
            P.dma(eG[m].v(), sp["eG"][m * 2 + d, :, c, :])
        P.dma(BT.v(), View(sp["BT"], sp["BT"].ap[:, :, t0:t0 + 128].rearrange("g p t -> p g t")))
        P.dma(x.v(), sp["x"][t0:t0 + 128, :])
        P.dma(a.v(), sp["a"][t0:t0 + 128, :])
        P.dma(dt.v(), sp["dt"][t0:t0 + 128, :])
        if do_y:
            qT = [B.qT[m][par] for m in range(2)]
            CT = B.CT[par]
            for m in range(2):
                P.dma(qT[m].v(), View(sp["qk"], sp["qk"].ap[m, :, 2 * d, :, t0:t0 + 128].rearrange("h p t -> p h t")))
            P.dma(CT.v(), View(sp["CT"], sp["CT"].ap[:, :, t0:t0 + 128].rearrange("g p t -> p g t")))
            yall = B.yall[par]
        if post:
            yb, gate = B.yb[0], B.gate[par]
            P.dma(yb.v(), sp["yb"][t0:t0 + 128, :])
            P.dma(gate.v(), sp["gate"][t0:t0 + 128, :])
        a_d = a[:, d * 32:(d + 1) * 32]
        dt_d = dt[:, d * 32:(d + 1) * 32]
        x3 = r3(x.v(), "p (h e) -> p h e", e=64)
        ktm = [B.ktm[0], B.ktm[1]]
        btm = B.btm[0]
        for m in range(2):
            pv = bfview(P.next_ps())
            for h in range(4):
                P.tr(pv[:, h * 128:(h + 1) * 128], kT[m][:, h, :], ident)
            P.copy(ktm[m].v(), pv[:, 0:512], eng="act")
        pv = bfview(P.next_ps())
        for g in range(4):
            P.tr(pv[:, g * 128:(g + 1) * 128], BT[:, g, :], ident)
        P.copy(btm.v(), pv[:, 0:512], eng="act")
        pss = P.next_ps()
        P.mm(pss[:, 0:32], env.ones_f.v(), a_d)
        P.mm(pss[:, 32:64], sl_f32, a_d)
        if do_y:
            P.mm(pss[:, 64:96], msk_f32, a_d)
        ne = 96 if do_y else 64
        eT = B.eT[par]
        P.act(eT[:, 0:ne], pss[:, 0:ne], AF.Exp)
        if Dt is not None:
            P.tt(Dt[2].v(), Dt[2].v(), pss[:, 0:32], ALU.add)
        if do_y:
            attT = [B.attT[0], B.attT[1]]
            for m in range(2):
                aps = P.next_ps()
                for h in range(4):
                    P.mm(aps[:, h * 128:(h + 1) * 128], kT[m][:, h, :], qT[m][:, h, :])
                P.tt(attT[m].v(), r3(aps.v(), "p (h t) -> p h t", h=4), bc(msk_f32, 1, [128, 4, 128]), ALU.mult)
        wv = B.wv[0]
        P.tt(wv.v(), eT[:, 32:64], dt_d, ALU.mult)
        vs = B.vs[0]
        P.tt(r3(vs.v(), "p (h e) -> p h e", e=64), x3, bc(wv.v(), 2, [128, 32, 64]), ALU.mult)
        cnt = {"e": 0}

        def retgla_y(m):
            if not do_y:
                return
            for hp in range(2):
                yp = P.next_ps()
                for hh in range(2):
                    h = hp * 2 + hh
                    o = yp[:, hh * 256:(hh + 1) * 256]
                    if post:
                        P.mm(o, ident, yb[:, m * 1024 + h * 256:m * 1024 + (h + 1) * 256], start=True, stop=False)
                    P.mm(o, attT[m][:, h, :], v[m][:, h * 256:(h + 1) * 256], start=(not post), stop=False)
                    P.mm(o, qT[m][:, h, :], SB[m][:, h * 256:(h + 1) * 256], start=False, stop=True)
                cs = slice(m * 1024 + hp * 512, m * 1024 + (hp + 1) * 512)
                cnt["e"] += 1
                P.copy(yall[:, cs], yp.v(), eng=("act" if cnt["e"] % 2 else "dve"))

        def retgla_state(m):
            for hp in range(2):
                ps2 = P.next_ps()
                for hh in range(2):
                    h = hp * 2 + hh
                    P.mm(ps2[:, hh * 256:(hh + 1) * 256], ktm[m][:, h * 128:(h + 1) * 128], v[m][:, h * 256:(h + 1) * 256])
                for hh in range(2):
                    h = hp * 2 + hh
                    hc = slice(h * 256, (h + 1) * 256)
                    P.act(S[m][:, hc], S[m][:, hc], AF.Copy, scale=eG[m][:, h:h + 1])
                    P.stt(S[m][:, hc], ps2[:, hh * 256:(hh + 1) * 256], eG[m][:, h:h + 1], S[m][:, hc], ALU.mult, ALU.add)
            P.copy(SB[m].v(), S[m].v(), eng="act")
            if Dt is not None:
                P.tt(Dt[m].v(), Dt[m].v(), eG[m].v(), ALU.mult)

        if do_y:
            vdt = B.vdt[0]
            vdt3 = r3(vdt.v(), "p (h e) -> p h e", e=64)
            P.tt(vdt3, x3, bc(dt_d, 2, [128, 32, 64]), ALU.mult)
            cb = P.next_ps()
            for g in range(4):
                P.mm(cb[:, g * 128:(g + 1) * 128], BT[:, g, :], CT[:, g, :])
            cbm = B.cbm[0]
            P.tt(cbm.v(), r3(cb.v(), "p (g t) -> p g t", g=4), bc(msk_f32, 1, [128, 4, 128]), ALU.mult)
            rhsD = B.rhsD[0]
            P.tt(rhsD.v(), bc(msk_bf, 1, [128, 32, 128]), bc(a_d, 2, [128, 32, 128]), ALU.mult)
            retgla_y(0)
            att = B.att[0]
            for g in range(4):
                for half in range(2):
                    dps = P.next_ps()
                    hsl = slice(g * 8 + half * 4, g * 8 + half * 4 + 4)
                    P.mm(dps.v(), sl_bf, r3(rhsD[:, hsl, :], "p h t -> p (h t)"))
                    L = B.L[(g * 2 + half) % 4]
                    P.act(r3(L.v(), "p h t -> p (h t)"), dps.v(), AF.Exp)
                    P.tt(att[:, hsl, :], L.v(), bc(cbm[:, g, :], 1, [128, 4, 128]), ALU.mult)
                if g == 0:
                    retgla_state(0)
                elif g == 1:
                    retgla_y(1)
                elif g == 2:
                    retgla_state(1)
            for g in range(4):
                yp = P.next_ps()
                cs = slice(2048 + g * 512, 2048 + (g + 1) * 512)
                if post:
                    P.mm(yp.v(), ident, yb[:, cs], start=True, stop=False)
                for h in range(8):
                    hs = slice(h * 64, (h + 1) * 64)
                    if post:
                        P.mm(yp[:, hs], att[:, g * 8 + h, :], vdt3[:, g * 8 + h, :], start=False, stop=False)
                        P.mm(yp[:, hs], lay["dI"][:, g * 8 + h, :], x3[:, g * 8 + h, :], start=False, stop=(h == 7))
                    else:
                        P.mm(yp[:, hs], att[:, g * 8 + h, :], vdt3[:, g * 8 + h, :])
                yi = P.next_ps()
                P.mm(yi.v(), CT[:, g, :], SB[2][:, g * 512:(g + 1) * 512])
                tmp = B.tmp[g % 2]
                P.tt(r3(tmp.v(), "p (h e) -> p h e", e=64), r3(yi.v(), "p (h e) -> p h e", e=64),
                     bc(eT[:, 64 + g * 8:64 + (g + 1) * 8], 2, [128, 8, 64]), ALU.mult)
                P.tt(yall[:, cs], tmp.v(), yp.v(), ALU.add)
        else:
            retgla_state(0)
            retgla_state(1)
        P.tt(r3(S[2].v(), "p (h e) -> p h e", e=64), r3(S[2].v(), "p (h e) -> p h e", e=64),
             bc(eT[:, 0:32], 2, [128, 32, 64]), ALU.mult)
        for g in range(4):
            ps = P.next_ps()
            P.mm(ps.v(), btm[:, g * 128:(g + 1) * 128], vs[:, g * 512:(g + 1) * 512])
            gc = slice(g * 512, (g + 1) * 512)
            P.tt(S[2][:, gc], S[2][:, gc], ps.v(), ALU.add)
        P.copy(SB[2].v(), S[2].v(), eng="act")

    def back(it, c):
        par = it % 2
        t0 = c * 128
        yall = B.yall[par]
        if mode == "yb":
            ybo = B.yb[0]
            P.copy(ybo[:, 0:2048], yall[:, 0:2048], eng="act")
            P.copy(ybo[:, 2048:4096], yall[:, 2048:4096], eng="dve")
            P.dma(sp["yb"][t0:t0 + 128, :], ybo.v())
            return
        gate = B.gate[par]
        ss, junk, ygate = B.ss[par], B.junk[0], B.ygate[0]
        P.tt(yall[:, 2048:4096], yall[:, 2048:4096], gate[:, 2048:4096], ALU.mult)
        for h in range(8):
            P.act(junk[:, 0:256], yall[:, h * 256:(h + 1) * 256], AF.Square, accum=ss[:, h:h + 1])
        for g in range(4):
            P.act(junk[:, 0:512], yall[:, 2048 + g * 512:2048 + (g + 1) * 512], AF.Square, accum=ss[:, 8 + g:9 + g])
        P.ts(ss[:, 0:8], ss[:, 0:8], 1.0 / 256, EPS, ALU.mult, ALU.add)
        P.ts(ss[:, 8:12], ss[:, 8:12], 1.0 / 512, EPS, ALU.mult, ALU.add)
        P.act(ss.v(), ss.v(), AF.Ln)
        P.act(ss.v(), ss.v(), AF.Exp, scale=-0.5)
        for h in range(4):
            hc = slice(h * 256, (h + 1) * 256)
            P.stt(ygate[:, hc], yall[:, hc], ss[:, h:h + 1], gate[:, hc], ALU.mult, ALU.mult)
        for h in range(4):
            hc = slice(1024 + h * 256, 1024 + (h + 1) * 256)
            P.stt(junk[:, 256:512], yall[:, hc], ss[:, 4 + h:5 + h], gate[:, hc], ALU.mult, ALU.mult)
            P.tt(ygate[:, hc], junk[:, 256:512], lay["gla_g"].v(), ALU.mult)
        for g in range(4):
            gc = slice(2048 + g * 512, 2048 + (g + 1) * 512)
            P.stt(ygate[:, gc], yall[:, gc], ss[:, 8 + g:9 + g], lay["ssd_ng"][:, g * 512:(g + 1) * 512], ALU.mult, ALU.mult)
        ygT = B.ygT[0]
        for q in range(4):
            pv = bfview(P.next_ps())
            for k8 in range(8):
                k = q * 8 + k8
                P.tr(pv[:, k8 * 128:(k8 + 1) * 128], ygate[:, k * 128:(k + 1) * 128], ident)
            P.copy(r3(ygT[:, q * 8:(q + 1) * 8, :], "p k t -> p (k t)"), pv[:, 0:1024], eng=("act" if q % 2 else "dve"))
        P.dma(View(sp["ygT"], sp["ygT"].ap[:, :, t0:t0 + 128].rearrange("k p t -> p k t")), ygT.v())

    prev = None
    for it, c in enumerate(chain):
        front(it, c)
        if mode != "state":
            if prev is not None:
                back(*prev)
            prev = (it, c)
    if prev is not None:
        back(*prev)


def phase_c(P, env, sp, X, spoff, gate5, wbL, woL, blks):
    with scope(P):
        yg_t = P.stack.enter_context(P.sb_ctx([128, 32, 512], BF16, "ygblk"))
        sg_t = P.stack.enter_context(P.sb_ctx([128, 24, 512], BF16, "sgblk"))
        wb_t = P.stack.enter_context(P.sb_ctx([128, 2, 32, 128], BF16, "wbf"))
        wo_t = P.stack.enter_context(P.sb_ctx([128, 2, 8, 128], BF16, "wof"))
        mg_t = P.stack.enter_context(P.sb_ctx([128, 8, 512], BF16, "mg"))
        t_t = P.stack.enter_context(P.sb_ctx([128, 3, 512], F32, "t123"))
        yg, sg, mg = Buf(yg_t[:]), Buf(sg_t[:]), Buf(mg_t[:])
        wbf = [Buf(wb_t[:, i]) for i in range(2)]
        wof = [Buf(wo_t[:, i]) for i in range(2)]
        t = [Buf(t_t[:, i]) for i in range(3)]
        nw = 0
        for b in blks:
            n = X.sizes[b]
            o = spoff[b]
            P.dma(yg[:, :, :n], View(sp["ygT"], sp["ygT"].ap[:, :, o:o + n].rearrange("k p t -> p k t")))
            P.dma(sg[:, :, :n], View(sp["sg"], sp["sg"].ap[:, :, o:o + n].rearrange("k p t -> p k t")))
            for f in range(KC):
                w = wbf[nw % 2]
                nw += 1
                P.dma(w.v(), wbL[f], q="pool")
                for i, (k0, k1) in enumerate(((0, 8), (8, 16), (16, 32))):
                    ps = P.next_ps()
                    for k in range(k0, k1):
                        P.mm(ps[:, :n], w[:, k, :], yg[:, k, :n], start=(k == k0), stop=(k == k1 - 1))
                    P.tt(t[i][:, :n], ps[:, :n], sg[:, i * 8 + f, :n], ALU.mult, eng=("dve" if i != 1 else "pool") if False else "dve")
                P.tt(t[0][:, :n], t[0][:, :n], t[1][:, :n], ALU.add)
                P.tt(mg[:, f, :n], t[0][:, :n], t[2][:, :n], ALU.add)
            hg = gate5[X.kinds[b]]
            for f in range(KC):
                w = wof[f % 2]
                P.dma(w.v(), woL[f], q="pool")
                ps = P.next_ps()
                for k in range(KC):
                    P.mm(ps[:, :n], w[:, k, :], mg[:, k, :n], start=(k == 0), stop=(k == KC - 1))
                P.stt(X.b[b][:, f, :], ps[:, :n], hg[:, f:f + 1], X.b[b][:, f, :], ALU.mult, ALU.add)


def load_small(P, ins, names):
    lay = {}
    for nm in names:
        b = ins[nm]
        shape = list(b.ap.shape)
        t = P.sb(shape, F32, "l_" + nm)
        P.dma(t.v(), b.v())
        lay[nm] = t
    return lay


def derive_layer(P, env, lay, need_post):
    lg = P.sb([128, 8], F32, "lg")
    t1 = P.sb([128, 8], F32, "lgt1")
    t2 = P.sb([128, 8], F32, "lgt2")
    P.act(t1.v(), lay["ret_logit"].v(), AF.Abs)
    P.act(t1.v(), t1.v(), AF.Exp, scale=-1.0)
    P.act(t1.v(), t1.v(), AF.Ln, bias=1.0)
    P.ts(t2.v(), lay["ret_logit"].v(), 0.0, None, ALU.min)
    P.tt(lg.v(), t2.v(), t1.v(), ALU.subtract)
    nlg = P.sb([128, 8], F32, "nlg")
    P.ts(nlg.v(), lg.v(), -1.0, None, ALU.mult)
    tab = P.sb([128, 2, 2, 4, 128], F32, "ret_tab")
    for d in range(2):
        row = env.row_up if d == 0 else env.row_dn
        for h in range(4):
            P.act(tab[:, d, 0, h, :], row, AF.Exp, scale=lg[:, d * 4 + h:d * 4 + h + 1])
            P.act(tab[:, d, 1, h, :], row, AF.Exp, scale=nlg[:, d * 4 + h:d * 4 + h + 1])
    lay["ret_tab"] = tab
    eg = P.sb([128, 2, 4], F32, "ret_eg")
    P.act(eg.v().m(lambda a: a.rearrange("p d h -> p (d h)")), lg.v(), AF.Exp, scale=128.0)
    lay["ret_eg"] = eg
    wa2b = P.sb([16, 2, 512], BF16, "wa2b")
    P.copy(wa2b.v(), lay["wa2"].v())
    lay["wa2"] = wa2b
    A = P.sb([128, 64], F32, "ssdA")
    P.act(A.v(), lay["a_log"].v(), AF.Exp)
    P.ts(A.v(), A.v(), -1.0, None, ALU.mult)
    lay["ssd_A"] = A


SMALL_COMMON = ["mod_lat", "mod_ctx", "norm_g", "ret_logit", "wa2", "gla_ba", "conv_w", "conv_b", "dt_bias", "a_log",
                "halo_valid"]
SMALL_SHAPES = {"mod_lat": [128, 9, 8], "mod_ctx": [128, 9, 8], "norm_g": [128, 3, 8], "ret_logit": [128, 8],
                "wa2": [16, 2, 512], "gla_ba": [128, 2, 4], "conv_w": [128, 5, 24], "conv_b": [128, 24],
                "dt_bias": [128, 64], "a_log": [128, 64], "halo_valid": [128, 2],
                "gla_g": [128, 256], "ssd_d": [128, 32], "ssd_ng": [128, 2048], "final_g": [128, 8]}
SIZES = [512, 512, 512, 512, TC, 4]
KINDS = ["lat", "lat", "lat", "lat", "ctx", "halo"]
TIN = TL + TC + 4


def make_spill(P, full):
    sp = {}
    sp["qk"] = P.dram("sp_qk", [2, 4, 4, 128, TT], BF16)
    sp["eG"] = P.dram("sp_eG", [4, 128, NCH, 4], F32)
    sp["v"] = P.dram("sp_v", [2, TT, 1024], BF16)
    sp["BT"] = P.dram("sp_BT", [4, 128, TT], BF16)
    sp["x"] = P.dram("sp_x", [TT, 2048], BF16)
    sp["a"] = P.dram("sp_a", [TT, 64], F32)
    sp["dt"] = P.dram("sp_dt", [TT, 64], F32)
    if full:
        sp["CT"] = P.dram("sp_CT", [4, 128, TT], BF16)
        sp["gate"] = P.dram("sp_gate", [TT, 4096], BF16)
        sp["sg"] = P.dram("sp_sg", [24, 128, TT], BF16)
        sp["yb"] = P.dram("sp_yb", [TT, 4096], BF16)
        sp["ygT"] = P.dram("sp_ygT", [32, 128, TT], BF16)
    return sp


def alloc_states(P):
    S = [P.sb([128, 1024], F32, "S_ret"), P.sb([128, 1024], F32, "S_gla"), P.sb([128, 2048], F32, "S_ssd")]
    SB = [P.sb([128, 1024], BF16, "SB_ret"), P.sb([128, 1024], BF16, "SB_gla"), P.sb([128, 2048], BF16, "SB_ssd")]
    return S, SB


def zero_states(P, S, SB):
    for i in range(3):
        P.memset(S[i].v(), 0.0, eng="pool")
        P.memset(SB[i].v(), 0.0, eng="pool")


def build_layer(phase, last=False):
    nc = bass.Bass("TRN2", target_bir_lowering=False)
    with contextlib.ExitStack() as stack:
        P = Prog(nc, stack)
        P.psum_banks()
        env = Env()
        ins = {}

        def inp(name, shape, dt=F32):
            ins[name] = P.dram(name, shape, dt, "ExternalInput")
            return ins[name]
        inp("cin", [128, 1024])
        inp("xin", [D, TIN])
        for nm in SMALL_COMMON:
            inp(nm, SMALL_SHAPES[nm])
        inp("rope_in", [128, 2, TL])
        inp("w_inL", [128, KC, DIN])
        if phase == 1:
            inp("wiL", [NFF, 128, KC, 256])
            inp("woL", [128, NFF, D])
            x1out = P.dram("x1T", [D, TT], F32, "ExternalOutput")
            Eout = P.dram("Eout", [2, 128, 4096], F32, "ExternalOutput")
            Dout = P.dram("Dout", [2, 128, 40], F32, "ExternalOutput")
        else:
            for nm in ("gla_g", "ssd_d", "ssd_ng", "final_g"):
                inp(nm, SMALL_SHAPES[nm])
            inp("wiL", [NFF, 128, KC, 256])
            inp("woL", [128, NFF, D])
            inp("wbL", [KC, 128, 32, 128])
            inp("w_outL", [KC, 128, KC, 128])
            inp("slotE", [2, 7, 128, 4096])
            inp("slotD", [2, 7, 128, 40])
            xout = P.dram("xoutT", [D, TT], F32, "ExternalOutput")
        load_consts(P, env, ins["cin"])
        lay = load_small(P, ins, SMALL_COMMON + ([] if phase == 1 else ["gla_g", "ssd_d", "ssd_ng", "final_g"]))
        lay["rope_in"] = ins["rope_in"]
        derive_layer(P, env, lay, phase == 2)
        modT = {"lat": lay["mod_lat"], "ctx": lay["mod_ctx"]}
        sp = make_spill(P, phase == 2)
        xv = ins["xin"].ap.rearrange("(k p) t -> p k t", p=128)
        offs = np.concatenate([[0], np.cumsum(SIZES)]).astype(int).tolist()
        spoff = {b: offs[b] for b in range(5)}

        def wi_view(buf):
            return buf

        with scope(P):
            H = Blocks(P, SIZES, KINDS, "hT", BF16)
            scr = (P.sb([128, KC, 512], BF16, "sq"), P.sb([128, KC, 512], F32, "tmpn"), P.sb([128, 512], F32, "rstd"))
            if phase == 1:
                with scope(P):
                    X = Blocks(P, SIZES, KINDS, "xTs")
                    for b in range(6):
                        P.dma(X.b[b].v(), View(ins["xin"], xv[:, :, offs[b]:offs[b] + SIZES[b]]))
                    m0 = prep_mod(P, modT, lay["norm_g"], 0)
                    m0["halo"] = m0["lat"]
                    for b in range(6):
                        gs, sh, hg = m0[KINDS[b]]
                        norm_mod(P, X.b[b], H.b[b], SIZES[b], gs, sh, env.ones_bf, scr)
                    ffn_tiled(P, X, H, m0, ins["wiL"], ins["woL"])
                    ov = x1out.ap.rearrange("(k p) t -> p k t", p=128)
                    for b in range(5):
                        P.dma(View(x1out, ov[:, :, offs[b]:offs[b] + SIZES[b]]), X.b[b].v())
                    m1 = prep_mod(P, modT, lay["norm_g"], 1)
                    m1["halo"] = m1["lat"]
                    for b in range(6):
                        gs, sh, hg = m1[KINDS[b]]
                        norm_mod(P, X.b[b], H.b[b], SIZES[b], gs, sh, env.ones_bf, scr)
            else:
                with scope(P):
                    xt_t = P.stack.enter_context(P.sb_ctx([128, 2, KC, 512], F32, "xtmp"))
                    xt = [Buf(xt_t[:, i]) for i in range(2)]
                    m1 = prep_mod(P, modT, lay["norm_g"], 1)
                    m1["halo"] = m1["lat"]
                    for b in range(6):
                        n = SIZES[b]
                        xb = xt[b % 2]
                        P.dma(xb[:, :, :n], View(ins["xin"], xv[:, :, offs[b]:offs[b] + n]))
                        gs, sh, hg = m1[KINDS[b]]
                        norm_mod(P, _sub(xb, n), H.b[b], n, gs, sh, env.ones_bf, scr)
            groups = ["ret", "gla", "v", "dt", "xB"] if phase == 1 else ["ret", "gla", "v", "gates", "dt", "xBC", "merge"]
            lay["halo_valid"] = lay["halo_valid"]
            inproj(P, env, H, ins["w_inL"], sp, lay, groups, do_q=(phase == 2))
        with scope(P):
            S, SB = alloc_states(P)
            if phase == 1:
                B = SweepBufs(P, False, False)
                Dt = [P.sb([128, 4], F32, "Dt0"), P.sb([128, 4], F32, "Dt1"), P.sb([128, 32], F32, "Dt2")]
                for d in range(2):
                    zero_states(P, S, SB)
                    P.memset(Dt[0].v(), 1.0)
                    P.memset(Dt[1].v(), 1.0)
                    P.memset(Dt[2].v(), 0.0)
                    chain = list(range(NCH_L)) if d == 0 else list(range(NCH_L - 1, -1, -1))
                    sweep(P, env, sp, lay, d, chain, S, SB, False, B, "state", Dt)
                    P.dma(Eout[d, :, 0:1024], S[0].v())
                    P.dma(Eout[d, :, 1024:2048], S[1].v())
                    P.dma(Eout[d, :, 2048:4096], S[2].v())
                    dd = P.sb([128, 40], F32, "dd")
                    P.copy(dd[:, 0:4], Dt[0].v())
                    P.copy(dd[:, 4:8], Dt[1].v())
                    P.act(dd[:, 8:40], Dt[2].v(), AF.Exp)
                    P.dma(Dout[d], dd.v())
            else:
                B = SweepBufs(P, True, True)
                dI = P.sb([128, 32, 128], BF16, "dI")
                for h in range(32):
                    P.ts(dI[:, h, :], env.cm("ident", bf=False), lay["ssd_d"][:, h:h + 1], None, ALU.mult)
                lay["dI"] = dI
                et = B.yall[0]
                dtile = P.sb([128, 40], F32, "slotD_t")
                for d in (1, 0):
                    zero_states(P, S, SB)
                    cchain = [NCH_L, NCH_L + 1] if d == 0 else [NCH_L + 1, NCH_L]
                    sweep(P, env, sp, lay, d, cchain, S, SB, True, B, "post" if d == 0 else "yb")
                    for s in range(7):
                        P.dma(et.v(), ins["slotE"][d, s])
                        P.dma(dtile.v(), ins["slotD"][d, s])
                        for h in range(4):
                            hc = slice(h * 256, (h + 1) * 256)
                            P.stt(S[0][:, hc], S[0][:, hc], dtile[:, h:h + 1], et[:, hc], ALU.mult, ALU.add)
                            hc2 = slice(1024 + h * 256, 1024 + (h + 1) * 256)
                            P.stt(S[1][:, hc], S[1][:, hc], dtile[:, 4 + h:5 + h], et[:, hc2], ALU.mult, ALU.add)
                        for h in range(32):
                            hc = slice(h * 64, (h + 1) * 64)
                            hc2 = slice(2048 + h * 64, 2048 + (h + 1) * 64)
                            P.stt(S[2][:, hc], S[2][:, hc], dtile[:, 8 + h:9 + h], et[:, hc2], ALU.mult, ALU.add)
                    for i in range(3):
                        P.copy(SB[i].v(), S[i].v(), eng="act")
                    chain = list(range(NCH_L)) if d == 0 else list(range(NCH_L - 1, -1, -1))
                    sweep(P, env, sp, lay, d, chain, S, SB, True, B, "post" if d == 0 else "yb")
        if phase == 2:
            with scope(P):
                X = Blocks(P, SIZES[:5], KINDS[:5], "xTs")
                for b in range(5):
                    P.dma(X.b[b].v(), View(ins["xin"], xv[:, :, offs[b]:offs[b] + SIZES[b]]))
                g5 = {}
                for kind in ("lat", "ctx"):
                    g5[kind] = modT[kind][:, 5, :]
                blks = [0, 1, 2, 3] + ([] if last else [4])
                phase_c(P, env, sp, X, spoff, g5, ins["wbL"], ins["w_outL"], blks)
                with scope(P):
                    H2 = Blocks(P, SIZES[:5], KINDS[:5], "h2T", BF16)
                    scr = (P.sb([128, KC, 512], BF16, "sq"), P.sb([128, KC, 512], F32, "tmpn"), P.sb([128, 512], F32, "rstd"))
                    m2 = prep_mod(P, modT, lay["norm_g"], 2)
                    for b in range(5):
                        gs, sh, hg = m2[KINDS[b]]
                        norm_mod(P, X.b[b], H2.b[b], SIZES[b], gs, sh, env.ones_bf, scr)
                    ffn_tiled(P, X, H2, m2, ins["wiL"], ins["woL"], G=4)
                ov = xout.ap.rearrange("(k p) t -> p k t", p=128)
                if last:
                    with scope(P):
                        scr = (P.sb([128, KC, 512], BF16, "sq"), P.sb([128, KC, 512], F32, "tmpn"), P.sb([128, 512], F32, "rstd"))
                        zt = P.sb([128, KC], F32, "zeros8")
                        P.memset(zt.v(), 0.0)
                        o_t = P.stack.enter_context(P.sb_ctx([128, 2, KC, 512], F32, "fo"))
                        ob = [Buf(o_t[:, i]) for i in range(2)]
                        for b in range(4):
                            norm_mod(P, X.b[b], ob[b % 2], 512, lay["final_g"], zt, env.ones_bf, scr)
                            P.dma(View(xout, ov[:, :, offs[b]:offs[b] + 512]), ob[b % 2].v())
                        P.dma(View(xout, ov[:, :, offs[4]:offs[4] + TC]), X.b[4].v())
                else:
                    for b in range(5):
                        P.dma(View(xout, ov[:, :, offs[b]:offs[b] + SIZES[b]]), X.b[b].v())
        P.emit()
    return nc


class _sub:
    def __init__(self, buf, n):
        self.buf = buf
        self.n = n

    def v(self):
        return View(self.buf, self.buf.ap[:, :, :self.n])

    def __getitem__(self, idx):
        return View(self.buf, self.buf.ap[:, :, :self.n][idx])


def ffn_tiled(P, X, H, mods, wiL, woL, G=6):
    nblk = len(X.sizes)
    T = X.T
    with scope(P):
        actT_t = P.stack.enter_context(P.sb_ctx([128, G, T], BF16, "actT"))
        wi_t = P.stack.enter_context(P.sb_ctx([128, 2, KC, 256], BF16, "wi_t"))
        wo_t = P.stack.enter_context(P.sb_ctx([128, G, D], BF16, "wo_t"))
        sil_t = P.stack.enter_context(P.sb_ctx([128, 2, 512], F32, "sil"))
        actb = [[Buf(actT_t[:, g, X.offs[b]:X.offs[b] + X.sizes[b]]) for b in range(nblk)] for g in range(G)]
        wib = [Buf(wi_t[:, i]) for i in range(2)]
        wob = Buf(wo_t[:])
        silb = [Buf(sil_t[:, i]) for i in range(2)]
        ns = 0
        nw = 0
        for g0 in range(0, NFF, G):
            gn = min(G, NFF - g0)
            for gi in range(gn):
                j = g0 + gi
                w = wib[nw % 2]
                nw += 1
                P.dma(w.v(), wiL[j], q="pool")
                for b in range(nblk):
                    n = X.sizes[b]
                    pa = P.next_ps()
                    pu = P.next_ps()
                    for k in range(KC):
                        P.mm(pa[:, :n], w[:, k, 0:128], H.b[b][:, k, :], start=(k == 0), stop=(k == KC - 1))
                    for k in range(KC):
                        P.mm(pu[:, :n], w[:, k, 128:256], H.b[b][:, k, :], start=(k == 0), stop=(k == KC - 1))
                    s = silb[ns % 2]
                    ns += 1
                    P.act(s[:, :n], pa[:, :n], AF.Silu)
                    P.tt(actb[gi][b].v(), s[:, :n], pu[:, :n], ALU.mult)
            P.dma(wob[:, 0:gn, :], woL[:, g0:g0 + gn, :], q="pool")
            for b in range(nblk):
                n = X.sizes[b]
                hg = mods[X.kinds[b]][2]
                for f in range(KC):
                    ps = P.next_ps()
                    for gi in range(gn):
                        P.mm(ps[:, :n], wob[:, gi, f * 128:(f + 1) * 128], actb[gi][b].v(), start=(gi == 0), stop=(gi == gn - 1))
                    P.stt(X.b[b][:, f, :], ps[:, :n], hg[:, f:f + 1], X.b[b][:, f, :], ALU.mult, ALU.add)


def build_p0():
    nc = bass.Bass("TRN2", target_bir_lowering=False)
    NCOL = 4608
    with contextlib.ExitStack() as stack:
        P = Prog(nc, stack)
        P.psum_banks()
        cc = P.dram("ccT", [128, KC, 2], F32, "ExternalInput")
        W = P.dram("adaW", [128, KC, NCOL], F32, "ExternalInput")
        bias = P.dram("adab", [2, NCOL], F32, "ExternalInput")
        out = P.dram("modout", [2, NCOL], F32, "ExternalOutput")
        s = P.sb([128, KC, 2], F32, "s")
        P.dma(s.v(), cc.v())
        P.act(s.v(), s.v(), AF.Silu)
        bt = P.sb([2, NCOL], F32, "bt")
        P.dma(bt.v(), bias.v())
        ot = P.sb([2, NCOL], F32, "ot")
        w_t = P.stack.enter_context(P.sb_ctx([128, 2, KC, 512], F32, "w0"))
        wb = [Buf(w_t[:, i]) for i in range(2)]
        for j in range(NCOL // 512):
            w = wb[j % 2]
            P.dma(w.v(), W[:, :, j * 512:(j + 1) * 512])
            ps = P.next_ps()
            for k in range(KC):
                P.mm(ps[0:2, :], s[:, k, :], w[:, k, :], start=(k == 0), stop=(k == KC - 1))
            P.tt(ot[:, j * 512:(j + 1) * 512], ps[0:2, :], bt[:, j * 512:(j + 1) * 512], ALU.add)
        P.dma(out.v(), ot.v())
        P.emit()
    return nc


_PROGS = {}


def get_prog(key):
    if key not in _PROGS:
        if key == "p0":
            _PROGS[key] = build_p0()
        elif key == "p1":
            _PROGS[key] = build_layer(1)
        elif key == "p2":
            _PROGS[key] = build_layer(2, last=False)
        else:
            _PROGS[key] = build_layer(2, last=True)
    return _PROGS[key]


def rep(v, n=128):
    v = np.asarray(v, np.float32).reshape(1, -1)
    return np.ascontiguousarray(np.broadcast_to(v, (n, v.shape[1])))


def rope_tables(core):
    t = core * TL + np.arange(TL)
    row = (t // 64).astype(np.float32)
    col = (t % 64).astype(np.float32)
    nf = 32
    freq = (np.float32(10000.0) ** (-np.arange(nf, dtype=np.float32) / np.float32(nf))).astype(np.float32)
    ang = np.concatenate([row[:, None] * freq[None, :], col[:, None] * freq[None, :]], axis=1).astype(np.float32)
    cos = np.cos(ang).astype(np.float32).T
    sin = np.sin(ang).astype(np.float32).T
    cos_tab = np.concatenate([cos, cos], axis=0)
    sin_tab = np.concatenate([-sin, sin], axis=0)
    return np.ascontiguousarray(np.stack([cos_tab, sin_tab], axis=1))


def run_model(inp, ncores, depth=4, run=None):
    from concourse.bass_utils import run_bass_kernel_spmd
    f32 = np.float32
    x = np.asarray(inp["x"], f32)[0]
    ctx = np.asarray(inp["ctx"], f32)[0]
    assert x.shape[0] == ncores * TL
    consts = host_consts()
    cc = np.stack([np.asarray(inp["c"], f32)[0], np.asarray(inp["c_ctx"], f32)], axis=1)
    ccT = np.ascontiguousarray(cc.reshape(KC, 128, 2).transpose(1, 0, 2))
    in_maps = []
    for c in range(8):
        l, half = (c // 2) % depth, c % 2
        Wl = np.asarray(inp["ada_w"][l], f32)[:, half * 4608:(half + 1) * 4608]
        in_maps.append({"ccT": ccT,
                        "adaW": np.ascontiguousarray(Wl.reshape(KC, 128, 4608).transpose(1, 0, 2)),
                        "adab": rep(np.asarray(inp["ada_b"][l], f32)[half * 4608:(half + 1) * 4608], 2)})
    res0 = run_bass_kernel_spmd(get_prog("p0"), in_maps, core_ids=list(range(8))).results
    mods = []
    for l in range(depth):
        m = np.concatenate([res0[2 * l]["modout"], res0[2 * l + 1]["modout"]], axis=1)
        mods.append([np.ascontiguousarray(m[s].reshape(9, KC, 128).transpose(2, 0, 1)) for s in range(2)])
    xT = [np.ascontiguousarray(x[c * TL:(c + 1) * TL].T) for c in range(ncores)]
    ctxT = np.ascontiguousarray(ctx.T)
    ropes = [rope_tables(c) for c in range(ncores)]
    zeros2 = np.zeros((D, 2), f32)

    def halos(xs):
        hs, vs = [], []
        for c in range(ncores):
            left = xs[c - 1][:, -2:] if c > 0 else zeros2
            right = xs[c + 1][:, :2] if c < ncores - 1 else zeros2
            hs.append(np.concatenate([left, right], axis=1))
            vs.append(rep(np.array([1.0 if c > 0 else 0.0, 1.0 if c < ncores - 1 else 0.0], f32)))
        return hs, vs

    for l in range(depth):
        last = l == depth - 1
        g = lambda k: np.asarray(inp[k][l], f32)
        common = {
            "cin": consts,
            "mod_lat": mods[l][0], "mod_ctx": mods[l][1],
            "norm_g": np.ascontiguousarray(g("norm_g").reshape(3, KC, 128).transpose(2, 0, 1)),
            "ret_logit": rep(g("ret_logit").reshape(-1)),
            "wa2": np.ascontiguousarray(g("gla_wa2").transpose(1, 0, 2)),
            "gla_ba": np.ascontiguousarray(g("gla_ba").reshape(2, 4, 128).transpose(2, 0, 1)),
            "conv_w": np.ascontiguousarray(g("conv_w").reshape(5, 24, 128).transpose(2, 0, 1)),
            "conv_b": np.ascontiguousarray(g("conv_b").reshape(24, 128).T),
            "dt_bias": rep(g("dt_bias").reshape(-1)),
            "a_log": rep(g("a_log").reshape(-1)),
            "w_inL": np.ascontiguousarray(g("w_in").reshape(KC, 128, DIN).transpose(1, 0, 2)),
        }

        def tile_ffn(wi, wo):
            wr = wi.reshape(KC, 128, 2 * DFF)
            a = wr[:, :, :DFF].reshape(KC, 128, NFF, 128).transpose(2, 1, 0, 3)
            u = wr[:, :, DFF:].reshape(KC, 128, NFF, 128).transpose(2, 1, 0, 3)
            return (np.ascontiguousarray(np.concatenate([a, u], axis=3)),
                    np.ascontiguousarray(wo.reshape(NFF, 128, D).transpose(1, 0, 2)))
        wiL, woL = tile_ffn(g("ffn1_wi"), g("ffn1_wo"))
        hs, vs = halos(xT)
        in_maps = []
        for c in range(ncores):
            m = dict(common)
            m.update({"xin": np.ascontiguousarray(np.concatenate([xT[c], ctxT, hs[c]], axis=1)),
                      "halo_valid": vs[c], "rope_in": ropes[c], "wiL": wiL, "woL": woL})
            in_maps.append(m)
        res1 = run_bass_kernel_spmd(get_prog("p1"), in_maps, core_ids=list(range(ncores))).results
        if run is not None:
            run["res1"] = res1
        x1 = [r["x1T"][:, :TL] for r in res1]
        ctx1 = res1[0]["x1T"][:, TL:]
        E = [r["Eout"] for r in res1]
        Dd = [r["Dout"] for r in res1]
        wiL, woL = tile_ffn(g("ffn2_wi"), g("ffn2_wo"))
        wb_all = np.concatenate([g("wb_ret"), g("wb_gla"), g("wb_ssd")], axis=0)
        wbL = np.ascontiguousarray(wb_all.reshape(32, 128, KC, 128).transpose(2, 1, 0, 3))
        w_outL = np.ascontiguousarray(g("w_out").reshape(KC, 128, KC, 128).transpose(2, 1, 0, 3))
        hs, vs = halos(x1)
        in_maps = []
        for c in range(ncores):
            slotE = np.zeros((2, 7, 128, 4096), f32)
            slotD = np.ones((2, 7, 128, 40), f32)
            for s, src in enumerate(range(0, c)):
                slotE[0, s] = E[src][0]
                slotD[0, s] = Dd[src][0]
            for s, src in enumerate(range(ncores - 1, c, -1)):
                slotE[1, s] = E[src][1]
                slotD[1, s] = Dd[src][1]
            m = dict(common)
            m.update({"xin": np.ascontiguousarray(np.concatenate([x1[c], ctx1, hs[c]], axis=1)),
                      "halo_valid": vs[c], "rope_in": ropes[c], "wiL": wiL, "woL": woL, "wbL": wbL, "w_outL": w_outL,
                      "gla_g": rep(g("gla_norm_g")), "ssd_d": rep(g("ssd_d")), "ssd_ng": rep(g("ssd_norm_g")),
                      "final_g": fm_vec(np.asarray(inp["final_norm_g"], f32)),
                      "slotE": slotE, "slotD": slotD})
            in_maps.append(m)
        res2 = run_bass_kernel_spmd(get_prog("p2last" if last else "p2"), in_maps, core_ids=list(range(ncores))).results
        if run is not None:
            run["res2"] = res2
        xT = [r["xoutT"][:, :TL] for r in res2]
        ctxT = np.ascontiguousarray(res2[0]["xoutT"][:, TL:])
    out = np.concatenate([t.T for t in xT], axis=0)[None]
    return np.ascontiguousarray(out.astype(np.float32))


def kernel(**inputs):
    return run_model(inputs, 8)
```

```python
import contextlib
import numpy as np
import concourse.bass as bass
import concourse.mybir as mybir

F32 = mybir.dt.float32
BF16 = mybir.dt.bfloat16
AF = mybir.ActivationFunctionType
ALU = mybir.AluOpType
KDMA = 14


class View:
    __slots__ = ("buf", "ap")

    def __init__(self, buf, ap):
        self.buf = buf
        self.ap = ap

    def __getitem__(self, idx):
        return View(self.buf, self.ap[idx])

    def m(self, f):
        return View(self.buf, f(self.ap))


class Buf:
    __slots__ = ("ap", "w", "r", "name", "psum")

    def __init__(self, ap, name="", psum=False):
        self.ap = ap
        self.w = None
        self.r = {}
        self.name = name
        self.psum = psum

    def __getitem__(self, idx):
        return View(self, self.ap[idx])

    def v(self):
        return View(self, self.ap)


class Prog:
    ENGS = ("pe", "act", "dve", "pool", "sp")

    def __init__(self, nc, stack):
        self.nc = nc
        self.stack = stack
        self.q = {e: [] for e in self.ENGS}
        self.cnt = {e: 0 for e in self.ENGS}
        self.dman = {e: 0 for e in self.ENGS}
        self.waited = {e: {} for e in self.ENGS}
        self.sems = {}
        for e in self.ENGS:
            self.sems[("eng", e)] = stack.enter_context(nc.semaphore("s_" + e))
        for e in ("sp", "pool", "act"):
            for i in range(KDMA):
                self.sems[("dma", e, i)] = stack.enter_context(nc.semaphore(f"d_{e}{i}"))
        self.nalloc = 0
        self.psn = 0

    def sb(self, shape, dtype=F32, name=None):
        self.nalloc += 1
        t = self.stack.enter_context(self.nc.sbuf_tensor(f"{name or 't'}_{self.nalloc}", list(shape), dtype))
        return Buf(t[:] if hasattr(t, "__getitem__") else t, name or "t")

    def sb_ctx(self, shape, dtype=F32, name=None):
        self.nalloc += 1
        return self.nc.sbuf_tensor(f"{name or 't'}_{self.nalloc}", list(shape), dtype)

    def psum_banks(self):
        self.ps = []
        for i in range(8):
            t = self.stack.enter_context(self.nc.psum_tensor(f"ps{i}", [128, 512], F32))
            self.ps.append(Buf(t[:], f"ps{i}", psum=True))

    def next_ps(self):
        b = self.ps[self.psn % 8]
        self.psn += 1
        return b

    def dram(self, name, shape, dtype=F32, kind="Internal"):
        t = self.nc.dram_tensor(name, list(shape), dtype, kind=kind)
        return Buf(t.ap(), name)

    def op(self, E, fn, reads, writes, dma=False):
        deps = {}

        def add(tok):
            sk, val, eng = tok
            if deps.get(sk, (0, None))[0] < val:
                deps[sk] = (val, eng)

        for b in reads:
            if b.w is not None:
                add(b.w)
            if b.psum:
                for sk, (val, eng) in b.r.items():
                    if eng != E:
                        add((sk, val, eng))
        for b in writes:
            if b.w is not None:
                add(b.w)
            for sk, (val, eng) in b.r.items():
                add((sk, val, eng))
        waits = []
        for sk, (val, eng) in deps.items():
            if eng == "pe" and E == "pe" and not dma:
                continue
            if self.waited[E].get(sk, 0) >= val:
                continue
            self.waited[E][sk] = val
            waits.append((sk, val))
        if dma:
            n = self.dman[E]
            self.dman[E] += 1
            sk = ("dma", E, n % KDMA)
            val = 16 * (n // KDMA + 1)
            if val > 16 and self.waited[E].get(sk, 0) < val - 16:
                waits.append((sk, val - 16))
                self.waited[E][sk] = val - 16
            tok = (sk, val, "dma")
            inc = (sk, 16)
        else:
            self.cnt[E] += 1
            sk = ("eng", E)
            tok = (sk, self.cnt[E], E)
            inc = (sk, 1)
        self.q[E].append((waits, fn, inc))
        wset = set(id(b) for b in writes)
        for b in writes:
            b.w = tok
            b.r = {}
        for b in reads:
            if id(b) not in wset:
                b.r[tok[0]] = (tok[1], tok[2])
        return tok

    def barrier(self):
        cur = []
        for e in self.ENGS:
            if self.cnt[e] > 0:
                cur.append((("eng", e), self.cnt[e]))
        for e in ("sp", "pool", "act"):
            n = self.dman[e]
            for i in range(KDMA):
                k = (n - 1 - i)
                if k >= 0:
                    cur.append((("dma", e, k % KDMA), 16 * (k // KDMA + 1)))
        for E in self.ENGS:
            waits = []
            for sk, val in cur:
                if sk == ("eng", E):
                    continue
                if self.waited[E].get(sk, 0) >= val:
                    continue
                self.waited[E][sk] = val
                waits.append((sk, val))
            if waits:
                self.q[E].append((waits, None, None))

    def emit(self):
        nc = self.nc
        self.barrier()
        with nc.Block() as block:
            def replay(E, e):
                for waits, fn, inc in self.q[E]:
                    for sk, val in waits:
                        e.wait_ge(self.sems[sk], val)
                    if fn is not None:
                        ins = fn(e)
                        ins.then_inc(self.sems[inc[0]], inc[1])

            @block.tensor
            def _(e):
                replay("pe", e)

            @block.scalar
            def _(e):
                replay("act", e)

            @block.vector
            def _(e):
                replay("dve", e)

            @block.gpsimd
            def _(e):
                replay("pool", e)

            @block.sync
            def _(e):
                replay("sp", e)

    @staticmethod
    def _b(*xs):
        return [x.buf for x in xs if isinstance(x, View)]

    @staticmethod
    def _a(x):
        return x.ap if isinstance(x, View) else x

    def mm(self, out, lhsT, rhs, start=True, stop=True):
        o, l, r = out.ap, lhsT.ap, rhs.ap
        return self.op("pe", lambda e: e.matmul(o, l, r, start=start, stop=stop), self._b(lhsT, rhs), self._b(out))

    def tr(self, out, in_, ident):
        o, i, d = out.ap, in_.ap, ident.ap
        return self.op("pe", lambda e: e.transpose(o, i, d), self._b(in_, ident), self._b(out))

    def act(self, out, in_, func, bias=0.0, scale=1.0, accum=None):
        o, i, b, s = out.ap, in_.ap, self._a(bias), self._a(scale)
        ac = self._a(accum) if accum is not None else None
        kw = {}
        if ac is not None:
            kw["accum_out"] = ac
        return self.op("act", lambda e: e.activation(o, i, func, bias=b, scale=s, **kw),
                       self._b(in_, bias, scale), self._b(out) + (self._b(accum) if accum is not None else []))

    def tt(self, out, a, b, op, eng="dve"):
        o, x, y = out.ap, a.ap, b.ap
        return self.op(eng, lambda e: e.tensor_tensor(o, x, y, op), self._b(a, b), self._b(out))

    def ts(self, out, a, s1, s2, op0, op1=None, eng="dve", accum=None):
        o, x, p1, p2 = out.ap, a.ap, self._a(s1), self._a(s2)
        kw = {}
        if op1 is not None:
            kw["op1"] = op1
        if accum is not None:
            kw["accum_out"] = accum.ap
        return self.op(eng, lambda e: e.tensor_scalar(o, x, p1, p2, op0, **kw), self._b(a, s1, s2),
                       self._b(out) + (self._b(accum) if accum is not None else []))

    def stt(self, out, a, s, b, op0, op1, eng="dve"):
        o, x, p, y = out.ap, a.ap, self._a(s), b.ap
        return self.op(eng, lambda e: e.scalar_tensor_tensor(o, x, p, y, op0, op1), self._b(a, s, b), self._b(out))

    def copy(self, out, a, eng="dve"):
        o, x = out.ap, a.ap
        if eng == "act":
            return self.op("act", lambda e: e.copy(o, x), self._b(a), self._b(out))
        return self.op(eng, lambda e: e.tensor_copy(o, x), self._b(a), self._b(out))

    def memset(self, out, val, eng="dve"):
        o = out.ap
        return self.op(eng, lambda e: e.memset(o, val), [], self._b(out))

    def scan(self, out, d0, d1, init, op0, op1):
        o, a, b, i = out.ap, d0.ap, d1.ap, self._a(init)
        return self.op("dve", lambda e: e.tensor_tensor_scan(o, a, b, i, op0, op1), self._b(d0, d1, init), self._b(out))

    def dma(self, out, in_, q="sp"):
        o, i = out.ap, in_.ap
        return self.op(q, lambda e: e.dma_start(out=o, in_=i), self._b(in_), self._b(out), dma=True)


@contextlib.contextmanager
def scope(P):
    old = P.stack
    with contextlib.ExitStack() as st:
        P.stack = st
        try:
            yield
        finally:
            P.barrier()
            P.stack = old


D = 1024
KC = 8
DFF = 2816
NFF = 22
EPS = 1e-6


def fm_vec(v):
    v = np.asarray(v, np.float32)
    return np.ascontiguousarray(v.reshape(-1, 128).T)


class Blocks:
    def __init__(self, P, sizes, kinds, name="xT", dtype=F32):
        self.sizes = sizes
        self.kinds = kinds
        self.offs = np.concatenate([[0], np.cumsum(sizes)]).astype(int).tolist()
        self.T = self.offs[-1]
        self.t = P.stack.enter_context(P.nc.sbuf_tensor(name, [128, KC, self.T], dtype))
        self.b = [Buf(self.t[:, :, self.offs[i]:self.offs[i] + n], f"{name}{i}") for i, n in enumerate(sizes)]


def prep_mod(P, modT, gT, idx):
    out = {}
    for kind, mt in modT.items():
        gs = P.sb([128, KC], F32, "gs")
        hg = P.sb([128, KC], F32, "hg")
        P.stt(gs.v(), mt[:, 3 * idx + 1, :], 1.0, gT[:, idx, :], ALU.add, ALU.mult)
        P.ts(hg.v(), mt[:, 3 * idx + 2, :], 0.5 if idx != 1 else 1.0, None, ALU.mult)
        out[kind] = (gs, mt[:, 3 * idx + 0, :], hg)
    return out


def norm_mod(P, xb, hb, n, gs, sh, ones_bf, scr):
    sq, tmp, rstd = scr
    P.act(sq[:, :, :n], xb.v(), AF.Square)
    ps = P.next_ps()
    for k in range(KC):
        P.mm(ps[:, :n], ones_bf.v(), sq[:, k, :n], start=(k == 0), stop=(k == KC - 1))
    P.ts(rstd[:, :n], ps[:, :n], 1.0 / D, EPS, ALU.mult, ALU.add)
    P.act(rstd[:, :n], rstd[:, :n], AF.Ln)
    P.act(rstd[:, :n], rstd[:, :n], AF.Exp, scale=-0.5)
    for k in range(KC):
        P.stt(tmp[:, k, :n], xb[:, k, :], gs[:, k:k + 1], rstd[:, :n], ALU.mult, ALU.mult)
        P.act(hb[:, k, :], tmp[:, k, :n], AF.Identity, bias=sh[:, k:k + 1], scale=1.0)


def ffn(P, X, H, mods, wi, wo, ones_bf, scr, G=6):
    nblk = len(X.sizes)
    T = X.T
    with P.sb_ctx([128, G, T], BF16, "actT") as actT_t, \
            P.sb_ctx([128, 2, KC, 256], BF16, "wi_t") as wi_t, \
            P.sb_ctx([128, G, D], BF16, "wo_t") as wo_t, \
            P.sb_ctx([128, 2, 512], F32, "sil") as sil_t:
        actb = [[Buf(actT_t[:, g, X.offs[b]:X.offs[b] + X.sizes[b]]) for b in range(nblk)] for g in range(G)]
        wib = [Buf(wi_t[:, i]) for i in range(2)]
        wob = Buf(wo_t[:])
        silb = [Buf(sil_t[:, i]) for i in range(2)]
        wiv = wi.ap.rearrange("(k p) c -> p k c", p=128)
        wov = wo.ap.rearrange("(j p) f -> p j f", p=128)
        ns = 0
        nw = 0
        for g0 in range(0, NFF, G):
            gn = min(G, NFF - g0)
            for gi in range(gn):
                j = g0 + gi
                w = wib[nw % 2]
                nw += 1
                P.dma(w[:, :, 0:128], View(wi, wiv[:, :, j * 128:(j + 1) * 128]), q="pool")
                P.dma(w[:, :, 128:256], View(wi, wiv[:, :, DFF + j * 128:DFF + (j + 1) * 128]), q="pool")
                for b in range(nblk):
                    n = X.sizes[b]
                    pa = P.next_ps()
                    pu = P.next_ps()
                    for k in range(KC):
                        P.mm(pa[:, :n], w[:, k, 0:128], H.b[b][:, k, :], start=(k == 0), stop=(k == KC - 1))
                    for k in range(KC):
                        P.mm(pu[:, :n], w[:, k, 128:256], H.b[b][:, k, :], start=(k == 0), stop=(k == KC - 1))
                    s = silb[ns % 2]
                    ns += 1
                    P.act(s[:, :n], pa[:, :n], AF.Silu)
                    P.tt(actb[gi][b].v(), s[:, :n], pu[:, :n], ALU.mult)
            P.dma(wob[:, 0:gn, :], View(wo, wov[:, g0:g0 + gn, :]), q="pool")
            for b in range(nblk):
                n = X.sizes[b]
                hg = mods[X.kinds[b]][2]
                for f in range(KC):
                    ps = P.next_ps()
                    for gi in range(gn):
                        P.mm(ps[:, :n], wob[:, gi, f * 128:(f + 1) * 128], actb[gi][b].v(), start=(gi == 0), stop=(gi == gn - 1))
                    P.stt(X.b[b][:, f, :], ps[:, :n], hg[:, f:f + 1], X.b[b][:, f, :], ALU.mult, ALU.add)
        P.barrier()


TL = 2048
TC = 256
NCH_L = TL // 128
NCH_C = TC // 128
NCH = NCH_L + NCH_C
TT = TL + TC
DIN = 14432
COLS = {}
_o = 0
for _n, _s in (("ret_q", 512), ("ret_k", 512), ("ret_v", 1024), ("ret_g", 1024), ("gla_q", 512), ("gla_k", 512),
               ("gla_v", 1024), ("gla_r", 1024), ("gla_af", 16), ("gla_ab", 16), ("ssd_z", 2048), ("ssd_xbc", 3072),
               ("ssd_dt", 64), ("merge", 3072)):
    COLS[_n] = (_o, _s)
    _o += _s
assert _o == DIN
CI = {"ident": 0, "perm": 1, "mle": 2, "mge": 3, "sgt": 4, "slt": 5}


def host_consts():
    p = np.arange(128)[:, None]
    c = np.arange(128)[None, :]
    mats = [p == c, c == (p + 64) % 128, p <= c, p >= c, p > c, p < c]
    m = np.concatenate([x.astype(np.float32) for x in mats], axis=1)
    rows = np.concatenate([np.broadcast_to(np.arange(1, 129, dtype=np.float32), (128, 128)),
                           np.broadcast_to(np.arange(128, 0, -1).astype(np.float32), (128, 128))], axis=1)
    return np.ascontiguousarray(np.concatenate([m, rows], axis=1))


class Env:
    pass


def load_consts(P, env, cin):
    cf = P.sb([128, 8 * 128], F32, "cf")
    P.dma(cf.v(), cin.v())
    cb = P.sb([128, 6 * 128], BF16, "cb")
    P.copy(cb.v(), cf[:, 0:768])
    env.cf, env.cb = cf, cb
    env.ones_bf = P.sb([128, 128], BF16, "ones")
    P.memset(env.ones_bf.v(), 1.0)
    env.ones_f = P.sb([128, 128], F32, "onesf")
    P.memset(env.ones_f.v(), 1.0)

    def cm(name, bf=True):
        i = CI[name]
        return (cb if bf else cf)[:, i * 128:(i + 1) * 128]
    env.cm = cm
    env.row_up = cf[:, 768:896]
    env.row_dn = cf[:, 896:1024]


def softplus_parts(P, z, n, tmp1, tmp2):
    P.act(tmp1, z, AF.Abs)
    P.act(tmp1, tmp1, AF.Exp, scale=-1.0)
    P.act(tmp1, tmp1, AF.Ln, bias=1.0)
    P.ts(tmp2, z, 0.0, None, ALU.max)
    return tmp2, tmp1


def inproj(P, env, H, W, sp, lay, groups, do_q):
    nblk = len(H.sizes)
    tokblocks = [b for b in range(nblk) if H.kinds[b] != "halo"]
    halo = [b for b in range(nblk) if H.kinds[b] == "halo"][0]
    spoff = {}
    o = 0
    for b in tokblocks:
        spoff[b] = o
        o += H.sizes[b]
    wt_t = P.stack.enter_context(P.sb_ctx([128, 2, KC, 512], BF16, "win_t"))
    wt = [Buf(wt_t[:, i]) for i in range(2)]
    st = {"nw": 0}

    def loadw(c0, n):
        w = wt[st["nw"] % 2]
        st["nw"] += 1
        P.dma(w[:, :, 0:n], W[:, :, c0:c0 + n], q="pool")
        return w

    def fm_mm(w, wc0, m, b, ps=None):
        n = H.sizes[b]
        ps = ps or P.next_ps()
        for k in range(KC):
            P.mm(ps[0:m, :n], w[:, k, wc0:wc0 + m], H.b[b][:, k, :], start=(k == 0), stop=(k == KC - 1))
        return ps

    def do_qk(mixer):
        mi = 0 if mixer == "ret" else 1
        names = (["q"] if do_q else []) + ["k"]
        with scope(P):
            ec = P.stack.enter_context
            qf_t = ec(P.sb_ctx([128, 2, 512], F32, "qk_f"))
            qb_t = ec(P.sb_ctx([128, 2, 512], BF16, "qk_b"))
            qo_t = ec(P.sb_ctx([128, 4, 512], BF16, "qk_o"))
            g_t = ec(P.sb_ctx([128, 6, 512], F32, "g_t"))
            rope_t = ec(P.sb_ctx([128, 2, TL if mixer == "ret" else 2], F32, "rope"))
            code_t = ec(P.sb_ctx([16, 2, 512], BF16, "code"))
            la_t = ec(P.sb_ctx([128, 2, 512], F32, "la_t"))
            epm_t = ec(P.sb_ctx([128, 4, 512], F32, "epm_t"))
            epm = [[Buf(epm_t[:, 0]), Buf(epm_t[:, 1])], [Buf(epm_t[:, 2]), Buf(epm_t[:, 3])]]
            qf = [Buf(qf_t[:, i]) for i in range(2)]
            qb = [Buf(qb_t[:, i]) for i in range(2)]
            qo = [Buf(qo_t[:, i]) for i in range(4)]
            gt = [Buf(g_t[:, i]) for i in range(6)]
            la = [Buf(la_t[:, i]) for i in range(2)]
            code = [Buf(code_t[:, i]) for i in range(2)]
            rope = Buf(rope_t[:])
            if mixer == "ret":
                P.dma(rope.v(), lay["rope_in"].v())
            wq = {}
            for nm in names:
                c0, _ = COLS[f"{mixer}_{nm}"]
                wq[nm] = (c0,)
            if mixer == "gla":
                wc = loadw(COLS["gla_af"][0], 32)
                wcode = P.sb([128, KC, 32], BF16, "wcode")
                P.copy(wcode.v(), wc[:, :, 0:32])
            cnt = 0
            if mixer == "gla":
                egg = P.sb([128, 2, NCH, 4], F32, "egg")
            for b in tokblocks:
                n = H.sizes[b]
                nc_ = n // 128
                kind = H.kinds[b]
                if mixer == "gla":
                    for d in range(2):
                        ps = P.next_ps()
                        for k in range(KC):
                            P.mm(ps[0:16, :n], wcode[:, k, d * 16:(d + 1) * 16], H.b[b][:, k, :], start=(k == 0), stop=(k == KC - 1))
                        P.copy(code[d][:, :n], ps[0:16, :n], eng="act")
                if mixer == "ret":
                    for nm in names:
                        w = loadw(wq[nm][0], 512)
                        var0 = 0 if nm == "q" else 1
                        for h in range(4):
                            ps = fm_mm(w, h * 128, 128, b)
                            cnt += 1
                            f32 = qf[cnt % 2]
                            sc = 128 ** -0.5 if nm == "k" else 1.0
                            if kind == "lat":
                                bfv = qb[cnt % 2]
                                P.act(bfv[:, :n], ps[:, :n], AF.Copy, scale=sc)
                                ps2 = P.next_ps()
                                P.mm(ps2[:, :n], env.cm("perm"), bfv[:, :n])
                                t0 = H.offs[b]
                                P.tt(f32[:, :n], bfv[:, :n], rope[:, 0, t0:t0 + n], ALU.mult)
                                P.tt(gt[0][:, :n], ps2[:, :n], rope[:, 1, t0:t0 + n], ALU.mult)
                                P.tt(f32[:, :n], f32[:, :n], gt[0][:, :n], ALU.add)
                            else:
                                P.act(f32[:, :n], ps[:, :n], AF.Copy, scale=sc)
                            for d in range(2):
                                sgn = 0 if nm == "q" else 1
                                tab = lay["ret_tab"][:, d, sgn, h, :].m(lambda a: a.unsqueeze(1).to_broadcast([128, nc_, 128]))
                                o = qo[(cnt * 2 + d) % 4]
                                P.tt(o[:, :n].m(lambda a: a.rearrange("p (c t) -> p c t", t=128)),
                                     f32[:, :n].m(lambda a: a.rearrange("p (c t) -> p c t", t=128)), tab, ALU.mult)
                                P.dma(sp["qk"][mi, h, 2 * d + var0, :, spoff[b]:spoff[b] + n], o[:, :n])
                else:
                    ws = {nm: loadw(wq[nm][0], 512) for nm in names}
                    for h in range(4):
                        for d in range(2):
                            psz = P.next_ps()
                            P.mm(psz[:, :n], lay["wa2"][:, d, h * 128:(h + 1) * 128], code[d][:, :n])
                            z = gt[1]
                            P.ts(z[:, :n], psz[:, :n], lay["gla_ba"][:, d, h:h + 1], None, ALU.add)
                            mx, l = softplus_parts(P, z[:, :n], n, gt[2][:, :n], gt[3][:, :n])
                            P.ts(gt[3][:, :n], z[:, :n], 0.0, None, ALU.min)
                            dd = gt[4]
                            P.tt(dd[:, :n], gt[3][:, :n], gt[2][:, :n], ALU.subtract)
                            cs = gt[5]
                            for c in range(nc_):
                                sl = slice(c * 128, (c + 1) * 128)
                                P.scan(cs[:, sl], env.ones_f[:, 0:128], dd[:, sl], 0.0, ALU.mult, ALU.add)
                            g = la[d]
                            if d == 0:
                                g = cs
                            else:
                                P.tt(g[:, :n], dd[:, :n], cs[:, :n], ALU.subtract)
                                for c in range(nc_):
                                    sl = slice(c * 128, (c + 1) * 128)
                                    P.ts(g[:, sl], g[:, sl], cs[:, c * 128 + 127:c * 128 + 128], None, ALU.add)
                            ch0 = spoff[b] // 128
                            for c in range(nc_):
                                P.act(egg[:, d, ch0 + c, h:h + 1], cs[:, c * 128 + 127:c * 128 + 128], AF.Exp, scale=1.0 / 16)
                            if do_q:
                                P.act(epm[0][d][:, :n], g[:, :n], AF.Exp, scale=1.0 / 16)
                            P.act(epm[1][d][:, :n], g[:, :n], AF.Exp, scale=-1.0 / 16)
                        for nm in names:
                            var0 = 0 if nm == "q" else 1
                            ps = fm_mm(ws[nm], h * 128, 128, b)
                            cnt += 1
                            f32 = qf[cnt % 2]
                            sc = 128 ** -0.5 if nm == "q" else 1.0
                            P.act(f32[:, :n], ps[:, :n], AF.Copy, scale=sc)
                            for d in range(2):
                                o = qo[(cnt * 2 + d) % 4]
                                P.tt(o[:, :n], f32[:, :n], epm[var0][d][:, :n], ALU.mult)
                                P.dma(sp["qk"][mi, h, 2 * d + var0, :, spoff[b]:spoff[b] + n], o[:, :n])
            if mixer == "gla":
                for d in range(2):
                    P.dma(sp["eG"][mi * 2 + d], egg[:, d])
            if mixer == "ret":
                egr = P.sb([128, 2, NCH, 4], F32, "egr")
                for d in range(2):
                    P.copy(egr[:, d, :, :], lay["ret_eg"][:, d, :].m(lambda a: a.unsqueeze(1).to_broadcast([128, NCH, 4])))
                    P.dma(sp["eG"][mi * 2 + d], egr[:, d])
            P.barrier()

    def do_tm(name, handler, width=512):
        c0, sz = COLS[name]
        for cc in range(0, sz, width):
            n_ = min(width, sz - cc)
            w = loadw(c0 + cc, n_)
            for b in tokblocks:
                for c in range(H.sizes[b] // 128):
                    ps = P.next_ps()
                    for k in range(KC):
                        P.mm(ps[:, :n_], H.b[b][:, k, c * 128:(c + 1) * 128], w[:, k, 0:n_], start=(k == 0), stop=(k == KC - 1))
                    handler(ps, n_, cc, spoff[b] + c * 128)

    ev = {"n": 0}
    evt_t = P.stack.enter_context(P.sb_ctx([128, 4, 512], BF16, "evt"))
    evt = [Buf(evt_t[:, i]) for i in range(4)]

    def h_copy(dst, col0, func=AF.Copy):
        def h(ps, n_, cc, tok):
            e = evt[ev["n"] % 4]
            ev["n"] += 1
            if func == AF.Copy and ev["n"] % 2 == 0:
                P.copy(e[:, :n_], ps[:, :n_])
            else:
                P.act(e[:, :n_], ps[:, :n_], func)
            P.dma(dst[tok:tok + 128, col0 + cc:col0 + cc + n_], e[:, :n_])
        return h

    def h_dt(ps, n_, cc, tok):
        z = P.sb([128, 64], F32, "dtz")
        t1 = P.sb([128, 64], F32, "dt1")
        t2 = P.sb([128, 64], F32, "dt2")
        P.tt(z.v(), ps[:, 0:64], lay["dt_bias"].v(), ALU.add)
        mx, l = softplus_parts(P, z.v(), 64, t1.v(), t2.v())
        P.tt(z.v(), mx, l, ALU.add)
        P.tt(t1.v(), z.v(), lay["ssd_A"].v(), ALU.mult)
        P.dma(sp["dt"][tok:tok + 128, :], z.v())
        P.dma(sp["a"][tok:tok + 128, :], t1.v())

    def do_xbc(chunks):
        c0, _ = COLS["ssd_xbc"]
        with scope(P):
            ec = P.stack.enter_context
            rawl_t = ec(P.sb_ctx([128, 2, TL + 4], F32, "rawl"))
            rawc_t = ec(P.sb_ctx([128, 2, TC + 4], F32, "rawc"))
            acc_t = ec(P.sb_ctx([128, 2, TL], F32, "acc"))
            xs_t = ec(P.sb_ctx([128, 4, TT], BF16, "xsT"))
            xtm_t = ec(P.sb_ctx([128, 2, 512], BF16, "xtm"))
            rawl = [Buf(rawl_t[:, i]) for i in range(2)]
            rawc = [Buf(rawc_t[:, i]) for i in range(2)]
            acc = [Buf(acc_t[:, i]) for i in range(2)]
            xs = [Buf(xs_t[:, i]) for i in range(4)]
            xtm = [Buf(xtm_t[:, i]) for i in range(2)]
            for i in range(2):
                P.memset(rawc[i][:, 0:2], 0.0)
                P.memset(rawc[i][:, TC + 2:TC + 4], 0.0)
            w = None
            nx = 0
            for ci, ch in enumerate(chunks):
                if ci % 4 == 0 or w is None:
                    w = loadw(c0 + ch * 128, 512)
                    wbase = ch
                rl, rc, ac = rawl[ci % 2], rawc[ci % 2], acc[ci % 2]
                for b in range(nblk):
                    n = H.sizes[b]
                    ps = fm_mm(w, (ch - wbase) * 128, 128, b)
                    if H.kinds[b] == "lat":
                        P.copy(rl[:, 2 + H.offs[b]:2 + H.offs[b] + n], ps[:, :n], eng=("act" if b % 2 else "dve"))
                    elif H.kinds[b] == "ctx":
                        P.copy(rc[:, 2:2 + n], ps[:, :n], eng="act")
                    else:
                        P.ts(rl[:, 0:2], ps[:, 0:2], lay["halo_valid"][:, 0:1], None, ALU.mult)
                        P.ts(rl[:, TL + 2:TL + 4], ps[:, 2:4], lay["halo_valid"][:, 1:2], None, ALU.mult)
                for (raw, T_, o_) in ((rl, TL, 0), (rc, TC, TL)):
                    a = ac[:, 0:T_]
                    P.act(a, raw[:, 0:T_], AF.Identity, bias=lay["conv_b"][:, ch:ch + 1], scale=lay["conv_w"][:, 0, ch:ch + 1])
                    for o in range(1, 5):
                        P.stt(a, raw[:, o:o + T_], lay["conv_w"][:, o, ch:ch + 1], a, ALU.mult, ALU.add,
                              eng=("dve" if o % 2 else "pool") if False else "dve")
                    if ch < 16:
                        P.act(xs[ci % 4][:, o_:o_ + T_], a, AF.Silu)
                    else:
                        e = xs[ci % 4]
                        P.act(e[:, o_:o_ + T_], a, AF.Silu)
                        which = "BT" if ch < 20 else "CT"
                        P.dma(sp[which][(ch - 16) % 4, :, o_:o_ + T_], e[:, o_:o_ + T_])
                if ch < 16 and ci % 4 == 3:
                    for tcn in range(NCH):
                        psb = P.next_ps()
                        pv = psb.v().m(lambda a: a.bitcast(BF16))
                        for q4 in range(4):
                            P.tr(pv[:, q4 * 128:(q4 + 1) * 128], xs[q4][:, tcn * 128:(tcn + 1) * 128], env.cm("ident"))
                        xo = xtm[nx % 2]
                        nx += 1
                        P.copy(xo.v(), pv[:, 0:512], eng=("act" if nx % 2 else "dve"))
                        P.dma(sp["x"][tcn * 128:(tcn + 1) * 128, (ch - 3) * 128:(ch + 1) * 128], xo.v())
            P.barrier()

    for grp in groups:
        if grp in ("ret", "gla"):
            do_qk(grp)
        elif grp == "v":
            do_tm("ret_v", h_copy(sp["v"][0], 0))
            do_tm("gla_v", h_copy(sp["v"][1], 0))
        elif grp == "gates":
            do_tm("ret_g", h_copy(sp["gate"], 0, AF.Silu))
            do_tm("gla_r", h_copy(sp["gate"], 1024, AF.Silu))
            do_tm("ssd_z", h_copy(sp["gate"], 2048, AF.Silu))
        elif grp == "dt":
            do_tm("ssd_dt", h_dt, width=64)
        elif grp == "xB":
            do_xbc(list(range(0, 20)))
        elif grp == "xBC":
            do_xbc(list(range(0, 24)))
        elif grp == "merge":
            c0, sz = COLS["merge"]
            for cc in range(0, sz, 512):
                w = loadw(c0 + cc, 512)
                for q4 in range(4):
                    for b in tokblocks:
                        n = H.sizes[b]
                        ps = fm_mm(w, q4 * 128, 128, b)
                        e = evt[ev["n"] % 4]
                        ev["n"] += 1
                        P.act(e[:, :n], ps[:, :n], AF.Sigmoid)
                        P.dma(sp["sg"][cc // 128 + q4, :, spoff[b]:spoff[b] + n], e[:, :n])
    P.barrier()


def bc(v, axis, shape):
    return v.m(lambda a: a.unsqueeze(axis).to_broadcast(list(shape)))


def r3(v, pat, **kw):
    return v.m(lambda a: a.rearrange(pat, **kw))


class SweepBufs:
    def __init__(self, P, do_y, with_post):
        st = P.stack

        def mk(shape, dt, n, name):
            t = st.enter_context(P.sb_ctx([128, n] + list(shape), dt, name))
            return [Buf(t[:, i]) for i in range(n)]
        self.kT = [mk([4, 128], BF16, 2, "kT0"), mk([4, 128], BF16, 2, "kT1")]
        self.v = [mk([1024], BF16, 2, "v0"), mk([1024], BF16, 2, "v1")]
        self.eG = [mk([4], F32, 2, "eG0"), mk([4], F32, 2, "eG1")]
        self.BT = mk([4, 128], BF16, 2, "BT")
        self.x = mk([2048], BF16, 2, "xtm")
        self.a = mk([64], F32, 2, "a")
        self.dt = mk([64], F32, 2, "dt")
        self.ktm = mk([512], BF16, 2, "ktm")
        self.btm = mk([512], BF16, 1, "btm")
        self.eT = mk([96], F32, 2, "eT")
        self.wv = mk([32], F32, 1, "wv")
        self.vs = mk([2048], BF16, 1, "vs")
        if do_y:
            self.qT = [mk([4, 128], BF16, 2, "qT0"), mk([4, 128], BF16, 2, "qT1")]
            self.CT = mk([4, 128], BF16, 2, "CT")
            self.attT = mk([4, 128], BF16, 2, "attT")
            self.vdt = mk([2048], BF16, 1, "vdt")
            self.cbm = mk([4, 128], BF16, 1, "cbm")
            self.rhsD = mk([32, 128], BF16, 1, "rhsD")
            self.L = mk([4, 128], BF16, 4, "L")
            self.att = mk([32, 128], BF16, 1, "att")
            self.tmp = mk([512], F32, 2, "tmp")
            self.yall = mk([4096], F32, 2, "yall")
            self.yb = mk([4096], BF16, 1, "yb")
        if with_post:
            self.gate = mk([4096], BF16, 2, "gate")
            self.ss = mk([12], F32, 2, "ss")
            self.junk = mk([512], F32, 1, "junk")
            self.ygate = mk([4096], BF16, 1, "ygate")
            self.ygT = mk([32, 128], BF16, 1, "ygT")


def sweep(P, env, sp, lay, d, chain, S, SB, do_y, B, mode, Dt=None):
    msk_f32 = env.cm("mle" if d == 0 else "mge", bf=False)
    msk_bf = env.cm("mle" if d == 0 else "mge")
    sl_f32 = env.cm("sgt" if d == 0 else "slt", bf=False)
    sl_bf = env.cm("sgt" if d == 0 else "slt")
    ident = env.cm("ident")
    post = mode == "post"

    def bfview(psb):
        return psb.v().m(lambda ap: ap.bitcast(BF16))

    def front(it, c):
        par = it % 2
        t0 = c * 128
        kT = [B.kT[m][par] for m in range(2)]
        v = [B.v[m][par] for m in range(2)]
        eG = [B.eG[m][par] for m in range(2)]
        BT, x, a, dt = B.BT[par], B.x[par], B.a[par], B.dt[par]
        for m in range(2):
            P.dma(kT[m].v(), View(sp["qk"], sp["qk"].ap[m, :, 2 * d + 1, :, t0:t0 + 128].rearrange("h p t -> p h t")))
            P.dma(v[m].v(), sp["v"][m, t0:t0 + 128, :])
            P.dma(eG[m].v(), sp["eG"][m * 2 + d, :, c, :])
        P.dma(BT.v(), View(sp["BT"], sp["BT"].ap[:, :, t0:t0 + 128].rearrange("g p t -> p g t")))
        P.dma(x.v(), sp["x"][t0:t0 + 128, :])
        P.dma(a.v(), sp["a"][t0:t0 + 128, :])
        P.dma(dt.v(), sp["dt"][t0:t0 + 128, :])
        if do_y:
            qT = [B.qT[m][par] for m in range(2)]
            CT = B.CT[par]
            for m in range(2):
                P.dma(qT[m].v(), View(sp["qk"], sp["qk"].ap[m, :, 2 * d, :, t0:t0 + 128].rearrange("h p t -> p h t")))
            P.dma(CT.v(), View(sp["CT"], sp["CT"].ap[:, :, t0:t0 + 128].rearrange("g p t -> p g t")))
            yall = B.yall[par]
        if post:
            yb, gate = B.yb[0], B.gate[par]
            P.dma(yb.v(), sp["yb"][t0:t0 + 128, :])
            P.dma(gate.v(), sp["gate"][t0:t0 + 128, :])
        a_d = a[:, d * 32:(d + 1) * 32]
        dt_d = dt[:, d * 32:(d + 1) * 32]
        x3 = r3(x.v(), "p (h e) -> p h e", e=64)
        ktm = [B.ktm[0], B.ktm[1]]
        btm = B.btm[0]
        for m in range(2):
            pv = bfview(P.next_ps())
            for h in range(4):
                P.tr(pv[:, h * 128:(h + 1) * 128], kT[m][:, h, :], ident)
            P.copy(ktm[m].v(), pv[:, 0:512], eng="act")
        pv = bfview(P.next_ps())
        for g in range(4):
            P.tr(pv[:, g * 128:(g + 1) * 128], BT[:, g, :], ident)
        P.copy(btm.v(), pv[:, 0:512], eng="act")
        pss = P.next_ps()
        P.mm(pss[:, 0:32], env.ones_f.v(), a_d)
        P.mm(pss[:, 32:64], sl_f32, a_d)
        if do_y:
            P.mm(pss[:, 64:96], msk_f32, a_d)
        ne = 96 if do_y else 64
        eT = B.eT[par]
        P.act(eT[:, 0:ne], pss[:, 0:ne], AF.Exp)
        if Dt is not None:
            P.tt(Dt[2].v(), Dt[2].v(), pss[:, 0:32], ALU.add)
        if do_y:
            attT = [B.attT[0], B.attT[1]]
            for m in range(2):
                aps = P.next_ps()
                for h in range(4):
                    P.mm(aps[:, h * 128:(h + 1) * 128], kT[m][:, h, :], qT[m][:, h, :])
                P.tt(attT[m].v(), r3(aps.v(), "p (h t) -> p h t", h=4), bc(msk_f32, 1, [128, 4, 128]), ALU.mult)
        wv = B.wv[0]
        P.tt(wv.v(), eT[:, 32:64], dt_d, ALU.mult)
        vs = B.vs[0]
        P.tt(r3(vs.v(), "p (h e) -> p h e", e=64), x3, bc(wv.v(), 2, [128, 32, 64]), ALU.mult)
        cnt = {"e": 0}

        def retgla_y(m):
            if not do_y:
                return
            for hp in range(2):
                yp = P.next_ps()
                for hh in range(2):
                    h = hp * 2 + hh
                    o = yp[:, hh * 256:(hh + 1) * 256]
                    if post:
                        P.mm(o, ident, yb[:, m * 1024 + h * 256:m * 1024 + (h + 1) * 256], start=True, stop=False)
                    P.mm(o, attT[m][:, h, :], v[m][:, h * 256:(h + 1) * 256], start=(not post), stop=False)
                    P.mm(o, qT[m][:, h, :], SB[m][:, h * 256:(h + 1) * 256], start=False, stop=True)
                cs = slice(m * 1024 + hp * 512, m * 1024 + (hp + 1) * 512)
                cnt["e"] += 1
                P.copy(yall[:, cs], yp.v(), eng=("act" if cnt["e"] % 2 else "dve"))

        def retgla_state(m):
            for hp in range(2):
                ps2 = P.next_ps()
                for hh in range(2):
                    h = hp * 2 + hh
                    P.mm(ps2[:, hh * 256:(hh + 1) * 256], ktm[m][:, h * 128:(h + 1) * 128], v[m][:, h * 256:(h + 1) * 256])
                for hh in range(2):
                    h = hp * 2 + hh
                    hc = slice(h * 256, (h + 1) * 256)
                    P.act(S[m][:, hc], S[m][:, hc], AF.Copy, scale=eG[m][:, h:h + 1])
                    P.stt(S[m][:, hc], ps2[:, hh * 256:(hh + 1) * 256], eG[m][:, h:h + 1], S[m][:, hc], ALU.mult, ALU.add)
            if do_y:
                P.copy(SB[m].v(), S[m].v(), eng="act")
            if Dt is not None:
                P.tt(Dt[m].v(), Dt[m].v(), eG[m].v(), ALU.mult)

        if do_y:
            vdt = B.vdt[0]
            vdt3 = r3(vdt.v(), "p (h e) -> p h e", e=64)
            P.tt(vdt3, x3, bc(dt_d, 2, [128, 32, 64]), ALU.mult)
            cb = P.next_ps()
            for g in range(4):
                P.mm(cb[:, g * 128:(g + 1) * 128], BT[:, g, :], CT[:, g, :])
            cbm = B.cbm[0]
            P.tt(cbm.v(), r3(cb.v(), "p (g t) -> p g t", g=4), bc(msk_f32, 1, [128, 4, 128]), ALU.mult)
            rhsD = B.rhsD[0]
            P.tt(rhsD.v(), bc(msk_bf, 1, [128, 32, 128]), bc(a_d, 2, [128, 32, 128]), ALU.mult)
            retgla_y(0)
            att = B.att[0]
            for g in range(4):
                for half in range(2):
                    dps = P.next_ps()
                    hsl = slice(g * 8 + half * 4, g * 8 + half * 4 + 4)
                    P.mm(dps.v(), sl_bf, r3(rhsD[:, hsl, :], "p h t -> p (h t)"))
                    L = B.L[(g * 2 + half) % 4]
                    P.act(r3(L.v(), "p h t -> p (h t)"), dps.v(), AF.Exp)
                    P.tt(att[:, hsl, :], L.v(), bc(cbm[:, g, :], 1, [128, 4, 128]), ALU.mult)
                if g == 0:
                    retgla_state(0)
                elif g == 1:
                    retgla_y(1)
                elif g == 2:
                    retgla_state(1)
            for g in range(4):
                yp = P.next_ps()
                cs = slice(2048 + g * 512, 2048 + (g + 1) * 512)
                if post:
                    P.mm(yp.v(), ident, yb[:, cs], start=True, stop=False)
                for h in range(8):
                    hs = slice(h * 64, (h + 1) * 64)
                    if post:
                        P.mm(yp[:, hs], att[:, g * 8 + h, :], vdt3[:, g * 8 + h, :], start=False, stop=False)
                        P.mm(yp[:, hs], lay["dI"][:, g * 8 + h, :], x3[:, g * 8 + h, :], start=False, stop=(h == 7))
                    else:
                        P.mm(yp[:, hs], att[:, g * 8 + h, :], vdt3[:, g * 8 + h, :])
                yi = P.next_ps()
                P.mm(yi.v(), CT[:, g, :], SB[2][:, g * 512:(g + 1) * 512])
                tmp = B.tmp[g % 2]
                P.tt(r3(tmp.v(), "p (h e) -> p h e", e=64), r3(yi.v(), "p (h e) -> p h e", e=64),
                     bc(eT[:, 64 + g * 8:64 + (g + 1) * 8], 2, [128, 8, 64]), ALU.mult)
                P.tt(yall[:, cs], tmp.v(), yp.v(), ALU.add)
        else:
            retgla_state(0)
            retgla_state(1)
        P.tt(r3(S[2].v(), "p (h e) -> p h e", e=64), r3(S[2].v(), "p (h e) -> p h e", e=64),
             bc(eT[:, 0:32], 2, [128, 32, 64]), ALU.mult)
        for g in range(4):
            ps = P.next_ps()
            P.mm(ps.v(), btm[:, g * 128:(g + 1) * 128], vs[:, g * 512:(g + 1) * 512])
            gc = slice(g * 512, (g + 1) * 512)
            P.tt(S[2][:, gc], S[2][:, gc], ps.v(), ALU.add)
        if do_y:
            P.copy(SB[2].v(), S[2].v(), eng="act")

    def back(it, c):
        par = it % 2
        t0 = c * 128
        yall = B.yall[par]
        if mode == "yb":
            ybo = B.yb[0]
            P.copy(ybo[:, 0:2048], yall[:, 0:2048], eng="act")
            P.copy(ybo[:, 2048:4096], yall[:, 2048:4096], eng="dve")
            P.dma(sp["yb"][t0:t0 + 128, :], ybo.v())
            return
        gate = B.gate[par]
        ss, junk, ygate = B.ss[par], B.junk[0], B.ygate[0]
        P.tt(yall[:, 2048:4096], yall[:, 2048:4096], gate[:, 2048:4096], ALU.mult)
        for h in range(8):
            P.act(junk[:, 0:256], yall[:, h * 256:(h + 1) * 256], AF.Square, accum=ss[:, h:h + 1])
        for g in range(4):
            P.act(junk[:, 0:512], yall[:, 2048 + g * 512:2048 + (g + 1) * 512], AF.Square, accum=ss[:, 8 + g:9 + g])
        P.ts(ss[:, 0:8], ss[:, 0:8], 1.0 / 256, EPS, ALU.mult, ALU.add)
        P.ts(ss[:, 8:12], ss[:, 8:12], 1.0 / 512, EPS, ALU.mult, ALU.add)
        P.act(ss.v(), ss.v(), AF.Ln)
        P.act(ss.v(), ss.v(), AF.Exp, scale=-0.5)
        for h in range(4):
            hc = slice(h * 256, (h + 1) * 256)
            P.stt(ygate[:, hc], yall[:, hc], ss[:, h:h + 1], gate[:, hc], ALU.mult, ALU.mult)
        for h in range(4):
            hc = slice(1024 + h * 256, 1024 + (h + 1) * 256)
            P.stt(junk[:, 256:512], yall[:, hc], ss[:, 4 + h:5 + h], gate[:, hc], ALU.mult, ALU.mult)
            P.tt(ygate[:, hc], junk[:, 256:512], lay["gla_g"].v(), ALU.mult)
        for g in range(4):
            gc = slice(2048 + g * 512, 2048 + (g + 1) * 512)
            P.stt(ygate[:, gc], yall[:, gc], ss[:, 8 + g:9 + g], lay["ssd_ng"][:, g * 512:(g + 1) * 512], ALU.mult, ALU.mult)
        ygT = B.ygT[0]
        for q in range(4):
            pv = bfview(P.next_ps())
            for k8 in range(8):
                k = q * 8 + k8
                P.tr(pv[:, k8 * 128:(k8 + 1) * 128], ygate[:, k * 128:(k + 1) * 128], ident)
            P.copy(r3(ygT[:, q * 8:(q + 1) * 8, :], "p k t -> p (k t)"), pv[:, 0:1024], eng=("act" if q % 2 else "dve"))
        P.dma(View(sp["ygT"], sp["ygT"].ap[:, :, t0:t0 + 128].rearrange("k p t -> p k t")), ygT.v())

    prev = None
    for it, c in enumerate(chain):
        front(it, c)
        if mode != "state":
            if prev is not None:
                back(*prev)
            prev = (it, c)
        yield it
    if prev is not None:
        back(*prev)


def phase_c(P, env, sp, X, spoff, gate5, wbL, woL, blks):
    with scope(P):
        yg_t = P.stack.enter_context(P.sb_ctx([128, 32, 512], BF16, "ygblk"))
        sg_t = P.stack.enter_context(P.sb_ctx([128, 24, 512], BF16, "sgblk"))
        wb_t = P.stack.enter_context(P.sb_ctx([128, 2, 32, 128], BF16, "wbf"))
        wo_t = P.stack.enter_context(P.sb_ctx([128, 2, 8, 128], BF16, "wof"))
        mg_t = P.stack.enter_context(P.sb_ctx([128, 8, 512], BF16, "mg"))
        t_t = P.stack.enter_context(P.sb_ctx([128, 3, 512], F32, "t123"))
        yg, sg, mg = Buf(yg_t[:]), Buf(sg_t[:]), Buf(mg_t[:])
        wbf = [Buf(wb_t[:, i]) for i in range(2)]
        wof = [Buf(wo_t[:, i]) for i in range(2)]
        t = [Buf(t_t[:, i]) for i in range(3)]
        nw = 0
        for b in blks:
            n = X.sizes[b]
            o = spoff[b]
            P.dma(yg[:, :, :n], View(sp["ygT"], sp["ygT"].ap[:, :, o:o + n].rearrange("k p t -> p k t")))
            P.dma(sg[:, :, :n], View(sp["sg"], sp["sg"].ap[:, :, o:o + n].rearrange("k p t -> p k t")))
            for f in range(KC):
                w = wbf[nw % 2]
                nw += 1
                P.dma(w.v(), wbL[f], q="pool")
                for i, (k0, k1) in enumerate(((0, 8), (8, 16), (16, 32))):
                    ps = P.next_ps()
                    for k in range(k0, k1):
                        P.mm(ps[:, :n], w[:, k, :], yg[:, k, :n], start=(k == k0), stop=(k == k1 - 1))
                    P.tt(t[i][:, :n], ps[:, :n], sg[:, i * 8 + f, :n], ALU.mult, eng=("dve" if i != 1 else "pool") if False else "dve")
                P.tt(t[0][:, :n], t[0][:, :n], t[1][:, :n], ALU.add)
                P.tt(mg[:, f, :n], t[0][:, :n], t[2][:, :n], ALU.add)
            hg = gate5[X.kinds[b]]
            for f in range(KC):
                w = wof[f % 2]
                P.dma(w.v(), woL[f], q="pool")
                ps = P.next_ps()
                for k in range(KC):
                    P.mm(ps[:, :n], w[:, k, :], mg[:, k, :n], start=(k == 0), stop=(k == KC - 1))
                P.stt(X.b[b][:, f, :], ps[:, :n], hg[:, f:f + 1], X.b[b][:, f, :], ALU.mult, ALU.add)


def load_small(P, ins, names):
    lay = {}
    for nm in names:
        b = ins[nm]
        shape = list(b.ap.shape)
        t = P.sb(shape, F32, "l_" + nm)
        P.dma(t.v(), b.v())
        lay[nm] = t
    return lay


def derive_layer(P, env, lay, need_post):
    lg = P.sb([128, 8], F32, "lg")
    t1 = P.sb([128, 8], F32, "lgt1")
    t2 = P.sb([128, 8], F32, "lgt2")
    P.act(t1.v(), lay["ret_logit"].v(), AF.Abs)
    P.act(t1.v(), t1.v(), AF.Exp, scale=-1.0)
    P.act(t1.v(), t1.v(), AF.Ln, bias=1.0)
    P.ts(t2.v(), lay["ret_logit"].v(), 0.0, None, ALU.min)
    P.tt(lg.v(), t2.v(), t1.v(), ALU.subtract)
    nlg = P.sb([128, 8], F32, "nlg")
    P.ts(nlg.v(), lg.v(), -1.0, None, ALU.mult)
    tab = P.sb([128, 2, 2, 4, 128], F32, "ret_tab")
    for d in range(2):
        row = env.row_up if d == 0 else env.row_dn
        for h in range(4):
            P.act(tab[:, d, 0, h, :], row, AF.Exp, scale=lg[:, d * 4 + h:d * 4 + h + 1])
            P.act(tab[:, d, 1, h, :], row, AF.Exp, scale=nlg[:, d * 4 + h:d * 4 + h + 1])
    lay["ret_tab"] = tab
    eg = P.sb([128, 2, 4], F32, "ret_eg")
    P.act(eg.v().m(lambda a: a.rearrange("p d h -> p (d h)")), lg.v(), AF.Exp, scale=128.0)
    lay["ret_eg"] = eg
    wa2b = P.sb([16, 2, 512], BF16, "wa2b")
    P.copy(wa2b.v(), lay["wa2"].v())
    lay["wa2"] = wa2b
    A = P.sb([128, 64], F32, "ssdA")
    P.act(A.v(), lay["a_log"].v(), AF.Exp)
    P.ts(A.v(), A.v(), -1.0, None, ALU.mult)
    lay["ssd_A"] = A


SMALL_COMMON = ["mod_lat", "mod_ctx", "norm_g", "ret_logit", "wa2", "gla_ba", "conv_w", "conv_b", "dt_bias", "a_log",
                "halo_valid"]
SMALL_SHAPES = {"mod_lat": [128, 9, 8], "mod_ctx": [128, 9, 8], "norm_g": [128, 3, 8], "ret_logit": [128, 8],
                "wa2": [16, 2, 512], "gla_ba": [128, 2, 4], "conv_w": [128, 5, 24], "conv_b": [128, 24],
                "dt_bias": [128, 64], "a_log": [128, 64], "halo_valid": [128, 2],
                "gla_g": [128, 256], "ssd_d": [128, 32], "ssd_ng": [128, 2048], "final_g": [128, 8]}
SIZES = [512, 512, 512, 512, TC, 4]
KINDS = ["lat", "lat", "lat", "lat", "ctx", "halo"]
TIN = TL + TC + 4


def make_spill(P, full):
    sp = {}
    sp["qk"] = P.dram("sp_qk", [2, 4, 4, 128, TT], BF16)
    sp["eG"] = P.dram("sp_eG", [4, 128, NCH, 4], F32)
    sp["v"] = P.dram("sp_v", [2, TT, 1024], BF16)
    sp["BT"] = P.dram("sp_BT", [4, 128, TT], BF16)
    sp["x"] = P.dram("sp_x", [TT, 2048], BF16)
    sp["a"] = P.dram("sp_a", [TT, 64], F32)
    sp["dt"] = P.dram("sp_dt", [TT, 64], F32)
    if full:
        sp["CT"] = P.dram("sp_CT", [4, 128, TT], BF16)
        sp["gate"] = P.dram("sp_gate", [TT, 4096], BF16)
        sp["sg"] = P.dram("sp_sg", [24, 128, TT], BF16)
        sp["yb"] = P.dram("sp_yb", [TT, 4096], BF16)
        sp["ygT"] = P.dram("sp_ygT", [32, 128, TT], BF16)
    return sp


def alloc_states(P):
    S = [P.sb([128, 1024], F32, "S_ret"), P.sb([128, 1024], F32, "S_gla"), P.sb([128, 2048], F32, "S_ssd")]
    SB = [P.sb([128, 1024], BF16, "SB_ret"), P.sb([128, 1024], BF16, "SB_gla"), P.sb([128, 2048], BF16, "SB_ssd")]
    return S, SB


def zero_states(P, S, SB):
    for i in range(3):
        P.memset(S[i].v(), 0.0, eng="pool")
        P.memset(SB[i].v(), 0.0, eng="pool")


def build_layer(phase, last=False):
    nc = bass.Bass("TRN2", target_bir_lowering=False)
    with contextlib.ExitStack() as stack:
        P = Prog(nc, stack)
        P.psum_banks()
        env = Env()
        ins = {}

        def inp(name, shape, dt=F32):
            ins[name] = P.dram(name, shape, dt, "ExternalInput")
            return ins[name]
        inp("cin", [128, 1024])
        inp("xin", [D, TIN])
        for nm in SMALL_COMMON:
            inp(nm, SMALL_SHAPES[nm])
        inp("rope_in", [128, 2, TL])
        inp("w_inL", [128, KC, DIN])
        if phase == 1:
            inp("wiL", [NFF, 128, KC, 256])
            inp("woL", [128, NFF, D])
            x1out = P.dram("x1T", [D, TT], F32, "ExternalOutput")
            Eout = P.dram("Eout", [2, 128, 4096], F32, "ExternalOutput")
            Dout = P.dram("Dout", [2, 128, 40], F32, "ExternalOutput")
        else:
            for nm in ("gla_g", "ssd_d", "ssd_ng", "final_g"):
                inp(nm, SMALL_SHAPES[nm])
            inp("wiL", [NFF, 128, KC, 256])
            inp("woL", [128, NFF, D])
            inp("wbL", [KC, 128, 32, 128])
            inp("w_outL", [KC, 128, KC, 128])
            inp("slotE", [2, 7, 128, 4096])
            inp("slotD", [2, 7, 128, 40])
            xout = P.dram("xoutT", [D, TT], F32, "ExternalOutput")
        load_consts(P, env, ins["cin"])
        lay = load_small(P, ins, SMALL_COMMON + ([] if phase == 1 else ["gla_g", "ssd_d", "ssd_ng", "final_g"]))
        lay["rope_in"] = ins["rope_in"]
        derive_layer(P, env, lay, phase == 2)
        modT = {"lat": lay["mod_lat"], "ctx": lay["mod_ctx"]}
        sp = make_spill(P, phase == 2)
        xv = ins["xin"].ap.rearrange("(k p) t -> p k t", p=128)
        offs = np.concatenate([[0], np.cumsum(SIZES)]).astype(int).tolist()
        spoff = {b: offs[b] for b in range(5)}

        def wi_view(buf):
            return buf

        with scope(P):
            H = Blocks(P, SIZES, KINDS, "hT", BF16)
            scr = (P.sb([128, KC, 512], BF16, "sq"), P.sb([128, KC, 512], F32, "tmpn"), P.sb([128, 512], F32, "rstd"))
            if phase == 1:
                with scope(P):
                    X = Blocks(P, SIZES, KINDS, "xTs")
                    for b in range(6):
                        P.dma(X.b[b].v(), View(ins["xin"], xv[:, :, offs[b]:offs[b] + SIZES[b]]))
                    m0 = prep_mod(P, modT, lay["norm_g"], 0)
                    m0["halo"] = m0["lat"]
                    for b in range(6):
                        gs, sh, hg = m0[KINDS[b]]
                        norm_mod(P, X.b[b], H.b[b], SIZES[b], gs, sh, env.ones_bf, scr)
                    ffn_tiled(P, X, H, m0, ins["wiL"], ins["woL"])
                    ov = x1out.ap.rearrange("(k p) t -> p k t", p=128)
                    for b in range(5):
                        P.dma(View(x1out, ov[:, :, offs[b]:offs[b] + SIZES[b]]), X.b[b].v())
                    m1 = prep_mod(P, modT, lay["norm_g"], 1)
                    m1["halo"] = m1["lat"]
                    for b in range(6):
                        gs, sh, hg = m1[KINDS[b]]
                        norm_mod(P, X.b[b], H.b[b], SIZES[b], gs, sh, env.ones_bf, scr)
            else:
                with scope(P):
                    xt_t = P.stack.enter_context(P.sb_ctx([128, 2, KC, 512], F32, "xtmp"))
                    xt = [Buf(xt_t[:, i]) for i in range(2)]
                    m1 = prep_mod(P, modT, lay["norm_g"], 1)
                    m1["halo"] = m1["lat"]
                    for b in range(6):
                        n = SIZES[b]
                        xb = xt[b % 2]
                        P.dma(xb[:, :, :n], View(ins["xin"], xv[:, :, offs[b]:offs[b] + n]))
                        gs, sh, hg = m1[KINDS[b]]
                        norm_mod(P, _sub(xb, n), H.b[b], n, gs, sh, env.ones_bf, scr)
            groups = ["ret", "gla", "v", "dt", "xB"] if phase == 1 else ["ret", "gla", "v", "gates", "dt", "xBC", "merge"]
            lay["halo_valid"] = lay["halo_valid"]
            inproj(P, env, H, ins["w_inL"], sp, lay, groups, do_q=(phase == 2))
        with scope(P):
            S, SB = alloc_states(P)
            if phase == 1:
                S2, SB2 = alloc_states(P)
                Ss, SBs = [S, S2], [SB, SB2]
                Bs = [SweepBufs(P, False, False), SweepBufs(P, False, False)]
                Dts = [[P.sb([128, 4], F32, "Dt0"), P.sb([128, 4], F32, "Dt1"), P.sb([128, 32], F32, "Dt2")] for _ in range(2)]
                gens = []
                for d in range(2):
                    zero_states(P, Ss[d], SBs[d])
                    P.memset(Dts[d][0].v(), 1.0)
                    P.memset(Dts[d][1].v(), 1.0)
                    P.memset(Dts[d][2].v(), 0.0)
                    chain = list(range(NCH_L)) if d == 0 else list(range(NCH_L - 1, -1, -1))
                    gens.append(sweep(P, env, sp, lay, d, chain, Ss[d], SBs[d], False, Bs[d], "state", Dts[d]))
                for _ in zip(*gens):
                    pass
                for g_ in gens:
                    for _ in g_:
                        pass
                for d in range(2):
                    S, SB, Dt = Ss[d], SBs[d], Dts[d]
                    P.dma(Eout[d, :, 0:1024], S[0].v())
                    P.dma(Eout[d, :, 1024:2048], S[1].v())
                    P.dma(Eout[d, :, 2048:4096], S[2].v())
                    dd = P.sb([128, 40], F32, "dd")
                    P.copy(dd[:, 0:4], Dt[0].v())
                    P.copy(dd[:, 4:8], Dt[1].v())
                    P.act(dd[:, 8:40], Dt[2].v(), AF.Exp)
                    P.dma(Dout[d], dd.v())
            else:
                B = SweepBufs(P, True, True)
                dI = P.sb([128, 32, 128], BF16, "dI")
                for h in range(32):
                    P.ts(dI[:, h, :], env.cm("ident", bf=False), lay["ssd_d"][:, h:h + 1], None, ALU.mult)
                lay["dI"] = dI
                et = B.yall[0]
                dtile = P.sb([128, 40], F32, "slotD_t")
                for d in (1, 0):
                    zero_states(P, S, SB)
                    cchain = [NCH_L, NCH_L + 1] if d == 0 else [NCH_L + 1, NCH_L]
                    for _ in sweep(P, env, sp, lay, d, cchain, S, SB, True, B, "post" if d == 0 else "yb"):
                        pass
                    for s in range(7):
                        P.dma(et.v(), ins["slotE"][d, s])
                        P.dma(dtile.v(), ins["slotD"][d, s])
                        for h in range(4):
                            hc = slice(h * 256, (h + 1) * 256)
                            P.stt(S[0][:, hc], S[0][:, hc], dtile[:, h:h + 1], et[:, hc], ALU.mult, ALU.add)
                            hc2 = slice(1024 + h * 256, 1024 + (h + 1) * 256)
                            P.stt(S[1][:, hc], S[1][:, hc], dtile[:, 4 + h:5 + h], et[:, hc2], ALU.mult, ALU.add)
                        for h in range(32):
                            hc = slice(h * 64, (h + 1) * 64)
                            hc2 = slice(2048 + h * 64, 2048 + (h + 1) * 64)
                            P.stt(S[2][:, hc], S[2][:, hc], dtile[:, 8 + h:9 + h], et[:, hc2], ALU.mult, ALU.add)
                    for i in range(3):
                        P.copy(SB[i].v(), S[i].v(), eng="act")
                    chain = list(range(NCH_L)) if d == 0 else list(range(NCH_L - 1, -1, -1))
                    for _ in sweep(P, env, sp, lay, d, chain, S, SB, True, B, "post" if d == 0 else "yb"):
                        pass
        if phase == 2:
            with scope(P):
                X = Blocks(P, SIZES[:5], KINDS[:5], "xTs")
                for b in range(5):
                    P.dma(X.b[b].v(), View(ins["xin"], xv[:, :, offs[b]:offs[b] + SIZES[b]]))
                g5 = {}
                for kind in ("lat", "ctx"):
                    g5[kind] = modT[kind][:, 5, :]
                blks = [0, 1, 2, 3] + ([] if last else [4])
                phase_c(P, env, sp, X, spoff, g5, ins["wbL"], ins["w_outL"], blks)
                with scope(P):
                    H2 = Blocks(P, SIZES[:5], KINDS[:5], "h2T", BF16)
                    scr = (P.sb([128, KC, 512], BF16, "sq"), P.sb([128, KC, 512], F32, "tmpn"), P.sb([128, 512], F32, "rstd"))
                    m2 = prep_mod(P, modT, lay["norm_g"], 2)
                    for b in range(5):
                        gs, sh, hg = m2[KINDS[b]]
                        norm_mod(P, X.b[b], H2.b[b], SIZES[b], gs, sh, env.ones_bf, scr)
                    ffn_tiled(P, X, H2, m2, ins["wiL"], ins["woL"], G=4)
                ov = xout.ap.rearrange("(k p) t -> p k t", p=128)
                if last:
                    with scope(P):
                        scr = (P.sb([128, KC, 512], BF16, "sq"), P.sb([128, KC, 512], F32, "tmpn"), P.sb([128, 512], F32, "rstd"))
                        zt = P.sb([128, KC], F32, "zeros8")
                        P.memset(zt.v(), 0.0)
                        o_t = P.stack.enter_context(P.sb_ctx([128, 2, KC, 512], F32, "fo"))
                        ob = [Buf(o_t[:, i]) for i in range(2)]
                        for b in range(4):
                            norm_mod(P, X.b[b], ob[b % 2], 512, lay["final_g"], zt, env.ones_bf, scr)
                            P.dma(View(xout, ov[:, :, offs[b]:offs[b] + 512]), ob[b % 2].v())
                        P.dma(View(xout, ov[:, :, offs[4]:offs[4] + TC]), X.b[4].v())
                else:
                    for b in range(5):
                        P.dma(View(xout, ov[:, :, offs[b]:offs[b] + SIZES[b]]), X.b[b].v())
        P.emit()
    return nc


class _sub:
    def __init__(self, buf, n):
        self.buf = buf
        self.n = n

    def v(self):
        return View(self.buf, self.buf.ap[:, :, :self.n])

    def __getitem__(self, idx):
        return View(self.buf, self.buf.ap[:, :, :self.n][idx])


def ffn_tiled(P, X, H, mods, wiL, woL, G=6):
    nblk = len(X.sizes)
    T = X.T
    with scope(P):
        actT_t = P.stack.enter_context(P.sb_ctx([128, G, T], BF16, "actT"))
        wi_t = P.stack.enter_context(P.sb_ctx([128, 2, KC, 256], BF16, "wi_t"))
        wo_t = P.stack.enter_context(P.sb_ctx([128, G, D], BF16, "wo_t"))
        sil_t = P.stack.enter_context(P.sb_ctx([128, 2, 512], F32, "sil"))
        actb = [[Buf(actT_t[:, g, X.offs[b]:X.offs[b] + X.sizes[b]]) for b in range(nblk)] for g in range(G)]
        wib = [Buf(wi_t[:, i]) for i in range(2)]
        wob = Buf(wo_t[:])
        silb = [Buf(sil_t[:, i]) for i in range(2)]
        ns = 0
        nw = 0
        for g0 in range(0, NFF, G):
            gn = min(G, NFF - g0)
            for gi in range(gn):
                j = g0 + gi
                w = wib[nw % 2]
                nw += 1
                P.dma(w.v(), wiL[j], q="pool")
                for b in range(nblk):
                    n = X.sizes[b]
                    pa = P.next_ps()
                    pu = P.next_ps()
                    for k in range(KC):
                        P.mm(pa[:, :n], w[:, k, 0:128], H.b[b][:, k, :], start=(k == 0), stop=(k == KC - 1))
                    for k in range(KC):
                        P.mm(pu[:, :n], w[:, k, 128:256], H.b[b][:, k, :], start=(k == 0), stop=(k == KC - 1))
                    s = silb[ns % 2]
                    ns += 1
                    P.act(s[:, :n], pa[:, :n], AF.Silu)
                    P.tt(actb[gi][b].v(), s[:, :n], pu[:, :n], ALU.mult)
            P.dma(wob[:, 0:gn, :], woL[:, g0:g0 + gn, :], q="pool")
            for b in range(nblk):
                n = X.sizes[b]
                hg = mods[X.kinds[b]][2]
                for f in range(KC):
                    ps = P.next_ps()
                    for gi in range(gn):
                        P.mm(ps[:, :n], wob[:, gi, f * 128:(f + 1) * 128], actb[gi][b].v(), start=(gi == 0), stop=(gi == gn - 1))
                    P.stt(X.b[b][:, f, :], ps[:, :n], hg[:, f:f + 1], X.b[b][:, f, :], ALU.mult, ALU.add)


def build_p0():
    nc = bass.Bass("TRN2", target_bir_lowering=False)
    NCOL = 4608
    with contextlib.ExitStack() as stack:
        P = Prog(nc, stack)
        P.psum_banks()
        cc = P.dram("ccT", [128, KC, 2], F32, "ExternalInput")
        W = P.dram("adaW", [128, KC, NCOL], F32, "ExternalInput")
        bias = P.dram("adab", [2, NCOL], F32, "ExternalInput")
        out = P.dram("modout", [2, NCOL], F32, "ExternalOutput")
        s = P.sb([128, KC, 2], F32, "s")
        P.dma(s.v(), cc.v())
        P.act(s.v(), s.v(), AF.Silu)
        bt = P.sb([2, NCOL], F32, "bt")
        P.dma(bt.v(), bias.v())
        ot = P.sb([2, NCOL], F32, "ot")
        w_t = P.stack.enter_context(P.sb_ctx([128, 2, KC, 512], F32, "w0"))
        wb = [Buf(w_t[:, i]) for i in range(2)]
        for j in range(NCOL // 512):
            w = wb[j % 2]
            P.dma(w.v(), W[:, :, j * 512:(j + 1) * 512])
            ps = P.next_ps()
            for k in range(KC):
                P.mm(ps[0:2, :], s[:, k, :], w[:, k, :], start=(k == 0), stop=(k == KC - 1))
            P.tt(ot[:, j * 512:(j + 1) * 512], ps[0:2, :], bt[:, j * 512:(j + 1) * 512], ALU.add)
        P.dma(out.v(), ot.v())
        P.emit()
    return nc


_PROGS = {}


def get_prog(key):
    if key not in _PROGS:
        if key == "p0":
            _PROGS[key] = build_p0()
        elif key == "p1":
            _PROGS[key] = build_layer(1)
        elif key == "p2":
            _PROGS[key] = build_layer(2, last=False)
        else:
            _PROGS[key] = build_layer(2, last=True)
    return _PROGS[key]


def rep(v, n=128):
    v = np.asarray(v, np.float32).reshape(1, -1)
    return np.ascontiguousarray(np.broadcast_to(v, (n, v.shape[1])))


def rope_tables(core):
    t = core * TL + np.arange(TL)
    row = (t // 64).astype(np.float32)
    col = (t % 64).astype(np.float32)
    nf = 32
    freq = (np.float32(10000.0) ** (-np.arange(nf, dtype=np.float32) / np.float32(nf))).astype(np.float32)
    ang = np.concatenate([row[:, None] * freq[None, :], col[:, None] * freq[None, :]], axis=1).astype(np.float32)
    cos = np.cos(ang).astype(np.float32).T
    sin = np.sin(ang).astype(np.float32).T
    cos_tab = np.concatenate([cos, cos], axis=0)
    sin_tab = np.concatenate([-sin, sin], axis=0)
    return np.ascontiguousarray(np.stack([cos_tab, sin_tab], axis=1))


def run_model(inp, ncores, depth=4, run=None):
    from concourse.bass_utils import run_bass_kernel_spmd
    f32 = np.float32
    x = np.asarray(inp["x"], f32)[0]
    ctx = np.asarray(inp["ctx"], f32)[0]
    assert x.shape[0] == ncores * TL
    consts = host_consts()
    cc = np.stack([np.asarray(inp["c"], f32)[0], np.asarray(inp["c_ctx"], f32)], axis=1)
    ccT = np.ascontiguousarray(cc.reshape(KC, 128, 2).transpose(1, 0, 2))
    in_maps = []
    for c in range(8):
        l, half = (c // 2) % depth, c % 2
        Wl = np.asarray(inp["ada_w"][l], f32)[:, half * 4608:(half + 1) * 4608]
        in_maps.append({"ccT": ccT,
                        "adaW": np.ascontiguousarray(Wl.reshape(KC, 128, 4608).transpose(1, 0, 2)),
                        "adab": rep(np.asarray(inp["ada_b"][l], f32)[half * 4608:(half + 1) * 4608], 2)})
    res0 = run_bass_kernel_spmd(get_prog("p0"), in_maps, core_ids=list(range(8))).results
    mods = []
    for l in range(depth):
        m = np.concatenate([res0[2 * l]["modout"], res0[2 * l + 1]["modout"]], axis=1)
        mods.append([np.ascontiguousarray(m[s].reshape(9, KC, 128).transpose(2, 0, 1)) for s in range(2)])
    xT = [np.ascontiguousarray(x[c * TL:(c + 1) * TL].T) for c in range(ncores)]
    ctxT = np.ascontiguousarray(ctx.T)
    ropes = [rope_tables(c) for c in range(ncores)]
    zeros2 = np.zeros((D, 2), f32)

    def halos(xs):
        hs, vs = [], []
        for c in range(ncores):
            left = xs[c - 1][:, -2:] if c > 0 else zeros2
            right = xs[c + 1][:, :2] if c < ncores - 1 else zeros2
            hs.append(np.concatenate([left, right], axis=1))
            vs.append(rep(np.array([1.0 if c > 0 else 0.0, 1.0 if c < ncores - 1 else 0.0], f32)))
        return hs, vs

    for l in range(depth):
        last = l == depth - 1
        g = lambda k: np.asarray(inp[k][l], f32)
        common = {
            "cin": consts,
            "mod_lat": mods[l][0], "mod_ctx": mods[l][1],
            "norm_g": np.ascontiguousarray(g("norm_g").reshape(3, KC, 128).transpose(2, 0, 1)),
            "ret_logit": rep(g("ret_logit").reshape(-1)),
            "wa2": np.ascontiguousarray(g("gla_wa2").transpose(1, 0, 2)),
            "gla_ba": np.ascontiguousarray(g("gla_ba").reshape(2, 4, 128).transpose(2, 0, 1)),
            "conv_w": np.ascontiguousarray(g("conv_w").reshape(5, 24, 128).transpose(2, 0, 1)),
            "conv_b": np.ascontiguousarray(g("conv_b").reshape(24, 128).T),
            "dt_bias": rep(g("dt_bias").reshape(-1)),
            "a_log": rep(g("a_log").reshape(-1)),
            "w_inL": np.ascontiguousarray(g("w_in").reshape(KC, 128, DIN).transpose(1, 0, 2)),
        }

        def tile_ffn(wi, wo):
            wr = wi.reshape(KC, 128, 2 * DFF)
            a = wr[:, :, :DFF].reshape(KC, 128, NFF, 128).transpose(2, 1, 0, 3)
            u = wr[:, :, DFF:].reshape(KC, 128, NFF, 128).transpose(2, 1, 0, 3)
            return (np.ascontiguousarray(np.concatenate([a, u], axis=3)),
                    np.ascontiguousarray(wo.reshape(NFF, 128, D).transpose(1, 0, 2)))
        wiL, woL = tile_ffn(g("ffn1_wi"), g("ffn1_wo"))
        hs, vs = halos(xT)
        in_maps = []
        for c in range(ncores):
            m = dict(common)
            m.update({"xin": np.ascontiguousarray(np.concatenate([xT[c], ctxT, hs[c]], axis=1)),
                      "halo_valid": vs[c], "rope_in": ropes[c], "wiL": wiL, "woL": woL})
            in_maps.append(m)
        res1 = run_bass_kernel_spmd(get_prog("p1"), in_maps, core_ids=list(range(ncores))).results
        if run is not None:
            run["res1"] = res1
        x1 = [r["x1T"][:, :TL] for r in res1]
        ctx1 = res1[0]["x1T"][:, TL:]
        E = [r["Eout"] for r in res1]
        Dd = [r["Dout"] for r in res1]
        wiL, woL = tile_ffn(g("ffn2_wi"), g("ffn2_wo"))
        wb_all = np.concatenate([g("wb_ret"), g("wb_gla"), g("wb_ssd")], axis=0)
        wbL = np.ascontiguousarray(wb_all.reshape(32, 128, KC, 128).transpose(2, 1, 0, 3))
        w_outL = np.ascontiguousarray(g("w_out").reshape(KC, 128, KC, 128).transpose(2, 1, 0, 3))
        hs, vs = halos(x1)
        in_maps = []
        for c in range(ncores):
            slotE = np.zeros((2, 7, 128, 4096), f32)
            slotD = np.ones((2, 7, 128, 40), f32)
            for s, src in enumerate(range(0, c)):
                slotE[0, s] = E[src][0]
                slotD[0, s] = Dd[src][0]
            for s, src in enumerate(range(ncores - 1, c, -1)):
                slotE[1, s] = E[src][1]
                slotD[1, s] = Dd[src][1]
            m = dict(common)
            m.update({"xin": np.ascontiguousarray(np.concatenate([x1[c], ctx1, hs[c]], axis=1)),
                      "halo_valid": vs[c], "rope_in": ropes[c], "wiL": wiL, "woL": woL, "wbL": wbL, "w_outL": w_outL,
                      "gla_g": rep(g("gla_norm_g")), "ssd_d": rep(g("ssd_d")), "ssd_ng": rep(g("ssd_norm_g")),
                      "final_g": fm_vec(np.asarray(inp["final_norm_g"], f32)),
                      "slotE": slotE, "slotD": slotD})
            in_maps.append(m)
        res2 = run_bass_kernel_spmd(get_prog("p2last" if last else "p2"), in_maps, core_ids=list(range(ncores))).results
        if run is not None:
            run["res2"] = res2
        xT = [r["xoutT"][:, :TL] for r in res2]
        ctxT = np.ascontiguousarray(res2[0]["xoutT"][:, TL:])
    out = np.concatenate([t.T for t in xT], axis=0)[None]
    return np.ascontiguousarray(out.astype(np.float32))


def kernel(**inputs):
    return run_model(inputs, 8)
```

```python
import contextlib
import numpy as np
import concourse.bass as bass
import concourse.mybir as mybir

F32 = mybir.dt.float32
BF16 = mybir.dt.bfloat16
AF = mybir.ActivationFunctionType
ALU = mybir.AluOpType
KDMA = 14


class View:
    __slots__ = ("buf", "ap")

    def __init__(self, buf, ap):
        self.buf = buf
        self.ap = ap

    def __getitem__(self, idx):
        return View(self.buf, self.ap[idx])

    def m(self, f):
        return View(self.buf, f(self.ap))


class Buf:
    __slots__ = ("ap", "w", "r", "name", "psum")

    def __init__(self, ap, name="", psum=False):
        self.ap = ap
        self.w = None
        self.r = {}
        self.name = name
        self.psum = psum

    def __getitem__(self, idx):
        return View(self, self.ap[idx])

    def v(self):
        return View(self, self.ap)


class Prog:
    ENGS = ("pe", "act", "dve", "pool", "sp")

    def __init__(self, nc, stack):
        self.nc = nc
        self.stack = stack
        self.q = {e: [] for e in self.ENGS}
        self.cnt = {e: 0 for e in self.ENGS}
        self.dman = {e: 0 for e in self.ENGS}
        self.waited = {e: {} for e in self.ENGS}
        self.sems = {}
        for e in self.ENGS:
            self.sems[("eng", e)] = stack.enter_context(nc.semaphore("s_" + e))
        for e in ("sp", "pool", "act"):
            for i in range(KDMA):
                self.sems[("dma", e, i)] = stack.enter_context(nc.semaphore(f"d_{e}{i}"))
        self.nalloc = 0
        self.psn = 0

    def sb(self, shape, dtype=F32, name=None):
        self.nalloc += 1
        t = self.stack.enter_context(self.nc.sbuf_tensor(f"{name or 't'}_{self.nalloc}", list(shape), dtype))
        return Buf(t[:] if hasattr(t, "__getitem__") else t, name or "t")

    def sb_ctx(self, shape, dtype=F32, name=None):
        self.nalloc += 1
        return self.nc.sbuf_tensor(f"{name or 't'}_{self.nalloc}", list(shape), dtype)

    def psum_banks(self):
        self.ps = []
        for i in range(8):
            t = self.stack.enter_context(self.nc.psum_tensor(f"ps{i}", [128, 512], F32))
            self.ps.append(Buf(t[:], f"ps{i}", psum=True))

    def next_ps(self):
        b = self.ps[self.psn % 8]
        self.psn += 1
        return b

    def dram(self, name, shape, dtype=F32, kind="Internal"):
        t = self.nc.dram_tensor(name, list(shape), dtype, kind=kind)
        return Buf(t.ap(), name)

    def op(self, E, fn, reads, writes, dma=False):
        deps = {}

        def add(tok):
            sk, val, eng = tok
            if deps.get(sk, (0, None))[0] < val:
                deps[sk] = (val, eng)

        for b in reads:
            if b.w is not None:
                add(b.w)
            if b.psum:
                for sk, (val, eng) in b.r.items():
                    if eng != E:
                        add((sk, val, eng))
        for b in writes:
            if b.w is not None:
                add(b.w)
            for sk, (val, eng) in b.r.items():
                add((sk, val, eng))
        waits = []
        for sk, (val, eng) in deps.items():
            if eng == "pe" and E == "pe" and not dma:
                continue
            if self.waited[E].get(sk, 0) >= val:
                continue
            self.waited[E][sk] = val
            waits.append((sk, val))
        if dma:
            n = self.dman[E]
            self.dman[E] += 1
            sk = ("dma", E, n % KDMA)
            val = 16 * (n // KDMA + 1)
            if val > 16 and self.waited[E].get(sk, 0) < val - 16:
                waits.append((sk, val - 16))
                self.waited[E][sk] = val - 16
            tok = (sk, val, "dma")
            inc = (sk, 16)
        else:
            self.cnt[E] += 1
            sk = ("eng", E)
            tok = (sk, self.cnt[E], E)
            inc = (sk, 1)
        self.q[E].append((waits, fn, inc))
        wset = set(id(b) for b in writes)
        for b in writes:
            b.w = tok
            b.r = {}
        for b in reads:
            if id(b) not in wset:
                b.r[tok[0]] = (tok[1], tok[2])
        return tok

    def barrier(self):
        cur = []
        for e in self.ENGS:
            if self.cnt[e] > 0:
                cur.append((("eng", e), self.cnt[e]))
        for e in ("sp", "pool", "act"):
            n = self.dman[e]
            for i in range(KDMA):
                k = (n - 1 - i)
                if k >= 0:
                    cur.append((("dma", e, k % KDMA), 16 * (k // KDMA + 1)))
        for E in self.ENGS:
            waits = []
            for sk, val in cur:
                if sk == ("eng", E):
                    continue
                if self.waited[E].get(sk, 0) >= val:
                    continue
                self.waited[E][sk] = val
                waits.append((sk, val))
            if waits:
                self.q[E].append((waits, None, None))

    def emit(self):
        nc = self.nc
        self.barrier()
        with nc.Block() as block:
            def replay(E, e):
                for waits, fn, inc in self.q[E]:
                    for sk, val in waits:
                        e.wait_ge(self.sems[sk], val)
                    if fn is not None:
                        ins = fn(e)
                        ins.then_inc(self.sems[inc[0]], inc[1])

            @block.tensor
            def _(e):
                replay("pe", e)

            @block.scalar
            def _(e):
                replay("act", e)

            @block.vector
            def _(e):
                replay("dve", e)

            @block.gpsimd
            def _(e):
                replay("pool", e)

            @block.sync
            def _(e):
                replay("sp", e)

    @staticmethod
    def _b(*xs):
        return [x.buf for x in xs if isinstance(x, View)]

    @staticmethod
    def _a(x):
        return x.ap if isinstance(x, View) else x

    def mm(self, out, lhsT, rhs, start=True, stop=True):
        o, l, r = out.ap, lhsT.ap, rhs.ap
        return self.op("pe", lambda e: e.matmul(o, l, r, start=start, stop=stop), self._b(lhsT, rhs), self._b(out))

    def tr(self, out, in_, ident):
        o, i, d = out.ap, in_.ap, ident.ap
        return self.op("pe", lambda e: e.transpose(o, i, d), self._b(in_, ident), self._b(out))

    def act(self, out, in_, func, bias=0.0, scale=1.0, accum=None):
        o, i, b, s = out.ap, in_.ap, self._a(bias), self._a(scale)
        ac = self._a(accum) if accum is not None else None
        kw = {}
        if ac is not None:
            kw["accum_out"] = ac
        return self.op("act", lambda e: e.activation(o, i, func, bias=b, scale=s, **kw),
                       self._b(in_, bias, scale), self._b(out) + (self._b(accum) if accum is not None else []))

    def tt(self, out, a, b, op, eng="dve"):
        o, x, y = out.ap, a.ap, b.ap
        return self.op(eng, lambda e: e.tensor_tensor(o, x, y, op), self._b(a, b), self._b(out))

    def ts(self, out, a, s1, s2, op0, op1=None, eng="dve", accum=None):
        o, x, p1, p2 = out.ap, a.ap, self._a(s1), self._a(s2)
        kw = {}
        if op1 is not None:
            kw["op1"] = op1
        if accum is not None:
            kw["accum_out"] = accum.ap
        return self.op(eng, lambda e: e.tensor_scalar(o, x, p1, p2, op0, **kw), self._b(a, s1, s2),
                       self._b(out) + (self._b(accum) if accum is not None else []))

    def stt(self, out, a, s, b, op0, op1, eng="dve"):
        o, x, p, y = out.ap, a.ap, self._a(s), b.ap
        return self.op(eng, lambda e: e.scalar_tensor_tensor(o, x, p, y, op0, op1), self._b(a, s, b), self._b(out))

    def copy(self, out, a, eng="dve"):
        o, x = out.ap, a.ap
        if eng == "act":
            return self.op("act", lambda e: e.copy(o, x), self._b(a), self._b(out))
        return self.op(eng, lambda e: e.tensor_copy(o, x), self._b(a), self._b(out))

    def memset(self, out, val, eng="dve"):
        o = out.ap
        return self.op(eng, lambda e: e.memset(o, val), [], self._b(out))

    def scan(self, out, d0, d1, init, op0, op1):
        o, a, b, i = out.ap, d0.ap, d1.ap, self._a(init)
        return self.op("dve", lambda e: e.tensor_tensor_scan(o, a, b, i, op0, op1), self._b(d0, d1, init), self._b(out))

    def dma(self, out, in_, q="sp"):
        o, i = out.ap, in_.ap
        return self.op(q, lambda e: e.dma_start(out=o, in_=i), self._b(in_), self._b(out), dma=True)


@contextlib.contextmanager
def scope(P):
    old = P.stack
    with contextlib.ExitStack() as st:
        P.stack = st
        try:
            yield
        finally:
            P.barrier()
            P.stack = old


D = 1024
KC = 8
DFF = 2816
NFF = 22
EPS = 1e-6


def fm_vec(v):
    v = np.asarray(v, np.float32)
    return np.ascontiguousarray(v.reshape(-1, 128).T)


class Blocks:
    def __init__(self, P, sizes, kinds, name="xT", dtype=F32):
        self.sizes = sizes
        self.kinds = kinds
        self.offs = np.concatenate([[0], np.cumsum(sizes)]).astype(int).tolist()
        self.T = self.offs[-1]
        self.t = P.stack.enter_context(P.nc.sbuf_tensor(name, [128, KC, self.T], dtype))
        self.b = [Buf(self.t[:, :, self.offs[i]:self.offs[i] + n], f"{name}{i}") for i, n in enumerate(sizes)]


def prep_mod(P, modT, gT, idx):
    out = {}
    for kind, mt in modT.items():
        gs = P.sb([128, KC], F32, "gs")
        hg = P.sb([128, KC], F32, "hg")
        P.stt(gs.v(), mt[:, 3 * idx + 1, :], 1.0, gT[:, idx, :], ALU.add, ALU.mult)
        P.ts(hg.v(), mt[:, 3 * idx + 2, :], 0.5 if idx != 1 else 1.0, None, ALU.mult)
        out[kind] = (gs, mt[:, 3 * idx + 0, :], hg)
    return out


def norm_mod(P, xb, hb, n, gs, sh, ones_bf, scr):
    sq, tmp, rstd = scr
    P.act(sq[:, :, :n], xb.v(), AF.Square)
    ps = P.next_ps()
    for k in range(KC):
        P.mm(ps[:, :n], ones_bf.v(), sq[:, k, :n], start=(k == 0), stop=(k == KC - 1))
    P.ts(rstd[:, :n], ps[:, :n], 1.0 / D, EPS, ALU.mult, ALU.add)
    P.act(rstd[:, :n], rstd[:, :n], AF.Ln)
    P.act(rstd[:, :n], rstd[:, :n], AF.Exp, scale=-0.5)
    for k in range(KC):
        P.stt(tmp[:, k, :n], xb[:, k, :], gs[:, k:k + 1], rstd[:, :n], ALU.mult, ALU.mult)
        P.act(hb[:, k, :], tmp[:, k, :n], AF.Identity, bias=sh[:, k:k + 1], scale=1.0)


def ffn(P, X, H, mods, wi, wo, ones_bf, scr, G=6):
    nblk = len(X.sizes)
    T = X.T
    with P.sb_ctx([128, G, T], BF16, "actT") as actT_t, \
            P.sb_ctx([128, 2, KC, 256], BF16, "wi_t") as wi_t, \
            P.sb_ctx([128, G, D], BF16, "wo_t") as wo_t, \
            P.sb_ctx([128, 2, 512], F32, "sil") as sil_t:
        actb = [[Buf(actT_t[:, g, X.offs[b]:X.offs[b] + X.sizes[b]]) for b in range(nblk)] for g in range(G)]
        wib = [Buf(wi_t[:, i]) for i in range(2)]
        wob = Buf(wo_t[:])
        silb = [Buf(sil_t[:, i]) for i in range(2)]
        wiv = wi.ap.rearrange("(k p) c -> p k c", p=128)
        wov = wo.ap.rearrange("(j p) f -> p j f", p=128)
        ns = 0
        nw = 0
        for g0 in range(0, NFF, G):
            gn = min(G, NFF - g0)
            for gi in range(gn):
                j = g0 + gi
                w = wib[nw % 2]
                nw += 1
                P.dma(w[:, :, 0:128], View(wi, wiv[:, :, j * 128:(j + 1) * 128]), q="pool")
                P.dma(w[:, :, 128:256], View(wi, wiv[:, :, DFF + j * 128:DFF + (j + 1) * 128]), q="pool")
                for b in range(nblk):
                    n = X.sizes[b]
                    pa = P.next_ps()
                    pu = P.next_ps()
                    for k in range(KC):
                        P.mm(pa[:, :n], w[:, k, 0:128], H.b[b][:, k, :], start=(k == 0), stop=(k == KC - 1))
                    for k in range(KC):
                        P.mm(pu[:, :n], w[:, k, 128:256], H.b[b][:, k, :], start=(k == 0), stop=(k == KC - 1))
                    s = silb[ns % 2]
                    ns += 1
                    P.act(s[:, :n], pa[:, :n], AF.Silu)
                    P.tt(actb[gi][b].v(), s[:, :n], pu[:, :n], ALU.mult)
            P.dma(wob[:, 0:gn, :], View(wo, wov[:, g0:g0 + gn, :]), q="pool")
            for b in range(nblk):
                n = X.sizes[b]
                hg = mods[X.kinds[b]][2]
                for f in range(KC):
                    ps = P.next_ps()
                    for gi in range(gn):
                        P.mm(ps[:, :n], wob[:, gi, f * 128:(f + 1) * 128], actb[gi][b].v(), start=(gi == 0), stop=(gi == gn - 1))
                    P.stt(X.b[b][:, f, :], ps[:, :n], hg[:, f:f + 1], X.b[b][:, f, :], ALU.mult, ALU.add)
        P.barrier()


TL = 2048
TC = 256
NCH_L = TL // 128
NCH_C = TC // 128
NCH = NCH_L + NCH_C
TT = TL + TC
DIN = 14432
COLS = {}
_o = 0
for _n, _s in (("ret_q", 512), ("ret_k", 512), ("ret_v", 1024), ("ret_g", 1024), ("gla_q", 512), ("gla_k", 512),
               ("gla_v", 1024), ("gla_r", 1024), ("gla_af", 16), ("gla_ab", 16), ("ssd_z", 2048), ("ssd_xbc", 3072),
               ("ssd_dt", 64), ("merge", 3072)):
    COLS[_n] = (_o, _s)
    _o += _s
assert _o == DIN
CI = {"ident": 0, "perm": 1, "mle": 2, "mge": 3, "sgt": 4, "slt": 5}


def host_consts():
    p = np.arange(128)[:, None]
    c = np.arange(128)[None, :]
    mats = [p == c, c == (p + 64) % 128, p <= c, p >= c, p > c, p < c]
    m = np.concatenate([x.astype(np.float32) for x in mats], axis=1)
    rows = np.concatenate([np.broadcast_to(np.arange(1, 129, dtype=np.float32), (128, 128)),
                           np.broadcast_to(np.arange(128, 0, -1).astype(np.float32), (128, 128))], axis=1)
    return np.ascontiguousarray(np.concatenate([m, rows], axis=1))


class Env:
    pass


def load_consts(P, env, cin):
    cf = P.sb([128, 8 * 128], F32, "cf")
    P.dma(cf.v(), cin.v())
    cb = P.sb([128, 6 * 128], BF16, "cb")
    P.copy(cb.v(), cf[:, 0:768])
    env.cf, env.cb = cf, cb
    env.ones_bf = P.sb([128, 128], BF16, "ones")
    P.memset(env.ones_bf.v(), 1.0)
    env.ones_f = P.sb([128, 128], F32, "onesf")
    P.memset(env.ones_f.v(), 1.0)

    def cm(name, bf=True):
        i = CI[name]
        return (cb if bf else cf)[:, i * 128:(i + 1) * 128]
    env.cm = cm
    env.row_up = cf[:, 768:896]
    env.row_dn = cf[:, 896:1024]


def softplus_parts(P, z, n, tmp1, tmp2):
    P.act(tmp1, z, AF.Abs)
    P.act(tmp1, tmp1, AF.Exp, scale=-1.0)
    P.act(tmp1, tmp1, AF.Ln, bias=1.0)
    P.ts(tmp2, z, 0.0, None, ALU.max)
    return tmp2, tmp1


def inproj(P, env, H, W, sp, lay, groups, do_q, skip_ctx=False):
    nblk = len(H.sizes)
    tokblocks = [b for b in range(nblk) if H.kinds[b] != "halo" and not (skip_ctx and H.kinds[b] == "ctx")]
    halo = [b for b in range(nblk) if H.kinds[b] == "halo"][0]
    spoff = {}
    o = 0
    for b in tokblocks:
        spoff[b] = o
        o += H.sizes[b]
    wt_t = P.stack.enter_context(P.sb_ctx([128, 2, KC, 512], BF16, "win_t"))
    wt = [Buf(wt_t[:, i]) for i in range(2)]
    st = {"nw": 0}

    def loadw(c0, n):
        w = wt[st["nw"] % 2]
        st["nw"] += 1
        P.dma(w[:, :, 0:n], W[:, :, c0:c0 + n], q="pool")
        return w

    def fm_mm(w, wc0, m, b, ps=None):
        n = H.sizes[b]
        ps = ps or P.next_ps()
        for k in range(KC):
            P.mm(ps[0:m, :n], w[:, k, wc0:wc0 + m], H.b[b][:, k, :], start=(k == 0), stop=(k == KC - 1))
        return ps

    def do_qk(mixer):
        mi = 0 if mixer == "ret" else 1
        names = (["q"] if do_q else []) + ["k"]
        with scope(P):
            ec = P.stack.enter_context
            qf_t = ec(P.sb_ctx([128, 2, 512], F32, "qk_f"))
            qb_t = ec(P.sb_ctx([128, 2, 512], BF16, "qk_b"))
            qo_t = ec(P.sb_ctx([128, 4, 512], BF16, "qk_o"))
            g_t = ec(P.sb_ctx([128, 6, 512], F32, "g_t"))
            rope_t = ec(P.sb_ctx([128, 2, TL if mixer == "ret" else 2], F32, "rope"))
            code_t = ec(P.sb_ctx([16, 2, 512], BF16, "code"))
            la_t = ec(P.sb_ctx([128, 2, 512], F32, "la_t"))
            epm_t = ec(P.sb_ctx([128, 4, 512], F32, "epm_t"))
            epm = [[Buf(epm_t[:, 0]), Buf(epm_t[:, 1])], [Buf(epm_t[:, 2]), Buf(epm_t[:, 3])]]
            qf = [Buf(qf_t[:, i]) for i in range(2)]
            qb = [Buf(qb_t[:, i]) for i in range(2)]
            qo = [Buf(qo_t[:, i]) for i in range(4)]
            gt = [Buf(g_t[:, i]) for i in range(6)]
            la = [Buf(la_t[:, i]) for i in range(2)]
            code = [Buf(code_t[:, i]) for i in range(2)]
            rope = Buf(rope_t[:])
            if mixer == "ret":
                P.dma(rope.v(), lay["rope_in"].v())
            wq = {}
            for nm in names:
                c0, _ = COLS[f"{mixer}_{nm}"]
                wq[nm] = (c0,)
            if mixer == "gla":
                wc = loadw(COLS["gla_af"][0], 32)
                wcode = P.sb([128, KC, 32], BF16, "wcode")
                P.copy(wcode.v(), wc[:, :, 0:32])
            cnt = 0
            if mixer == "gla":
                egg = P.sb([128, 2, NCH, 4], F32, "egg")
            for b in tokblocks:
                n = H.sizes[b]
                nc_ = n // 128
                kind = H.kinds[b]
                if mixer == "gla":
                    for d in range(2):
                        ps = P.next_ps()
                        for k in range(KC):
                            P.mm(ps[0:16, :n], wcode[:, k, d * 16:(d + 1) * 16], H.b[b][:, k, :], start=(k == 0), stop=(k == KC - 1))
                        P.copy(code[d][:, :n], ps[0:16, :n], eng="act")
                if mixer == "ret":
                    for nm in names:
                        w = loadw(wq[nm][0], 512)
                        var0 = 0 if nm == "q" else 1
                        for h in range(4):
                            ps = fm_mm(w, h * 128, 128, b)
                            cnt += 1
                            f32 = qf[cnt % 2]
                            sc = 128 ** -0.5 if nm == "k" else 1.0
                            if kind == "lat":
                                bfv = qb[cnt % 2]
                                P.act(bfv[:, :n], ps[:, :n], AF.Copy, scale=sc)
                                ps2 = P.next_ps()
                                P.mm(ps2[:, :n], env.cm("perm"), bfv[:, :n])
                                t0 = H.offs[b]
                                P.tt(f32[:, :n], bfv[:, :n], rope[:, 0, t0:t0 + n], ALU.mult)
                                P.tt(gt[0][:, :n], ps2[:, :n], rope[:, 1, t0:t0 + n], ALU.mult)
                                P.tt(f32[:, :n], f32[:, :n], gt[0][:, :n], ALU.add)
                            else:
                                P.act(f32[:, :n], ps[:, :n], AF.Copy, scale=sc)
                            for d in range(2):
                                sgn = 0 if nm == "q" else 1
                                tab = lay["ret_tab"][:, d, sgn, h, :].m(lambda a: a.unsqueeze(1).to_broadcast([128, nc_, 128]))
                                o = qo[(cnt * 2 + d) % 4]
                                P.tt(o[:, :n].m(lambda a: a.rearrange("p (c t) -> p c t", t=128)),
                                     f32[:, :n].m(lambda a: a.rearrange("p (c t) -> p c t", t=128)), tab, ALU.mult)
                                P.dma(sp["qk"][mi, h, 2 * d + var0, :, spoff[b]:spoff[b] + n], o[:, :n])
                else:
                    ws = {nm: loadw(wq[nm][0], 512) for nm in names}
                    for h in range(4):
                        for d in range(2):
                            psz = P.next_ps()
                            P.mm(psz[:, :n], lay["wa2"][:, d, h * 128:(h + 1) * 128], code[d][:, :n])
                            z = gt[1]
                            P.ts(z[:, :n], psz[:, :n], lay["gla_ba"][:, d, h:h + 1], None, ALU.add)
                            mx, l = softplus_parts(P, z[:, :n], n, gt[2][:, :n], gt[3][:, :n])
                            P.ts(gt[3][:, :n], z[:, :n], 0.0, None, ALU.min)
                            dd = gt[4]
                            P.tt(dd[:, :n], gt[3][:, :n], gt[2][:, :n], ALU.subtract)
                            cs = gt[5]
                            for c in range(nc_):
                                sl = slice(c * 128, (c + 1) * 128)
                                P.scan(cs[:, sl], env.ones_f[:, 0:128], dd[:, sl], 0.0, ALU.mult, ALU.add)
                            g = la[d]
                            if d == 0:
                                g = cs
                            else:
                                P.tt(g[:, :n], dd[:, :n], cs[:, :n], ALU.subtract)
                                for c in range(nc_):
                                    sl = slice(c * 128, (c + 1) * 128)
                                    P.ts(g[:, sl], g[:, sl], cs[:, c * 128 + 127:c * 128 + 128], None, ALU.add)
                            ch0 = spoff[b] // 128
                            for c in range(nc_):
                                P.act(egg[:, d, ch0 + c, h:h + 1], cs[:, c * 128 + 127:c * 128 + 128], AF.Exp, scale=1.0 / 16)
                            if do_q:
                                P.act(epm[0][d][:, :n], g[:, :n], AF.Exp, scale=1.0 / 16)
                            P.act(epm[1][d][:, :n], g[:, :n], AF.Exp, scale=-1.0 / 16)
                        for nm in names:
                            var0 = 0 if nm == "q" else 1
                            ps = fm_mm(ws[nm], h * 128, 128, b)
                            cnt += 1
                            f32 = qf[cnt % 2]
                            sc = 128 ** -0.5 if nm == "q" else 1.0
                            P.act(f32[:, :n], ps[:, :n], AF.Copy, scale=sc)
                            for d in range(2):
                                o = qo[(cnt * 2 + d) % 4]
                                P.tt(o[:, :n], f32[:, :n], epm[var0][d][:, :n], ALU.mult)
                                P.dma(sp["qk"][mi, h, 2 * d + var0, :, spoff[b]:spoff[b] + n], o[:, :n])
            if mixer == "gla":
                for d in range(2):
                    P.dma(sp["eG"][mi * 2 + d], egg[:, d])
            if mixer == "ret":
                egr = P.sb([128, 2, NCH, 4], F32, "egr")
                for d in range(2):
                    P.copy(egr[:, d, :, :], lay["ret_eg"][:, d, :].m(lambda a: a.unsqueeze(1).to_broadcast([128, NCH, 4])))
                    P.dma(sp["eG"][mi * 2 + d], egr[:, d])
            P.barrier()

    def do_tm(name, handler, width=512):
        c0, sz = COLS[name]
        for cc in range(0, sz, width):
            n_ = min(width, sz - cc)
            w = loadw(c0 + cc, n_)
            for b in tokblocks:
                for c in range(H.sizes[b] // 128):
                    ps = P.next_ps()
                    for k in range(KC):
                        P.mm(ps[:, :n_], H.b[b][:, k, c * 128:(c + 1) * 128], w[:, k, 0:n_], start=(k == 0), stop=(k == KC - 1))
                    handler(ps, n_, cc, spoff[b] + c * 128)

    ev = {"n": 0}
    evt_t = P.stack.enter_context(P.sb_ctx([128, 4, 512], BF16, "evt"))
    evt = [Buf(evt_t[:, i]) for i in range(4)]

    def h_copy(dst, col0, func=AF.Copy):
        def h(ps, n_, cc, tok):
            e = evt[ev["n"] % 4]
            ev["n"] += 1
            if func == AF.Copy and ev["n"] % 2 == 0:
                P.copy(e[:, :n_], ps[:, :n_])
            else:
                P.act(e[:, :n_], ps[:, :n_], func)
            P.dma(dst[tok:tok + 128, col0 + cc:col0 + cc + n_], e[:, :n_])
        return h

    def h_dt(ps, n_, cc, tok):
        z = P.sb([128, 64], F32, "dtz")
        t1 = P.sb([128, 64], F32, "dt1")
        t2 = P.sb([128, 64], F32, "dt2")
        P.tt(z.v(), ps[:, 0:64], lay["dt_bias"].v(), ALU.add)
        mx, l = softplus_parts(P, z.v(), 64, t1.v(), t2.v())
        P.tt(z.v(), mx, l, ALU.add)
        P.tt(t1.v(), z.v(), lay["ssd_A"].v(), ALU.mult)
        P.dma(sp["dt"][tok:tok + 128, :], z.v())
        P.dma(sp["a"][tok:tok + 128, :], t1.v())

    def do_xbc(chunks):
        c0, _ = COLS["ssd_xbc"]
        with scope(P):
            ec = P.stack.enter_context
            rawl_t = ec(P.sb_ctx([128, 2, TL + 4], F32, "rawl"))
            rawc_t = ec(P.sb_ctx([128, 2, TC + 4], F32, "rawc"))
            acc_t = ec(P.sb_ctx([128, 2, TL], F32, "acc"))
            xs_t = ec(P.sb_ctx([128, 4, TT], BF16, "xsT"))
            xtm_t = ec(P.sb_ctx([128, 2, 512], BF16, "xtm"))
            rawl = [Buf(rawl_t[:, i]) for i in range(2)]
            rawc = [Buf(rawc_t[:, i]) for i in range(2)]
            acc = [Buf(acc_t[:, i]) for i in range(2)]
            xs = [Buf(xs_t[:, i]) for i in range(4)]
            xtm = [Buf(xtm_t[:, i]) for i in range(2)]
            for i in range(2):
                P.memset(rawc[i][:, 0:2], 0.0)
                P.memset(rawc[i][:, TC + 2:TC + 4], 0.0)
            w = None
            nx = 0
            for ci, ch in enumerate(chunks):
                if ci % 4 == 0 or w is None:
                    w = loadw(c0 + ch * 128, 512)
                    wbase = ch
                rl, rc, ac = rawl[ci % 2], rawc[ci % 2], acc[ci % 2]
                for b in range(nblk):
                    n = H.sizes[b]
                    if skip_ctx and H.kinds[b] == "ctx":
                        continue
                    ps = fm_mm(w, (ch - wbase) * 128, 128, b)
                    if H.kinds[b] == "lat":
                        P.copy(rl[:, 2 + H.offs[b]:2 + H.offs[b] + n], ps[:, :n], eng=("act" if b % 2 else "dve"))
                    elif H.kinds[b] == "ctx":
                        P.copy(rc[:, 2:2 + n], ps[:, :n], eng="act")
                    else:
                        P.ts(rl[:, 0:2], ps[:, 0:2], lay["halo_valid"][:, 0:1], None, ALU.mult)
                        P.ts(rl[:, TL + 2:TL + 4], ps[:, 2:4], lay["halo_valid"][:, 1:2], None, ALU.mult)
                for (raw, T_, o_) in (((rl, TL, 0),) if skip_ctx else ((rl, TL, 0), (rc, TC, TL))):
                    a = ac[:, 0:T_]
                    P.act(a, raw[:, 0:T_], AF.Identity, bias=lay["conv_b"][:, ch:ch + 1], scale=lay["conv_w"][:, 0, ch:ch + 1])
                    for o in range(1, 5):
                        P.stt(a, raw[:, o:o + T_], lay["conv_w"][:, o, ch:ch + 1], a, ALU.mult, ALU.add,
                              eng=("dve" if o % 2 else "pool") if False else "dve")
                    if ch < 16:
                        P.act(xs[ci % 4][:, o_:o_ + T_], a, AF.Silu)
                    else:
                        e = xs[ci % 4]
                        P.act(e[:, o_:o_ + T_], a, AF.Silu)
                        which = "BT" if ch < 20 else "CT"
                        P.dma(sp[which][(ch - 16) % 4, :, o_:o_ + T_], e[:, o_:o_ + T_])
                if ch < 16 and ci % 4 == 3:
                    for tcn in range(NCH_L if skip_ctx else NCH):
                        psb = P.next_ps()
                        pv = psb.v().m(lambda a: a.bitcast(BF16))
                        for q4 in range(4):
                            P.tr(pv[:, q4 * 128:(q4 + 1) * 128], xs[q4][:, tcn * 128:(tcn + 1) * 128], env.cm("ident"))
                        xo = xtm[nx % 2]
                        nx += 1
                        P.copy(xo.v(), pv[:, 0:512], eng=("act" if nx % 2 else "dve"))
                        P.dma(sp["x"][tcn * 128:(tcn + 1) * 128, (ch - 3) * 128:(ch + 1) * 128], xo.v())
            P.barrier()

    for grp in groups:
        if grp in ("ret", "gla"):
            do_qk(grp)
        elif grp == "v":
            do_tm("ret_v", h_copy(sp["v"][0], 0))
            do_tm("gla_v", h_copy(sp["v"][1], 0))
        elif grp == "gates":
            do_tm("ret_g", h_copy(sp["gate"], 0, AF.Silu))
            do_tm("gla_r", h_copy(sp["gate"], 1024, AF.Silu))
            do_tm("ssd_z", h_copy(sp["gate"], 2048, AF.Silu))
        elif grp == "dt":
            do_tm("ssd_dt", h_dt, width=64)
        elif grp == "xB":
            do_xbc(list(range(0, 20)))
        elif grp == "xBC":
            do_xbc(list(range(0, 24)))
        elif grp == "merge":
            c0, sz = COLS["merge"]
            for cc in range(0, sz, 512):
                w = loadw(c0 + cc, 512)
                for q4 in range(4):
                    for b in tokblocks:
                        n = H.sizes[b]
                        ps = fm_mm(w, q4 * 128, 128, b)
                        e = evt[ev["n"] % 4]
                        ev["n"] += 1
                        P.act(e[:, :n], ps[:, :n], AF.Sigmoid)
                        P.dma(sp["sg"][cc // 128 + q4, :, spoff[b]:spoff[b] + n], e[:, :n])
    P.barrier()


def bc(v, axis, shape):
    return v.m(lambda a: a.unsqueeze(axis).to_broadcast(list(shape)))


def r3(v, pat, **kw):
    return v.m(lambda a: a.rearrange(pat, **kw))


class SweepBufs:
    def __init__(self, P, do_y, with_post):
        st = P.stack

        def mk(shape, dt, n, name):
            t = st.enter_context(P.sb_ctx([128, n] + list(shape), dt, name))
            return [Buf(t[:, i]) for i in range(n)]
        self.kT = [mk([4, 128], BF16, 2, "kT0"), mk([4, 128], BF16, 2, "kT1")]
        self.v = [mk([1024], BF16, 2, "v0"), mk([1024], BF16, 2, "v1")]
        self.eG = [mk([4], F32, 2, "eG0"), mk([4], F32, 2, "eG1")]
        self.BT = mk([4, 128], BF16, 2, "BT")
        self.x = mk([2048], BF16, 2, "xtm")
        self.a = mk([64], F32, 2, "a")
        self.dt = mk([64], F32, 2, "dt")
        self.ktm = mk([512], BF16, 2, "ktm")
        self.btm = mk([512], BF16, 1, "btm")
        self.eT = mk([96], F32, 2, "eT")
        self.wv = mk([32], F32, 1, "wv")
        self.vs = mk([2048], BF16, 1, "vs")
        if do_y:
            self.qT = [mk([4, 128], BF16, 2, "qT0"), mk([4, 128], BF16, 2, "qT1")]
            self.CT = mk([4, 128], BF16, 2, "CT")
            self.attT = mk([4, 128], BF16, 2, "attT")
            self.vdt = mk([2048], BF16, 1, "vdt")
            self.cbm = mk([4, 128], BF16, 1, "cbm")
            self.rhsD = mk([32, 128], BF16, 1, "rhsD")
            self.L = mk([4, 128], BF16, 4, "L")
            self.att = mk([32, 128], BF16, 1, "att")
            self.tmp = mk([512], F32, 2, "tmp")
            self.yall = mk([4096], F32, 2, "yall")
            self.yb = mk([4096], BF16, 1, "yb")
        if with_post:
            self.gate = mk([4096], BF16, 2, "gate")
            self.ss = mk([12], F32, 2, "ss")
            self.junk = mk([512], F32, 1, "junk")
            self.ygate = mk([4096], BF16, 1, "ygate")
            self.ygT = mk([32, 128], BF16, 1, "ygT")


def sweep(P, env, sp, lay, d, chain, S, SB, do_y, B, mode, Dt=None):
    msk_f32 = env.cm("mle" if d == 0 else "mge", bf=False)
    msk_bf = env.cm("mle" if d == 0 else "mge")
    sl_f32 = env.cm("sgt" if d == 0 else "slt", bf=False)
    sl_bf = env.cm("sgt" if d == 0 else "slt")
    ident = env.cm("ident")
    post = mode == "post"

    def bfview(psb):
        return psb.v().m(lambda ap: ap.bitcast(BF16))

    def front(it, c):
        par = it % 2
        t0 = c * 128
        kT = [B.kT[m][par] for m in range(2)]
        v = [B.v[m][par] for m in range(2)]
        eG = [B.eG[m][par] for m in range(2)]
        BT, x, a, dt = B.BT[par], B.x[par], B.a[par], B.dt[par]
        for m in range(2):
            P.dma(kT[m].v(), View(sp["qk"], sp["qk"].ap[m, :, 2 * d + 1, :, t0:t0 + 128].rearrange("h p t -> p h t")))
            P.dma(v[m].v(), sp["v"][m, t0:t0 + 128, :])
            P.dma(eG[m].v(), sp["eG"][m * 2 + d, :, c, :])
        P.dma(BT.v(), View(sp["BT"], sp["BT"].ap[:, :, t0:t0 + 128].rearrange("g p t -> p g t")))
        P.dma(x.v(), sp["x"][t0:t0 + 128, :])
        P.dma(a.v(), sp["a"][t0:t0 + 128, :])
        P.dma(dt.v(), sp["dt"][t0:t0 + 128, :])
        if do_y:
            qT = [B.qT[m][par] for m in range(2)]
            CT = B.CT[par]
            for m in range(2):
                P.dma(qT[m].v(), View(sp["qk"], sp["qk"].ap[m, :, 2 * d, :, t0:t0 + 128].rearrange("h p t -> p h t")))
            P.dma(CT.v(), View(sp["CT"], sp["CT"].ap[:, :, t0:t0 + 128].rearrange("g p t -> p g t")))
            yall = B.yall[par]
        if post:
            yb, gate = B.yb[0], B.gate[par]
            P.dma(yb.v(), sp["yb"][t0:t0 + 128, :])
            P.dma(gate.v(), sp["gate"][t0:t0 + 128, :])
        a_d = a[:, d * 32:(d + 1) * 32]
        dt_d = dt[:, d * 32:(d + 1) * 32]
        x3 = r3(x.v(), "p (h e) -> p h e", e=64)
        ktm = [B.ktm[0], B.ktm[1]]
        btm = B.btm[0]
        for m in range(2):
            pv = bfview(P.next_ps())
            for h in range(4):
                P.tr(pv[:, h * 128:(h + 1) * 128], kT[m][:, h, :], ident)
            P.copy(ktm[m].v(), pv[:, 0:512], eng="act")
        pv = bfview(P.next_ps())
        for g in range(4):
            P.tr(pv[:, g * 128:(g + 1) * 128], BT[:, g, :], ident)
        P.copy(btm.v(), pv[:, 0:512], eng="act")
        pss = P.next_ps()
        P.mm(pss[:, 0:32], env.ones_f.v(), a_d)
        P.mm(pss[:, 32:64], sl_f32, a_d)
        if do_y:
            P.mm(pss[:, 64:96], msk_f32, a_d)
        ne = 96 if do_y else 64
        eT = B.eT[par]
        P.act(eT[:, 0:ne], pss[:, 0:ne], AF.Exp)
        if Dt is not None:
            P.tt(Dt[2].v(), Dt[2].v(), pss[:, 0:32], ALU.add)
        if do_y:
            attT = [B.attT[0], B.attT[1]]
            for m in range(2):
                aps = P.next_ps()
                for h in range(4):
                    P.mm(aps[:, h * 128:(h + 1) * 128], kT[m][:, h, :], qT[m][:, h, :])
                P.tt(attT[m].v(), r3(aps.v(), "p (h t) -> p h t", h=4), bc(msk_f32, 1, [128, 4, 128]), ALU.mult)
        wv = B.wv[0]
        P.tt(wv.v(), eT[:, 32:64], dt_d, ALU.mult)
        vs = B.vs[0]
        P.tt(r3(vs.v(), "p (h e) -> p h e", e=64), x3, bc(wv.v(), 2, [128, 32, 64]), ALU.mult)
        cnt = {"e": 0}

        def retgla_y(m):
            if not do_y:
                return
            for hp in range(2):
                yp = P.next_ps()
                for hh in range(2):
                    h = hp * 2 + hh
                    o = yp[:, hh * 256:(hh + 1) * 256]
                    if post:
                        P.mm(o, ident, yb[:, m * 1024 + h * 256:m * 1024 + (h + 1) * 256], start=True, stop=False)
                    P.mm(o, attT[m][:, h, :], v[m][:, h * 256:(h + 1) * 256], start=(not post), stop=False)
                    P.mm(o, qT[m][:, h, :], SB[m][:, h * 256:(h + 1) * 256], start=False, stop=True)
                cs = slice(m * 1024 + hp * 512, m * 1024 + (hp + 1) * 512)
                cnt["e"] += 1
                P.copy(yall[:, cs], yp.v(), eng=("act" if cnt["e"] % 2 else "dve"))

        def retgla_state(m):
            for hp in range(2):
                ps2 = P.next_ps()
                for hh in range(2):
                    h = hp * 2 + hh
                    P.mm(ps2[:, hh * 256:(hh + 1) * 256], ktm[m][:, h * 128:(h + 1) * 128], v[m][:, h * 256:(h + 1) * 256])
                for hh in range(2):
                    h = hp * 2 + hh
                    hc = slice(h * 256, (h + 1) * 256)
                    P.act(S[m][:, hc], S[m][:, hc], AF.Copy, scale=eG[m][:, h:h + 1])
                    P.stt(S[m][:, hc], ps2[:, hh * 256:(hh + 1) * 256], eG[m][:, h:h + 1], S[m][:, hc], ALU.mult, ALU.add)
            if do_y:
                P.copy(SB[m].v(), S[m].v(), eng="act")
            if Dt is not None:
                P.tt(Dt[m].v(), Dt[m].v(), eG[m].v(), ALU.mult)

        if do_y:
            vdt = B.vdt[0]
            vdt3 = r3(vdt.v(), "p (h e) -> p h e", e=64)
            P.tt(vdt3, x3, bc(dt_d, 2, [128, 32, 64]), ALU.mult)
            cb = P.next_ps()
            for g in range(4):
                P.mm(cb[:, g * 128:(g + 1) * 128], BT[:, g, :], CT[:, g, :])
            cbm = B.cbm[0]
            P.tt(cbm.v(), r3(cb.v(), "p (g t) -> p g t", g=4), bc(msk_f32, 1, [128, 4, 128]), ALU.mult)
            rhsD = B.rhsD[0]
            P.tt(rhsD.v(), bc(msk_bf, 1, [128, 32, 128]), bc(a_d, 2, [128, 32, 128]), ALU.mult)
            retgla_y(0)
            att = B.att[0]
            for g in range(4):
                for half in range(2):
                    dps = P.next_ps()
                    hsl = slice(g * 8 + half * 4, g * 8 + half * 4 + 4)
                    P.mm(dps.v(), sl_bf, r3(rhsD[:, hsl, :], "p h t -> p (h t)"))
                    L = B.L[(g * 2 + half) % 4]
                    P.act(r3(L.v(), "p h t -> p (h t)"), dps.v(), AF.Exp)
                    P.tt(att[:, hsl, :], L.v(), bc(cbm[:, g, :], 1, [128, 4, 128]), ALU.mult)
                if g == 0:
                    retgla_state(0)
                elif g == 1:
                    retgla_y(1)
                elif g == 2:
                    retgla_state(1)
            for g in range(4):
                yp = P.next_ps()
                cs = slice(2048 + g * 512, 2048 + (g + 1) * 512)
                if post:
                    P.mm(yp.v(), ident, yb[:, cs], start=True, stop=False)
                for h in range(8):
                    hs = slice(h * 64, (h + 1) * 64)
                    if post:
                        P.mm(yp[:, hs], att[:, g * 8 + h, :], vdt3[:, g * 8 + h, :], start=False, stop=False)
                        P.mm(yp[:, hs], lay["dI"][:, g * 8 + h, :], x3[:, g * 8 + h, :], start=False, stop=(h == 7))
                    else:
                        P.mm(yp[:, hs], att[:, g * 8 + h, :], vdt3[:, g * 8 + h, :])
                yi = P.next_ps()
                P.mm(yi.v(), CT[:, g, :], SB[2][:, g * 512:(g + 1) * 512])
                tmp = B.tmp[g % 2]
                P.tt(r3(tmp.v(), "p (h e) -> p h e", e=64), r3(yi.v(), "p (h e) -> p h e", e=64),
                     bc(eT[:, 64 + g * 8:64 + (g + 1) * 8], 2, [128, 8, 64]), ALU.mult)
                P.tt(yall[:, cs], tmp.v(), yp.v(), ALU.add)
        else:
            retgla_state(0)
            retgla_state(1)
        P.tt(r3(S[2].v(), "p (h e) -> p h e", e=64), r3(S[2].v(), "p (h e) -> p h e", e=64),
             bc(eT[:, 0:32], 2, [128, 32, 64]), ALU.mult)
        for g in range(4):
            ps = P.next_ps()
            P.mm(ps.v(), btm[:, g * 128:(g + 1) * 128], vs[:, g * 512:(g + 1) * 512])
            gc = slice(g * 512, (g + 1) * 512)
            P.tt(S[2][:, gc], S[2][:, gc], ps.v(), ALU.add)
        if do_y:
            P.copy(SB[2].v(), S[2].v(), eng="act")

    def back(it, c):
        par = it % 2
        t0 = c * 128
        yall = B.yall[par]
        if mode == "yb":
            ybo = B.yb[0]
            P.copy(ybo[:, 0:2048], yall[:, 0:2048], eng="act")
            P.copy(ybo[:, 2048:4096], yall[:, 2048:4096], eng="dve")
            P.dma(sp["yb"][t0:t0 + 128, :], ybo.v())
            return
        gate = B.gate[par]
        ss, junk, ygate = B.ss[par], B.junk[0], B.ygate[0]
        P.tt(yall[:, 2048:4096], yall[:, 2048:4096], gate[:, 2048:4096], ALU.mult)
        for h in range(8):
            P.act(junk[:, 0:256], yall[:, h * 256:(h + 1) * 256], AF.Square, accum=ss[:, h:h + 1])
        for g in range(4):
            P.act(junk[:, 0:512], yall[:, 2048 + g * 512:2048 + (g + 1) * 512], AF.Square, accum=ss[:, 8 + g:9 + g])
        P.ts(ss[:, 0:8], ss[:, 0:8], 1.0 / 256, EPS, ALU.mult, ALU.add)
        P.ts(ss[:, 8:12], ss[:, 8:12], 1.0 / 512, EPS, ALU.mult, ALU.add)
        P.act(ss.v(), ss.v(), AF.Ln)
        P.act(ss.v(), ss.v(), AF.Exp, scale=-0.5)
        for h in range(4):
            hc = slice(h * 256, (h + 1) * 256)
            P.stt(ygate[:, hc], yall[:, hc], ss[:, h:h + 1], gate[:, hc], ALU.mult, ALU.mult)
        for h in range(4):
            hc = slice(1024 + h * 256, 1024 + (h + 1) * 256)
            P.stt(junk[:, 256:512], yall[:, hc], ss[:, 4 + h:5 + h], gate[:, hc], ALU.mult, ALU.mult)
            P.tt(ygate[:, hc], junk[:, 256:512], lay["gla_g"].v(), ALU.mult)
        for g in range(4):
            gc = slice(2048 + g * 512, 2048 + (g + 1) * 512)
            P.stt(ygate[:, gc], yall[:, gc], ss[:, 8 + g:9 + g], lay["ssd_ng"][:, g * 512:(g + 1) * 512], ALU.mult, ALU.mult)
        ygT = B.ygT[0]
        for q in range(4):
            pv = bfview(P.next_ps())
            for k8 in range(8):
                k = q * 8 + k8
                P.tr(pv[:, k8 * 128:(k8 + 1) * 128], ygate[:, k * 128:(k + 1) * 128], ident)
            P.copy(r3(ygT[:, q * 8:(q + 1) * 8, :], "p k t -> p (k t)"), pv[:, 0:1024], eng=("act" if q % 2 else "dve"))
        P.dma(View(sp["ygT"], sp["ygT"].ap[:, :, t0:t0 + 128].rearrange("k p t -> p k t")), ygT.v())

    prev = None
    for it, c in enumerate(chain):
        front(it, c)
        if mode != "state":
            if prev is not None:
                back(*prev)
            prev = (it, c)
        yield it
    if prev is not None:
        back(*prev)


def phase_c(P, env, sp, X, spoff, gate5, wbL, woL, blks):
    with scope(P):
        yg_t = P.stack.enter_context(P.sb_ctx([128, 32, 512], BF16, "ygblk"))
        sg_t = P.stack.enter_context(P.sb_ctx([128, 24, 512], BF16, "sgblk"))
        wb_t = P.stack.enter_context(P.sb_ctx([128, 2, 32, 128], BF16, "wbf"))
        wo_t = P.stack.enter_context(P.sb_ctx([128, 2, 8, 128], BF16, "wof"))
        mg_t = P.stack.enter_context(P.sb_ctx([128, 8, 512], BF16, "mg"))
        t_t = P.stack.enter_context(P.sb_ctx([128, 3, 512], F32, "t123"))
        yg, sg, mg = Buf(yg_t[:]), Buf(sg_t[:]), Buf(mg_t[:])
        wbf = [Buf(wb_t[:, i]) for i in range(2)]
        wof = [Buf(wo_t[:, i]) for i in range(2)]
        t = [Buf(t_t[:, i]) for i in range(3)]
        nw = 0
        for b in blks:
            n = X.sizes[b]
            o = spoff[b]
            P.dma(yg[:, :, :n], View(sp["ygT"], sp["ygT"].ap[:, :, o:o + n].rearrange("k p t -> p k t")))
            P.dma(sg[:, :, :n], View(sp["sg"], sp["sg"].ap[:, :, o:o + n].rearrange("k p t -> p k t")))
            for f in range(KC):
                w = wbf[nw % 2]
                nw += 1
                P.dma(w.v(), wbL[f], q="pool")
                for i, (k0, k1) in enumerate(((0, 8), (8, 16), (16, 32))):
                    ps = P.next_ps()
                    for k in range(k0, k1):
                        P.mm(ps[:, :n], w[:, k, :], yg[:, k, :n], start=(k == k0), stop=(k == k1 - 1))
                    P.tt(t[i][:, :n], ps[:, :n], sg[:, i * 8 + f, :n], ALU.mult, eng=("dve" if i != 1 else "pool") if False else "dve")
                P.tt(t[0][:, :n], t[0][:, :n], t[1][:, :n], ALU.add)
                P.tt(mg[:, f, :n], t[0][:, :n], t[2][:, :n], ALU.add)
            hg = gate5[X.kinds[b]]
            for f in range(KC):
                w = wof[f % 2]
                P.dma(w.v(), woL[f], q="pool")
                ps = P.next_ps()
                for k in range(KC):
                    P.mm(ps[:, :n], w[:, k, :], mg[:, k, :n], start=(k == 0), stop=(k == KC - 1))
                P.stt(X.b[b][:, f, :], ps[:, :n], hg[:, f:f + 1], X.b[b][:, f, :], ALU.mult, ALU.add)


def load_small(P, ins, names):
    lay = {}
    for nm in names:
        b = ins[nm]
        shape = list(b.ap.shape)
        t = P.sb(shape, F32, "l_" + nm)
        P.dma(t.v(), b.v())
        lay[nm] = t
    return lay


def derive_layer(P, env, lay, need_post):
    lg = P.sb([128, 8], F32, "lg")
    t1 = P.sb([128, 8], F32, "lgt1")
    t2 = P.sb([128, 8], F32, "lgt2")
    P.act(t1.v(), lay["ret_logit"].v(), AF.Abs)
    P.act(t1.v(), t1.v(), AF.Exp, scale=-1.0)
    P.act(t1.v(), t1.v(), AF.Ln, bias=1.0)
    P.ts(t2.v(), lay["ret_logit"].v(), 0.0, None, ALU.min)
    P.tt(lg.v(), t2.v(), t1.v(), ALU.subtract)
    nlg = P.sb([128, 8], F32, "nlg")
    P.ts(nlg.v(), lg.v(), -1.0, None, ALU.mult)
    tab = P.sb([128, 2, 2, 4, 128], F32, "ret_tab")
    for d in range(2):
        row = env.row_up if d == 0 else env.row_dn
        for h in range(4):
            P.act(tab[:, d, 0, h, :], row, AF.Exp, scale=lg[:, d * 4 + h:d * 4 + h + 1])
            P.act(tab[:, d, 1, h, :], row, AF.Exp, scale=nlg[:, d * 4 + h:d * 4 + h + 1])
    lay["ret_tab"] = tab
    eg = P.sb([128, 2, 4], F32, "ret_eg")
    P.act(eg.v().m(lambda a: a.rearrange("p d h -> p (d h)")), lg.v(), AF.Exp, scale=128.0)
    lay["ret_eg"] = eg
    wa2b = P.sb([16, 2, 512], BF16, "wa2b")
    P.copy(wa2b.v(), lay["wa2"].v())
    lay["wa2"] = wa2b
    A = P.sb([128, 64], F32, "ssdA")
    P.act(A.v(), lay["a_log"].v(), AF.Exp)
    P.ts(A.v(), A.v(), -1.0, None, ALU.mult)
    lay["ssd_A"] = A


SMALL_COMMON = ["mod_lat", "mod_ctx", "norm_g", "ret_logit", "wa2", "gla_ba", "conv_w", "conv_b", "dt_bias", "a_log",
                "halo_valid"]
SMALL_SHAPES = {"mod_lat": [128, 9, 8], "mod_ctx": [128, 9, 8], "norm_g": [128, 3, 8], "ret_logit": [128, 8],
                "wa2": [16, 2, 512], "gla_ba": [128, 2, 4], "conv_w": [128, 5, 24], "conv_b": [128, 24],
                "dt_bias": [128, 64], "a_log": [128, 64], "halo_valid": [128, 2],
                "gla_g": [128, 256], "ssd_d": [128, 32], "ssd_ng": [128, 2048], "final_g": [128, 8]}
SIZES = [512, 512, 512, 512, TC, 4]
KINDS = ["lat", "lat", "lat", "lat", "ctx", "halo"]
TIN = TL + TC + 4


def make_spill(P, full):
    sp = {}
    sp["qk"] = P.dram("sp_qk", [2, 4, 4, 128, TT], BF16)
    sp["eG"] = P.dram("sp_eG", [4, 128, NCH, 4], F32)
    sp["v"] = P.dram("sp_v", [2, TT, 1024], BF16)
    sp["BT"] = P.dram("sp_BT", [4, 128, TT], BF16)
    sp["x"] = P.dram("sp_x", [TT, 2048], BF16)
    sp["a"] = P.dram("sp_a", [TT, 64], F32)
    sp["dt"] = P.dram("sp_dt", [TT, 64], F32)
    if full:
        sp["CT"] = P.dram("sp_CT", [4, 128, TT], BF16)
        sp["gate"] = P.dram("sp_gate", [TT, 4096], BF16)
        sp["sg"] = P.dram("sp_sg", [24, 128, TT], BF16)
        sp["yb"] = P.dram("sp_yb", [TT, 4096], BF16)
        sp["ygT"] = P.dram("sp_ygT", [32, 128, TT], BF16)
    return sp


def alloc_states(P):
    S = [P.sb([128, 1024], F32, "S_ret"), P.sb([128, 1024], F32, "S_gla"), P.sb([128, 2048], F32, "S_ssd")]
    SB = [P.sb([128, 1024], BF16, "SB_ret"), P.sb([128, 1024], BF16, "SB_gla"), P.sb([128, 2048], BF16, "SB_ssd")]
    return S, SB


def zero_states(P, S, SB):
    for i in range(3):
        P.memset(S[i].v(), 0.0, eng="pool")
        P.memset(SB[i].v(), 0.0, eng="pool")


def build_layer(phase, last=False):
    nc = bass.Bass("TRN2", target_bir_lowering=False)
    with contextlib.ExitStack() as stack:
        P = Prog(nc, stack)
        P.psum_banks()
        env = Env()
        ins = {}

        def inp(name, shape, dt=F32):
            ins[name] = P.dram(name, shape, dt, "ExternalInput")
            return ins[name]
        inp("cin", [128, 1024])
        inp("xin", [D, TIN])
        for nm in SMALL_COMMON:
            inp(nm, SMALL_SHAPES[nm])
        inp("rope_in", [128, 2, TL])
        inp("w_inL", [128, KC, DIN])
        if phase == 1:
            inp("wiL", [NFF, 128, KC, 256])
            inp("woL", [128, NFF, D])
            x1out = P.dram("x1T", [D, TT], F32, "ExternalOutput")
            Eout = P.dram("Eout", [2, 128, 4096], F32, "ExternalOutput")
            Dout = P.dram("Dout", [2, 128, 40], F32, "ExternalOutput")
        else:
            for nm in ("gla_g", "ssd_d", "ssd_ng", "final_g"):
                inp(nm, SMALL_SHAPES[nm])
            inp("wiL", [NFF, 128, KC, 256])
            inp("woL", [128, NFF, D])
            inp("wbL", [KC, 128, 32, 128])
            inp("w_outL", [KC, 128, KC, 128])
            inp("slotE", [2, 7, 128, 4096])
            inp("slotD", [2, 7, 128, 40])
            xout = P.dram("xoutT", [D, TT], F32, "ExternalOutput")
        load_consts(P, env, ins["cin"])
        lay = load_small(P, ins, SMALL_COMMON + ([] if phase == 1 else ["gla_g", "ssd_d", "ssd_ng", "final_g"]))
        lay["rope_in"] = ins["rope_in"]
        derive_layer(P, env, lay, phase == 2)
        modT = {"lat": lay["mod_lat"], "ctx": lay["mod_ctx"]}
        sp = make_spill(P, phase == 2)
        xv = ins["xin"].ap.rearrange("(k p) t -> p k t", p=128)
        offs = np.concatenate([[0], np.cumsum(SIZES)]).astype(int).tolist()
        spoff = {b: offs[b] for b in range(5)}

        def wi_view(buf):
            return buf

        with scope(P):
            H = Blocks(P, SIZES, KINDS, "hT", BF16)
            scr = (P.sb([128, KC, 512], BF16, "sq"), P.sb([128, KC, 512], F32, "tmpn"), P.sb([128, 512], F32, "rstd"))
            if phase == 1:
                with scope(P):
                    X = Blocks(P, SIZES, KINDS, "xTs")
                    for b in range(6):
                        P.dma(X.b[b].v(), View(ins["xin"], xv[:, :, offs[b]:offs[b] + SIZES[b]]))
                    m0 = prep_mod(P, modT, lay["norm_g"], 0)
                    m0["halo"] = m0["lat"]
                    for b in range(6):
                        gs, sh, hg = m0[KINDS[b]]
                        norm_mod(P, X.b[b], H.b[b], SIZES[b], gs, sh, env.ones_bf, scr)
                    ffn_tiled(P, X, H, m0, ins["wiL"], ins["woL"])
                    ov = x1out.ap.rearrange("(k p) t -> p k t", p=128)
                    for b in range(5):
                        P.dma(View(x1out, ov[:, :, offs[b]:offs[b] + SIZES[b]]), X.b[b].v())
                    m1 = prep_mod(P, modT, lay["norm_g"], 1)
                    m1["halo"] = m1["lat"]
                    for b in range(6):
                        gs, sh, hg = m1[KINDS[b]]
                        norm_mod(P, X.b[b], H.b[b], SIZES[b], gs, sh, env.ones_bf, scr)
            else:
                with scope(P):
                    xt_t = P.stack.enter_context(P.sb_ctx([128, 2, KC, 512], F32, "xtmp"))
                    xt = [Buf(xt_t[:, i]) for i in range(2)]
                    m1 = prep_mod(P, modT, lay["norm_g"], 1)
                    m1["halo"] = m1["lat"]
                    for b in range(6):
                        n = SIZES[b]
                        xb = xt[b % 2]
                        P.dma(xb[:, :, :n], View(ins["xin"], xv[:, :, offs[b]:offs[b] + n]))
                        gs, sh, hg = m1[KINDS[b]]
                        norm_mod(P, _sub(xb, n), H.b[b], n, gs, sh, env.ones_bf, scr)
            groups = ["ret", "gla", "v", "dt", "xB"] if phase == 1 else ["ret", "gla", "v", "gates", "dt", "xBC", "merge"]
            lay["halo_valid"] = lay["halo_valid"]
            inproj(P, env, H, ins["w_inL"], sp, lay, groups, do_q=(phase == 2), skip_ctx=(phase == 1))
        with scope(P):
            S, SB = alloc_states(P)
            if phase == 1:
                S2, SB2 = alloc_states(P)
                Ss, SBs = [S, S2], [SB, SB2]
                Bs = [SweepBufs(P, False, False), SweepBufs(P, False, False)]
                Dts = [[P.sb([128, 4], F32, "Dt0"), P.sb([128, 4], F32, "Dt1"), P.sb([128, 32], F32, "Dt2")] for _ in range(2)]
                gens = []
                for d in range(2):
                    zero_states(P, Ss[d], SBs[d])
                    P.memset(Dts[d][0].v(), 1.0)
                    P.memset(Dts[d][1].v(), 1.0)
                    P.memset(Dts[d][2].v(), 0.0)
                    chain = list(range(NCH_L)) if d == 0 else list(range(NCH_L - 1, -1, -1))
                    gens.append(sweep(P, env, sp, lay, d, chain, Ss[d], SBs[d], False, Bs[d], "state", Dts[d]))
                for _ in zip(*gens):
                    pass
                for g_ in gens:
                    for _ in g_:
                        pass
                for d in range(2):
                    S, SB, Dt = Ss[d], SBs[d], Dts[d]
                    P.dma(Eout[d, :, 0:1024], S[0].v())
                    P.dma(Eout[d, :, 1024:2048], S[1].v())
                    P.dma(Eout[d, :, 2048:4096], S[2].v())
                    dd = P.sb([128, 40], F32, "dd")
                    P.copy(dd[:, 0:4], Dt[0].v())
                    P.copy(dd[:, 4:8], Dt[1].v())
                    P.act(dd[:, 8:40], Dt[2].v(), AF.Exp)
                    P.dma(Dout[d], dd.v())
            else:
                B = SweepBufs(P, True, True)
                dI = P.sb([128, 32, 128], BF16, "dI")
                for h in range(32):
                    P.ts(dI[:, h, :], env.cm("ident", bf=False), lay["ssd_d"][:, h:h + 1], None, ALU.mult)
                lay["dI"] = dI
                et = B.yall[0]
                dtile = P.sb([128, 40], F32, "slotD_t")
                for d in (1, 0):
                    zero_states(P, S, SB)
                    cchain = [NCH_L, NCH_L + 1] if d == 0 else [NCH_L + 1, NCH_L]
                    for _ in sweep(P, env, sp, lay, d, cchain, S, SB, True, B, "post" if d == 0 else "yb"):
                        pass
                    for s in range(7):
                        P.dma(et.v(), ins["slotE"][d, s])
                        P.dma(dtile.v(), ins["slotD"][d, s])
                        for h in range(4):
                            hc = slice(h * 256, (h + 1) * 256)
                            P.stt(S[0][:, hc], S[0][:, hc], dtile[:, h:h + 1], et[:, hc], ALU.mult, ALU.add)
                            hc2 = slice(1024 + h * 256, 1024 + (h + 1) * 256)
                            P.stt(S[1][:, hc], S[1][:, hc], dtile[:, 4 + h:5 + h], et[:, hc2], ALU.mult, ALU.add)
                        for h in range(32):
                            hc = slice(h * 64, (h + 1) * 64)
                            hc2 = slice(2048 + h * 64, 2048 + (h + 1) * 64)
                            P.stt(S[2][:, hc], S[2][:, hc], dtile[:, 8 + h:9 + h], et[:, hc2], ALU.mult, ALU.add)
                    for i in range(3):
                        P.copy(SB[i].v(), S[i].v(), eng="act")
                    chain = list(range(NCH_L)) if d == 0 else list(range(NCH_L - 1, -1, -1))
                    for _ in sweep(P, env, sp, lay, d, chain, S, SB, True, B, "post" if d == 0 else "yb"):
                        pass
        if phase == 2:
            with scope(P):
                X = Blocks(P, SIZES[:5], KINDS[:5], "xTs")
                for b in range(5):
                    P.dma(X.b[b].v(), View(ins["xin"], xv[:, :, offs[b]:offs[b] + SIZES[b]]))
                g5 = {}
                for kind in ("lat", "ctx"):
                    g5[kind] = modT[kind][:, 5, :]
                blks = [0, 1, 2, 3] + ([] if last else [4])
                phase_c(P, env, sp, X, spoff, g5, ins["wbL"], ins["w_outL"], blks)
                with scope(P):
                    H2 = Blocks(P, SIZES[:5], KINDS[:5], "h2T", BF16)
                    scr = (P.sb([128, KC, 512], BF16, "sq"), P.sb([128, KC, 512], F32, "tmpn"), P.sb([128, 512], F32, "rstd"))
                    m2 = prep_mod(P, modT, lay["norm_g"], 2)
                    for b in range(5):
                        gs, sh, hg = m2[KINDS[b]]
                        norm_mod(P, X.b[b], H2.b[b], SIZES[b], gs, sh, env.ones_bf, scr)
                    ffn_tiled(P, X, H2, m2, ins["wiL"], ins["woL"], G=4)
                ov = xout.ap.rearrange("(k p) t -> p k t", p=128)
                if last:
                    with scope(P):
                        scr = (P.sb([128, KC, 512], BF16, "sq"), P.sb([128, KC, 512], F32, "tmpn"), P.sb([128, 512], F32, "rstd"))
                        zt = P.sb([128, KC], F32, "zeros8")
                        P.memset(zt.v(), 0.0)
                        o_t = P.stack.enter_context(P.sb_ctx([128, 2, KC, 512], F32, "fo"))
                        ob = [Buf(o_t[:, i]) for i in range(2)]
                        for b in range(4):
                            norm_mod(P, X.b[b], ob[b % 2], 512, lay["final_g"], zt, env.ones_bf, scr)
                            P.dma(View(xout, ov[:, :, offs[b]:offs[b] + 512]), ob[b % 2].v())
                        P.dma(View(xout, ov[:, :, offs[4]:offs[4] + TC]), X.b[4].v())
                else:
                    for b in range(5):
                        P.dma(View(xout, ov[:, :, offs[b]:offs[b] + SIZES[b]]), X.b[b].v())
        P.emit()
    return nc


class _sub:
    def __init__(self, buf, n):
        self.buf = buf
        self.n = n

    def v(self):
        return View(self.buf, self.buf.ap[:, :, :self.n])

    def __getitem__(self, idx):
        return View(self.buf, self.buf.ap[:, :, :self.n][idx])


def ffn_tiled(P, X, H, mods, wiL, woL, G=6):
    nblk = len(X.sizes)
    T = X.T
    with scope(P):
        actT_t = P.stack.enter_context(P.sb_ctx([128, G, T], BF16, "actT"))
        wi_t = P.stack.enter_context(P.sb_ctx([128, 2, KC, 256], BF16, "wi_t"))
        wo_t = P.stack.enter_context(P.sb_ctx([128, G, D], BF16, "wo_t"))
        sil_t = P.stack.enter_context(P.sb_ctx([128, 2, 512], F32, "sil"))
        actb = [[Buf(actT_t[:, g, X.offs[b]:X.offs[b] + X.sizes[b]]) for b in range(nblk)] for g in range(G)]
        wib = [Buf(wi_t[:, i]) for i in range(2)]
        wob = Buf(wo_t[:])
        silb = [Buf(sil_t[:, i]) for i in range(2)]
        ns = 0
        nw = 0
        for g0 in range(0, NFF, G):
            gn = min(G, NFF - g0)
            for gi in range(gn):
                j = g0 + gi
                w = wib[nw % 2]
                nw += 1
                P.dma(w.v(), wiL[j], q="pool")
                for b in range(nblk):
                    n = X.sizes[b]
                    pa = P.next_ps()
                    pu = P.next_ps()
                    for k in range(KC):
                        P.mm(pa[:, :n], w[:, k, 0:128], H.b[b][:, k, :], start=(k == 0), stop=(k == KC - 1))
                    for k in range(KC):
                        P.mm(pu[:, :n], w[:, k, 128:256], H.b[b][:, k, :], start=(k == 0), stop=(k == KC - 1))
                    s = silb[ns % 2]
                    ns += 1
                    P.act(s[:, :n], pa[:, :n], AF.Silu)
                    P.tt(actb[gi][b].v(), s[:, :n], pu[:, :n], ALU.mult)
            P.dma(wob[:, 0:gn, :], woL[:, g0:g0 + gn, :], q="pool")
            for b in range(nblk):
                n = X.sizes[b]
                hg = mods[X.kinds[b]][2]
                for f in range(KC):
                    ps = P.next_ps()
                    for gi in range(gn):
                        P.mm(ps[:, :n], wob[:, gi, f * 128:(f + 1) * 128], actb[gi][b].v(), start=(gi == 0), stop=(gi == gn - 1))
                    P.stt(X.b[b][:, f, :], ps[:, :n], hg[:, f:f + 1], X.b[b][:, f, :], ALU.mult, ALU.add)


def build_p0():
    nc = bass.Bass("TRN2", target_bir_lowering=False)
    NCOL = 4608
    with contextlib.ExitStack() as stack:
        P = Prog(nc, stack)
        P.psum_banks()
        cc = P.dram("ccT", [128, KC, 2], F32, "ExternalInput")
        W = P.dram("adaW", [128, KC, NCOL], F32, "ExternalInput")
        bias = P.dram("adab", [2, NCOL], F32, "ExternalInput")
        out = P.dram("modout", [2, NCOL], F32, "ExternalOutput")
        s = P.sb([128, KC, 2], F32, "s")
        P.dma(s.v(), cc.v())
        P.act(s.v(), s.v(), AF.Silu)
        bt = P.sb([2, NCOL], F32, "bt")
        P.dma(bt.v(), bias.v())
        ot = P.sb([2, NCOL], F32, "ot")
        w_t = P.stack.enter_context(P.sb_ctx([128, 2, KC, 512], F32, "w0"))
        wb = [Buf(w_t[:, i]) for i in range(2)]
        for j in range(NCOL // 512):
            w = wb[j % 2]
            P.dma(w.v(), W[:, :, j * 512:(j + 1) * 512])
            ps = P.next_ps()
            for k in range(KC):
                P.mm(ps[0:2, :], s[:, k, :], w[:, k, :], start=(k == 0), stop=(k == KC - 1))
            P.tt(ot[:, j * 512:(j + 1) * 512], ps[0:2, :], bt[:, j * 512:(j + 1) * 512], ALU.add)
        P.dma(out.v(), ot.v())
        P.emit()
    return nc


_PROGS = {}


def get_prog(key):
    if key not in _PROGS:
        if key == "p0":
            _PROGS[key] = build_p0()
        elif key == "p1":
            _PROGS[key] = build_layer(1)
        elif key == "p2":
            _PROGS[key] = build_layer(2, last=False)
        else:
            _PROGS[key] = build_layer(2, last=True)
    return _PROGS[key]


def rep(v, n=128):
    v = np.asarray(v, np.float32).reshape(1, -1)
    return np.ascontiguousarray(np.broadcast_to(v, (n, v.shape[1])))


def rope_tables(core):
    t = core * TL + np.arange(TL)
    row = (t // 64).astype(np.float32)
    col = (t % 64).astype(np.float32)
    nf = 32
    freq = (np.float32(10000.0) ** (-np.arange(nf, dtype=np.float32) / np.float32(nf))).astype(np.float32)
    ang = np.concatenate([row[:, None] * freq[None, :], col[:, None] * freq[None, :]], axis=1).astype(np.float32)
    cos = np.cos(ang).astype(np.float32).T
    sin = np.sin(ang).astype(np.float32).T
    cos_tab = np.concatenate([cos, cos], axis=0)
    sin_tab = np.concatenate([-sin, sin], axis=0)
    return np.ascontiguousarray(np.stack([cos_tab, sin_tab], axis=1))


def run_model(inp, ncores, depth=4, run=None):
    from concourse.bass_utils import run_bass_kernel_spmd
    f32 = np.float32
    x = np.asarray(inp["x"], f32)[0]
    ctx = np.asarray(inp["ctx"], f32)[0]
    assert x.shape[0] == ncores * TL
    consts = host_consts()
    cc = np.stack([np.asarray(inp["c"], f32)[0], np.asarray(inp["c_ctx"], f32)], axis=1)
    ccT = np.ascontiguousarray(cc.reshape(KC, 128, 2).transpose(1, 0, 2))
    in_maps = []
    for c in range(8):
        l, half = (c // 2) % depth, c % 2
        Wl = np.asarray(inp["ada_w"][l], f32)[:, half * 4608:(half + 1) * 4608]
        in_maps.append({"ccT": ccT,
                        "adaW": np.ascontiguousarray(Wl.reshape(KC, 128, 4608).transpose(1, 0, 2)),
                        "adab": rep(np.asarray(inp["ada_b"][l], f32)[half * 4608:(half + 1) * 4608], 2)})
    res0 = run_bass_kernel_spmd(get_prog("p0"), in_maps, core_ids=list(range(8))).results
    mods = []
    for l in range(depth):
        m = np.concatenate([res0[2 * l]["modout"], res0[2 * l + 1]["modout"]], axis=1)
        mods.append([np.ascontiguousarray(m[s].reshape(9, KC, 128).transpose(2, 0, 1)) for s in range(2)])
    xT = [np.ascontiguousarray(x[c * TL:(c + 1) * TL].T) for c in range(ncores)]
    ctxT = np.ascontiguousarray(ctx.T)
    ropes = [rope_tables(c) for c in range(ncores)]
    zeros2 = np.zeros((D, 2), f32)

    def halos(xs):
        hs, vs = [], []
        for c in range(ncores):
            left = xs[c - 1][:, -2:] if c > 0 else zeros2
            right = xs[c + 1][:, :2] if c < ncores - 1 else zeros2
            hs.append(np.concatenate([left, right], axis=1))
            vs.append(rep(np.array([1.0 if c > 0 else 0.0, 1.0 if c < ncores - 1 else 0.0], f32)))
        return hs, vs

    for l in range(depth):
        last = l == depth - 1
        g = lambda k: np.asarray(inp[k][l], f32)
        common = {
            "cin": consts,
            "mod_lat": mods[l][0], "mod_ctx": mods[l][1],
            "norm_g": np.ascontiguousarray(g("norm_g").reshape(3, KC, 128).transpose(2, 0, 1)),
            "ret_logit": rep(g("ret_logit").reshape(-1)),
            "wa2": np.ascontiguousarray(g("gla_wa2").transpose(1, 0, 2)),
            "gla_ba": np.ascontiguousarray(g("gla_ba").reshape(2, 4, 128).transpose(2, 0, 1)),
            "conv_w": np.ascontiguousarray(g("conv_w").reshape(5, 24, 128).transpose(2, 0, 1)),
            "conv_b": np.ascontiguousarray(g("conv_b").reshape(24, 128).T),
            "dt_bias": rep(g("dt_bias").reshape(-1)),
            "a_log": rep(g("a_log").reshape(-1)),
            "w_inL": np.ascontiguousarray(g("w_in").reshape(KC, 128, DIN).transpose(1, 0, 2)),
        }

        def tile_ffn(wi, wo):
            wr = wi.reshape(KC, 128, 2 * DFF)
            a = wr[:, :, :DFF].reshape(KC, 128, NFF, 128).transpose(2, 1, 0, 3)
            u = wr[:, :, DFF:].reshape(KC, 128, NFF, 128).transpose(2, 1, 0, 3)
            return (np.ascontiguousarray(np.concatenate([a, u], axis=3)),
                    np.ascontiguousarray(wo.reshape(NFF, 128, D).transpose(1, 0, 2)))
        wiL, woL = tile_ffn(g("ffn1_wi"), g("ffn1_wo"))
        hs, vs = halos(xT)
        in_maps = []
        for c in range(ncores):
            m = dict(common)
            m.update({"xin": np.ascontiguousarray(np.concatenate([xT[c], ctxT, hs[c]], axis=1)),
                      "halo_valid": vs[c], "rope_in": ropes[c], "wiL": wiL, "woL": woL})
            in_maps.append(m)
        res1 = run_bass_kernel_spmd(get_prog("p1"), in_maps, core_ids=list(range(ncores))).results
        if run is not None:
            run["res1"] = res1
        x1 = [r["x1T"][:, :TL] for r in res1]
        ctx1 = res1[0]["x1T"][:, TL:]
        E = [r["Eout"] for r in res1]
        Dd = [r["Dout"] for r in res1]
        wiL, woL = tile_ffn(g("ffn2_wi"), g("ffn2_wo"))
        wb_all = np.concatenate([g("wb_ret"), g("wb_gla"), g("wb_ssd")], axis=0)
        wbL = np.ascontiguousarray(wb_all.reshape(32, 128, KC, 128).transpose(2, 1, 0, 3))
        w_outL = np.ascontiguousarray(g("w_out").reshape(KC, 128, KC, 128).transpose(2, 1, 0, 3))
        hs, vs = halos(x1)
        in_maps = []
        for c in range(ncores):
            slotE = np.zeros((2, 7, 128, 4096), f32)
            slotD = np.ones((2, 7, 128, 40), f32)
            for s, src in enumerate(range(0, c)):
                slotE[0, s] = E[src][0]
                slotD[0, s] = Dd[src][0]
            for s, src in enumerate(range(ncores - 1, c, -1)):
                slotE[1, s] = E[src][1]
                slotD[1, s] = Dd[src][1]
            m = dict(common)
            m.update({"xin": np.ascontiguousarray(np.concatenate([x1[c], ctx1, hs[c]], axis=1)),
                      "halo_valid": vs[c], "rope_in": ropes[c], "wiL": wiL, "woL": woL, "wbL": wbL, "w_outL": w_outL,
                      "gla_g": rep(g("gla_norm_g")), "ssd_d": rep(g("ssd_d")), "ssd_ng": rep(g("ssd_norm_g")),
                      "final_g": fm_vec(np.asarray(inp["final_norm_g"], f32)),
                      "slotE": slotE, "slotD": slotD})
            in_maps.append(m)
        res2 = run_bass_kernel_spmd(get_prog("p2last" if last else "p2"), in_maps, core_ids=list(range(ncores))).results
        if run is not None:
            run["res2"] = res2
        xT = [r["xoutT"][:, :TL] for r in res2]
        ctxT = np.ascontiguousarray(res2[0]["xoutT"][:, TL:])
    out = np.concatenate([t.T for t in xT], axis=0)[None]
    return np.ascontiguousarray(out.astype(np.float32))


def kernel(**inputs):
    return run_model(inputs, 8)
```

```python
import contextlib
import numpy as np
import concourse.bass as bass
import concourse.mybir as mybir

F32 = mybir.dt.float32
BF16 = mybir.dt.bfloat16
AF = mybir.ActivationFunctionType
ALU = mybir.AluOpType
KDMA = 14


class View:
    __slots__ = ("buf", "ap")

    def __init__(self, buf, ap):
        self.buf = buf
        self.ap = ap

    def __getitem__(self, idx):
        return View(self.buf, self.ap[idx])

    def m(self, f):
        return View(self.buf, f(self.ap))


class Buf:
    __slots__ = ("ap", "w", "r", "name", "psum")

    def __init__(self, ap, name="", psum=False):
        self.ap = ap
        self.w = None
        self.r = {}
        self.name = name
        self.psum = psum

    def __getitem__(self, idx):
        return View(self, self.ap[idx])

    def v(self):
        return View(self, self.ap)


class Prog:
    ENGS = ("pe", "act", "dve", "pool", "sp")

    def __init__(self, nc, stack):
        self.nc = nc
        self.stack = stack
        self.q = {e: [] for e in self.ENGS}
        self.cnt = {e: 0 for e in self.ENGS}
        self.dman = {e: 0 for e in self.ENGS}
        self.waited = {e: {} for e in self.ENGS}
        self.sems = {}
        for e in self.ENGS:
            self.sems[("eng", e)] = stack.enter_context(nc.semaphore("s_" + e))
        for e in ("sp", "pool", "act"):
            for i in range(KDMA):
                self.sems[("dma", e, i)] = stack.enter_context(nc.semaphore(f"d_{e}{i}"))
        self.nalloc = 0
        self.psn = 0

    def sb(self, shape, dtype=F32, name=None):
        self.nalloc += 1
        t = self.stack.enter_context(self.nc.sbuf_tensor(f"{name or 't'}_{self.nalloc}", list(shape), dtype))
        return Buf(t[:] if hasattr(t, "__getitem__") else t, name or "t")

    def sb_ctx(self, shape, dtype=F32, name=None):
        self.nalloc += 1
        return self.nc.sbuf_tensor(f"{name or 't'}_{self.nalloc}", list(shape), dtype)

    def psum_banks(self):
        self.ps = []
        for i in range(8):
            t = self.stack.enter_context(self.nc.psum_tensor(f"ps{i}", [128, 512], F32))
            self.ps.append(Buf(t[:], f"ps{i}", psum=True))

    def next_ps(self):
        b = self.ps[self.psn % 8]
        self.psn += 1
        return b

    def dram(self, name, shape, dtype=F32, kind="Internal"):
        t = self.nc.dram_tensor(name, list(shape), dtype, kind=kind)
        return Buf(t.ap(), name)

    def op(self, E, fn, reads, writes, dma=False):
        deps = {}

        def add(tok):
            sk, val, eng = tok
            if deps.get(sk, (0, None))[0] < val:
                deps[sk] = (val, eng)

        for b in reads:
            if b.w is not None:
                add(b.w)
            if b.psum:
                for sk, (val, eng) in b.r.items():
                    if eng != E:
                        add((sk, val, eng))
        for b in writes:
            if b.w is not None:
                add(b.w)
            for sk, (val, eng) in b.r.items():
                add((sk, val, eng))
        waits = []
        for sk, (val, eng) in deps.items():
            if eng == "pe" and E == "pe" and not dma:
                continue
            if self.waited[E].get(sk, 0) >= val:
                continue
            self.waited[E][sk] = val
            waits.append((sk, val))
        if dma:
            n = self.dman[E]
            self.dman[E] += 1
            sk = ("dma", E, n % KDMA)
            val = 16 * (n // KDMA + 1)
            if val > 16 and self.waited[E].get(sk, 0) < val - 16:
                waits.append((sk, val - 16))
                self.waited[E][sk] = val - 16
            tok = (sk, val, "dma")
            inc = (sk, 16)
        else:
            self.cnt[E] += 1
            sk = ("eng", E)
            tok = (sk, self.cnt[E], E)
            inc = (sk, 1)
        self.q[E].append((waits, fn, inc))
        wset = set(id(b) for b in writes)
        for b in writes:
            b.w = tok
            b.r = {}
        for b in reads:
            if id(b) not in wset:
                b.r[tok[0]] = (tok[1], tok[2])
        return tok

    def barrier(self):
        cur = []
        for e in self.ENGS:
            if self.cnt[e] > 0:
                cur.append((("eng", e), self.cnt[e]))
        for e in ("sp", "pool", "act"):
            n = self.dman[e]
            for i in range(KDMA):
                k = (n - 1 - i)
                if k >= 0:
                    cur.append((("dma", e, k % KDMA), 16 * (k // KDMA + 1)))
        for E in self.ENGS:
            waits = []
            for sk, val in cur:
                if sk == ("eng", E):
                    continue
                if self.waited[E].get(sk, 0) >= val:
                    continue
                self.waited[E][sk] = val
                waits.append((sk, val))
            if waits:
                self.q[E].append((waits, None, None))

    def emit(self):
        nc = self.nc
        self.barrier()
        with nc.Block() as block:
            def replay(E, e):
                for waits, fn, inc in self.q[E]:
                    for sk, val in waits:
                        e.wait_ge(self.sems[sk], val)
                    if fn is not None:
                        ins = fn(e)
                        ins.then_inc(self.sems[inc[0]], inc[1])

            @block.tensor
            def _(e):
                replay("pe", e)

            @block.scalar
            def _(e):
                replay("act", e)

            @block.vector
            def _(e):
                replay("dve", e)

            @block.gpsimd
            def _(e):
                replay("pool", e)

            @block.sync
            def _(e):
                replay("sp", e)

    @staticmethod
    def _b(*xs):
        return [x.buf for x in xs if isinstance(x, View)]

    @staticmethod
    def _a(x):
        return x.ap if isinstance(x, View) else x

    def mm(self, out, lhsT, rhs, start=True, stop=True):
        o, l, r = out.ap, lhsT.ap, rhs.ap
        return self.op("pe", lambda e: e.matmul(o, l, r, start=start, stop=stop), self._b(lhsT, rhs), self._b(out))

    def tr(self, out, in_, ident):
        o, i, d = out.ap, in_.ap, ident.ap
        return self.op("pe", lambda e: e.transpose(o, i, d), self._b(in_, ident), self._b(out))

    def act(self, out, in_, func, bias=0.0, scale=1.0, accum=None):
        o, i, b, s = out.ap, in_.ap, self._a(bias), self._a(scale)
        ac = self._a(accum) if accum is not None else None
        kw = {}
        if ac is not None:
            kw["accum_out"] = ac
        return self.op("act", lambda e: e.activation(o, i, func, bias=b, scale=s, **kw),
                       self._b(in_, bias, scale), self._b(out) + (self._b(accum) if accum is not None else []))

    def tt(self, out, a, b, op, eng="dve"):
        o, x, y = out.ap, a.ap, b.ap
        return self.op(eng, lambda e: e.tensor_tensor(o, x, y, op), self._b(a, b), self._b(out))

    def ts(self, out, a, s1, s2, op0, op1=None, eng="dve", accum=None):
        o, x, p1, p2 = out.ap, a.ap, self._a(s1), self._a(s2)
        kw = {}
        if op1 is not None:
            kw["op1"] = op1
        if accum is not None:
            kw["accum_out"] = accum.ap
        return self.op(eng, lambda e: e.tensor_scalar(o, x, p1, p2, op0, **kw), self._b(a, s1, s2),
                       self._b(out) + (self._b(accum) if accum is not None else []))

    def stt(self, out, a, s, b, op0, op1, eng="dve"):
        o, x, p, y = out.ap, a.ap, self._a(s), b.ap
        return self.op(eng, lambda e: e.scalar_tensor_tensor(o, x, p, y, op0, op1), self._b(a, s, b), self._b(out))

    def copy(self, out, a, eng="dve"):
        o, x = out.ap, a.ap
        if eng == "act":
            return self.op("act", lambda e: e.copy(o, x), self._b(a), self._b(out))
        return self.op(eng, lambda e: e.tensor_copy(o, x), self._b(a), self._b(out))

    def memset(self, out, val, eng="dve"):
        o = out.ap
        return self.op(eng, lambda e: e.memset(o, val), [], self._b(out))

    def scan(self, out, d0, d1, init, op0, op1):
        o, a, b, i = out.ap, d0.ap, d1.ap, self._a(init)
        return self.op("dve", lambda e: e.tensor_tensor_scan(o, a, b, i, op0, op1), self._b(d0, d1, init), self._b(out))

    def dma(self, out, in_, q="sp"):
        o, i = out.ap, in_.ap
        return self.op(q, lambda e: e.dma_start(out=o, in_=i), self._b(in_), self._b(out), dma=True)


@contextlib.contextmanager
def scope(P):
    old = P.stack
    with contextlib.ExitStack() as st:
        P.stack = st
        try:
            yield
        finally:
            P.barrier()
            P.stack = old


D = 1024
KC = 8
DFF = 2816
NFF = 22
EPS = 1e-6


def fm_vec(v):
    v = np.asarray(v, np.float32)
    return np.ascontiguousarray(v.reshape(-1, 128).T)


class Blocks:
    def __init__(self, P, sizes, kinds, name="xT", dtype=F32):
        self.sizes = sizes
        self.kinds = kinds
        self.offs = np.concatenate([[0], np.cumsum(sizes)]).astype(int).tolist()
        self.T = self.offs[-1]
        self.t = P.stack.enter_context(P.nc.sbuf_tensor(name, [128, KC, self.T], dtype))
        self.b = [Buf(self.t[:, :, self.offs[i]:self.offs[i] + n], f"{name}{i}") for i, n in enumerate(sizes)]


def prep_mod(P, modT, gT, idx):
    out = {}
    for kind, mt in modT.items():
        gs = P.sb([128, KC], F32, "gs")
        hg = P.sb([128, KC], F32, "hg")
        P.stt(gs.v(), mt[:, 3 * idx + 1, :], 1.0, gT[:, idx, :], ALU.add, ALU.mult)
        P.ts(hg.v(), mt[:, 3 * idx + 2, :], 0.5 if idx != 1 else 1.0, None, ALU.mult)
        out[kind] = (gs, mt[:, 3 * idx + 0, :], hg)
    return out


def norm_mod(P, xb, hb, n, gs, sh, ones_bf, scr):
    sq, tmp, rstd = scr
    P.act(sq[:, :, :n], xb.v(), AF.Square)
    ps = P.next_ps()
    for k in range(KC):
        P.mm(ps[:, :n], ones_bf.v(), sq[:, k, :n], start=(k == 0), stop=(k == KC - 1))
    P.ts(rstd[:, :n], ps[:, :n], 1.0 / D, EPS, ALU.mult, ALU.add)
    P.act(rstd[:, :n], rstd[:, :n], AF.Ln)
    P.act(rstd[:, :n], rstd[:, :n], AF.Exp, scale=-0.5)
    for k in range(KC):
        P.stt(tmp[:, k, :n], xb[:, k, :], gs[:, k:k + 1], rstd[:, :n], ALU.mult, ALU.mult)
        P.act(hb[:, k, :], tmp[:, k, :n], AF.Identity, bias=sh[:, k:k + 1], scale=1.0)


def ffn(P, X, H, mods, wi, wo, ones_bf, scr, G=6):
    nblk = len(X.sizes)
    T = X.T
    with P.sb_ctx([128, G, T], BF16, "actT") as actT_t, \
            P.sb_ctx([128, 2, KC, 256], BF16, "wi_t") as wi_t, \
            P.sb_ctx([128, G, D], BF16, "wo_t") as wo_t, \
            P.sb_ctx([128, 2, 512], F32, "sil") as sil_t:
        actb = [[Buf(actT_t[:, g, X.offs[b]:X.offs[b] + X.sizes[b]]) for b in range(nblk)] for g in range(G)]
        wib = [Buf(wi_t[:, i]) for i in range(2)]
        wob = Buf(wo_t[:])
        silb = [Buf(sil_t[:, i]) for i in range(2)]
        wiv = wi.ap.rearrange("(k p) c -> p k c", p=128)
        wov = wo.ap.rearrange("(j p) f -> p j f", p=128)
        ns = 0
        nw = 0
        for g0 in range(0, NFF, G):
            gn = min(G, NFF - g0)
            for gi in range(gn):
                j = g0 + gi
                w = wib[nw % 2]
                nw += 1
                P.dma(w[:, :, 0:128], View(wi, wiv[:, :, j * 128:(j + 1) * 128]), q="pool")
                P.dma(w[:, :, 128:256], View(wi, wiv[:, :, DFF + j * 128:DFF + (j + 1) * 128]), q="pool")
                for b in range(nblk):
                    n = X.sizes[b]
                    pa = P.next_ps()
                    pu = P.next_ps()
                    for k in range(KC):
                        P.mm(pa[:, :n], w[:, k, 0:128], H.b[b][:, k, :], start=(k == 0), stop=(k == KC - 1))
                    for k in range(KC):
                        P.mm(pu[:, :n], w[:, k, 128:256], H.b[b][:, k, :], start=(k == 0), stop=(k == KC - 1))
                    s = silb[ns % 2]
                    ns += 1
                    P.act(s[:, :n], pa[:, :n], AF.Silu)
                    P.tt(actb[gi][b].v(), s[:, :n], pu[:, :n], ALU.mult)
            P.dma(wob[:, 0:gn, :], View(wo, wov[:, g0:g0 + gn, :]), q="pool")
            for b in range(nblk):
                n = X.sizes[b]
                hg = mods[X.kinds[b]][2]
                for f in range(KC):
                    ps = P.next_ps()
                    for gi in range(gn):
                        P.mm(ps[:, :n], wob[:, gi, f * 128:(f + 1) * 128], actb[gi][b].v(), start=(gi == 0), stop=(gi == gn - 1))
                    P.stt(X.b[b][:, f, :], ps[:, :n], hg[:, f:f + 1], X.b[b][:, f, :], ALU.mult, ALU.add)
        P.barrier()


TL = 2048
TC = 256
NCH_L = TL // 128
NCH_C = TC // 128
NCH = NCH_L + NCH_C
TT = TL + TC
DIN = 14432
COLS = {}
_o = 0
for _n, _s in (("ret_q", 512), ("ret_k", 512), ("ret_v", 1024), ("ret_g", 1024), ("gla_q", 512), ("gla_k", 512),
               ("gla_v", 1024), ("gla_r", 1024), ("gla_af", 16), ("gla_ab", 16), ("ssd_z", 2048), ("ssd_xbc", 3072),
               ("ssd_dt", 64), ("merge", 3072)):
    COLS[_n] = (_o, _s)
    _o += _s
assert _o == DIN
CI = {"ident": 0, "perm": 1, "mle": 2, "mge": 3, "sgt": 4, "slt": 5}


def host_consts():
    p = np.arange(128)[:, None]
    c = np.arange(128)[None, :]
    mats = [p == c, c == (p + 64) % 128, p <= c, p >= c, p > c, p < c]
    m = np.concatenate([x.astype(np.float32) for x in mats], axis=1)
    rows = np.concatenate([np.broadcast_to(np.arange(1, 129, dtype=np.float32), (128, 128)),
                           np.broadcast_to(np.arange(128, 0, -1).astype(np.float32), (128, 128))], axis=1)
    return np.ascontiguousarray(np.concatenate([m, rows], axis=1))


class Env:
    pass


def load_consts(P, env, cin):
    cf = P.sb([128, 8 * 128], F32, "cf")
    P.dma(cf.v(), cin.v())
    cb = P.sb([128, 6 * 128], BF16, "cb")
    P.copy(cb.v(), cf[:, 0:768])
    env.cf, env.cb = cf, cb
    env.ones_bf = P.sb([128, 128], BF16, "ones")
    P.memset(env.ones_bf.v(), 1.0)
    env.ones_f = P.sb([128, 128], F32, "onesf")
    P.memset(env.ones_f.v(), 1.0)

    def cm(name, bf=True):
        i = CI[name]
        return (cb if bf else cf)[:, i * 128:(i + 1) * 128]
    env.cm = cm
    env.row_up = cf[:, 768:896]
    env.row_dn = cf[:, 896:1024]


def softplus_parts(P, z, n, tmp1, tmp2):
    P.act(tmp1, z, AF.Abs)
    P.act(tmp1, tmp1, AF.Exp, scale=-1.0)
    P.act(tmp1, tmp1, AF.Ln, bias=1.0)
    P.ts(tmp2, z, 0.0, None, ALU.max)
    return tmp2, tmp1


def inproj(P, env, H, W, sp, lay, groups, do_q, skip_ctx=False):
    nblk = len(H.sizes)
    tokblocks = [b for b in range(nblk) if H.kinds[b] != "halo" and not (skip_ctx and H.kinds[b] == "ctx")]
    halo = [b for b in range(nblk) if H.kinds[b] == "halo"][0]
    spoff = {}
    o = 0
    for b in tokblocks:
        spoff[b] = o
        o += H.sizes[b]
    wt_t = P.stack.enter_context(P.sb_ctx([128, 2, KC, 512], BF16, "win_t"))
    wt = [Buf(wt_t[:, i]) for i in range(2)]
    st = {"nw": 0}

    def loadw(c0, n):
        w = wt[st["nw"] % 2]
        st["nw"] += 1
        P.dma(w[:, :, 0:n], W[:, :, c0:c0 + n], q="pool")
        return w

    def fm_mm(w, wc0, m, b, ps=None):
        n = H.sizes[b]
        ps = ps or P.next_ps()
        for k in range(KC):
            P.mm(ps[0:m, :n], w[:, k, wc0:wc0 + m], H.b[b][:, k, :], start=(k == 0), stop=(k == KC - 1))
        return ps

    def do_qk(mixer):
        mi = 0 if mixer == "ret" else 1
        names = (["q"] if do_q else []) + ["k"]
        with scope(P):
            ec = P.stack.enter_context
            qf_t = ec(P.sb_ctx([128, 2, 512], F32, "qk_f"))
            qb_t = ec(P.sb_ctx([128, 2, 512], BF16, "qk_b"))
            qo_t = ec(P.sb_ctx([128, 4, 512], BF16, "qk_o"))
            g_t = ec(P.sb_ctx([128, 6, 512], F32, "g_t"))
            rope_t = ec(P.sb_ctx([128, 2, TL if mixer == "ret" else 2], F32, "rope"))
            code_t = ec(P.sb_ctx([16, 2, 512], BF16, "code"))
            la_t = ec(P.sb_ctx([128, 2, 512], F32, "la_t"))
            epm_t = ec(P.sb_ctx([128, 4, 512], F32, "epm_t"))
            epm = [[Buf(epm_t[:, 0]), Buf(epm_t[:, 1])], [Buf(epm_t[:, 2]), Buf(epm_t[:, 3])]]
            qf = [Buf(qf_t[:, i]) for i in range(2)]
            qb = [Buf(qb_t[:, i]) for i in range(2)]
            qo = [Buf(qo_t[:, i]) for i in range(4)]
            gt = [Buf(g_t[:, i]) for i in range(6)]
            la = [Buf(la_t[:, i]) for i in range(2)]
            code = [Buf(code_t[:, i]) for i in range(2)]
            rope = Buf(rope_t[:])
            if mixer == "ret":
                P.dma(rope.v(), lay["rope_in"].v())
            wq = {}
            for nm in names:
                c0, _ = COLS[f"{mixer}_{nm}"]
                wq[nm] = (c0,)
            if mixer == "gla":
                wc = loadw(COLS["gla_af"][0], 32)
                wcode = P.sb([128, KC, 32], BF16, "wcode")
                P.copy(wcode.v(), wc[:, :, 0:32])
            cnt = 0
            if mixer == "gla":
                egg = P.sb([128, 2, NCH, 4], F32, "egg")
            for b in tokblocks:
                n = H.sizes[b]
                nc_ = n // 128
                kind = H.kinds[b]
                if mixer == "gla":
                    for d in range(2):
                        ps = P.next_ps()
                        for k in range(KC):
                            P.mm(ps[0:16, :n], wcode[:, k, d * 16:(d + 1) * 16], H.b[b][:, k, :], start=(k == 0), stop=(k == KC - 1))
                        P.copy(code[d][:, :n], ps[0:16, :n], eng="act")
                if mixer == "ret":
                    for nm in names:
                        w = loadw(wq[nm][0], 512)
                        var0 = 0 if nm == "q" else 1
                        for h in range(4):
                            ps = fm_mm(w, h * 128, 128, b)
                            cnt += 1
                            f32 = qf[cnt % 2]
                            sc = 128 ** -0.5 if nm == "k" else 1.0
                            if kind == "lat":
                                bfv = qb[cnt % 2]
                                P.act(bfv[:, :n], ps[:, :n], AF.Copy, scale=sc)
                                ps2 = P.next_ps()
                                P.mm(ps2[:, :n], env.cm("perm"), bfv[:, :n])
                                t0 = H.offs[b]
                                P.tt(f32[:, :n], bfv[:, :n], rope[:, 0, t0:t0 + n], ALU.mult)
                                P.tt(gt[0][:, :n], ps2[:, :n], rope[:, 1, t0:t0 + n], ALU.mult)
                                P.tt(f32[:, :n], f32[:, :n], gt[0][:, :n], ALU.add)
                            else:
                                P.act(f32[:, :n], ps[:, :n], AF.Copy, scale=sc)
                            for d in range(2):
                                sgn = 0 if nm == "q" else 1
                                tab = lay["ret_tab"][:, d, sgn, h, :].m(lambda a: a.unsqueeze(1).to_broadcast([128, nc_, 128]))
                                o = qo[(cnt * 2 + d) % 4]
                                P.tt(o[:, :n].m(lambda a: a.rearrange("p (c t) -> p c t", t=128)),
                                     f32[:, :n].m(lambda a: a.rearrange("p (c t) -> p c t", t=128)), tab, ALU.mult)
                                P.dma(sp["qk"][mi, h, 2 * d + var0, :, spoff[b]:spoff[b] + n], o[:, :n])
                else:
                    ws = {nm: loadw(wq[nm][0], 512) for nm in names}
                    for h in range(4):
                        for d in range(2):
                            psz = P.next_ps()
                            P.mm(psz[:, :n], lay["wa2"][:, d, h * 128:(h + 1) * 128], code[d][:, :n])
                            z = gt[1]
                            P.ts(z[:, :n], psz[:, :n], lay["gla_ba"][:, d, h:h + 1], None, ALU.add)
                            mx, l = softplus_parts(P, z[:, :n], n, gt[2][:, :n], gt[3][:, :n])
                            P.ts(gt[3][:, :n], z[:, :n], 0.0, None, ALU.min)
                            dd = gt[4]
                            P.tt(dd[:, :n], gt[3][:, :n], gt[2][:, :n], ALU.subtract)
                            cs = gt[5]
                            for c in range(nc_):
                                sl = slice(c * 128, (c + 1) * 128)
                                P.scan(cs[:, sl], env.ones_f[:, 0:128], dd[:, sl], 0.0, ALU.mult, ALU.add)
                            g = la[d]
                            if d == 0:
                                g = cs
                            else:
                                P.tt(g[:, :n], dd[:, :n], cs[:, :n], ALU.subtract)
                                for c in range(nc_):
                                    sl = slice(c * 128, (c + 1) * 128)
                                    P.ts(g[:, sl], g[:, sl], cs[:, c * 128 + 127:c * 128 + 128], None, ALU.add)
                            ch0 = spoff[b] // 128
                            for c in range(nc_):
                                P.act(egg[:, d, ch0 + c, h:h + 1], cs[:, c * 128 + 127:c * 128 + 128], AF.Exp, scale=1.0 / 16)
                            if do_q:
                                P.act(epm[0][d][:, :n], g[:, :n], AF.Exp, scale=1.0 / 16)
                            P.act(epm[1][d][:, :n], g[:, :n], AF.Exp, scale=-1.0 / 16)
                        for nm in names:
                            var0 = 0 if nm == "q" else 1
                            ps = fm_mm(ws[nm], h * 128, 128, b)
                            cnt += 1
                            f32 = qf[cnt % 2]
                            sc = 128 ** -0.5 if nm == "q" else 1.0
                            P.act(f32[:, :n], ps[:, :n], AF.Copy, scale=sc)
                            for d in range(2):
                                o = qo[(cnt * 2 + d) % 4]
                                P.tt(o[:, :n], f32[:, :n], epm[var0][d][:, :n], ALU.mult)
                                P.dma(sp["qk"][mi, h, 2 * d + var0, :, spoff[b]:spoff[b] + n], o[:, :n])
            if mixer == "gla":
                for d in range(2):
                    P.dma(sp["eG"][mi * 2 + d], egg[:, d])
            if mixer == "ret":
                egr = P.sb([128, 2, NCH, 4], F32, "egr")
                for d in range(2):
                    P.copy(egr[:, d, :, :], lay["ret_eg"][:, d, :].m(lambda a: a.unsqueeze(1).to_broadcast([128, NCH, 4])))
                    P.dma(sp["eG"][mi * 2 + d], egr[:, d])
            P.barrier()

    def do_tm(name, handler, width=512):
        c0, sz = COLS[name]
        for cc in range(0, sz, width):
            n_ = min(width, sz - cc)
            w = loadw(c0 + cc, n_)
            for b in tokblocks:
                for c in range(H.sizes[b] // 128):
                    ps = P.next_ps()
                    for k in range(KC):
                        P.mm(ps[:, :n_], H.b[b][:, k, c * 128:(c + 1) * 128], w[:, k, 0:n_], start=(k == 0), stop=(k == KC - 1))
                    handler(ps, n_, cc, spoff[b] + c * 128)

    ev = {"n": 0}
    evt_t = P.stack.enter_context(P.sb_ctx([128, 4, 512], BF16, "evt"))
    evt = [Buf(evt_t[:, i]) for i in range(4)]

    def h_copy(dst, col0, func=AF.Copy):
        def h(ps, n_, cc, tok):
            e = evt[ev["n"] % 4]
            ev["n"] += 1
            if func == AF.Copy and ev["n"] % 2 == 0:
                P.copy(e[:, :n_], ps[:, :n_])
            else:
                P.act(e[:, :n_], ps[:, :n_], func)
            P.dma(dst[tok:tok + 128, col0 + cc:col0 + cc + n_], e[:, :n_])
        return h

    def h_dt(ps, n_, cc, tok):
        z = P.sb([128, 64], F32, "dtz")
        t1 = P.sb([128, 64], F32, "dt1")
        t2 = P.sb([128, 64], F32, "dt2")
        P.tt(z.v(), ps[:, 0:64], lay["dt_bias"].v(), ALU.add)
        mx, l = softplus_parts(P, z.v(), 64, t1.v(), t2.v())
        P.tt(z.v(), mx, l, ALU.add)
        P.tt(t1.v(), z.v(), lay["ssd_A"].v(), ALU.mult)
        P.dma(sp["dt"][tok:tok + 128, :], z.v())
        P.dma(sp["a"][tok:tok + 128, :], t1.v())

    def do_xbc(chunks):
        c0, _ = COLS["ssd_xbc"]
        with scope(P):
            ec = P.stack.enter_context
            rawl_t = ec(P.sb_ctx([128, 2, TL + 4], F32, "rawl"))
            rawc_t = ec(P.sb_ctx([128, 2, TC + 4], F32, "rawc"))
            acc_t = ec(P.sb_ctx([128, 2, TL], F32, "acc"))
            xs_t = ec(P.sb_ctx([128, 4, TT], BF16, "xsT"))
            xtm_t = ec(P.sb_ctx([128, 2, 512], BF16, "xtm"))
            rawl = [Buf(rawl_t[:, i]) for i in range(2)]
            rawc = [Buf(rawc_t[:, i]) for i in range(2)]
            acc = [Buf(acc_t[:, i]) for i in range(2)]
            xs = [Buf(xs_t[:, i]) for i in range(4)]
            xtm = [Buf(xtm_t[:, i]) for i in range(2)]
            for i in range(2):
                P.memset(rawc[i][:, 0:2], 0.0)
                P.memset(rawc[i][:, TC + 2:TC + 4], 0.0)
            w = None
            nx = 0
            for ci, ch in enumerate(chunks):
                if ci % 4 == 0 or w is None:
                    w = loadw(c0 + ch * 128, 512)
                    wbase = ch
                rl, rc, ac = rawl[ci % 2], rawc[ci % 2], acc[ci % 2]
                for b in range(nblk):
                    n = H.sizes[b]
                    if skip_ctx and H.kinds[b] == "ctx":
                        continue
                    ps = fm_mm(w, (ch - wbase) * 128, 128, b)
                    if H.kinds[b] == "lat":
                        P.copy(rl[:, 2 + H.offs[b]:2 + H.offs[b] + n], ps[:, :n], eng=("act" if b % 2 else "dve"))
                    elif H.kinds[b] == "ctx":
                        P.copy(rc[:, 2:2 + n], ps[:, :n], eng="act")
                    else:
                        P.ts(rl[:, 0:2], ps[:, 0:2], lay["halo_valid"][:, 0:1], None, ALU.mult)
                        P.ts(rl[:, TL + 2:TL + 4], ps[:, 2:4], lay["halo_valid"][:, 1:2], None, ALU.mult)
                for (raw, T_, o_) in (((rl, TL, 0),) if skip_ctx else ((rl, TL, 0), (rc, TC, TL))):
                    a = ac[:, 0:T_]
                    P.act(a, raw[:, 0:T_], AF.Identity, bias=lay["conv_b"][:, ch:ch + 1], scale=lay["conv_w"][:, 0, ch:ch + 1])
                    for o in range(1, 5):
                        P.stt(a, raw[:, o:o + T_], lay["conv_w"][:, o, ch:ch + 1], a, ALU.mult, ALU.add,
                              eng=("dve" if o % 2 else "pool") if False else "dve")
                    if ch < 16:
                        P.act(xs[ci % 4][:, o_:o_ + T_], a, AF.Silu)
                    else:
                        e = xs[ci % 4]
                        P.act(e[:, o_:o_ + T_], a, AF.Silu)
                        which = "BT" if ch < 20 else "CT"
                        P.dma(sp[which][(ch - 16) % 4, :, o_:o_ + T_], e[:, o_:o_ + T_])
                if ch < 16 and ci % 4 == 3:
                    for tcn in range(NCH_L if skip_ctx else NCH):
                        psb = P.next_ps()
                        pv = psb.v().m(lambda a: a.bitcast(BF16))
                        for q4 in range(4):
                            P.tr(pv[:, q4 * 128:(q4 + 1) * 128], xs[q4][:, tcn * 128:(tcn + 1) * 128], env.cm("ident"))
                        xo = xtm[nx % 2]
                        nx += 1
                        P.copy(xo.v(), pv[:, 0:512], eng=("act" if nx % 2 else "dve"))
                        P.dma(sp["x"][tcn * 128:(tcn + 1) * 128, (ch - 3) * 128:(ch + 1) * 128], xo.v())
            P.barrier()

    for grp in groups:
        if grp in ("ret", "gla"):
            do_qk(grp)
        elif grp == "v":
            do_tm("ret_v", h_copy(sp["v"][0], 0))
            do_tm("gla_v", h_copy(sp["v"][1], 0))
        elif grp == "gates":
            do_tm("ret_g", h_copy(sp["gate"], 0, AF.Silu))
            do_tm("gla_r", h_copy(sp["gate"], 1024, AF.Silu))
            do_tm("ssd_z", h_copy(sp["gate"], 2048, AF.Silu))
        elif grp == "dt":
            do_tm("ssd_dt", h_dt, width=64)
        elif grp == "xB":
            do_xbc(list(range(0, 20)))
        elif grp == "xBC":
            do_xbc(list(range(0, 24)))
        elif grp == "merge":
            c0, sz = COLS["merge"]
            for cc in range(0, sz, 512):
                w = loadw(c0 + cc, 512)
                for q4 in range(4):
                    for b in tokblocks:
                        n = H.sizes[b]
                        ps = fm_mm(w, q4 * 128, 128, b)
                        e = evt[ev["n"] % 4]
                        ev["n"] += 1
                        P.act(e[:, :n], ps[:, :n], AF.Sigmoid)
                        P.dma(sp["sg"][cc // 128 + q4, :, spoff[b]:spoff[b] + n], e[:, :n])
    P.barrier()


def bc(v, axis, shape):
    return v.m(lambda a: a.unsqueeze(axis).to_broadcast(list(shape)))


def r3(v, pat, **kw):
    return v.m(lambda a: a.rearrange(pat, **kw))


class SweepBufs:
    def __init__(self, P, do_y, with_post):
        st = P.stack

        def mk(shape, dt, n, name):
            t = st.enter_context(P.sb_ctx([128, n] + list(shape), dt, name))
            return [Buf(t[:, i]) for i in range(n)]
        self.kT = [mk([4, 128], BF16, 2, "kT0"), mk([4, 128], BF16, 2, "kT1")]
        self.v = [mk([1024], BF16, 2, "v0"), mk([1024], BF16, 2, "v1")]
        self.eG = [mk([4], F32, 2, "eG0"), mk([4], F32, 2, "eG1")]
        self.BT = mk([4, 128], BF16, 2, "BT")
        self.x = mk([2048], BF16, 2, "xtm")
        self.a = mk([64], F32, 2, "a")
        self.dt = mk([64], F32, 2, "dt")
        self.ktm = mk([512], BF16, 2, "ktm")
        self.btm = mk([512], BF16, 1, "btm")
        self.eT = mk([96], F32, 2, "eT")
        self.wv = mk([32], F32, 1, "wv")
        self.vs = mk([2048], BF16, 1, "vs")
        if do_y:
            self.qT = [mk([4, 128], BF16, 2, "qT0"), mk([4, 128], BF16, 2, "qT1")]
            self.CT = mk([4, 128], BF16, 2, "CT")
            self.attT = mk([4, 128], BF16, 2, "attT")
            self.vdt = mk([2048], BF16, 1, "vdt")
            self.cbm = mk([4, 128], BF16, 1, "cbm")
            self.rhsD = mk([32, 128], BF16, 1, "rhsD")
            self.L = mk([4, 128], BF16, 4, "L")
            self.att = mk([32, 128], BF16, 1, "att")
            self.tmp = mk([512], F32, 2, "tmp")
            self.yall = mk([4096], F32, 2, "yall")
            self.yb = mk([4096], BF16, 1, "yb")
        if with_post:
            self.gate = mk([4096], BF16, 2, "gate")
            self.ss = mk([12], F32, 2, "ss")
            self.junk = mk([512], F32, 1, "junk")
            self.ygate = mk([4096], BF16, 1, "ygate")
            self.ygT = mk([32, 128], BF16, 1, "ygT")


def sweep(P, env, sp, lay, d, chain, S, SB, do_y, B, mode, Dt=None):
    msk_f32 = env.cm("mle" if d == 0 else "mge", bf=False)
    msk_bf = env.cm("mle" if d == 0 else "mge")
    sl_f32 = env.cm("sgt" if d == 0 else "slt", bf=False)
    sl_bf = env.cm("sgt" if d == 0 else "slt")
    ident = env.cm("ident")
    post = mode == "post"

    def bfview(psb):
        return psb.v().m(lambda ap: ap.bitcast(BF16))

    def front(it, c):
        par = it % 2
        t0 = c * 128
        kT = [B.kT[m][par] for m in range(2)]
        v = [B.v[m][par] for m in range(2)]
        eG = [B.eG[m][par] for m in range(2)]
        BT, x, a, dt = B.BT[par], B.x[par], B.a[par], B.dt[par]
        for m in range(2):
            P.dma(kT[m].v(), View(sp["qk"], sp["qk"].ap[m, :, 2 * d + 1, :, t0:t0 + 128].rearrange("h p t -> p h t")))
            P.dma(v[m].v(), sp["v"][m, t0:t0 + 128, :])
            P.dma(eG[m].v(), sp["eG"][m * 2 + d, :, c, :])
        P.dma(BT.v(), View(sp["BT"], sp["BT"].ap[:, :, t0:t0 + 128].rearrange("g p t -> p g t")))
        P.dma(x.v(), sp["x"][t0:t0 + 128, :])
        P.dma(a.v(), sp["a"][t0:t0 + 128, :])
        P.dma(dt.v(), sp["dt"][t0:t0 + 128, :])
        if do_y:
            qT = [B.qT[m][par] for m in range(2)]
            CT = B.CT[par]
            for m in range(2):
                P.dma(qT[m].v(), View(sp["qk"], sp["qk"].ap[m, :, 2 * d, :, t0:t0 + 128].rearrange("h p t -> p h t")))
            P.dma(CT.v(), View(sp["CT"], sp["CT"].ap[:, :, t0:t0 + 128].rearrange("g p t -> p g t")))
            yall = B.yall[par]
        if post:
            yb, gate = B.yb[0], B.gate[par]
            P.dma(yb.v(), sp["yb"][t0:t0 + 128, :])
            P.dma(gate.v(), sp["gate"][t0:t0 + 128, :])
        a_d = a[:, d * 32:(d + 1) * 32]
        dt_d = dt[:, d * 32:(d + 1) * 32]
        x3 = r3(x.v(), "p (h e) -> p h e", e=64)
        ktm = [B.ktm[0], B.ktm[1]]
        btm = B.btm[0]
        for m in range(2):
            pv = bfview(P.next_ps())
            for h in range(4):
                P.tr(pv[:, h * 128:(h + 1) * 128], kT[m][:, h, :], ident)
            P.copy(ktm[m].v(), pv[:, 0:512], eng="act")
        pv = bfview(P.next_ps())
        for g in range(4):
            P.tr(pv[:, g * 128:(g + 1) * 128], BT[:, g, :], ident)
        P.copy(btm.v(), pv[:, 0:512], eng="act")
        pss = P.next_ps()
        P.mm(pss[:, 0:32], env.ones_f.v(), a_d)
        P.mm(pss[:, 32:64], sl_f32, a_d)
        if do_y:
            P.mm(pss[:, 64:96], msk_f32, a_d)
        ne = 96 if do_y else 64
        eT = B.eT[par]
        P.act(eT[:, 0:ne], pss[:, 0:ne], AF.Exp)
        if Dt is not None:
            P.tt(Dt[2].v(), Dt[2].v(), pss[:, 0:32], ALU.add)
        if do_y:
            attT = [B.attT[0], B.attT[1]]
            for m in range(2):
                aps = P.next_ps()
                for h in range(4):
                    P.mm(aps[:, h * 128:(h + 1) * 128], kT[m][:, h, :], qT[m][:, h, :])
                P.tt(attT[m].v(), r3(aps.v(), "p (h t) -> p h t", h=4), bc(msk_f32, 1, [128, 4, 128]), ALU.mult)
        wv = B.wv[0]
        P.tt(wv.v(), eT[:, 32:64], dt_d, ALU.mult)
        vs = B.vs[0]
        P.tt(r3(vs.v(), "p (h e) -> p h e", e=64), x3, bc(wv.v(), 2, [128, 32, 64]), ALU.mult)
        cnt = {"e": 0}

        def retgla_y(m):
            if not do_y:
                return
            for hp in range(2):
                yp = P.next_ps()
                for hh in range(2):
                    h = hp * 2 + hh
                    o = yp[:, hh * 256:(hh + 1) * 256]
                    if post:
                        P.mm(o, ident, yb[:, m * 1024 + h * 256:m * 1024 + (h + 1) * 256], start=True, stop=False)
                    P.mm(o, attT[m][:, h, :], v[m][:, h * 256:(h + 1) * 256], start=(not post), stop=False)
                    P.mm(o, qT[m][:, h, :], SB[m][:, h * 256:(h + 1) * 256], start=False, stop=True)
                cs = slice(m * 1024 + hp * 512, m * 1024 + (hp + 1) * 512)
                cnt["e"] += 1
                P.copy(yall[:, cs], yp.v(), eng=("act" if cnt["e"] % 2 else "dve"))

        def retgla_state(m):
            for hp in range(2):
                ps2 = P.next_ps()
                for hh in range(2):
                    h = hp * 2 + hh
                    P.mm(ps2[:, hh * 256:(hh + 1) * 256], ktm[m][:, h * 128:(h + 1) * 128], v[m][:, h * 256:(h + 1) * 256])
                for hh in range(2):
                    h = hp * 2 + hh
                    hc = slice(h * 256, (h + 1) * 256)
                    P.act(S[m][:, hc], S[m][:, hc], AF.Copy, scale=eG[m][:, h:h + 1])
                    P.stt(S[m][:, hc], ps2[:, hh * 256:(hh + 1) * 256], eG[m][:, h:h + 1], S[m][:, hc], ALU.mult, ALU.add)
            if do_y:
                P.copy(SB[m].v(), S[m].v(), eng="act")
            if Dt is not None:
                P.tt(Dt[m].v(), Dt[m].v(), eG[m].v(), ALU.mult)

        if do_y:
            vdt = B.vdt[0]
            vdt3 = r3(vdt.v(), "p (h e) -> p h e", e=64)
            P.tt(vdt3, x3, bc(dt_d, 2, [128, 32, 64]), ALU.mult)
            cb = P.next_ps()
            for g in range(4):
                P.mm(cb[:, g * 128:(g + 1) * 128], BT[:, g, :], CT[:, g, :])
            cbm = B.cbm[0]
            P.tt(cbm.v(), r3(cb.v(), "p (g t) -> p g t", g=4), bc(msk_f32, 1, [128, 4, 128]), ALU.mult)
            rhsD = B.rhsD[0]
            P.tt(rhsD.v(), bc(msk_bf, 1, [128, 32, 128]), bc(a_d, 2, [128, 32, 128]), ALU.mult)
            retgla_y(0)
            att = B.att[0]
            for g in range(4):
                for half in range(2):
                    dps = P.next_ps()
                    hsl = slice(g * 8 + half * 4, g * 8 + half * 4 + 4)
                    P.mm(dps.v(), sl_bf, r3(rhsD[:, hsl, :], "p h t -> p (h t)"))
                    L = B.L[(g * 2 + half) % 4]
                    P.act(r3(L.v(), "p h t -> p (h t)"), dps.v(), AF.Exp)
                    P.tt(att[:, hsl, :], L.v(), bc(cbm[:, g, :], 1, [128, 4, 128]), ALU.mult)
                if g == 0:
                    retgla_state(0)
                elif g == 1:
                    retgla_y(1)
                elif g == 2:
                    retgla_state(1)
            for g in range(4):
                yp = P.next_ps()
                cs = slice(2048 + g * 512, 2048 + (g + 1) * 512)
                if post:
                    P.mm(yp.v(), ident, yb[:, cs], start=True, stop=False)
                for h in range(8):
                    hs = slice(h * 64, (h + 1) * 64)
                    if post:
                        P.mm(yp[:, hs], att[:, g * 8 + h, :], vdt3[:, g * 8 + h, :], start=False, stop=False)
                        P.mm(yp[:, hs], lay["dI"][:, g * 8 + h, :], x3[:, g * 8 + h, :], start=False, stop=(h == 7))
                    else:
                        P.mm(yp[:, hs], att[:, g * 8 + h, :], vdt3[:, g * 8 + h, :])
                yi = P.next_ps()
                P.mm(yi.v(), CT[:, g, :], SB[2][:, g * 512:(g + 1) * 512])
                tmp = B.tmp[g % 2]
                P.tt(r3(tmp.v(), "p (h e) -> p h e", e=64), r3(yi.v(), "p (h e) -> p h e", e=64),
                     bc(eT[:, 64 + g * 8:64 + (g + 1) * 8], 2, [128, 8, 64]), ALU.mult)
                P.tt(yall[:, cs], tmp.v(), yp.v(), ALU.add)
        else:
            retgla_state(0)
            retgla_state(1)
        P.tt(r3(S[2].v(), "p (h e) -> p h e", e=64), r3(S[2].v(), "p (h e) -> p h e", e=64),
             bc(eT[:, 0:32], 2, [128, 32, 64]), ALU.mult)
        for g in range(4):
            ps = P.next_ps()
            P.mm(ps.v(), btm[:, g * 128:(g + 1) * 128], vs[:, g * 512:(g + 1) * 512])
            gc = slice(g * 512, (g + 1) * 512)
            P.tt(S[2][:, gc], S[2][:, gc], ps.v(), ALU.add)
        if do_y:
            P.copy(SB[2].v(), S[2].v(), eng="act")

    def back(it, c):
        par = it % 2
        t0 = c * 128
        yall = B.yall[par]
        if mode == "yb":
            ybo = B.yb[0]
            P.copy(ybo[:, 0:2048], yall[:, 0:2048], eng="act")
            P.copy(ybo[:, 2048:4096], yall[:, 2048:4096], eng="dve")
            P.dma(sp["yb"][t0:t0 + 128, :], ybo.v())
            return
        gate = B.gate[par]
        ss, junk, ygate = B.ss[par], B.junk[0], B.ygate[0]
        P.tt(yall[:, 2048:4096], yall[:, 2048:4096], gate[:, 2048:4096], ALU.mult)
        for h in range(8):
            P.act(junk[:, 0:256], yall[:, h * 256:(h + 1) * 256], AF.Square, accum=ss[:, h:h + 1])
        for g in range(4):
            P.act(junk[:, 0:512], yall[:, 2048 + g * 512:2048 + (g + 1) * 512], AF.Square, accum=ss[:, 8 + g:9 + g])
        P.ts(ss[:, 0:8], ss[:, 0:8], 1.0 / 256, EPS, ALU.mult, ALU.add)
        P.ts(ss[:, 8:12], ss[:, 8:12], 1.0 / 512, EPS, ALU.mult, ALU.add)
        P.act(ss.v(), ss.v(), AF.Ln)
        P.act(ss.v(), ss.v(), AF.Exp, scale=-0.5)
        for h in range(4):
            hc = slice(h * 256, (h + 1) * 256)
            P.stt(ygate[:, hc], yall[:, hc], ss[:, h:h + 1], gate[:, hc], ALU.mult, ALU.mult)
        for h in range(4):
            hc = slice(1024 + h * 256, 1024 + (h + 1) * 256)
            P.stt(junk[:, 256:512], yall[:, hc], ss[:, 4 + h:5 + h], gate[:, hc], ALU.mult, ALU.mult)
            P.tt(ygate[:, hc], junk[:, 256:512], lay["gla_g"].v(), ALU.mult)
        for g in range(4):
            gc = slice(2048 + g * 512, 2048 + (g + 1) * 512)
            P.stt(ygate[:, gc], yall[:, gc], ss[:, 8 + g:9 + g], lay["ssd_ng"][:, g * 512:(g + 1) * 512], ALU.mult, ALU.mult)
        ygT = B.ygT[0]
        for q in range(4):
            pv = bfview(P.next_ps())
            for k8 in range(8):
                k = q * 8 + k8
                P.tr(pv[:, k8 * 128:(k8 + 1) * 128], ygate[:, k * 128:(k + 1) * 128], ident)
            P.copy(r3(ygT[:, q * 8:(q + 1) * 8, :], "p k t -> p (k t)"), pv[:, 0:1024], eng=("act" if q % 2 else "dve"))
        P.dma(View(sp["ygT"], sp["ygT"].ap[:, :, t0:t0 + 128].rearrange("k p t -> p k t")), ygT.v())

    prev = None
    for it, c in enumerate(chain):
        front(it, c)
        if mode != "state":
            if prev is not None:
                back(*prev)
            prev = (it, c)
        yield it
    if prev is not None:
        back(*prev)


def phase_c(P, env, sp, X, spoff, gate5, wbL, woL, blks):
    with scope(P):
        yg_t = P.stack.enter_context(P.sb_ctx([128, 32, 512], BF16, "ygblk"))
        sg_t = P.stack.enter_context(P.sb_ctx([128, 24, 512], BF16, "sgblk"))
        wb_t = P.stack.enter_context(P.sb_ctx([128, 2, 32, 128], BF16, "wbf"))
        wo_t = P.stack.enter_context(P.sb_ctx([128, 2, 8, 128], BF16, "wof"))
        mg_t = P.stack.enter_context(P.sb_ctx([128, 8, 512], BF16, "mg"))
        t_t = P.stack.enter_context(P.sb_ctx([128, 3, 512], F32, "t123"))
        yg, sg, mg = Buf(yg_t[:]), Buf(sg_t[:]), Buf(mg_t[:])
        wbf = [Buf(wb_t[:, i]) for i in range(2)]
        wof = [Buf(wo_t[:, i]) for i in range(2)]
        t = [Buf(t_t[:, i]) for i in range(3)]
        nw = 0
        for b in blks:
            n = X.sizes[b]
            o = spoff[b]
            P.dma(yg[:, :, :n], View(sp["ygT"], sp["ygT"].ap[:, :, o:o + n].rearrange("k p t -> p k t")))
            P.dma(sg[:, :, :n], View(sp["sg"], sp["sg"].ap[:, :, o:o + n].rearrange("k p t -> p k t")))
            for f in range(KC):
                w = wbf[nw % 2]
                nw += 1
                P.dma(w.v(), wbL[f], q="pool")
                for i, (k0, k1) in enumerate(((0, 8), (8, 16), (16, 32))):
                    ps = P.next_ps()
                    for k in range(k0, k1):
                        P.mm(ps[:, :n], w[:, k, :], yg[:, k, :n], start=(k == k0), stop=(k == k1 - 1))
                    P.tt(t[i][:, :n], ps[:, :n], sg[:, i * 8 + f, :n], ALU.mult, eng=("dve" if i != 1 else "pool") if False else "dve")
                P.tt(t[0][:, :n], t[0][:, :n], t[1][:, :n], ALU.add)
                P.tt(mg[:, f, :n], t[0][:, :n], t[2][:, :n], ALU.add)
            hg = gate5[X.kinds[b]]
            for f in range(KC):
                w = wof[f % 2]
                P.dma(w.v(), woL[f], q="pool")
                ps = P.next_ps()
                for k in range(KC):
                    P.mm(ps[:, :n], w[:, k, :], mg[:, k, :n], start=(k == 0), stop=(k == KC - 1))
                P.stt(X.b[b][:, f, :], ps[:, :n], hg[:, f:f + 1], X.b[b][:, f, :], ALU.mult, ALU.add)


def load_small(P, ins, names):
    lay = {}
    for nm in names:
        b = ins[nm]
        shape = list(b.ap.shape)
        t = P.sb(shape, F32, "l_" + nm)
        P.dma(t.v(), b.v())
        lay[nm] = t
    return lay


def derive_layer(P, env, lay, need_post):
    lg = P.sb([128, 8], F32, "lg")
    t1 = P.sb([128, 8], F32, "lgt1")
    t2 = P.sb([128, 8], F32, "lgt2")
    P.act(t1.v(), lay["ret_logit"].v(), AF.Abs)
    P.act(t1.v(), t1.v(), AF.Exp, scale=-1.0)
    P.act(t1.v(), t1.v(), AF.Ln, bias=1.0)
    P.ts(t2.v(), lay["ret_logit"].v(), 0.0, None, ALU.min)
    P.tt(lg.v(), t2.v(), t1.v(), ALU.subtract)
    nlg = P.sb([128, 8], F32, "nlg")
    P.ts(nlg.v(), lg.v(), -1.0, None, ALU.mult)
    tab = P.sb([128, 2, 2, 4, 128], F32, "ret_tab")
    for d in range(2):
        row = env.row_up if d == 0 else env.row_dn
        for h in range(4):
            P.act(tab[:, d, 0, h, :], row, AF.Exp, scale=lg[:, d * 4 + h:d * 4 + h + 1])
            P.act(tab[:, d, 1, h, :], row, AF.Exp, scale=nlg[:, d * 4 + h:d * 4 + h + 1])
    lay["ret_tab"] = tab
    eg = P.sb([128, 2, 4], F32, "ret_eg")
    P.act(eg.v().m(lambda a: a.rearrange("p d h -> p (d h)")), lg.v(), AF.Exp, scale=128.0)
    lay["ret_eg"] = eg
    wa2b = P.sb([16, 2, 512], BF16, "wa2b")
    P.copy(wa2b.v(), lay["wa2"].v())
    lay["wa2"] = wa2b
    A = P.sb([128, 64], F32, "ssdA")
    P.act(A.v(), lay["a_log"].v(), AF.Exp)
    P.ts(A.v(), A.v(), -1.0, None, ALU.mult)
    lay["ssd_A"] = A


SMALL_COMMON = ["mod_lat", "mod_ctx", "norm_g", "ret_logit", "wa2", "gla_ba", "conv_w", "conv_b", "dt_bias", "a_log",
                "halo_valid"]
SMALL_SHAPES = {"mod_lat": [128, 9, 8], "mod_ctx": [128, 9, 8], "norm_g": [128, 3, 8], "ret_logit": [128, 8],
                "wa2": [16, 2, 512], "gla_ba": [128, 2, 4], "conv_w": [128, 5, 24], "conv_b": [128, 24],
                "dt_bias": [128, 64], "a_log": [128, 64], "halo_valid": [128, 2],
                "gla_g": [128, 256], "ssd_d": [128, 32], "ssd_ng": [128, 2048], "final_g": [128, 8]}
SIZES = [512, 512, 512, 512, TC, 4]
KINDS = ["lat", "lat", "lat", "lat", "ctx", "halo"]
TIN = TL + TC + 4


def make_spill(P, full):
    sp = {}
    sp["qk"] = P.dram("sp_qk", [2, 4, 4, 128, TT], BF16)
    sp["eG"] = P.dram("sp_eG", [4, 128, NCH, 4], F32)
    sp["v"] = P.dram("sp_v", [2, TT, 1024], BF16)
    sp["BT"] = P.dram("sp_BT", [4, 128, TT], BF16)
    sp["x"] = P.dram("sp_x", [TT, 2048], BF16)
    sp["a"] = P.dram("sp_a", [TT, 64], F32)
    sp["dt"] = P.dram("sp_dt", [TT, 64], F32)
    if full:
        sp["CT"] = P.dram("sp_CT", [4, 128, TT], BF16)
        sp["gate"] = P.dram("sp_gate", [TT, 4096], BF16)
        sp["sg"] = P.dram("sp_sg", [24, 128, TT], BF16)
        sp["yb"] = P.dram("sp_yb", [TT, 4096], BF16)
        sp["ygT"] = P.dram("sp_ygT", [32, 128, TT], BF16)
    return sp


def alloc_states(P):
    S = [P.sb([128, 1024], F32, "S_ret"), P.sb([128, 1024], F32, "S_gla"), P.sb([128, 2048], F32, "S_ssd")]
    SB = [P.sb([128, 1024], BF16, "SB_ret"), P.sb([128, 1024], BF16, "SB_gla"), P.sb([128, 2048], BF16, "SB_ssd")]
    return S, SB


def zero_states(P, S, SB):
    for i in range(3):
        P.memset(S[i].v(), 0.0, eng="pool")
        P.memset(SB[i].v(), 0.0, eng="pool")


def build_layer(phase, last=False):
    nc = bass.Bass("TRN2", target_bir_lowering=False)
    with contextlib.ExitStack() as stack:
        P = Prog(nc, stack)
        P.psum_banks()
        env = Env()
        ins = {}

        def inp(name, shape, dt=F32):
            ins[name] = P.dram(name, shape, dt, "ExternalInput")
            return ins[name]
        inp("cin", [128, 1024])
        inp("xin", [D, TIN])
        for nm in SMALL_COMMON:
            inp(nm, SMALL_SHAPES[nm])
        inp("rope_in", [128, 2, TL])
        inp("w_inL", [128, KC, DIN])
        if phase == 1:
            inp("wiL", [NFF, 128, KC, 256])
            inp("woL", [128, NFF, D])
            x1out = P.dram("x1T", [D, TT], F32, "ExternalOutput")
            Eout = P.dram("Eout", [2, 128, 4096], F32, "ExternalOutput")
            Dout = P.dram("Dout", [2, 128, 40], F32, "ExternalOutput")
        else:
            for nm in ("gla_g", "ssd_d", "ssd_ng", "final_g"):
                inp(nm, SMALL_SHAPES[nm])
            inp("wiL", [NFF, 128, KC, 256])
            inp("woL", [128, NFF, D])
            inp("wbL", [KC, 128, 32, 128])
            inp("w_outL", [KC, 128, KC, 128])
            inp("slotE", [2, 7, 128, 4096])
            inp("slotD", [2, 7, 128, 40])
            xout = P.dram("xoutT", [D, TT], F32, "ExternalOutput")
        load_consts(P, env, ins["cin"])
        lay = load_small(P, ins, SMALL_COMMON + ([] if phase == 1 else ["gla_g", "ssd_d", "ssd_ng", "final_g"]))
        lay["rope_in"] = ins["rope_in"]
        derive_layer(P, env, lay, phase == 2)
        modT = {"lat": lay["mod_lat"], "ctx": lay["mod_ctx"]}
        sp = make_spill(P, phase == 2)
        xv = ins["xin"].ap.rearrange("(k p) t -> p k t", p=128)
        offs = np.concatenate([[0], np.cumsum(SIZES)]).astype(int).tolist()
        spoff = {b: offs[b] for b in range(5)}

        def wi_view(buf):
            return buf

        with scope(P):
            H = Blocks(P, SIZES, KINDS, "hT", BF16)
            scr = (P.sb([128, KC, 512], BF16, "sq"), P.sb([128, KC, 512], F32, "tmpn"), P.sb([128, 512], F32, "rstd"))
            if phase == 1:
                with scope(P):
                    X = Blocks(P, SIZES, KINDS, "xTs")
                    for b in range(6):
                        P.dma(X.b[b].v(), View(ins["xin"], xv[:, :, offs[b]:offs[b] + SIZES[b]]))
                    m0 = prep_mod(P, modT, lay["norm_g"], 0)
                    m0["halo"] = m0["lat"]
                    for b in range(6):
                        gs, sh, hg = m0[KINDS[b]]
                        norm_mod(P, X.b[b], H.b[b], SIZES[b], gs, sh, env.ones_bf, scr)
                    ffn_tiled(P, X, H, m0, ins["wiL"], ins["woL"])
                    ov = x1out.ap.rearrange("(k p) t -> p k t", p=128)
                    for b in range(5):
                        P.dma(View(x1out, ov[:, :, offs[b]:offs[b] + SIZES[b]]), X.b[b].v())
                    m1 = prep_mod(P, modT, lay["norm_g"], 1)
                    m1["halo"] = m1["lat"]
                    for b in range(6):
                        gs, sh, hg = m1[KINDS[b]]
                        norm_mod(P, X.b[b], H.b[b], SIZES[b], gs, sh, env.ones_bf, scr)
            else:
                with scope(P):
                    xt_t = P.stack.enter_context(P.sb_ctx([128, 2, KC, 512], F32, "xtmp"))
                    xt = [Buf(xt_t[:, i]) for i in range(2)]
                    m1 = prep_mod(P, modT, lay["norm_g"], 1)
                    m1["halo"] = m1["lat"]
                    for b in range(6):
                        n = SIZES[b]
                        xb = xt[b % 2]
                        P.dma(xb[:, :, :n], View(ins["xin"], xv[:, :, offs[b]:offs[b] + n]))
                        gs, sh, hg = m1[KINDS[b]]
                        norm_mod(P, _sub(xb, n), H.b[b], n, gs, sh, env.ones_bf, scr)
            groups = ["ret", "gla", "v", "dt", "xB"] if phase == 1 else ["ret", "gla", "v", "gates", "dt", "xBC", "merge"]
            lay["halo_valid"] = lay["halo_valid"]
            inproj(P, env, H, ins["w_inL"], sp, lay, groups, do_q=(phase == 2), skip_ctx=(phase == 1))
        with scope(P):
            S, SB = alloc_states(P)
            if phase == 1:
                S2, SB2 = alloc_states(P)
                Ss, SBs = [S, S2], [SB, SB2]
                Bs = [SweepBufs(P, False, False), SweepBufs(P, False, False)]
                Dts = [[P.sb([128, 4], F32, "Dt0"), P.sb([128, 4], F32, "Dt1"), P.sb([128, 32], F32, "Dt2")] for _ in range(2)]
                gens = []
                for d in range(2):
                    zero_states(P, Ss[d], SBs[d])
                    P.memset(Dts[d][0].v(), 1.0)
                    P.memset(Dts[d][1].v(), 1.0)
                    P.memset(Dts[d][2].v(), 0.0)
                    chain = list(range(NCH_L)) if d == 0 else list(range(NCH_L - 1, -1, -1))
                    gens.append(sweep(P, env, sp, lay, d, chain, Ss[d], SBs[d], False, Bs[d], "state", Dts[d]))
                for _ in zip(*gens):
                    pass
                for g_ in gens:
                    for _ in g_:
                        pass
                for d in range(2):
                    S, SB, Dt = Ss[d], SBs[d], Dts[d]
                    P.dma(Eout[d, :, 0:1024], S[0].v())
                    P.dma(Eout[d, :, 1024:2048], S[1].v())
                    P.dma(Eout[d, :, 2048:4096], S[2].v())
                    dd = P.sb([128, 40], F32, "dd")
                    P.copy(dd[:, 0:4], Dt[0].v())
                    P.copy(dd[:, 4:8], Dt[1].v())
                    P.act(dd[:, 8:40], Dt[2].v(), AF.Exp)
                    P.dma(Dout[d], dd.v())
            else:
                B = SweepBufs(P, True, True)
                dI = P.sb([128, 32, 128], BF16, "dI")
                for h in range(32):
                    P.ts(dI[:, h, :], env.cm("ident", bf=False), lay["ssd_d"][:, h:h + 1], None, ALU.mult)
                lay["dI"] = dI
                et = B.yall[0]
                dtile = P.sb([128, 40], F32, "slotD_t")
                for d in (1, 0):
                    zero_states(P, S, SB)
                    cchain = [NCH_L, NCH_L + 1] if d == 0 else [NCH_L + 1, NCH_L]
                    if last:
                        for _ in sweep(P, env, sp, lay, d, cchain, S, SB, False, B, "state"):
                            pass
                    else:
                        for _ in sweep(P, env, sp, lay, d, cchain, S, SB, True, B, "post" if d == 0 else "yb"):
                            pass
                    for s in range(7):
                        P.dma(et.v(), ins["slotE"][d, s])
                        P.dma(dtile.v(), ins["slotD"][d, s])
                        for h in range(4):
                            hc = slice(h * 256, (h + 1) * 256)
                            P.stt(S[0][:, hc], S[0][:, hc], dtile[:, h:h + 1], et[:, hc], ALU.mult, ALU.add)
                            hc2 = slice(1024 + h * 256, 1024 + (h + 1) * 256)
                            P.stt(S[1][:, hc], S[1][:, hc], dtile[:, 4 + h:5 + h], et[:, hc2], ALU.mult, ALU.add)
                        for h in range(32):
                            hc = slice(h * 64, (h + 1) * 64)
                            hc2 = slice(2048 + h * 64, 2048 + (h + 1) * 64)
                            P.stt(S[2][:, hc], S[2][:, hc], dtile[:, 8 + h:9 + h], et[:, hc2], ALU.mult, ALU.add)
                    for i in range(3):
                        P.copy(SB[i].v(), S[i].v(), eng="act")
                    chain = list(range(NCH_L)) if d == 0 else list(range(NCH_L - 1, -1, -1))
                    for _ in sweep(P, env, sp, lay, d, chain, S, SB, True, B, "post" if d == 0 else "yb"):
                        pass
        if phase == 2:
            with scope(P):
                X = Blocks(P, SIZES[:5], KINDS[:5], "xTs")
                for b in range(5):
                    P.dma(X.b[b].v(), View(ins["xin"], xv[:, :, offs[b]:offs[b] + SIZES[b]]))
                g5 = {}
                for kind in ("lat", "ctx"):
                    g5[kind] = modT[kind][:, 5, :]
                blks = [0, 1, 2, 3] + ([] if last else [4])
                phase_c(P, env, sp, X, spoff, g5, ins["wbL"], ins["w_outL"], blks)
                with scope(P):
                    H2 = Blocks(P, SIZES[:5], KINDS[:5], "h2T", BF16)
                    scr = (P.sb([128, KC, 512], BF16, "sq"), P.sb([128, KC, 512], F32, "tmpn"), P.sb([128, 512], F32, "rstd"))
                    m2 = prep_mod(P, modT, lay["norm_g"], 2)
                    for b in range(5):
                        gs, sh, hg = m2[KINDS[b]]
                        norm_mod(P, X.b[b], H2.b[b], SIZES[b], gs, sh, env.ones_bf, scr)
                    ffn_tiled(P, X, H2, m2, ins["wiL"], ins["woL"], G=4)
                ov = xout.ap.rearrange("(k p) t -> p k t", p=128)
                if last:
                    with scope(P):
                        scr = (P.sb([128, KC, 512], BF16, "sq"), P.sb([128, KC, 512], F32, "tmpn"), P.sb([128, 512], F32, "rstd"))
                        zt = P.sb([128, KC], F32, "zeros8")
                        P.memset(zt.v(), 0.0)
                        o_t = P.stack.enter_context(P.sb_ctx([128, 2, KC, 512], F32, "fo"))
                        ob = [Buf(o_t[:, i]) for i in range(2)]
                        for b in range(4):
                            norm_mod(P, X.b[b], ob[b % 2], 512, lay["final_g"], zt, env.ones_bf, scr)
                            P.dma(View(xout, ov[:, :, offs[b]:offs[b] + 512]), ob[b % 2].v())
                        P.dma(View(xout, ov[:, :, offs[4]:offs[4] + TC]), X.b[4].v())
                else:
                    for b in range(5):
                        P.dma(View(xout, ov[:, :, offs[b]:offs[b] + SIZES[b]]), X.b[b].v())
        P.emit()
    return nc


class _sub:
    def __init__(self, buf, n):
        self.buf = buf
        self.n = n

    def v(self):
        return View(self.buf, self.buf.ap[:, :, :self.n])

    def __getitem__(self, idx):
        return View(self.buf, self.buf.ap[:, :, :self.n][idx])


def ffn_tiled(P, X, H, mods, wiL, woL, G=6):
    nblk = len(X.sizes)
    T = X.T
    with scope(P):
        actT_t = P.stack.enter_context(P.sb_ctx([128, G, T], BF16, "actT"))
        wi_t = P.stack.enter_context(P.sb_ctx([128, 2, KC, 256], BF16, "wi_t"))
        wo_t = P.stack.enter_context(P.sb_ctx([128, G, D], BF16, "wo_t"))
        sil_t = P.stack.enter_context(P.sb_ctx([128, 2, 512], F32, "sil"))
        actb = [[Buf(actT_t[:, g, X.offs[b]:X.offs[b] + X.sizes[b]]) for b in range(nblk)] for g in range(G)]
        wib = [Buf(wi_t[:, i]) for i in range(2)]
        wob = Buf(wo_t[:])
        silb = [Buf(sil_t[:, i]) for i in range(2)]
        ns = 0
        nw = 0
        for g0 in range(0, NFF, G):
            gn = min(G, NFF - g0)
            for gi in range(gn):
                j = g0 + gi
                w = wib[nw % 2]
                nw += 1
                P.dma(w.v(), wiL[j], q="pool")
                for b in range(nblk):
                    n = X.sizes[b]
                    pa = P.next_ps()
                    pu = P.next_ps()
                    for k in range(KC):
                        P.mm(pa[:, :n], w[:, k, 0:128], H.b[b][:, k, :], start=(k == 0), stop=(k == KC - 1))
                    for k in range(KC):
                        P.mm(pu[:, :n], w[:, k, 128:256], H.b[b][:, k, :], start=(k == 0), stop=(k == KC - 1))
                    s = silb[ns % 2]
                    ns += 1
                    P.act(s[:, :n], pa[:, :n], AF.Silu)
                    P.tt(actb[gi][b].v(), s[:, :n], pu[:, :n], ALU.mult)
            P.dma(wob[:, 0:gn, :], woL[:, g0:g0 + gn, :], q="pool")
            for b in range(nblk):
                n = X.sizes[b]
                hg = mods[X.kinds[b]][2]
                for f in range(KC):
                    ps = P.next_ps()
                    for gi in range(gn):
                        P.mm(ps[:, :n], wob[:, gi, f * 128:(f + 1) * 128], actb[gi][b].v(), start=(gi == 0), stop=(gi == gn - 1))
                    P.stt(X.b[b][:, f, :], ps[:, :n], hg[:, f:f + 1], X.b[b][:, f, :], ALU.mult, ALU.add)


def build_p0():
    nc = bass.Bass("TRN2", target_bir_lowering=False)
    NCOL = 4608
    with contextlib.ExitStack() as stack:
        P = Prog(nc, stack)
        P.psum_banks()
        cc = P.dram("ccT", [128, KC, 2], F32, "ExternalInput")
        W = P.dram("adaW", [128, KC, NCOL], F32, "ExternalInput")
        bias = P.dram("adab", [2, NCOL], F32, "ExternalInput")
        out = P.dram("modout", [2, NCOL], F32, "ExternalOutput")
        s = P.sb([128, KC, 2], F32, "s")
        P.dma(s.v(), cc.v())
        P.act(s.v(), s.v(), AF.Silu)
        bt = P.sb([2, NCOL], F32, "bt")
        P.dma(bt.v(), bias.v())
        ot = P.sb([2, NCOL], F32, "ot")
        w_t = P.stack.enter_context(P.sb_ctx([128, 2, KC, 512], F32, "w0"))
        wb = [Buf(w_t[:, i]) for i in range(2)]
        for j in range(NCOL // 512):
            w = wb[j % 2]
            P.dma(w.v(), W[:, :, j * 512:(j + 1) * 512])
            ps = P.next_ps()
            for k in range(KC):
                P.mm(ps[0:2, :], s[:, k, :], w[:, k, :], start=(k == 0), stop=(k == KC - 1))
            P.tt(ot[:, j * 512:(j + 1) * 512], ps[0:2, :], bt[:, j * 512:(j + 1) * 512], ALU.add)
        P.dma(out.v(), ot.v())
        P.emit()
    return nc


_PROGS = {}


def get_prog(key):
    if key not in _PROGS:
        if key == "p0":
            _PROGS[key] = build_p0()
        elif key == "p1":
            _PROGS[key] = build_layer(1)
        elif key == "p2":
            _PROGS[key] = build_layer(2, last=False)
        else:
            _PROGS[key] = build_layer(2, last=True)
    return _PROGS[key]


def rep(v, n=128):
    v = np.asarray(v, np.float32).reshape(1, -1)
    return np.ascontiguousarray(np.broadcast_to(v, (n, v.shape[1])))


def rope_tables(core):
    t = core * TL + np.arange(TL)
    row = (t // 64).astype(np.float32)
    col = (t % 64).astype(np.float32)
    nf = 32
    freq = (np.float32(10000.0) ** (-np.arange(nf, dtype=np.float32) / np.float32(nf))).astype(np.float32)
    ang = np.concatenate([row[:, None] * freq[None, :], col[:, None] * freq[None, :]], axis=1).astype(np.float32)
    cos = np.cos(ang).astype(np.float32).T
    sin = np.sin(ang).astype(np.float32).T
    cos_tab = np.concatenate([cos, cos], axis=0)
    sin_tab = np.concatenate([-sin, sin], axis=0)
    return np.ascontiguousarray(np.stack([cos_tab, sin_tab], axis=1))


def run_model(inp, ncores, depth=4, run=None):
    from concourse.bass_utils import run_bass_kernel_spmd
    f32 = np.float32
    x = np.asarray(inp["x"], f32)[0]
    ctx = np.asarray(inp["ctx"], f32)[0]
    assert x.shape[0] == ncores * TL
    consts = host_consts()
    cc = np.stack([np.asarray(inp["c"], f32)[0], np.asarray(inp["c_ctx"], f32)], axis=1)
    ccT = np.ascontiguousarray(cc.reshape(KC, 128, 2).transpose(1, 0, 2))
    in_maps = []
    for c in range(8):
        l, half = (c // 2) % depth, c % 2
        Wl = np.asarray(inp["ada_w"][l], f32)[:, half * 4608:(half + 1) * 4608]
        in_maps.append({"ccT": ccT,
                        "adaW": np.ascontiguousarray(Wl.reshape(KC, 128, 4608).transpose(1, 0, 2)),
                        "adab": rep(np.asarray(inp["ada_b"][l], f32)[half * 4608:(half + 1) * 4608], 2)})
    res0 = run_bass_kernel_spmd(get_prog("p0"), in_maps, core_ids=list(range(8))).results
    mods = []
    for l in range(depth):
        m = np.concatenate([res0[2 * l]["modout"], res0[2 * l + 1]["modout"]], axis=1)
        mods.append([np.ascontiguousarray(m[s].reshape(9, KC, 128).transpose(2, 0, 1)) for s in range(2)])
    xT = [np.ascontiguousarray(x[c * TL:(c + 1) * TL].T) for c in range(ncores)]
    ctxT = np.ascontiguousarray(ctx.T)
    ropes = [rope_tables(c) for c in range(ncores)]
    zeros2 = np.zeros((D, 2), f32)

    def halos(xs):
        hs, vs = [], []
        for c in range(ncores):
            left = xs[c - 1][:, -2:] if c > 0 else zeros2
            right = xs[c + 1][:, :2] if c < ncores - 1 else zeros2
            hs.append(np.concatenate([left, right], axis=1))
            vs.append(rep(np.array([1.0 if c > 0 else 0.0, 1.0 if c < ncores - 1 else 0.0], f32)))
        return hs, vs

    for l in range(depth):
        last = l == depth - 1
        g = lambda k: np.asarray(inp[k][l], f32)
        common = {
            "cin": consts,
            "mod_lat": mods[l][0], "mod_ctx": mods[l][1],
            "norm_g": np.ascontiguousarray(g("norm_g").reshape(3, KC, 128).transpose(2, 0, 1)),
            "ret_logit": rep(g("ret_logit").reshape(-1)),
            "wa2": np.ascontiguousarray(g("gla_wa2").transpose(1, 0, 2)),
            "gla_ba": np.ascontiguousarray(g("gla_ba").reshape(2, 4, 128).transpose(2, 0, 1)),
            "conv_w": np.ascontiguousarray(g("conv_w").reshape(5, 24, 128).transpose(2, 0, 1)),
            "conv_b": np.ascontiguousarray(g("conv_b").reshape(24, 128).T),
            "dt_bias": rep(g("dt_bias").reshape(-1)),
            "a_log": rep(g("a_log").reshape(-1)),
            "w_inL": np.ascontiguousarray(g("w_in").reshape(KC, 128, DIN).transpose(1, 0, 2)),
        }

        def tile_ffn(wi, wo):
            wr = wi.reshape(KC, 128, 2 * DFF)
            a = wr[:, :, :DFF].reshape(KC, 128, NFF, 128).transpose(2, 1, 0, 3)
            u = wr[:, :, DFF:].reshape(KC, 128, NFF, 128).transpose(2, 1, 0, 3)
            return (np.ascontiguousarray(np.concatenate([a, u], axis=3)),
                    np.ascontiguousarray(wo.reshape(NFF, 128, D).transpose(1, 0, 2)))
        wiL, woL = tile_ffn(g("ffn1_wi"), g("ffn1_wo"))
        hs, vs = halos(xT)
        in_maps = []
        for c in range(ncores):
            m = dict(common)
            m.update({"xin": np.ascontiguousarray(np.concatenate([xT[c], ctxT, hs[c]], axis=1)),
                      "halo_valid": vs[c], "rope_in": ropes[c], "wiL": wiL, "woL": woL})
            in_maps.append(m)
        res1 = run_bass_kernel_spmd(get_prog("p1"), in_maps, core_ids=list(range(ncores))).results
        if run is not None:
            run["res1"] = res1
        x1 = [r["x1T"][:, :TL] for r in res1]
        ctx1 = res1[0]["x1T"][:, TL:]
        E = [r["Eout"] for r in res1]
        Dd = [r["Dout"] for r in res1]
        wiL, woL = tile_ffn(g("ffn2_wi"), g("ffn2_wo"))
        wb_all = np.concatenate([g("wb_ret"), g("wb_gla"), g("wb_ssd")], axis=0)
        wbL = np.ascontiguousarray(wb_all.reshape(32, 128, KC, 128).transpose(2, 1, 0, 3))
        w_outL = np.ascontiguousarray(g("w_out").reshape(KC, 128, KC, 128).transpose(2, 1, 0, 3))
        hs, vs = halos(x1)
        in_maps = []
        for c in range(ncores):
            slotE = np.zeros((2, 7, 128, 4096), f32)
            slotD = np.ones((2, 7, 128, 40), f32)
            for s, src in enumerate(range(0, c)):
                slotE[0, s] = E[src][0]
                slotD[0, s] = Dd[src][0]
            for s, src in enumerate(range(ncores - 1, c, -1)):
                slotE[1, s] = E[src][1]
                slotD[1, s] = Dd[src][1]
            m = dict(common)
            m.update({"xin": np.ascontiguousarray(np.concatenate([x1[c], ctx1, hs[c]], axis=1)),
                      "halo_valid": vs[c], "rope_in": ropes[c], "wiL": wiL, "woL": woL, "wbL": wbL, "w_outL": w_outL,
                      "gla_g": rep(g("gla_norm_g")), "ssd_d": rep(g("ssd_d")), "ssd_ng": rep(g("ssd_norm_g")),
                      "final_g": fm_vec(np.asarray(inp["final_norm_g"], f32)),
                      "slotE": slotE, "slotD": slotD})
            in_maps.append(m)
        res2 = run_bass_kernel_spmd(get_prog("p2last" if last else "p2"), in_maps, core_ids=list(range(ncores))).results
        if run is not None:
            run["res2"] = res2
        xT = [r["xoutT"][:, :TL] for r in res2]
        ctxT = np.ascontiguousarray(res2[0]["xoutT"][:, TL:])
    out = np.concatenate([t.T for t in xT], axis=0)[None]
    return np.ascontiguousarray(out.astype(np.float32))


def kernel(**inputs):
    return run_model(inputs, 8)
```
